# Optimizing a Trainium2 kernel written in Bass

```python
import jax, jax.numpy as jnp
from jax import lax
import numpy as np

D_MODEL = 1024
BATCH = 4
SEQ = 4096
DEPTH = 2

MLA_HEADS = 8
MLA_Q_RANK = 256
MLA_KV_RANK = 128
MLA_NOPE = 64
MLA_ROPE = 32
MLA_V = 64
MOBA_HEADS = 8
MOBA_HEAD_DIM = 64
MOBA_BLOCK = 256
MOBA_TOPK = 3
MOBA_Q_CHUNK = 32
RET_HEADS = 4
RET_QK_DIM = 64
RET_V_DIM = 128
RET_CHUNK = 128
RET_THETA = 10000.0
ROPE_THETA = 500000.0
PARTIAL_ROPE_DIV = 4
ATTN_Q_BLOCK = 128
D_FF = 4 * D_MODEL
NORM_EPS = 1e-6
GN_EPS = 1e-5
NEG_INF = -1e30

MOBA_W = MOBA_HEADS * MOBA_HEAD_DIM
RET_QK_W = RET_HEADS * RET_QK_DIM
RET_V_W = RET_HEADS * RET_V_DIM
MLA_OUT_W = MLA_HEADS * MLA_V
IN_SPLIT_SIZES = (MLA_Q_RANK, MLA_KV_RANK, MLA_ROPE,
                  MOBA_W, MOBA_W, MOBA_W,
                  RET_QK_W, RET_QK_W, RET_V_W, RET_V_W,
                  D_MODEL, D_MODEL, D_MODEL)
IN_COLS = sum(IN_SPLIT_SIZES)
IN_SPLIT_IDX = tuple(int(i) for i in np.cumsum(IN_SPLIT_SIZES)[:-1])

kernel_name = "hybrid_mla_moba_retention_block"


def rms_norm(x, g, eps=NORM_EPS):
    xf = x.astype(jnp.float32)
    y = xf * lax.rsqrt(jnp.mean(xf * xf, axis=-1, keepdims=True) + eps)
    return (y * g.astype(jnp.float32)).astype(x.dtype)


def rope_tables(seq, dim, theta):
    inv = 1.0 / (theta ** (jnp.arange(0, dim, 2, dtype=jnp.float32) / dim))
    ang = jnp.arange(seq, dtype=jnp.float32)[:, None] * inv[None, :]
    return jnp.cos(ang), jnp.sin(ang)


def apply_rope(x, cos, sin):
    half = x.shape[-1] // 2
    x1, x2 = x[..., :half], x[..., half:]
    c, s = cos.astype(x.dtype), sin.astype(x.dtype)
    return jnp.concatenate([x1 * c - x2 * s, x1 * s + x2 * c], axis=-1)


def causal_attention(q, k, v, scale):
    B, H, S, _ = q.shape
    kpos = jnp.arange(S)

    def block(i):
        start = i * ATTN_Q_BLOCK
        qb = lax.dynamic_slice_in_dim(q, start, ATTN_Q_BLOCK, axis=2)
        s = jnp.einsum('bhqd,bhkd->bhqk', qb, k).astype(jnp.float32) * scale
        qpos = start + jnp.arange(ATTN_Q_BLOCK)
        s = jnp.where(kpos[None, :] <= qpos[:, None], s, NEG_INF)
        p = jax.nn.softmax(s, axis=-1).astype(v.dtype)
        return jnp.einsum('bhqk,bhkd->bhqd', p, v)

    o = lax.map(block, jnp.arange(S // ATTN_Q_BLOCK))
    return o.transpose(1, 2, 0, 3, 4).reshape(B, H, S, v.shape[-1])


def mla_branch(cq, ckv, k_rope, q_norm_g, kv_norm_g, w_q_up, w_kv_up, cos, sin):
    B, S, _ = cq.shape
    cq = rms_norm(cq, q_norm_g)
    ckv = rms_norm(ckv, kv_norm_g)
    q = (cq @ w_q_up).reshape(B, S, MLA_HEADS, MLA_NOPE + MLA_ROPE).transpose(0, 2, 1, 3)
    q = jnp.concatenate([q[..., :MLA_NOPE], apply_rope(q[..., MLA_NOPE:], cos, sin)], axis=-1)
    kv = (ckv @ w_kv_up).reshape(B, S, MLA_HEADS, MLA_NOPE + MLA_V).transpose(0, 2, 1, 3)
    k_pe = apply_rope(k_rope, cos, sin)[:, None]
    k = jnp.concatenate([kv[..., :MLA_NOPE],
                         jnp.broadcast_to(k_pe, (B, MLA_HEADS, S, MLA_ROPE))], axis=-1)
    v = kv[..., MLA_NOPE:]
    o = causal_attention(q, k, v, (MLA_NOPE + MLA_ROPE) ** -0.5)
    return o.transpose(0, 2, 1, 3).reshape(B, S, MLA_OUT_W)


def moba_branch(q, k, v, cos, sin):
    B, S, _ = q.shape
    H, Dh, BS, QC = MOBA_HEADS, MOBA_HEAD_DIM, MOBA_BLOCK, MOBA_Q_CHUNK
    rd = Dh // PARTIAL_ROPE_DIV

    def heads(t):
        return t.reshape(B, S, H, Dh).transpose(0, 2, 1, 3)

    def partial_rope(t):
        return jnp.concatenate([apply_rope(t[..., :rd], cos, sin), t[..., rd:]], axis=-1)

    q, k, v = partial_rope(heads(q)), partial_rope(heads(k)), heads(v)
    nb = -(-S // BS)
    pad = nb * BS - S
    padw = ((0, 0), (0, 0), (0, pad), (0, 0))
    q, k, v = jnp.pad(q, padw), jnp.pad(k, padw), jnp.pad(v, padw)
    kb = k.reshape(B, H, nb, BS, Dh)
    vb = v.reshape(B, H, nb, BS, Dh)
    k_mean = jnp.mean(kb.astype(jnp.float32), axis=3)
    topk = min(MOBA_TOPK, nb)
    scale = Dh ** -0.5
    bi = jnp.arange(B)[:, None, None, None]
    hi = jnp.arange(H)[None, :, None, None]

    def chunk(c):
        start = c * QC
        blk = start // BS
        qc = lax.dynamic_slice_in_dim(q, start, QC, axis=2)
        gate = jnp.einsum('bhqd,bhnd->bhqn', qc.astype(jnp.float32), k_mean)
        gate = jnp.where(jnp.arange(nb) < blk, gate, NEG_INF)
        _, idx = lax.top_k(gate, topk)
        sel_valid = jnp.arange(topk) < blk
        k_sel = kb[bi, hi, idx]
        v_sel = vb[bi, hi, idx]
        s_sel = jnp.einsum('bhqd,bhqnkd->bhqnk', qc, k_sel).astype(jnp.float32) * scale
        s_sel = jnp.where(sel_valid[:, None], s_sel, NEG_INF).reshape(B, H, QC, topk * BS)
        k_own = lax.dynamic_index_in_dim(kb, blk, axis=2, keepdims=False)
        v_own = lax.dynamic_index_in_dim(vb, blk, axis=2, keepdims=False)
        s_own = jnp.einsum('bhqd,bhkd->bhqk', qc, k_own).astype(jnp.float32) * scale
        qpos = start + jnp.arange(QC)
        kpos = blk * BS + jnp.arange(BS)
        s_own = jnp.where(kpos[None, :] <= qpos[:, None], s_own, NEG_INF)
        p = jax.nn.softmax(jnp.concatenate([s_sel, s_own], axis=-1), axis=-1).astype(v.dtype)
        o_sel = jnp.einsum('bhqn,bhqnd->bhqd', p[..., :topk * BS],
                           v_sel.reshape(B, H, QC, topk * BS, Dh))
        o_own = jnp.einsum('bhqk,bhkd->bhqd', p[..., topk * BS:], v_own)
        return o_sel + o_own

    o = lax.map(chunk, jnp.arange(nb * BS // QC))
    o = o.transpose(1, 2, 0, 3, 4).reshape(B, H, nb * BS, Dh)[:, :, :S]
    return o.transpose(0, 2, 1, 3).reshape(B, S, MOBA_W)


def retention_branch(q, k, v, g, gn_w, gn_b, cos, sin):
    B, S, _ = q.shape
    H, Dk, Dv, C = RET_HEADS, RET_QK_DIM, RET_V_DIM, RET_CHUNK
    nc = S // C
    f32 = jnp.float32
    q = apply_rope(q.astype(f32).reshape(B, S, H, Dk).transpose(0, 2, 1, 3), cos, sin)
    k = apply_rope(k.astype(f32).reshape(B, S, H, Dk).transpose(0, 2, 1, 3), cos, sin) * (Dk ** -0.5)
    v = v.astype(f32).reshape(B, S, H, Dv).transpose(0, 2, 1, 3)
    log_gamma = jnp.log(1.0 - 2.0 ** (-5.0 - jnp.arange(H, dtype=f32)))
    pos = jnp.arange(C, dtype=f32)
    diff = pos[:, None] - pos[None, :]
    decay_in = jnp.where(diff >= 0, jnp.exp(log_gamma[:, None, None] * diff), 0.0)
    xi = jnp.exp(log_gamma[:, None] * (pos + 1.0))
    zeta = jnp.exp(log_gamma[:, None] * (C - 1.0 - pos))
    gamma_c = jnp.exp(log_gamma * C)[None, :, None, None]
    qc = q.reshape(B, H, nc, C, Dk)
    kc = k.reshape(B, H, nc, C, Dk)
    vc = v.reshape(B, H, nc, C, Dv)
    scores = jnp.einsum('bhnid,bhnjd->bhnij', qc, kc) * decay_in[:, None]
    inner = jnp.einsum('bhnij,bhnje->bhnie', scores, vc)
    kv_chunk = jnp.einsum('bhnjd,bhnje->bhnde', kc * zeta[:, None, :, None], vc)

    def step(R, kv):
        return gamma_c * R + kv, R

    _, R_prev = lax.scan(step, jnp.zeros((B, H, Dk, Dv), f32), kv_chunk.transpose(2, 0, 1, 3, 4))
    R_prev = R_prev.transpose(1, 2, 0, 3, 4)
    cross = jnp.einsum('bhnid,bhnde->bhnie', qc * xi[:, None, :, None], R_prev)
    o = (inner + cross).reshape(B, H, S, Dv)
    mu = jnp.mean(o, axis=-1, keepdims=True)
    var = jnp.mean(jnp.square(o - mu), axis=-1, keepdims=True)
    o = (o - mu) * lax.rsqrt(var + GN_EPS)
    o = o.transpose(0, 2, 1, 3).reshape(B, S, RET_V_W) * gn_w.astype(f32) + gn_b.astype(f32)
    return jax.nn.silu(g) * o.astype(g.dtype)


def hybrid_mixer(h, w_in, mla_q_norm, mla_kv_norm, mla_w_q_up, mla_w_kv_up, ret_gn_w, ret_gn_b,
                 w_branch_mla, w_branch_moba, w_branch_ret, w_out, ropes):
    (cos_mla, sin_mla), (cos_moba, sin_moba), (cos_ret, sin_ret) = ropes
    p = h @ w_in
    (cq, ckv, k_rope, mq, mk, mv, rq, rk, rv, rg, g_mla, g_moba, g_ret) = jnp.split(p, IN_SPLIT_IDX, axis=-1)
    o_mla = mla_branch(cq, ckv, k_rope, mla_q_norm, mla_kv_norm, mla_w_q_up, mla_w_kv_up, cos_mla, sin_mla)
    o_moba = moba_branch(mq, mk, mv, cos_moba, sin_moba)
    o_ret = retention_branch(rq, rk, rv, rg, ret_gn_w, ret_gn_b, cos_ret, sin_ret)
    merged = (jax.nn.sigmoid(g_mla) * (o_mla @ w_branch_mla)
              + jax.nn.sigmoid(g_moba) * (o_moba @ w_branch_moba)
              + jax.nn.sigmoid(g_ret) * (o_ret @ w_branch_ret))
    return merged @ w_out


def squared_relu_mlp(h, w_up, w_down):
    return jnp.square(jax.nn.relu(h @ w_up)) @ w_down


def setup_inputs(seed: int = 0) -> dict:
    key = jax.random.key(seed)
    ks = jax.random.split(key, 18)

    def dense(k, shape):
        return jax.random.normal(k, shape, jnp.float32) * (shape[-2] ** -0.5)

    def gain(k, shape):
        return 1.0 + 0.02 * jax.random.normal(k, shape, jnp.float32)

    L = DEPTH
    return {
        "x": jax.random.normal(ks[0], (BATCH, SEQ, D_MODEL), jnp.float32),
        "attn_norm": gain(ks[1], (L, D_MODEL)),
        "w_in": dense(ks[2], (L, D_MODEL, IN_COLS)),
        "mla_q_norm": gain(ks[3], (L, MLA_Q_RANK)),
        "mla_kv_norm": gain(ks[4], (L, MLA_KV_RANK)),
        "mla_w_q_up": dense(ks[5], (L, MLA_Q_RANK, MLA_HEADS * (MLA_NOPE + MLA_ROPE))),
        "mla_w_kv_up": dense(ks[6], (L, MLA_KV_RANK, MLA_HEADS * (MLA_NOPE + MLA_V))),
        "ret_gn_w": gain(ks[7], (L, RET_V_W)),
        "ret_gn_b": 0.02 * jax.random.normal(ks[8], (L, RET_V_W), jnp.float32),
        "w_branch_mla": dense(ks[9], (L, MLA_OUT_W, D_MODEL)),
        "w_branch_moba": dense(ks[10], (L, MOBA_W, D_MODEL)),
        "w_branch_ret": dense(ks[11], (L, RET_V_W, D_MODEL)),
        "w_out": dense(ks[12], (L, D_MODEL, D_MODEL)),
        "mlp_norm": gain(ks[13], (L, D_MODEL)),
        "w_mlp_up": dense(ks[14], (L, D_MODEL, D_FF)),
        "w_mlp_down": dense(ks[15], (L, D_FF, D_MODEL)),
        "final_norm": gain(ks[16], (D_MODEL,)),
    }


def reference(x, attn_norm, w_in, mla_q_norm, mla_kv_norm, mla_w_q_up, mla_w_kv_up, ret_gn_w, ret_gn_b,
              w_branch_mla, w_branch_moba, w_branch_ret, w_out, mlp_norm, w_mlp_up, w_mlp_down, final_norm):
    S = x.shape[1]
    ropes = (rope_tables(S, MLA_ROPE, ROPE_THETA),
             rope_tables(S, MOBA_HEAD_DIM // PARTIAL_ROPE_DIV, ROPE_THETA),
             rope_tables(S, RET_QK_DIM, RET_THETA))
    for l in range(DEPTH):
        h = rms_norm(x, attn_norm[l])
        x = x + hybrid_mixer(h, w_in[l], mla_q_norm[l], mla_kv_norm[l], mla_w_q_up[l], mla_w_kv_up[l],
                             ret_gn_w[l], ret_gn_b[l], w_branch_mla[l], w_branch_moba[l], w_branch_ret[l],
                             w_out[l], ropes)
        h = rms_norm(x, mlp_norm[l])
        x = x + squared_relu_mlp(h, w_mlp_up[l], w_mlp_down[l])
    return rms_norm(x, final_norm)
```

```python
import numpy as np
import concourse.bass as bass
import concourse.mybir as mybir
from concourse.bass_utils import run_bass_kernel_spmd

F32 = mybir.dt.float32
BF16 = mybir.dt.bfloat16
AF = mybir.ActivationFunctionType
ALU = mybir.AluOpType
AX = mybir.AxisListType

SEQ = 4096
DM = 1024
TT = 512
NT = SEQ // TT
DEPTH = 2
NCORES = 4
EPS = 1e-6
GN_EPS = 1e-5
NEGB = -30000.0


class Buf:
    __slots__ = ("name", "w", "r", "rd")

    def __init__(self, name=""):
        self.name = name
        self.w = None
        self.r = {}
        self.rd = []


class _Op:
    __slots__ = ("eng", "fn", "deps", "is_dma", "idx", "need_inc", "semidx", "semval")


class Sched:
    ENG = ("pe", "act", "dve", "pool", "sp")

    def __init__(self, nc, n_dma_sems=40, same_eng_sync=("act", "dve", "pool")):
        import os
        if os.environ.get("K_VAR", "") == "pesync":
            same_eng_sync = ("act", "dve", "pool", "pe")
        self.nc = nc
        self.ops = []
        self.n_dma_sems = n_dma_sems
        self.same_eng_sync = set(same_eng_sync)
        self.last_on_eng = {}
        self.dmas_open = []
        self.cur_barrier = None

    def op(self, eng, fn, reads=(), writes=(), dma=False):
        o = _Op()
        o.eng = eng
        o.fn = fn
        o.is_dma = dma
        o.idx = len(self.ops)
        o.need_inc = False
        deps = set()
        for b in reads:
            if b.w is not None:
                deps.add(b.w)
        for b in writes:
            if b.w is not None:
                deps.add(b.w)
            deps.update(b.r.values())
            deps.update(b.rd)
        for b in reads:
            if dma:
                b.rd.append(o.idx)
            else:
                b.r[eng] = o.idx
        for b in writes:
            b.w = o.idx
            b.r = {}
            b.rd = []
        if self.cur_barrier is not None:
            deps.add(self.cur_barrier)
        deps.discard(o.idx)
        o.deps = deps
        self.ops.append(o)
        if dma:
            self.dmas_open.append(o.idx)
        else:
            self.last_on_eng[eng] = o.idx
        return o

    def barrier(self):
        deps = set(self.last_on_eng.values()) | set(self.dmas_open)
        o = self.op("sp", lambda e: e.nop())
        o.deps |= deps
        o.deps.discard(o.idx)
        self.dmas_open = []
        self.cur_barrier = o.idx
        return o

    def emit(self):
        nc = self.nc
        ops = self.ops
        engobj = {"pe": nc.tensor, "act": nc.scalar, "dve": nc.vector, "pool": nc.gpsimd, "sp": nc.sync}
        ses = self.same_eng_sync

        def needs_sync(o, od):
            if od.is_dma or o.is_dma:
                return True
            if od.eng != o.eng:
                return True
            return od.eng in ses

        for o in ops:
            for d in o.deps:
                od = ops[d]
                if not od.is_dma and needs_sync(o, od):
                    od.need_inc = True
        cnt = {e: 0 for e in self.ENG}
        ndma = 0
        qpool = {"sp": (0, 32), "pool": (32, 24), "act": (56, 8)}
        qcnt = {q: 0 for q in qpool}
        for o in ops:
            if o.is_dma:
                base, n = qpool[o.eng]
                k = qcnt[o.eng]
                qcnt[o.eng] += 1
                o.semidx = base + (k % n)
                o.semval = 16 * (k // n + 1)
                ndma += 1
            elif o.need_inc:
                cnt[o.eng] += 1
                o.semval = cnt[o.eng]
        engsem = {e: nc.alloc_semaphore(name=f"s_{e}") for e in self.ENG}
        dmasem = [nc.alloc_semaphore(name=f"s_dma{i}") for i in range(64)]
        waited = {}
        nwaits = 0
        self.trace = {e: [] for e in self.ENG}
        for o in ops:
            e = engobj[o.eng]
            waits = {}
            for d in o.deps:
                od = ops[d]
                if not needs_sync(o, od):
                    continue
                key = ("dma", od.semidx) if od.is_dma else ("eng", od.eng)
                if waits.get(key, 0) < od.semval:
                    waits[key] = od.semval
            if o.is_dma and o.semval > 16:
                key = ("dma", o.semidx)
                if waits.get(key, 0) < o.semval - 16:
                    waits[key] = o.semval - 16
            for key, val in waits.items():
                wk = (o.eng, key)
                if waited.get(wk, 0) >= val:
                    continue
                sem = dmasem[key[1]] if key[0] == "dma" else engsem[key[1]]
                e.wait_ge(sem, val)
                self.trace[o.eng].append(("w", key, val, o.idx))
                waited[wk] = val
                nwaits += 1
            ins = o.fn(e)
            if o.is_dma:
                ins.then_inc(dmasem[o.semidx], 16)
                self.trace[o.eng].append(("i", ("dma", o.semidx), 16, o.idx))
            elif o.need_inc:
                ins.then_inc(engsem[o.eng], 1)
                self.trace[o.eng].append(("i", ("eng", o.eng), 1, o.idx))
        self.stats = dict(n_ops=len(ops), n_waits=nwaits, incs=dict(cnt), n_dma=ndma)
        return self.stats


class Ring:
    def __init__(self, items):
        self.items = items
        self.i = 0

    def next(self):
        x = self.items[self.i % len(self.items)]
        self.i += 1
        return x


def _rope_tab(dim, theta):
    inv = (1.0 / (np.float32(theta) ** (np.arange(0, dim, 2, dtype=np.float32) / np.float32(dim)))).astype(np.float32)
    ang = (np.arange(SEQ, dtype=np.float32)[:, None] * inv[None, :]).astype(np.float32)
    c = np.cos(ang).astype(np.float32).T
    s = np.sin(ang).astype(np.float32).T
    cc = np.concatenate([c, c], 0)
    ss = np.concatenate([-s, s], 0)
    return np.ascontiguousarray(np.stack([cc, ss], 0))


def make_consts():
    c = {}
    c["identF"] = np.eye(128, dtype=np.float32)
    c["rope_mla"] = _rope_tab(32, 500000.0)
    rm = _rope_tab(16, 500000.0)
    pad = np.zeros((2, 32, SEQ), np.float32)
    pad[:, 0:16] = rm
    pad[0, 16:32] = 1.0
    c["rope_moba"] = pad
    c["rope_ret"] = _rope_tab(64, 10000.0)
    k = np.arange(128)
    c["mask01"] = (k[:, None] <= k[None, :]).astype(np.float32)
    c["onehot"] = (np.arange(16)[:, None] == (np.arange(SEQ)[None, :] // 256)).astype(np.float32)
    lg = np.log(np.float32(1.0) - np.float32(2.0) ** (np.float32(-5.0) - np.arange(4, dtype=np.float32))).astype(np.float32)
    pos = np.arange(128, dtype=np.float32)
    diff = pos[:, None] - pos[None, :]
    dec = np.where(diff >= 0, np.exp(lg[:, None, None] * diff), 0.0).astype(np.float32)
    c["decayT"] = np.ascontiguousarray(dec.transpose(0, 2, 1) * np.float32(0.125)).astype(np.float32)
    xi = np.exp(lg[:, None] * (pos + 1.0)).astype(np.float32)
    c["xi"] = np.ascontiguousarray(np.broadcast_to(np.tile(xi, (1, 4))[:, None, :], (4, 64, 512))).astype(np.float32)
    zeta = np.exp(lg[:, None] * (127.0 - pos)).astype(np.float32) * np.float32(0.125)
    c["zeta"] = np.ascontiguousarray(zeta.T).astype(np.float32)
    gam = [float(np.exp(lg[h] * np.float32(128.0))) for h in range(4)]
    return c, gam


VSTR = 27
NV = DEPTH * VSTR + 8


def pack_vecs(inp):
    v = np.zeros((128, NV), np.float32)
    for l in range(DEPTH):
        b = l * VSTR
        v[:, b + 0:b + 8] = inp["attn_norm"][l].reshape(8, 128).T
        v[:, b + 8:b + 16] = inp["mlp_norm"][l].reshape(8, 128).T
        v[:, b + 16:b + 18] = inp["mla_q_norm"][l].reshape(2, 128).T
        v[:, b + 18:b + 19] = inp["mla_kv_norm"][l].reshape(1, 128).T
        v[:, b + 19:b + 23] = inp["ret_gn_w"][l].reshape(4, 128).T
        v[:, b + 23:b + 27] = inp["ret_gn_b"][l].reshape(4, 128).T
    v[:, DEPTH * VSTR:DEPTH * VSTR + 8] = inp["final_norm"].reshape(8, 128).T
    return v


class KB:
    SB_BASE = 20480
    SB_TOP = 229376

    def __init__(self, n_layers=DEPTH, debug=False, phases=None):
        self.nc = nc = bass.Bass("TRN2", target_bir_lowering=False)
        self.S = Sched(nc)
        self.debug = debug
        self.n_layers = n_layers
        self.phases = phases
        self.top = self.SB_BASE
        self.uid = 0
        L = DEPTH
        inp = lambda name, shape: nc.dram_tensor(name, shape, F32, kind="ExternalInput").ap()
        self.x = inp("x", [SEQ, DM])
        self.w_in = inp("w_in", [L, DM, 6560])
        self.w_q_up = inp("mla_w_q_up", [L, 256, 768])
        self.w_kv_up = inp("mla_w_kv_up", [L, 128, 1024])
        self.w_b = [inp("w_branch_mla", [L, 512, DM]), inp("w_branch_moba", [L, 512, DM]), inp("w_branch_ret", [L, 512, DM])]
        self.w_out = inp("w_out", [L, DM, DM])
        self.w_up = inp("w_mlp_up", [L, DM, 4096])
        self.w_down = inp("w_mlp_down", [L, 4096, DM])
        self.vecs_d = inp("vecs", [128, NV])
        self.c_identF = inp("identF", [128, 128])
        self.c_rope_mla = inp("rope_mla", [2, 32, SEQ])
        self.c_rope_moba = inp("rope_moba", [2, 32, SEQ])
        self.c_rope_ret = inp("rope_ret", [2, 64, SEQ])
        self.c_mask01 = inp("mask01", [128, 128])
        self.c_onehot = inp("onehot", [16, SEQ])
        self.c_decayT = inp("decayT", [4, 128, 128])
        self.c_xi = inp("xi", [4, 64, 512])
        self.c_zeta = inp("zeta", [128, 4])
        self.out = nc.dram_tensor("out", [SEQ, DM], F32, kind="ExternalOutput").ap()
        kind = "ExternalOutput" if debug else "Internal"
        scr = lambda name, shape, dt: nc.dram_tensor(name, shape, dt, kind=kind).ap()
        self.XT = scr("XT", [8, 128, SEQ], F32)
        self.HT = scr("HT", [8, 128, SEQ], BF16)
        self.CQ = scr("CQ", [2, 128, SEQ], BF16)
        self.CKV = scr("CKV", [128, SEQ], BF16)
        self.KPE = scr("KPE", [32, SEQ], BF16)
        self.MQ = scr("MQ", [8, 64, SEQ], BF16)
        self.MK = scr("MK", [8, 64, SEQ], BF16)
        self.MV = scr("MV", [SEQ, 512], BF16)
        self.RQ = scr("RQ", [4, 64, SEQ], BF16)
        self.RK = scr("RK", [4, 64, SEQ], BF16)
        self.RV = scr("RV", [SEQ, 512], BF16)
        self.RG = scr("RG", [4, 128, SEQ], F32)
        self.OB = [scr("OMLA", [4, 128, SEQ], BF16), scr("OMOBA", [4, 128, SEQ], BF16), scr("ORET", [4, 128, SEQ], BF16)]
        self.ACTS = scr("ACTS", [32, 128, SEQ], BF16)
        self.dbuf = {}
        self.psF = nc.alloc_psum_tensor("psF", [128, 7, 512], F32)
        self.psB = nc.alloc_psum_tensor("psB", [128, 1024], BF16)
        self.pbuf = [Buf(f"ps{i}") for i in range(7)]
        self.pbufB = Buf("psB")
        self.bank_i = 0
        _, self.gam = make_consts()
        self.identF = self.sb([128, 128], F32)
        self.identB = self.sb([128, 128], BF16)
        self.onesB = self.sb([128, 128], BF16)
        self.mask01 = self.sb([128, 128], BF16)
        self.vecs = self.sb([128, NV], F32)
        self.bconst = Buf("const")
        self.dma("sp", self.identF[:], self.c_identF, w=[self.bconst])
        self.dma("pool", self.identB[:], self.c_identF, w=[self.bconst])
        self.dma("pool", self.mask01[:], self.c_mask01, w=[self.bconst])
        self.dma("sp", self.vecs[:], self.vecs_d, w=[self.bconst])
        self.I("pool", "memset", [], [self.bconst], ap=self.onesB[:], constant=1.0)
        self.phase_base = self.top

    def sb(self, shape, dtype, name=None):
        sz = int(np.prod(shape[1:])) * (4 if dtype == F32 else 2)
        sz = (sz + 63) // 64 * 64
        self.uid += 1
        t = self.nc.alloc_sbuf_tensor_at(name or f"t{self.uid}", list(shape), dtype, offset=self.top)
        self.top += sz
        assert self.top <= self.SB_TOP, f"SBUF overflow {self.top}"
        return t

    def sbr(self, n, shape, dtype):
        return Ring([(self.sb(shape, dtype), Buf()) for _ in range(n)])

    def begin_phase(self):
        self.S.barrier()
        self.top = self.phase_base

    def I(self, eng, meth, r, w, **kw):
        return self.S.op(eng, lambda e: getattr(e, meth)(**kw), r, w)

    def dma(self, eng, out, in_, r=(), w=()):
        return self.S.op(eng, lambda e: e.dma_start(out=out, in_=in_), r, w, dma=True)

    def bank(self):
        i = self.bank_i % 7
        self.bank_i += 1
        return self.psF[:, i, :], self.pbuf[i]

    def db(self, name, t=None):
        k = (name, t)
        if k not in self.dbuf:
            self.dbuf[k] = Buf(f"{name}{t}")
        return self.dbuf[k]

    def dball(self, name):
        return [self.db(name, t) for t in range(NT)]

    def vcol(self, l, off, n=1):
        b = l * VSTR + off
        return self.vecs[:, b:b + n]

    def evac_eng(self):
        self._ev = getattr(self, "_ev", 0) + 1
        return "act" if self._ev % 2 else "dve"

    def copy(self, eng, out, in_, r, w):
        if eng == "act":
            return self.I("act", "copy", r, w, out=out, in_=in_)
        return self.I(eng, "tensor_copy", r, w, out=out, in_=in_)

    def rmsnorm(self, xaps, gaps, nfeat, eps, outaps, rb, wb, sq, bsq, rs, brs, nparts=128):
        n = len(xaps)
        for c in range(n):
            self.I("act", "activation", rb, [bsq], out=sq[0:nparts, c, :], in_=xaps[c], func=AF.Square)
        ps, bps = self.bank()
        for c in range(n):
            self.I("pe", "matmul", [bsq, self.bconst], [bps], out=ps, lhsT=self.onesB[0:nparts, :], rhs=sq[0:nparts, c, :],
                   start=(c == 0), stop=(c == n - 1))
        self.I("act", "activation", [bps], [brs], out=rs[:], in_=ps, func=AF.Sqrt, scale=1.0 / nfeat, bias=float(eps))
        self.I("dve", "reciprocal", [brs], [brs], out=rs[:], in_=rs[:])
        for c in range(n):
            self.I("dve", "scalar_tensor_tensor", list(rb) + [brs, self.bconst], wb, out=outaps[c], in0=xaps[c], scalar=gaps[c],
                   in1=rs[0:nparts, :], op0=ALU.mult, op1=ALU.mult)

    def p_transpose_in(self):
        self.begin_phase()
        xin = self.sbr(2, [128, DM], F32)
        xo = self.sbr(2, [128, 8, TT], F32)
        for t in range(NT):
            xot, bxo = xo.next()
            tiles = []
            for s in range(4):
                xt, bx = xin.next()
                r0 = t * TT + s * 128
                self.dma("sp", xt[:], self.x[r0:r0 + 128, :], w=[bx])
                for g in range(2):
                    ps, bps = self.bank()
                    for j in range(4):
                        c = g * 4 + j
                        self.I("pe", "transpose", [bx, self.bconst], [bps], out=ps[:, j * 128:(j + 1) * 128], in_=xt[:, c * 128:(c + 1) * 128],
                               identity=self.identF[:])
                    self.copy(self.evac_eng(), xot[:, g * 4:(g + 1) * 4, s * 128:(s + 1) * 128],
                              ps.rearrange("p (j n) -> p j n", n=128), [bps], [bxo])
            self.dma("sp", self.XT.rearrange("c p n -> p c n")[:, :, t * TT:(t + 1) * TT], xot[:], r=[bxo], w=[self.db("XT", t)])

    def p_inproj(self, l):
        self.begin_phase()
        W1 = self.sb([128, 8, 4288], BF16)
        bW = []
        win = self.w_in[l].rearrange("(c p) n -> p c n", p=128)

        def wl(d0, s0, n):
            b = Buf()
            self.dma("pool", W1[:, :, d0:d0 + n], win[:, :, s0:s0 + n], w=[b])
            bW.append(b)

        wl(0, 0, 416)
        wl(416, 400, 16)
        wl(432, 384, 16)
        wl(448, 416, 512)
        wl(1088, 928, 512)
        wl(1728, 1440, 512)
        wl(2240, 1952, 256)
        wl(2752, 2208, 256)
        wl(3264, 2464, 512)
        wl(3776, 2976, 512)
        for (dst, src, H, dh, half) in ((960, 416, 8, 16, 8), (1600, 928, 8, 16, 8), (2496, 1952, 4, 64, 32), (3008, 2208, 4, 64, 32)):
            for c in range(8):
                dstv = W1[:, c, dst:dst + H * dh].rearrange("p (h e) -> p h e", e=dh)
                srcv = win[:, c, src:src + H * 64].rearrange("p (h d) -> p h d", d=64)
                b = Buf()
                self.dma("pool", dstv[:, :, 0:half], srcv[:, :, half:2 * half], w=[b])
                self.dma("pool", dstv[:, :, half:2 * half], srcv[:, :, 0:half], w=[b])
                bW.append(b)
        xTr = self.sbr(1, [128, 8, TT], F32)
        hTr = self.sbr(2, [128, 8, TT], BF16)
        sq = self.sb([128, 8, TT], BF16)
        bsq = Buf()
        rsr = self.sbr(2, [128, TT], F32)
        cqf = self.sb([128, 3, TT], F32)
        bcqf = Buf()
        cqn = self.sb([128, 3, TT], BF16)
        bcqn = Buf()
        sq2 = self.sb([128, 2, TT], BF16)
        bsq2 = Buf()
        tabk = self.sb([32, 2, TT], F32)
        tabm = self.sb([32, 2, TT], F32)
        tabr = self.sb([64, 2, TT], F32)
        btab = Buf()
        t1r = self.sbr(3, [64, TT], F32)
        t2r = self.sbr(3, [64, TT], F32)
        kpe = self.sb([32, TT], BF16)
        bkpe = Buf()
        stq = self.sb([64, 8, TT], BF16)
        stk = self.sb([64, 8, TT], BF16)
        strq = self.sb([64, 4, TT], BF16)
        strk = self.sb([64, 4, TT], BF16)
        stg = self.sb([128, 4, TT], F32)
        stv = self.sbr(2, [128, 4, 512], BF16)
        bst = {k: Buf() for k in ("q", "k", "rq", "rk", "g")}
        import os
        dbg_nt = int(os.environ.get("K_NT", NT))
        dbg_stage = int(os.environ.get("K_STAGE", 99))
        for t in range(dbg_nt):
            tok = slice(t * TT, (t + 1) * TT)
            xT, bx = xTr.next()
            hT, bh = hTr.next()
            rs, brs = rsr.next()
            self.dma("sp", xT[:], self.XT.rearrange("c p n -> p c n")[:, :, tok], r=[self.db("XT", t)], w=[bx])
            self.dma("sp", tabk[:], self.c_rope_mla.rearrange("a p n -> p a n")[:, :, tok], w=[btab])
            self.dma("sp", tabm[:], self.c_rope_moba.rearrange("a p n -> p a n")[:, :, tok], w=[btab])
            self.dma("sp", tabr[:], self.c_rope_ret.rearrange("a p n -> p a n")[:, :, tok], w=[btab])
            self.rmsnorm([xT[:, c, :] for c in range(8)], [self.vcol(l, c) for c in range(8)], DM, EPS,
                         [hT[:, c, :] for c in range(8)], [bx], [bh], sq, bsq, rs, brs)
            self.dma("sp", self.HT.rearrange("c p n -> p c n")[:, :, tok], hT[:], r=[bh], w=[self.db("HT", t)])
            if dbg_stage < 1:
                continue

            def mm(M, col, N0=0, N1=TT):
                ps, bps = self.bank()
                for c in range(8):
                    self.I("pe", "matmul", [bh] + bW, [bps], out=ps[0:M, 0:N1 - N0], lhsT=W1[:, c, col:col + M], rhs=hT[:, c, N0:N1],
                           start=(c == 0), stop=(c == 7))
                return ps, bps

            for j in range(3):
                ps, bps = mm(128, j * 128)
                self.copy(self.evac_eng(), cqf[:, j, :], ps, [bps], [bcqf])
            rs2, brs2 = rsr.next()
            self.rmsnorm([cqf[:, c, :] for c in range(2)], [self.vcol(l, 16 + c) for c in range(2)], 256, EPS,
                         [cqn[:, c, :] for c in range(2)], [bcqf], [bcqn], sq2, bsq2, rs2, brs2)
            self.dma("sp", self.CQ.rearrange("c p n -> p c n")[:, :, tok], cqn[:, 0:2, :], r=[bcqn], w=[self.db("CQ", t)])
            rs3, brs3 = rsr.next()
            self.rmsnorm([cqf[:, 2, :]], [self.vcol(l, 18)], 128, EPS, [cqn[:, 2, :]], [bcqf], [bcqn], sq2, bsq2, rs3, brs3)
            self.dma("sp", self.CKV[:, tok], cqn[:, 2, :], r=[bcqn], w=[self.db("CKV", t)])

            def rope(psm, bpm, psr, bpr, n, tab, outap, wb):
                t1, b1 = t1r.next()
                t2, b2 = t2r.next()
                self.I("dve", "tensor_tensor", [bpm, btab], [b1], out=t1[0:n, :], in0=psm[0:n, :], in1=tab[0:n, 0, :], op=ALU.mult)
                self.I("dve", "tensor_tensor", [bpr, btab], [b2], out=t2[0:n, :], in0=psr[0:n, :], in1=tab[0:n, 1, :], op=ALU.mult)
                self.I("pool", "tensor_tensor", [b1, b2], wb, out=outap, in0=t1[0:n, :], in1=t2[0:n, :], op=ALU.add)

            if dbg_stage < 2:
                continue
            psm, bpm = mm(32, 384)
            psr, bpr = mm(32, 416)
            rope(psm, bpm, psr, bpr, 32, tabk, kpe[:], [bkpe])
            self.dma("sp", self.KPE[:, tok], kpe[:], r=[bkpe], w=[self.db("KPE", t)])
            if dbg_stage < 3:
                continue
            for (st, key, base, rbase, dst) in ((stq, "q", 448, 960, self.MQ), (stk, "k", 1088, 1600, self.MK)):
                for h in range(8):
                    psm, bpm = mm(64, base + h * 64)
                    psr, bpr = mm(32, rbase + h * 16)
                    self.I("act", "copy", [bpm], [bst[key]], out=st[32:64, h, :], in_=psm[32:64, :])
                    rope(psm, bpm, psr, bpr, 32, tabm, st[0:32, h, :], [bst[key]])
                self.dma("sp", dst.rearrange("h p n -> p h n")[:, :, tok], st[:], r=[bst[key]], w=[self.db("M" + key, t)])
            if dbg_stage < 4:
                continue
            for (st, key, base, rbase, dst) in ((strq, "rq", 2240, 2496, self.RQ), (strk, "rk", 2752, 3008, self.RK)):
                for h in range(4):
                    psm, bpm = mm(64, base + h * 64)
                    psr, bpr = mm(64, rbase + h * 64)
                    rope(psm, bpm, psr, bpr, 64, tabr, st[:, h, :], [bst[key]])
                self.dma("sp", dst.rearrange("h p n -> p h n")[:, :, tok], st[:], r=[bst[key]], w=[self.db(key, t)])
            if dbg_stage < 5:
                continue
            for j in range(4):
                ps, bps = mm(128, 3776 + j * 128)
                self.I("act", "activation", [bps], [bst["g"]], out=stg[:, j, :], in_=ps, func=AF.Silu)
            self.dma("sp", self.RG.rearrange("c p n -> p c n")[:, :, tok], stg[:], r=[bst["g"]], w=[self.db("RG", t)])
            if dbg_stage < 6:
                continue
            for (col, dst, key) in ((1728, self.MV, "MV"), (3264, self.RV, "RV")):
                sv, bsv = stv.next()
                for s in range(4):
                    ps, bps = self.bank()
                    for c in range(8):
                        self.I("pe", "matmul", [bh] + bW, [bps], out=ps, lhsT=hT[:, c, s * 128:(s + 1) * 128], rhs=W1[:, c, col:col + 512],
                               start=(c == 0), stop=(c == 7))
                    self.copy(self.evac_eng(), sv[:, s, :], ps, [bps], [bsv])
                self.dma("sp", dst[t * TT:(t + 1) * TT, :].rearrange("(s p) n -> p s n", p=128), sv[:], r=[bsv], w=[self.db(key, t)])

    def attn_tile(self, t, Kb, bK, Vb, bV, qT, bq, dk, scale, pss_r, pT_r, pso, bpso):
        nkt = 4 * t + 4
        for kt in range(nkt):
            j = kt - 4 * t
            q0 = max(0, j) * 128
            N = TT - q0
            (pi, bpi) = pss_r.next()
            pss = self.psF[:, pi, :]
            self.I("pe", "matmul", [bK, bq], [bpi], out=pss[:, 0:N], lhsT=Kb[0:dk, kt * 128:(kt + 1) * 128], rhs=qT[0:dk, q0:TT],
                   start=True, stop=True)
            pT, bpT = pT_r.next()
            self.I("act", "activation", [bpi], [bpT], out=pT[:, 0:N], in_=pss[:, 0:N], func=AF.Exp, scale=float(scale))
            if j >= 0:
                self.I("pool", "tensor_tensor", [bpT, self.bconst], [bpT], out=pT[:, 0:128], in0=pT[:, 0:128], in1=self.mask01[:], op=ALU.mult)
            self.I("pe", "matmul", [bV, bpT], [bpso], out=pso[:, q0:TT], lhsT=Vb[:, kt, :], rhs=pT[:, 0:N],
                   start=(kt == 0), stop=(kt == nkt - 1))

    def attn_finish(self, pso, bpso, rec_r, ost_r, dst_ap, dbuf):
        rec, brec = rec_r.next()
        ost, bost = ost_r.next()
        self.I("dve", "reciprocal", [bpso], [brec], out=rec[0:64, :], in_=pso[64:128, :])
        self.I("dve", "tensor_tensor", [bpso, brec], [bost], out=ost[0:64, :], in0=pso[0:64, :], in1=rec[0:64, :], op=ALU.mult)
        self.dma("sp", dst_ap, ost[0:64, :], r=[bost], w=[dbuf])

    def p_mla(self, l):
        self.begin_phase()
        CQs = self.sb([128, 2, SEQ], BF16)
        CKVs = self.sb([128, SEQ], BF16)
        bin_ = Buf()
        self.dma("sp", CQs[:], self.CQ.rearrange("c p n -> p c n"), r=self.dball("CQ"), w=[bin_])
        self.dma("sp", CKVs[:], self.CKV, r=self.dball("CKV"), w=[bin_])
        Wq = self.sb([128, 2, 8, 96], BF16)
        Wqr = self.sb([128, 2, 8, 96], BF16)
        Wkv = self.sb([128, 8, 128], BF16)
        bw = Buf()
        wq = self.w_q_up[l].rearrange("(c p) (h d) -> p c h d", p=128, d=96)
        self.I("pool", "memset", [], [bw], ap=Wqr[:], constant=0.0)
        for c in range(2):
            self.dma("pool", Wq[:, c, :, :], wq[:, c, :, :], w=[bw])
            self.dma("pool", Wqr[:, c, :, 64:80], wq[:, c, :, 80:96], w=[bw])
            self.dma("pool", Wqr[:, c, :, 80:96], wq[:, c, :, 64:80], w=[bw])
        self.dma("pool", Wkv[:], self.w_kv_up[l].rearrange("p (h d) -> p h d", d=128), w=[bw])
        tab = self.sb([96, 2, SEQ], F32)
        btab = Buf()
        self.dma("sp", tab[64:96, :, :], self.c_rope_mla.rearrange("a p n -> p a n"), w=[btab])
        Kr = self.sbr(2, [96, SEQ], BF16)
        Vr = self.sbr(2, [128, 32, 128], BF16)
        for (Kb, bK) in Kr.items:
            self.dma("sp", Kb[64:96, :], self.KPE, r=self.dball("KPE"), w=[bK])
        for (Vb, bV) in Vr.items:
            self.I("pool", "memset", [], [bV], ap=Vb[:, :, 64:128], constant=1.0)
        qTr = self.sbr(2, [96, TT], BF16)
        t1r = self.sbr(2, [96, TT], F32)
        t2r = self.sbr(2, [96, TT], F32)
        pT_r = self.sbr(3, [128, TT], BF16)
        rec_r = self.sbr(2, [64, TT], F32)
        ost_r = self.sbr(2, [64, TT], BF16)
        pss_r = Ring([(0, self.pbuf[0]), (1, self.pbuf[1]), (2, self.pbuf[2])])
        pso_r = Ring([(3, self.pbuf[3]), (4, self.pbuf[4])])
        aux_r = Ring([(5, self.pbuf[5]), (6, self.pbuf[6])])
        scale = 96 ** -0.5
        for h in range(8):
            Kb, bK = Kr.next()
            Vb, bV = Vr.next()
            for t in range(NT):
                pi, bpi = aux_r.next()
                ps = self.psF[:, pi, :]
                self.I("pe", "matmul", [bin_, bw], [bpi], out=ps[0:64, :], lhsT=Wkv[:, h, 0:64], rhs=CKVs[:, t * TT:(t + 1) * TT], start=True, stop=True)
                self.copy(self.evac_eng(), Kb[0:64, t * TT:(t + 1) * TT], ps[0:64, :], [bpi], [bK])
            for g in range(4):
                pi, bpi = aux_r.next()
                ps = self.psF[:, pi, :]
                for j in range(8):
                    kt = g * 8 + j
                    self.I("pe", "matmul", [bin_, bw], [bpi], out=ps[:, j * 64:(j + 1) * 64], lhsT=CKVs[:, kt * 128:(kt + 1) * 128],
                           rhs=Wkv[:, h, 64:128], start=True, stop=True)
                self.copy(self.evac_eng(), Vb[:, g * 8:(g + 1) * 8, 0:64], ps.rearrange("p (j d) -> p j d", d=64), [bpi], [bV])
            for t in range(NT):
                tok = slice(t * TT, (t + 1) * TT)
                qT, bq = qTr.next()
                pi, bpm = aux_r.next()
                psm = self.psF[:, pi, :]
                pi2, bpr = aux_r.next()
                psr = self.psF[:, pi2, :]
                for c in range(2):
                    self.I("pe", "matmul", [bin_, bw], [bpm], out=psm[0:96, :], lhsT=Wq[:, c, h, :], rhs=CQs[:, c, tok], start=(c == 0), stop=(c == 1))
                for c in range(2):
                    self.I("pe", "matmul", [bin_, bw], [bpr], out=psr[0:96, :], lhsT=Wqr[:, c, h, :], rhs=CQs[:, c, tok], start=(c == 0), stop=(c == 1))
                self.I("act", "copy", [bpm], [bq], out=qT[0:64, :], in_=psm[0:64, :])
                t1, b1 = t1r.next()
                t2, b2 = t2r.next()
                self.I("dve", "tensor_tensor", [bpm, btab], [b1], out=t1[64:96, :], in0=psm[64:96, :], in1=tab[64:96, 0, tok], op=ALU.mult)
                self.I("dve", "tensor_tensor", [bpr, btab], [b2], out=t2[64:96, :], in0=psr[64:96, :], in1=tab[64:96, 1, tok], op=ALU.mult)
                self.I("pool", "tensor_tensor", [b1, b2], [bq], out=qT[64:96, :], in0=t1[64:96, :], in1=t2[64:96, :], op=ALU.add)
                po, bpso = pso_r.next()
                pso = self.psF[:, po, :]
                self.attn_tile(t, Kb, bK, Vb, bV, qT, bq, 96, scale, pss_r, pT_r, pso, bpso)
                self.attn_finish(pso, bpso, rec_r, ost_r, self.OB[0][h // 2, (h % 2) * 64:(h % 2) * 64 + 64, tok], self.db("OMLA", t))

    def p_moba(self, l):
        self.begin_phase()
        Qr = self.sbr(2, [80, SEQ], BF16)
        Kr = self.sbr(2, [80, SEQ], BF16)
        Vr = self.sbr(2, [128, 32, 128], BF16)
        for (Kb, bK) in Kr.items:
            self.dma("pool", Kb[64:80, :], self.c_onehot, w=[bK])
        for (Vb, bV) in Vr.items:
            self.I("pool", "memset", [], [bV], ap=Vb[:, :, 64:128], constant=1.0)
        kmf = self.sb([64, 16], F32)
        kmb = self.sb([64, 16], BF16)
        bkm = Buf()
        gate_r = self.sbr(2, [128, 16], F32)
        top_r = self.sbr(2, [128, 8], F32)
        mbp_r = self.sbr(2, [128, 4, 80], BF16)
        for (m, bm) in mbp_r.items:
            self.I("pool", "memset", [], [bm], ap=m[:], constant=0.0)
        pT_r = self.sbr(3, [128, TT], BF16)
        rec_r = self.sbr(2, [64, TT], F32)
        ost_r = self.sbr(2, [64, TT], BF16)
        pss_r = Ring([(0, self.pbuf[0]), (1, self.pbuf[1]), (2, self.pbuf[2])])
        pso_r = Ring([(3, self.pbuf[3]), (4, self.pbuf[4])])
        aux_r = Ring([(5, self.pbuf[5]), (6, self.pbuf[6])])
        for h in range(8):
            Qb, bQ = Qr.next()
            Kb, bK = Kr.next()
            Vb, bV = Vr.next()
            self.dma("sp", Qb[0:64, :], self.MQ[h], r=self.dball("Mq"), w=[bQ])
            self.dma("sp", Kb[0:64, :], self.MK[h], r=self.dball("Mk"), w=[bK])
            for g4 in range(4):
                self.dma("sp", Vb[:, g4 * 8:(g4 + 1) * 8, 0:64],
                         self.MV[g4 * 1024:(g4 + 1) * 1024, h * 64:(h + 1) * 64].rearrange("(k p) d -> p k d", p=128), r=self.dball("MV"), w=[bV])
            self.I("dve", "tensor_reduce", [bK], [bkm], out=kmf[:], in_=Kb[0:64, :].rearrange("p (n k) -> p n k", k=256), axis=AX.X, op=ALU.add)
            self.I("dve", "tensor_copy", [bkm], [bkm], out=kmb[:], in_=kmf[:])
            for t in range(NT):
                tok = slice(t * TT, (t + 1) * TT)
                mbp, bmb = mbp_r.next()
                pi, bpg = aux_r.next()
                psg = self.psF[:, pi, :]
                for s in range(4):
                    g = 4 * t + s
                    blk = g // 2
                    if blk <= 3:
                        self.I("pool", "memset", [], [bmb], ap=mbp[:, s, 64:80], constant=0.0)
                        continue
                    self.I("pe", "matmul", [bQ, bkm], [bpg], out=psg[:, s * 16:(s + 1) * 16], lhsT=Qb[0:64, g * 128:(g + 1) * 128], rhs=kmb[:],
                           start=True, stop=True)
                    gt, bg = gate_r.next()
                    tp, btp = top_r.next()
                    self.I("dve", "memset", [], [bg], ap=gt[:], constant=-1e30)
                    self.I("dve", "tensor_copy", [bpg], [bg], out=gt[:, 0:blk], in_=psg[:, s * 16:s * 16 + blk])
                    self.I("dve", "max", [bg], [btp], out=tp[:], in_=gt[:])
                    self.I("dve", "tensor_scalar", [bg, btp], [bmb], out=mbp[:, s, 64:80], in0=gt[:], scalar1=tp[:, 2:3], scalar2=NEGB,
                           op0=ALU.is_lt, op1=ALU.mult)
                    self.I("dve", "memset", [], [bmb], ap=mbp[:, s, 64 + blk:80], constant=0.0)
                for s in range(4):
                    self.I("pe", "transpose", [bmb, self.bconst], [self.pbufB], out=self.psB[0:80, s * 128:(s + 1) * 128], in_=mbp[:, s, :],
                           identity=self.identB[:])
                self.I("act", "copy", [self.pbufB], [bQ], out=Qb[64:80, tok], in_=self.psB[64:80, 0:TT])
                po, bpso = pso_r.next()
                pso = self.psF[:, po, :]
                self.attn_tile(t, Kb, bK, Vb, bV, Qb[:, tok], bQ, 80, 0.125, pss_r, pT_r, pso, bpso)
                self.attn_finish(pso, bpso, rec_r, ost_r, self.OB[1][h // 2, (h % 2) * 64:(h % 2) * 64 + 64, tok], self.db("OMOBA", t))

    def p_ret(self, l):
        self.begin_phase()
        Qr = self.sbr(2, [64, SEQ], BF16)
        Kr = self.sbr(2, [64, SEQ], BF16)
        Vr = self.sbr(2, [128, 32, 128], BF16)
        dec = self.sb([128, 4, 128], F32)
        xi = self.sb([64, 4, 512], F32)
        zeta = self.sb([128, 4], F32)
        bc = Buf()
        self.dma("sp", dec[:], self.c_decayT.rearrange("h j i -> j h i"), w=[bc])
        self.dma("sp", xi[:], self.c_xi.rearrange("h p n -> p h n"), w=[bc])
        self.dma("sp", zeta[:], self.c_zeta, w=[bc])
        qx_r = self.sbr(2, [64, TT], BF16)
        kz_r = self.sbr(3, [128, 64], BF16)
        sc_r = self.sbr(3, [128, 128], BF16)
        Rf = self.sb([64, 128], F32)
        bRf = Buf()
        Rb_r = self.sbr(3, [64, 128], BF16)
        G_r = self.sbr(2, [128, TT], F32)
        of_r = self.sbr(2, [128, TT], F32)
        ob_r = self.sbr(2, [128, 2, TT], BF16)
        m_r = self.sbr(2, [128, TT], F32)
        v_r = self.sbr(2, [128, TT], F32)
        y_r = self.sbr(2, [128, TT], BF16)
        po_r = Ring([(0, self.pbuf[0]), (1, self.pbuf[1])])
        ps_r = Ring([(2, self.pbuf[2]), (3, self.pbuf[3])])
        pk_r = Ring([(4, self.pbuf[4]), (5, self.pbuf[5])])
        for h in range(4):
            Qb, bQ = Qr.next()
            Kb, bK = Kr.next()
            Vb, bV = Vr.next()
            self.dma("sp", Qb[:], self.RQ[h], r=self.dball("rq"), w=[bQ])
            self.dma("sp", Kb[:], self.RK[h], r=self.dball("rk"), w=[bK])
            for g4 in range(4):
                self.dma("sp", Vb[:, g4 * 8:(g4 + 1) * 8, :],
                         self.RV[g4 * 1024:(g4 + 1) * 1024, h * 128:(h + 1) * 128].rearrange("(k p) d -> p k d", p=128), r=self.dball("RV"), w=[bV])
            self.I("dve", "memset", [], [bRf], ap=Rf[:], constant=0.0)
            Rb, bRb = Rb_r.next()
            self.I("dve", "tensor_copy", [bRf], [bRb], out=Rb[:], in_=Rf[:])
            for t in range(NT):
                tok = slice(t * TT, (t + 1) * TT)
                G, bG = G_r.next()
                self.dma("sp", G[:], self.RG[h, :, tok], r=[self.db("RG", t)], w=[bG])
                qx, bqx = qx_r.next()
                self.I("pool", "tensor_tensor", [bQ, bc], [bqx], out=qx[:], in0=Qb[:, tok], in1=xi[:, h, :], op=ALU.mult)
                pi_, bpo = po_r.next()
                po = self.psF[:, pi_, :]
                for s in range(4):
                    n = 4 * t + s
                    ch = slice(n * 128, (n + 1) * 128)
                    pi_, bps = ps_r.next()
                    ps = self.psF[:, pi_, :]
                    self.I("pe", "matmul", [bK, bQ], [bps], out=ps[:, 0:128], lhsT=Kb[:, ch], rhs=Qb[:, ch], start=True, stop=True)
                    sc, bsc = sc_r.next()
                    self.I("dve", "tensor_tensor", [bps, bc], [bsc], out=sc[:], in0=ps[:, 0:128], in1=dec[:, h, :], op=ALU.mult)
                    self.I("pe", "matmul", [bV, bsc], [bpo], out=po[:, s * 128:(s + 1) * 128], lhsT=Vb[:, n, :], rhs=sc[:], start=True, stop=False)
                    self.I("pe", "matmul", [bRb, bqx], [bpo], out=po[:, s * 128:(s + 1) * 128], lhsT=Rb[:], rhs=qx[:, s * 128:(s + 1) * 128],
                           start=False, stop=True)
                    self.I("pe", "transpose", [bK, self.bconst], [self.pbufB], out=self.psB[:, 0:64], in_=Kb[:, ch], identity=self.identB[0:64, 0:64])
                    kz, bkz = kz_r.next()
                    self.I("act", "activation", [self.pbufB, bc], [bkz], out=kz[:], in_=self.psB[:, 0:64], func=AF.Copy, scale=zeta[:, h:h + 1])
                    pi_, bpk = pk_r.next()
                    pk = self.psF[:, pi_, :]
                    self.I("pe", "matmul", [bkz, bV], [bpk], out=pk[0:64, 0:128], lhsT=kz[:], rhs=Vb[:, n, :], start=True, stop=True)
                    self.I("dve", "scalar_tensor_tensor", [bRf, bpk], [bRf], out=Rf[:], in0=Rf[:], scalar=float(self.gam[h]), in1=pk[0:64, 0:128],
                           op0=ALU.mult, op1=ALU.add)
                    Rb, bRb = Rb_r.next()
                    self.I("dve", "tensor_copy", [bRf], [bRb], out=Rb[:], in_=Rf[:])
                of, bof = of_r.next()
                ob, bob = ob_r.next()
                self.I("act", "copy", [bpo], [bof], out=of[:], in_=po)
                self.I("dve", "tensor_copy", [bof], [bob], out=ob[:, 0, :], in_=of[:])
                self.I("act", "activation", [bof], [bob], out=ob[:, 1, :], in_=of[:], func=AF.Square)
                pm, bpm = self.psF[:, 6, :], self.pbuf[6]
                self.I("pe", "matmul", [bob, self.bconst], [bpm], out=pm, lhsT=self.onesB[:], rhs=ob[:, 0, :], start=True, stop=True)
                pi_, bpq = ps_r.next()
                pq = self.psF[:, pi_, :]
                self.I("pe", "matmul", [bob, self.bconst], [bpq], out=pq, lhsT=self.onesB[:], rhs=ob[:, 1, :], start=True, stop=True)
                m, bm = m_r.next()
                v, bv = v_r.next()
                self.I("act", "mul", [bpm], [bm], out=m[:], in_=pm, mul=1.0 / 128)
                self.I("dve", "tensor_tensor", [bm], [bv], out=v[:], in0=m[:], in1=m[:], op=ALU.mult)
                self.I("dve", "scalar_tensor_tensor", [bpq, bv], [bv], out=v[:], in0=pq, scalar=1.0 / 128, in1=v[:], op0=ALU.mult, op1=ALU.subtract)
                self.I("act", "activation", [bv], [bv], out=v[:], in_=v[:], func=AF.Sqrt, scale=1.0, bias=float(GN_EPS))
                self.I("dve", "reciprocal", [bv], [bv], out=v[:], in_=v[:])
                self.I("pool", "tensor_tensor", [bof, bm], [bof], out=of[:], in0=of[:], in1=m[:], op=ALU.subtract)
                self.I("dve", "tensor_tensor", [bof, bv], [bof], out=of[:], in0=of[:], in1=v[:], op=ALU.mult)
                self.I("dve", "tensor_scalar", [bof, self.bconst], [bof], out=of[:], in0=of[:], scalar1=self.vcol(l, 19 + h), scalar2=self.vcol(l, 23 + h),
                       op0=ALU.mult, op1=ALU.add)
                y, by = y_r.next()
                self.I("pool", "tensor_tensor", [bof, bG], [by], out=y[:], in0=of[:], in1=G[:], op=ALU.mult)
                self.dma("sp", self.OB[2][h, :, tok], y[:], r=[by], w=[self.db("ORET", t)])

    def p_merge(self, l):
        self.begin_phase()
        Wg = self.sb([128, 8, 3072], BF16)
        Wb = self.sb([128, 3, 4, DM], BF16)
        Wo = self.sb([128, 8, DM], BF16)
        bw = Buf()
        win = self.w_in[l].rearrange("(c p) n -> p c n", p=128)
        for j in range(3):
            self.dma("pool", Wg[:, :, j * 1024:(j + 1) * 1024], win[:, :, 3488 + j * 1024:3488 + (j + 1) * 1024], w=[bw])
            self.dma("pool", Wb[:, j, :, :], self.w_b[j][l].rearrange("(c p) n -> p c n", p=128), w=[bw])
        self.dma("pool", Wo[:], self.w_out[l].rearrange("(c p) n -> p c n", p=128), w=[bw])
        hT_r = self.sbr(2, [128, 8, TT], BF16)
        o_r = self.sbr(2, [128, 3, 4, TT], BF16)
        xT_r = self.sbr(2, [128, 8, TT], F32)
        sig_r = self.sbr(3, [128, TT], F32)
        acc_r = self.sbr(2, [128, TT], F32)
        mg_r = self.sbr(2, [128, 8, TT], BF16)
        names = ("OMLA", "OMOBA", "ORET")
        for t in range(NT):
            tok = slice(t * TT, (t + 1) * TT)
            hT, bh = hT_r.next()
            o, bo = o_r.next()
            xT, bx = xT_r.next()
            mg, bmg = mg_r.next()
            self.dma("sp", hT[:], self.HT.rearrange("c p n -> p c n")[:, :, tok], r=[self.db("HT", t)], w=[bh])
            for j in range(3):
                self.dma("sp", o[:, j, :, :], self.OB[j].rearrange("c p n -> p c n")[:, :, tok], r=[self.db(names[j], t)], w=[bo])
            self.dma("sp", xT[:], self.XT.rearrange("c p n -> p c n")[:, :, tok], r=[self.db("XT", t)], w=[bx])
            for oc in range(8):
                acc, bacc = acc_r.next()
                for j in range(3):
                    pg, bpg = self.bank()
                    for c in range(8):
                        self.I("pe", "matmul", [bh, bw], [bpg], out=pg, lhsT=Wg[:, c, j * 1024 + oc * 128:j * 1024 + (oc + 1) * 128], rhs=hT[:, c, :],
                               start=(c == 0), stop=(c == 7))
                    pb, bpb = self.bank()
                    for c in range(4):
                        self.I("pe", "matmul", [bo, bw], [bpb], out=pb, lhsT=Wb[:, j, c, oc * 128:(oc + 1) * 128], rhs=o[:, j, c, :],
                               start=(c == 0), stop=(c == 3))
                    sg, bsg = sig_r.next()
                    self.I("act", "activation", [bpg], [bsg], out=sg[:], in_=pg, func=AF.Sigmoid)
                    if j == 0:
                        self.I("dve", "tensor_tensor", [bsg, bpb], [bacc], out=acc[:], in0=pb, in1=sg[:], op=ALU.mult)
                    else:
                        self.I("dve", "tensor_tensor", [bsg, bpb], [bsg], out=sg[:], in0=pb, in1=sg[:], op=ALU.mult)
                        if j == 1:
                            self.I("pool", "tensor_tensor", [bsg, bacc], [bacc], out=acc[:], in0=acc[:], in1=sg[:], op=ALU.add)
                        else:
                            self.I("pool", "tensor_tensor", [bsg, bacc], [bmg], out=mg[:, oc, :], in0=acc[:], in1=sg[:], op=ALU.add)
            for oc in range(8):
                po, bpo = self.bank()
                for c in range(8):
                    self.I("pe", "matmul", [bmg, bw], [bpo], out=po, lhsT=Wo[:, c, oc * 128:(oc + 1) * 128], rhs=mg[:, c, :], start=(c == 0), stop=(c == 7))
                self.I("dve", "tensor_tensor", [bpo, bx], [bx], out=xT[:, oc, :], in0=po, in1=xT[:, oc, :], op=ALU.add)
            self.dma("sp", self.XT.rearrange("c p n -> p c n")[:, :, tok], xT[:], r=[bx], w=[self.db("XT", t)])

    def p_mlp_up(self, l):
        self.begin_phase()
        Wu = self.sb([128, 8, 4096], BF16)
        bw = Buf()
        wu = self.w_up[l].rearrange("(c p) n -> p c n", p=128)
        for j in range(4):
            self.dma("pool", Wu[:, :, j * 1024:(j + 1) * 1024], wu[:, :, j * 1024:(j + 1) * 1024], w=[bw])
        xT_r = self.sbr(2, [128, 8, TT], F32)
        hT_r = self.sbr(2, [128, 8, TT], BF16)
        sq = self.sb([128, 8, TT], BF16)
        bsq = Buf()
        rs_r = self.sbr(2, [128, TT], F32)
        r_r = self.sbr(3, [128, TT], F32)
        a_r = self.sbr(2, [128, 8, TT], BF16)
        for t in range(NT):
            tok = slice(t * TT, (t + 1) * TT)
            xT, bx = xT_r.next()
            hT, bh = hT_r.next()
            rs, brs = rs_r.next()
            self.dma("sp", xT[:], self.XT.rearrange("c p n -> p c n")[:, :, tok], r=[self.db("XT", t)], w=[bx])
            self.rmsnorm([xT[:, c, :] for c in range(8)], [self.vcol(l, 8 + c) for c in range(8)], DM, EPS,
                         [hT[:, c, :] for c in range(8)], [bx], [bh], sq, bsq, rs, brs)
            for fg in range(4):
                a, ba = a_r.next()
                for fj in range(8):
                    f = fg * 8 + fj
                    ps, bps = self.bank()
                    for c in range(8):
                        self.I("pe", "matmul", [bh, bw], [bps], out=ps, lhsT=Wu[:, c, f * 128:(f + 1) * 128], rhs=hT[:, c, :], start=(c == 0), stop=(c == 7))
                    r, br = r_r.next()
                    self.I("act", "activation", [bps], [br], out=r[:], in_=ps, func=AF.Relu)
                    self.I("pool" if fj % 2 else "dve", "tensor_tensor", [br], [ba], out=a[:, fj, :], in0=r[:], in1=r[:], op=ALU.mult)
                self.dma("sp", self.ACTS[fg * 8:(fg + 1) * 8].rearrange("c p n -> p c n")[:, :, tok], a[:], r=[ba], w=[self.db("ACTS%d" % fg, t)])

    def p_mlp_down(self, l):
        self.begin_phase()
        Wd = self.sb([128, 32, DM], BF16)
        bw = Buf()
        wd = self.w_down[l].rearrange("(c p) n -> p c n", p=128)
        for j in range(4):
            self.dma("pool", Wd[:, j * 8:(j + 1) * 8, :], wd[:, j * 8:(j + 1) * 8, :], w=[bw])
        xT_r = self.sbr(2, [128, 8, TT], F32)
        a_r = self.sbr(2, [128, 32, TT], BF16)
        for t in range(NT):
            tok = slice(t * TT, (t + 1) * TT)
            xT, bx = xT_r.next()
            a, ba = a_r.next()
            self.dma("sp", xT[:], self.XT.rearrange("c p n -> p c n")[:, :, tok], r=[self.db("XT", t)], w=[bx])
            for fg in range(4):
                self.dma("sp", a[:, fg * 8:(fg + 1) * 8, :], self.ACTS[fg * 8:(fg + 1) * 8].rearrange("c p n -> p c n")[:, :, tok],
                         r=[self.db("ACTS%d" % fg, t)], w=[ba])
            for oc in range(8):
                po, bpo = self.bank()
                for f in range(32):
                    self.I("pe", "matmul", [ba, bw], [bpo], out=po, lhsT=Wd[:, f, oc * 128:(oc + 1) * 128], rhs=a[:, f, :], start=(f == 0), stop=(f == 31))
                self.I("dve", "tensor_tensor", [bpo, bx], [bx], out=xT[:, oc, :], in0=po, in1=xT[:, oc, :], op=ALU.add)
            self.dma("sp", self.XT.rearrange("c p n -> p c n")[:, :, tok], xT[:], r=[bx], w=[self.db("XT", t)])

    def p_final(self):
        self.begin_phase()
        xT_r = self.sbr(2, [128, 8, TT], F32)
        yT_r = self.sbr(2, [128, 8, TT], F32)
        sq = self.sb([128, 8, TT], BF16)
        bsq = Buf()
        rs_r = self.sbr(2, [128, TT], F32)
        o_r = self.sbr(2, [128, DM], F32)
        for t in range(NT):
            tok = slice(t * TT, (t + 1) * TT)
            xT, bx = xT_r.next()
            yT, by = yT_r.next()
            rs, brs = rs_r.next()
            self.dma("sp", xT[:], self.XT.rearrange("c p n -> p c n")[:, :, tok], r=[self.db("XT", t)], w=[bx])
            fb = DEPTH * VSTR
            self.rmsnorm([xT[:, c, :] for c in range(8)], [self.vecs[:, fb + c:fb + c + 1] for c in range(8)], DM, EPS,
                         [yT[:, c, :] for c in range(8)], [bx], [by], sq, bsq, rs, brs)
            for s in range(4):
                o, bo = o_r.next()
                for g in range(2):
                    ps, bps = self.bank()
                    for j in range(4):
                        c = g * 4 + j
                        self.I("pe", "transpose", [by, self.bconst], [bps], out=ps[:, j * 128:(j + 1) * 128], in_=yT[:, c, s * 128:(s + 1) * 128],
                               identity=self.identF[:])
                    self.copy(self.evac_eng(), o[:, g * 512:(g + 1) * 512], ps, [bps], [bo])
                r0 = t * TT + s * 128
                self.dma("sp", self.out[r0:r0 + 128, :], o[:], r=[bo], w=[self.db("out", t)])

    def build(self):
        ph = self.phases
        want = lambda name: ph is None or name in ph
        if want("t_in"):
            self.p_transpose_in()
        for l in range(self.n_layers):
            if want(f"inproj{l}"):
                self.p_inproj(l)
            if want(f"mla{l}"):
                self.p_mla(l)
            if want(f"moba{l}"):
                self.p_moba(l)
            if want(f"ret{l}"):
                self.p_ret(l)
            if want(f"merge{l}"):
                self.p_merge(l)
            if want(f"mlpup{l}"):
                self.p_mlp_up(l)
            if want(f"mlpdown{l}"):
                self.p_mlp_down(l)
        if want("final"):
            self.p_final()
        self.S.barrier()
        stats = self.S.emit()
        return self.nc, stats


_CACHE = {}


def get_program():
    if "nc" not in _CACHE:
        kb = KB()
        _CACHE["nc"], _CACHE["stats"] = kb.build()
    return _CACHE["nc"]


def make_in_map(inputs, b, consts, vecs):
    m = {"x": np.ascontiguousarray(inputs["x"][b])}
    for k in ("w_in", "mla_w_q_up", "mla_w_kv_up", "w_branch_mla", "w_branch_moba", "w_branch_ret", "w_out", "w_mlp_up", "w_mlp_down"):
        m[k] = np.ascontiguousarray(inputs[k], dtype=np.float32)
    m["vecs"] = vecs
    m.update(consts)
    return m


def kernel(**inputs):
    inputs = {k: np.asarray(v) for k, v in inputs.items()}
    consts, _ = make_consts()
    vecs = pack_vecs(inputs)
    nc = get_program()
    in_maps = [make_in_map(inputs, b, consts, vecs) for b in range(NCORES)]
    res = run_bass_kernel_spmd(nc, in_maps, core_ids=list(range(NCORES)))
    out = np.stack([np.asarray(res.results[b]["out"], dtype=np.float32) for b in range(NCORES)], axis=0)
    return out
```

```python
import numpy as np
import concourse.bass as bass
import concourse.mybir as mybir
from concourse.bass_utils import run_bass_kernel_spmd

F32 = mybir.dt.float32
BF16 = mybir.dt.bfloat16
AF = mybir.ActivationFunctionType
ALU = mybir.AluOpType
AX = mybir.AxisListType

SEQ = 4096
DM = 1024
TT = 512
NT = SEQ // TT
DEPTH = 2
NCORES = 4
EPS = 1e-6
GN_EPS = 1e-5
NEGB = -30000.0


class Buf:
    __slots__ = ("name", "w", "r", "rd")

    def __init__(self, name=""):
        self.name = name
        self.w = None
        self.r = {}
        self.rd = []


class _Op:
    __slots__ = ("eng", "fn", "deps", "is_dma", "idx", "need_inc", "semidx", "semval")


class Sched:
    ENG = ("pe", "act", "dve", "pool", "sp")

    def __init__(self, nc, n_dma_sems=40, same_eng_sync=("act", "dve", "pool")):
        import os
        if os.environ.get("K_VAR", "") == "pesync":
            same_eng_sync = ("act", "dve", "pool", "pe")
        self.nc = nc
        self.ops = []
        self.n_dma_sems = n_dma_sems
        self.same_eng_sync = set(same_eng_sync)
        self.last_on_eng = {}
        self.dmas_open = []
        self.cur_barrier = None

    def op(self, eng, fn, reads=(), writes=(), dma=False):
        o = _Op()
        o.eng = eng
        o.fn = fn
        o.is_dma = dma
        o.idx = len(self.ops)
        o.need_inc = False
        deps = set()
        for b in reads:
            if b.w is not None:
                deps.add(b.w)
        for b in writes:
            if b.w is not None:
                deps.add(b.w)
            deps.update(b.r.values())
            deps.update(b.rd)
        for b in reads:
            if dma:
                b.rd.append(o.idx)
            else:
                b.r[eng] = o.idx
        for b in writes:
            b.w = o.idx
            b.r = {}
            b.rd = []
        if self.cur_barrier is not None:
            deps.add(self.cur_barrier)
        deps.discard(o.idx)
        o.deps = deps
        self.ops.append(o)
        if dma:
            self.dmas_open.append(o.idx)
        else:
            self.last_on_eng[eng] = o.idx
        return o

    def barrier(self):
        deps = set(self.last_on_eng.values()) | set(self.dmas_open)
        o = self.op("sp", lambda e: e.nop())
        o.deps |= deps
        o.deps.discard(o.idx)
        self.dmas_open = []
        self.cur_barrier = o.idx
        return o

    def emit(self):
        nc = self.nc
        ops = self.ops
        engobj = {"pe": nc.tensor, "act": nc.scalar, "dve": nc.vector, "pool": nc.gpsimd, "sp": nc.sync}
        ses = self.same_eng_sync

        def needs_sync(o, od):
            if od.is_dma or o.is_dma:
                return True
            if od.eng != o.eng:
                return True
            return od.eng in ses

        for o in ops:
            for d in o.deps:
                od = ops[d]
                if not od.is_dma and needs_sync(o, od):
                    od.need_inc = True
        cnt = {e: 0 for e in self.ENG}
        ndma = 0
        qpool = {"sp": (0, 32), "pool": (32, 24), "act": (56, 8)}
        qcnt = {q: 0 for q in qpool}
        for o in ops:
            if o.is_dma:
                base, n = qpool[o.eng]
                k = qcnt[o.eng]
                qcnt[o.eng] += 1
                o.semidx = base + (k % n)
                o.semval = 16 * (k // n + 1)
                ndma += 1
            elif o.need_inc:
                cnt[o.eng] += 1
                o.semval = cnt[o.eng]
        engsem = {e: nc.alloc_semaphore(name=f"s_{e}") for e in self.ENG}
        dmasem = [nc.alloc_semaphore(name=f"s_dma{i}") for i in range(64)]
        waited = {}
        nwaits = 0
        self.trace = {e: [] for e in self.ENG}
        for o in ops:
            e = engobj[o.eng]
            waits = {}
            for d in o.deps:
                od = ops[d]
                if not needs_sync(o, od):
                    continue
                key = ("dma", od.semidx) if od.is_dma else ("eng", od.eng)
                if waits.get(key, 0) < od.semval:
                    waits[key] = od.semval
            if o.is_dma and o.semval > 16:
                key = ("dma", o.semidx)
                if waits.get(key, 0) < o.semval - 16:
                    waits[key] = o.semval - 16
            for key, val in waits.items():
                wk = (o.eng, key)
                if waited.get(wk, 0) >= val:
                    continue
                sem = dmasem[key[1]] if key[0] == "dma" else engsem[key[1]]
                e.wait_ge(sem, val)
                self.trace[o.eng].append(("w", key, val, o.idx))
                waited[wk] = val
                nwaits += 1
            ins = o.fn(e)
            if o.is_dma:
                ins.then_inc(dmasem[o.semidx], 16)
                self.trace[o.eng].append(("i", ("dma", o.semidx), 16, o.idx))
            elif o.need_inc:
                ins.then_inc(engsem[o.eng], 1)
                self.trace[o.eng].append(("i", ("eng", o.eng), 1, o.idx))
        self.stats = dict(n_ops=len(ops), n_waits=nwaits, incs=dict(cnt), n_dma=ndma)
        return self.stats


class Ring:
    def __init__(self, items):
        self.items = items
        self.i = 0

    def next(self):
        x = self.items[self.i % len(self.items)]
        self.i += 1
        return x


def _rope_tab(dim, theta):
    inv = (1.0 / (np.float32(theta) ** (np.arange(0, dim, 2, dtype=np.float32) / np.float32(dim)))).astype(np.float32)
    ang = (np.arange(SEQ, dtype=np.float32)[:, None] * inv[None, :]).astype(np.float32)
    c = np.cos(ang).astype(np.float32).T
    s = np.sin(ang).astype(np.float32).T
    cc = np.concatenate([c, c], 0)
    ss = np.concatenate([-s, s], 0)
    return np.ascontiguousarray(np.stack([cc, ss], 0))


def make_consts():
    c = {}
    c["identF"] = np.eye(128, dtype=np.float32)
    c["rope_mla"] = _rope_tab(32, 500000.0)
    rm = _rope_tab(16, 500000.0)
    pad = np.zeros((2, 32, SEQ), np.float32)
    pad[:, 0:16] = rm
    pad[0, 16:32] = 1.0
    c["rope_moba"] = pad
    c["rope_ret"] = _rope_tab(64, 10000.0)
    k = np.arange(128)
    c["mask01"] = (k[:, None] <= k[None, :]).astype(np.float32)
    c["onehot"] = (np.arange(16)[:, None] == (np.arange(SEQ)[None, :] // 256)).astype(np.float32)
    lg = np.log(np.float32(1.0) - np.float32(2.0) ** (np.float32(-5.0) - np.arange(4, dtype=np.float32))).astype(np.float32)
    pos = np.arange(128, dtype=np.float32)
    diff = pos[:, None] - pos[None, :]
    dec = np.where(diff >= 0, np.exp(lg[:, None, None] * diff), 0.0).astype(np.float32)
    c["decayT"] = np.ascontiguousarray(dec.transpose(0, 2, 1) * np.float32(0.125)).astype(np.float32)
    xi = np.exp(lg[:, None] * (pos + 1.0)).astype(np.float32)
    c["xi"] = np.ascontiguousarray(np.broadcast_to(np.tile(xi, (1, 4))[:, None, :], (4, 64, 512))).astype(np.float32)
    zeta = np.exp(lg[:, None] * (127.0 - pos)).astype(np.float32) * np.float32(0.125)
    c["zeta"] = np.ascontiguousarray(zeta.T).astype(np.float32)
    gam = [float(np.exp(lg[h] * np.float32(128.0))) for h in range(4)]
    return c, gam


VSTR = 27
NV = DEPTH * VSTR + 8


def pack_vecs(inp):
    v = np.zeros((128, NV), np.float32)
    for l in range(DEPTH):
        b = l * VSTR
        v[:, b + 0:b + 8] = inp["attn_norm"][l].reshape(8, 128).T
        v[:, b + 8:b + 16] = inp["mlp_norm"][l].reshape(8, 128).T
        v[:, b + 16:b + 18] = inp["mla_q_norm"][l].reshape(2, 128).T
        v[:, b + 18:b + 19] = inp["mla_kv_norm"][l].reshape(1, 128).T
        v[:, b + 19:b + 23] = inp["ret_gn_w"][l].reshape(4, 128).T
        v[:, b + 23:b + 27] = inp["ret_gn_b"][l].reshape(4, 128).T
    v[:, DEPTH * VSTR:DEPTH * VSTR + 8] = inp["final_norm"].reshape(8, 128).T
    return v


class KB:
    SB_BASE = 20480
    SB_TOP = 229376

    def __init__(self, n_layers=DEPTH, debug=False, phases=None):
        self.nc = nc = bass.Bass("TRN2", target_bir_lowering=False)
        self.S = Sched(nc)
        self.debug = debug
        self.n_layers = n_layers
        self.phases = phases
        self.top = self.SB_BASE
        self.uid = 0
        L = DEPTH
        inp = lambda name, shape: nc.dram_tensor(name, shape, F32, kind="ExternalInput").ap()
        self.x = inp("x", [SEQ, DM])
        self.w_in = inp("w_in", [L, DM, 6560])
        self.w_q_up = inp("mla_w_q_up", [L, 256, 768])
        self.w_kv_up = inp("mla_w_kv_up", [L, 128, 1024])
        self.w_b = [inp("w_branch_mla", [L, 512, DM]), inp("w_branch_moba", [L, 512, DM]), inp("w_branch_ret", [L, 512, DM])]
        self.w_out = inp("w_out", [L, DM, DM])
        self.w_up = inp("w_mlp_up", [L, DM, 4096])
        self.w_down = inp("w_mlp_down", [L, 4096, DM])
        self.vecs_d = inp("vecs", [128, NV])
        self.c_identF = inp("identF", [128, 128])
        self.c_rope_mla = inp("rope_mla", [2, 32, SEQ])
        self.c_rope_moba = inp("rope_moba", [2, 32, SEQ])
        self.c_rope_ret = inp("rope_ret", [2, 64, SEQ])
        self.c_mask01 = inp("mask01", [128, 128])
        self.c_onehot = inp("onehot", [16, SEQ])
        self.c_decayT = inp("decayT", [4, 128, 128])
        self.c_xi = inp("xi", [4, 64, 512])
        self.c_zeta = inp("zeta", [128, 4])
        self.out = nc.dram_tensor("out", [SEQ, DM], F32, kind="ExternalOutput").ap()
        kind = "ExternalOutput" if debug else "Internal"
        scr = lambda name, shape, dt: nc.dram_tensor(name, shape, dt, kind=kind).ap()
        self.XT = scr("XT", [8, 128, SEQ], F32)
        self.HT = scr("HT", [8, 128, SEQ], BF16)
        self.CQ = scr("CQ", [2, 128, SEQ], BF16)
        self.CKV = scr("CKV", [128, SEQ], BF16)
        self.KPE = scr("KPE", [32, SEQ], BF16)
        self.MQ = scr("MQ", [8, 64, SEQ], BF16)
        self.MK = scr("MK", [8, 64, SEQ], BF16)
        self.MV = scr("MV", [SEQ, 512], BF16)
        self.RQ = scr("RQ", [4, 64, SEQ], BF16)
        self.RK = scr("RK", [4, 64, SEQ], BF16)
        self.RV = scr("RV", [SEQ, 512], BF16)
        self.RG = scr("RG", [4, 128, SEQ], F32)
        self.OB = [scr("OMLA", [4, 128, SEQ], BF16), scr("OMOBA", [4, 128, SEQ], BF16), scr("ORET", [4, 128, SEQ], BF16)]
        self.ACTS = scr("ACTS", [32, 128, SEQ], BF16)
        self.dbuf = {}
        self.psF = nc.alloc_psum_tensor("psF", [128, 7, 512], F32)
        self.psB = nc.alloc_psum_tensor("psB", [128, 1024], BF16)
        self.pbuf = [Buf(f"ps{i}") for i in range(7)]
        self.pbufB = Buf("psB")
        self.bank_i = 0
        _, self.gam = make_consts()
        self.identF = self.sb([128, 128], F32)
        self.identB = self.sb([128, 128], BF16)
        self.onesB = self.sb([128, 128], BF16)
        self.mask01 = self.sb([128, 128], BF16)
        self.vecs = self.sb([128, NV], F32)
        self.bconst = Buf("const")
        self.dma("sp", self.identF[:], self.c_identF, w=[self.bconst])
        self.dma("pool", self.identB[:], self.c_identF, w=[self.bconst])
        self.dma("pool", self.mask01[:], self.c_mask01, w=[self.bconst])
        self.dma("sp", self.vecs[:], self.vecs_d, w=[self.bconst])
        self.I("pool", "memset", [], [self.bconst], ap=self.onesB[:], constant=1.0)
        self.phase_base = self.top

    def sb(self, shape, dtype, name=None):
        sz = int(np.prod(shape[1:])) * (4 if dtype == F32 else 2)
        sz = (sz + 63) // 64 * 64
        self.uid += 1
        t = self.nc.alloc_sbuf_tensor_at(name or f"t{self.uid}", list(shape), dtype, offset=self.top)
        self.top += sz
        assert self.top <= self.SB_TOP, f"SBUF overflow {self.top}"
        return t

    def sbr(self, n, shape, dtype):
        return Ring([(self.sb(shape, dtype), Buf()) for _ in range(n)])

    def begin_phase(self):
        self.S.barrier()
        self.top = self.phase_base

    def I(self, eng, meth, r, w, **kw):
        return self.S.op(eng, lambda e: getattr(e, meth)(**kw), r, w)

    def dma(self, eng, out, in_, r=(), w=()):
        return self.S.op(eng, lambda e: e.dma_start(out=out, in_=in_), r, w, dma=True)

    def bank(self):
        i = self.bank_i % 7
        self.bank_i += 1
        return self.psF[:, i, :], self.pbuf[i]

    def db(self, name, t=None):
        k = (name, t)
        if k not in self.dbuf:
            self.dbuf[k] = Buf(f"{name}{t}")
        return self.dbuf[k]

    def dball(self, name):
        return [self.db(name, t) for t in range(NT)]

    def vcol(self, l, off, n=1):
        b = l * VSTR + off
        return self.vecs[:, b:b + n]

    def evac_eng(self):
        self._ev = getattr(self, "_ev", 0) + 1
        return "act" if self._ev % 2 else "dve"

    def copy(self, eng, out, in_, r, w):
        if eng == "act":
            return self.I("act", "copy", r, w, out=out, in_=in_)
        return self.I(eng, "tensor_copy", r, w, out=out, in_=in_)

    def rmsnorm(self, xaps, gaps, nfeat, eps, outaps, rb, wb, sq, bsq, rs, brs, nparts=128):
        n = len(xaps)
        for c in range(n):
            self.I("act", "activation", rb, [bsq], out=sq[0:nparts, c, :], in_=xaps[c], func=AF.Square)
        ps, bps = self.bank()
        for c in range(n):
            self.I("pe", "matmul", [bsq, self.bconst], [bps], out=ps, lhsT=self.onesB[0:nparts, :], rhs=sq[0:nparts, c, :],
                   start=(c == 0), stop=(c == n - 1))
        self.I("act", "activation", [bps], [brs], out=rs[:], in_=ps, func=AF.Sqrt, scale=1.0 / nfeat, bias=float(eps))
        self.I("dve", "reciprocal", [brs], [brs], out=rs[:], in_=rs[:])
        for c in range(n):
            self.I("dve", "scalar_tensor_tensor", list(rb) + [brs, self.bconst], wb, out=outaps[c], in0=xaps[c], scalar=gaps[c],
                   in1=rs[0:nparts, :], op0=ALU.mult, op1=ALU.mult)

    def p_transpose_in(self):
        self.begin_phase()
        xin = self.sbr(2, [128, DM], F32)
        xo = self.sbr(2, [128, 8, TT], F32)
        for t in range(NT):
            xot, bxo = xo.next()
            tiles = []
            for s in range(4):
                xt, bx = xin.next()
                r0 = t * TT + s * 128
                self.dma("sp", xt[:], self.x[r0:r0 + 128, :], w=[bx])
                for g in range(2):
                    ps, bps = self.bank()
                    for j in range(4):
                        c = g * 4 + j
                        self.I("pe", "transpose", [bx, self.bconst], [bps], out=ps[:, j * 128:(j + 1) * 128], in_=xt[:, c * 128:(c + 1) * 128],
                               identity=self.identF[:])
                    self.copy(self.evac_eng(), xot[:, g * 4:(g + 1) * 4, s * 128:(s + 1) * 128],
                              ps.rearrange("p (j n) -> p j n", n=128), [bps], [bxo])
            self.dma("sp", self.XT.rearrange("c p n -> p c n")[:, :, t * TT:(t + 1) * TT], xot[:], r=[bxo], w=[self.db("XT", t)])

    def p_inproj(self, l):
        self.begin_phase()
        W1 = self.sb([128, 8, 4288], BF16)
        bW = []
        win = self.w_in[l].rearrange("(c p) n -> p c n", p=128)

        def wl(d0, s0, n):
            b = Buf()
            self.dma("pool", W1[:, :, d0:d0 + n], win[:, :, s0:s0 + n], w=[b])
            bW.append(b)

        wl(0, 0, 416)
        wl(416, 400, 16)
        wl(432, 384, 16)
        wl(448, 416, 512)
        wl(1088, 928, 512)
        wl(1728, 1440, 512)
        wl(2240, 1952, 256)
        wl(2752, 2208, 256)
        wl(3264, 2464, 512)
        wl(3776, 2976, 512)
        for (dst, src, H, dh, half) in ((960, 416, 8, 16, 8), (1600, 928, 8, 16, 8), (2496, 1952, 4, 64, 32), (3008, 2208, 4, 64, 32)):
            for c in range(8):
                dstv = W1[:, c, dst:dst + H * dh].rearrange("p (h e) -> p h e", e=dh)
                srcv = win[:, c, src:src + H * 64].rearrange("p (h d) -> p h d", d=64)
                b = Buf()
                self.dma("pool", dstv[:, :, 0:half], srcv[:, :, half:2 * half], w=[b])
                self.dma("pool", dstv[:, :, half:2 * half], srcv[:, :, 0:half], w=[b])
                bW.append(b)
        xTr = self.sbr(1, [128, 8, TT], F32)
        hTr = self.sbr(2, [128, 8, TT], BF16)
        sq = self.sb([128, 8, TT], BF16)
        bsq = Buf()
        rsr = self.sbr(2, [128, TT], F32)
        cqf = self.sb([128, 3, TT], F32)
        bcqf = Buf()
        cqn = self.sb([128, 3, TT], BF16)
        bcqn = Buf()
        sq2 = self.sb([128, 2, TT], BF16)
        bsq2 = Buf()
        tabk = self.sb([32, 2, TT], F32)
        tabm = self.sb([32, 2, TT], F32)
        tabr = self.sb([64, 2, TT], F32)
        btab = Buf()
        t1r = self.sbr(3, [64, TT], F32)
        t2r = self.sbr(3, [64, TT], F32)
        kpe = self.sb([32, TT], BF16)
        bkpe = Buf()
        stq = self.sb([64, 8, TT], BF16)
        stk = self.sb([64, 8, TT], BF16)
        strq = self.sb([64, 4, TT], BF16)
        strk = self.sb([64, 4, TT], BF16)
        stg = self.sb([128, 4, TT], F32)
        stv = self.sbr(2, [128, 4, 512], BF16)
        bst = {k: Buf() for k in ("q", "k", "rq", "rk", "g")}
        import os
        dbg_nt = int(os.environ.get("K_NT", NT))
        dbg_stage = int(os.environ.get("K_STAGE", 99))
        for t in range(dbg_nt):
            tok = slice(t * TT, (t + 1) * TT)
            xT, bx = xTr.next()
            hT, bh = hTr.next()
            rs, brs = rsr.next()
            self.dma("sp", xT[:], self.XT.rearrange("c p n -> p c n")[:, :, tok], r=[self.db("XT", t)], w=[bx])
            self.dma("sp", tabk[:], self.c_rope_mla.rearrange("a p n -> p a n")[:, :, tok], w=[btab])
            self.dma("sp", tabm[:], self.c_rope_moba.rearrange("a p n -> p a n")[:, :, tok], w=[btab])
            self.dma("sp", tabr[:], self.c_rope_ret.rearrange("a p n -> p a n")[:, :, tok], w=[btab])
            self.rmsnorm([xT[:, c, :] for c in range(8)], [self.vcol(l, c) for c in range(8)], DM, EPS,
                         [hT[:, c, :] for c in range(8)], [bx], [bh], sq, bsq, rs, brs)
            self.dma("sp", self.HT.rearrange("c p n -> p c n")[:, :, tok], hT[:], r=[bh], w=[self.db("HT", t)])
            if dbg_stage < 1:
                continue

            def mm(M, col, N0=0, N1=TT):
                ps, bps = self.bank()
                for c in range(8):
                    self.I("pe", "matmul", [bh] + bW, [bps], out=ps[0:M, 0:N1 - N0], lhsT=W1[:, c, col:col + M], rhs=hT[:, c, N0:N1],
                           start=(c == 0), stop=(c == 7))
                return ps, bps

            for j in range(3):
                ps, bps = mm(128, j * 128)
                self.copy(self.evac_eng(), cqf[:, j, :], ps, [bps], [bcqf])
            rs2, brs2 = rsr.next()
            self.rmsnorm([cqf[:, c, :] for c in range(2)], [self.vcol(l, 16 + c) for c in range(2)], 256, EPS,
                         [cqn[:, c, :] for c in range(2)], [bcqf], [bcqn], sq2, bsq2, rs2, brs2)
            self.dma("sp", self.CQ.rearrange("c p n -> p c n")[:, :, tok], cqn[:, 0:2, :], r=[bcqn], w=[self.db("CQ", t)])
            rs3, brs3 = rsr.next()
            self.rmsnorm([cqf[:, 2, :]], [self.vcol(l, 18)], 128, EPS, [cqn[:, 2, :]], [bcqf], [bcqn], sq2, bsq2, rs3, brs3)
            self.dma("sp", self.CKV[:, tok], cqn[:, 2, :], r=[bcqn], w=[self.db("CKV", t)])

            def rope(psm, bpm, psr, bpr, n, tab, outap, wb):
                t1, b1 = t1r.next()
                t2, b2 = t2r.next()
                self.I("dve", "tensor_tensor", [bpm, btab], [b1], out=t1[0:n, :], in0=psm[0:n, :], in1=tab[0:n, 0, :], op=ALU.mult)
                self.I("dve", "tensor_tensor", [bpr, btab], [b2], out=t2[0:n, :], in0=psr[0:n, :], in1=tab[0:n, 1, :], op=ALU.mult)
                self.I("pool", "tensor_tensor", [b1, b2], wb, out=outap, in0=t1[0:n, :], in1=t2[0:n, :], op=ALU.add)

            if dbg_stage < 2:
                continue
            psm, bpm = mm(32, 384)
            psr, bpr = mm(32, 416)
            rope(psm, bpm, psr, bpr, 32, tabk, kpe[:], [bkpe])
            self.dma("sp", self.KPE[:, tok], kpe[:], r=[bkpe], w=[self.db("KPE", t)])
            if dbg_stage < 3:
                continue
            for (st, key, base, rbase, dst) in ((stq, "q", 448, 960, self.MQ), (stk, "k", 1088, 1600, self.MK)):
                for h in range(8):
                    psm, bpm = mm(64, base + h * 64)
                    psr, bpr = mm(32, rbase + h * 16)
                    self.I("act", "copy", [bpm], [bst[key]], out=st[32:64, h, :], in_=psm[32:64, :])
                    rope(psm, bpm, psr, bpr, 32, tabm, st[0:32, h, :], [bst[key]])
                self.dma("sp", dst.rearrange("h p n -> p h n")[:, :, tok], st[:], r=[bst[key]], w=[self.db("M" + key, t)])
            if dbg_stage < 4:
                continue
            for (st, key, base, rbase, dst) in ((strq, "rq", 2240, 2496, self.RQ), (strk, "rk", 2752, 3008, self.RK)):
                for h in range(4):
                    psm, bpm = mm(64, base + h * 64)
                    psr, bpr = mm(64, rbase + h * 64)
                    rope(psm, bpm, psr, bpr, 64, tabr, st[:, h, :], [bst[key]])
                self.dma("sp", dst.rearrange("h p n -> p h n")[:, :, tok], st[:], r=[bst[key]], w=[self.db(key, t)])
            if dbg_stage < 5:
                continue
            for j in range(4):
                ps, bps = mm(128, 3776 + j * 128)
                self.I("act", "activation", [bps], [bst["g"]], out=stg[:, j, :], in_=ps, func=AF.Silu)
            self.dma("sp", self.RG.rearrange("c p n -> p c n")[:, :, tok], stg[:], r=[bst["g"]], w=[self.db("RG", t)])
            if dbg_stage < 6:
                continue
            for (col, dst, key) in ((1728, self.MV, "MV"), (3264, self.RV, "RV")):
                sv, bsv = stv.next()
                for s in range(4):
                    ps, bps = self.bank()
                    for c in range(8):
                        self.I("pe", "matmul", [bh] + bW, [bps], out=ps, lhsT=hT[:, c, s * 128:(s + 1) * 128], rhs=W1[:, c, col:col + 512],
                               start=(c == 0), stop=(c == 7))
                    self.copy(self.evac_eng(), sv[:, s, :], ps, [bps], [bsv])
                self.dma("sp", dst[t * TT:(t + 1) * TT, :].rearrange("(s p) n -> p s n", p=128), sv[:], r=[bsv], w=[self.db(key, t)])

    def attn_tile(self, t, Kb, bK, Vb, bV, qT, bq, dk, scale, pss_r, pT_r, pso, bpso, LA=2, extra_reads=()):
        nkt = 4 * t + 4
        info = {}

        def emit_s(kt):
            j = kt - 4 * t
            q0 = max(0, j) * 128
            N = TT - q0
            (pi, bpi) = pss_r.next()
            pss = self.psF[:, pi, :]
            self.I("pe", "matmul", [bK, bq] + list(extra_reads), [bpi], out=pss[:, 0:N], lhsT=Kb[0:dk, kt * 128:(kt + 1) * 128], rhs=qT[0:dk, q0:TT],
                   start=True, stop=True)
            info[kt] = (pss, bpi, j, q0, N)

        for kt in range(min(LA, nkt)):
            emit_s(kt)
        for kt in range(nkt):
            if kt + LA < nkt:
                emit_s(kt + LA)
            pss, bpi, j, q0, N = info.pop(kt)
            pT, bpT = pT_r.next()
            self.I("act", "activation", [bpi], [bpT], out=pT[:, 0:N], in_=pss[:, 0:N], func=AF.Exp, scale=float(scale))
            if j >= 0:
                self.I("pool", "tensor_tensor", [bpT, self.bconst], [bpT], out=pT[:, 0:128], in0=pT[:, 0:128], in1=self.mask01[:], op=ALU.mult)
            self.I("pe", "matmul", [bV, bpT], [bpso], out=pso[:, q0:TT], lhsT=Vb[:, kt, :], rhs=pT[:, 0:N],
                   start=(kt == 0), stop=(kt == nkt - 1))

    def attn_finish(self, pso, bpso, rec_r, ost_r, dst_ap, dbuf):
        rec, brec = rec_r.next()
        ost, bost = ost_r.next()
        self.I("dve", "reciprocal", [bpso], [brec], out=rec[0:64, :], in_=pso[64:128, :])
        self.I("dve", "tensor_tensor", [bpso, brec], [bost], out=ost[0:64, :], in0=pso[0:64, :], in1=rec[0:64, :], op=ALU.mult)
        self.dma("sp", dst_ap, ost[0:64, :], r=[bost], w=[dbuf])

    def p_mla(self, l):
        self.begin_phase()
        CQs = self.sb([128, 2, SEQ], BF16)
        CKVs = self.sb([128, SEQ], BF16)
        bin_ = Buf()
        self.dma("sp", CQs[:], self.CQ.rearrange("c p n -> p c n"), r=self.dball("CQ"), w=[bin_])
        self.dma("sp", CKVs[:], self.CKV, r=self.dball("CKV"), w=[bin_])
        Wq = self.sb([128, 2, 8, 96], BF16)
        Wqr = self.sb([128, 2, 8, 96], BF16)
        Wkv = self.sb([128, 8, 128], BF16)
        bw = Buf()
        wq = self.w_q_up[l].rearrange("(c p) (h d) -> p c h d", p=128, d=96)
        self.I("pool", "memset", [], [bw], ap=Wqr[:], constant=0.0)
        for c in range(2):
            self.dma("pool", Wq[:, c, :, :], wq[:, c, :, :], w=[bw])
            self.dma("pool", Wqr[:, c, :, 64:80], wq[:, c, :, 80:96], w=[bw])
            self.dma("pool", Wqr[:, c, :, 80:96], wq[:, c, :, 64:80], w=[bw])
        self.dma("pool", Wkv[:], self.w_kv_up[l].rearrange("p (h d) -> p h d", d=128), w=[bw])
        tab = self.sb([96, 2, SEQ], F32)
        btab = Buf()
        self.dma("sp", tab[64:96, :, :], self.c_rope_mla.rearrange("a p n -> p a n"), w=[btab])
        Kr = self.sbr(2, [128, SEQ], BF16)
        Vr = self.sbr(2, [128, 32, 128], BF16)
        for (Kb, bK) in Kr.items:
            self.I("pool", "memset", [], [bK], ap=Kb[96:128, :], constant=0.0)
            self.dma("sp", Kb[64:96, :], self.KPE, r=self.dball("KPE"), w=[bK])
        for (Vb, bV) in Vr.items:
            self.I("pool", "memset", [], [bV], ap=Vb[:, :, 64:128], constant=1.0)
        qTr = self.sbr(3, [128, TT], BF16)
        for (q_, bq_) in qTr.items:
            self.I("pool", "memset", [], [bq_], ap=q_[96:128, :], constant=0.0)
        t1r = self.sbr(2, [96, TT], F32)
        t2r = self.sbr(2, [96, TT], F32)
        pT_r = self.sbr(4, [128, TT], BF16)
        rec_r = self.sbr(2, [64, TT], F32)
        ost_r = self.sbr(2, [64, TT], BF16)
        pss_r = Ring([(0, self.pbuf[0]), (1, self.pbuf[1]), (2, self.pbuf[2])])
        pso_r = Ring([(3, self.pbuf[3]), (4, self.pbuf[4])])
        aux_r = Ring([(5, self.pbuf[5]), (6, self.pbuf[6])])
        scale = 96 ** -0.5
        KV = {}

        def build_kv(h):
            Kb, bK = Kr.next()
            Vb, bV = Vr.next()
            KV[h] = (Kb, bK, Vb, bV)
            for t in range(NT):
                pi, bpi = aux_r.next()
                ps = self.psF[:, pi, :]
                self.I("pe", "matmul", [bin_, bw], [bpi], out=ps[0:64, :], lhsT=Wkv[:, h, 0:64], rhs=CKVs[:, t * TT:(t + 1) * TT], start=True, stop=True)
                self.copy(self.evac_eng(), Kb[0:64, t * TT:(t + 1) * TT], ps[0:64, :], [bpi], [bK])
            for g in range(4):
                pi, bpi = aux_r.next()
                ps = self.psF[:, pi, :]
                for j in range(8):
                    kt = g * 8 + j
                    self.I("pe", "matmul", [bin_, bw], [bpi], out=ps[:, j * 64:(j + 1) * 64], lhsT=CKVs[:, kt * 128:(kt + 1) * 128],
                           rhs=Wkv[:, h, 64:128], start=True, stop=True)
                self.copy(self.evac_eng(), Vb[:, g * 8:(g + 1) * 8, 0:64], ps.rearrange("p (j d) -> p j d", d=64), [bpi], [bV])

        def build_q(h, t):
            tok = slice(t * TT, (t + 1) * TT)
            qT, bq = qTr.next()
            pi, bpm = aux_r.next()
            psm = self.psF[:, pi, :]
            pi2, bpr = aux_r.next()
            psr = self.psF[:, pi2, :]
            for c in range(2):
                self.I("pe", "matmul", [bin_, bw], [bpm], out=psm[0:96, :], lhsT=Wq[:, c, h, :], rhs=CQs[:, c, tok], start=(c == 0), stop=(c == 1))
            for c in range(2):
                self.I("pe", "matmul", [bin_, bw], [bpr], out=psr[0:96, :], lhsT=Wqr[:, c, h, :], rhs=CQs[:, c, tok], start=(c == 0), stop=(c == 1))
            self.I("act", "copy", [bpm], [bq], out=qT[0:64, :], in_=psm[0:64, :])
            t1, b1 = t1r.next()
            t2, b2 = t2r.next()
            self.I("dve", "tensor_tensor", [bpm, btab], [b1], out=t1[64:96, :], in0=psm[64:96, :], in1=tab[64:96, 0, tok], op=ALU.mult)
            self.I("dve", "tensor_tensor", [bpr, btab], [b2], out=t2[64:96, :], in0=psr[64:96, :], in1=tab[64:96, 1, tok], op=ALU.mult)
            self.I("pool", "tensor_tensor", [b1, b2], [bq], out=qT[64:96, :], in0=t1[64:96, :], in1=t2[64:96, :], op=ALU.add)
            return qT, bq

        items = [(h, t) for h in range(8) for t in range(NT)]
        build_kv(0)
        prepared = {items[0]: build_q(*items[0])}
        for idx, (h, t) in enumerate(items):
            if idx + 1 < len(items):
                hn, tn = items[idx + 1]
                if tn == 0:
                    build_kv(hn)
                prepared[(hn, tn)] = build_q(hn, tn)
            qT, bq = prepared.pop((h, t))
            Kb, bK, Vb, bV = KV[h]
            tok = slice(t * TT, (t + 1) * TT)
            po, bpso = pso_r.next()
            pso = self.psF[:, po, :]
            self.attn_tile(t, Kb, bK, Vb, bV, qT, bq, 128, scale, pss_r, pT_r, pso, bpso)
            self.attn_finish(pso, bpso, rec_r, ost_r, self.OB[0][h // 2, (h % 2) * 64:(h % 2) * 64 + 64, tok], self.db("OMLA", t))

    def p_moba(self, l):
        self.begin_phase()
        Qr = self.sbr(2, [128, SEQ], BF16)
        Kr = self.sbr(2, [128, SEQ], BF16)
        Vr = self.sbr(2, [128, 32, 128], BF16)
        for (Kb, bK) in Kr.items:
            self.I("pool", "memset", [], [bK], ap=Kb[64:128, :], constant=0.0)
            self.dma("pool", Kb[64:80, :], self.c_onehot, w=[bK])
        for (Qb, bQ) in Qr.items:
            self.I("pool", "memset", [], [bQ], ap=Qb[64:128, :], constant=0.0)
        for (Vb, bV) in Vr.items:
            self.I("pool", "memset", [], [bV], ap=Vb[:, :, 64:128], constant=1.0)
        km_r = [(self.sb([64, 16], F32), self.sb([64, 16], BF16), Buf()) for _ in range(2)]
        gate_r = self.sbr(3, [128, 16], F32)
        top_r = self.sbr(3, [128, 8], F32)
        mbp_r = self.sbr(3, [128, 4, 80], BF16)
        for (m, bm) in mbp_r.items:
            self.I("pool", "memset", [], [bm], ap=m[:], constant=0.0)
        pT_r = self.sbr(4, [128, TT], BF16)
        rec_r = self.sbr(2, [64, TT], F32)
        ost_r = self.sbr(2, [64, TT], BF16)
        pss_r = Ring([(0, self.pbuf[0]), (1, self.pbuf[1]), (2, self.pbuf[2])])
        pso_r = Ring([(3, self.pbuf[3]), (4, self.pbuf[4])])
        aux_r = Ring([(5, self.pbuf[5]), (6, self.pbuf[6])])
        bQm = [[Buf() for _ in range(NT)] for _ in range(2)]
        H = {}

        def load_head(h):
            Qb, bQ = Qr.items[h % 2]
            Kb, bK = Kr.items[h % 2]
            Vb, bV = Vr.items[h % 2]
            H[h] = (Qb, bQ, Kb, bK, Vb, bV)
            self.dma("sp", Qb[0:64, :], self.MQ[h], r=self.dball("Mq"), w=[bQ])
            self.dma("sp", Kb[0:64, :], self.MK[h], r=self.dball("Mk"), w=[bK])
            for g4 in range(4):
                self.dma("sp", Vb[:, g4 * 8:(g4 + 1) * 8, 0:64],
                         self.MV[g4 * 1024:(g4 + 1) * 1024, h * 64:(h + 1) * 64].rearrange("(k p) d -> p k d", p=128), r=self.dball("MV"), w=[bV])

        def kmean(h):
            Qb, bQ, Kb, bK, Vb, bV = H[h]
            kmf, kmb, bkm = km_r[h % 2]
            self.I("dve", "tensor_reduce", [bK], [bkm], out=kmf[:], in_=Kb[0:64, :].rearrange("p (n k) -> p n k", k=256), axis=AX.X, op=ALU.add)
            self.I("dve", "tensor_copy", [bkm], [bkm], out=kmb[:], in_=kmf[:])

        def prepare(h, t):
            Qb, bQ, Kb, bK, Vb, bV = H[h]
            kmf, kmb, bkm = km_r[h % 2]
            tok = slice(t * TT, (t + 1) * TT)
            mbp, bmb = mbp_r.next()
            pi, bpg = aux_r.next()
            psg = self.psF[:, pi, :]
            for s in range(4):
                g = 4 * t + s
                blk = g // 2
                if blk <= 3:
                    self.I("pool", "memset", [], [bmb], ap=mbp[:, s, 64:80], constant=0.0)
                    continue
                self.I("pe", "matmul", [bQ, bkm], [bpg], out=psg[:, s * 16:(s + 1) * 16], lhsT=Qb[0:64, g * 128:(g + 1) * 128], rhs=kmb[:],
                       start=True, stop=True)
                gt, bg = gate_r.next()
                tp, btp = top_r.next()
                self.I("dve", "memset", [], [bg], ap=gt[:], constant=-1e30)
                self.I("dve", "tensor_copy", [bpg], [bg], out=gt[:, 0:blk], in_=psg[:, s * 16:s * 16 + blk])
                self.I("dve", "max", [bg], [btp], out=tp[:], in_=gt[:])
                self.I("dve", "tensor_scalar", [bg, btp], [bmb], out=mbp[:, s, 64:80], in0=gt[:], scalar1=tp[:, 2:3], scalar2=NEGB,
                       op0=ALU.is_lt, op1=ALU.mult)
                self.I("dve", "memset", [], [bmb], ap=mbp[:, s, 64 + blk:80], constant=0.0)
            for s in range(4):
                self.I("pe", "transpose", [bmb, self.bconst], [self.pbufB], out=self.psB[0:80, s * 128:(s + 1) * 128], in_=mbp[:, s, :],
                       identity=self.identB[:])
            self.I("act", "copy", [self.pbufB, bQ], [bQm[h % 2][t]], out=Qb[64:80, tok], in_=self.psB[64:80, 0:TT])

        items = [(h, t) for h in range(8) for t in range(NT)]
        load_head(0)
        load_head(1)
        kmean(0)
        prepare(0, 0)
        for idx, (h, t) in enumerate(items):
            if t == 1 and h + 1 < 8 and h >= 1:
                load_head(h + 1)
            if idx + 1 < len(items):
                hn, tn = items[idx + 1]
                if tn == 0:
                    kmean(hn)
                prepare(hn, tn)
            Qb, bQ, Kb, bK, Vb, bV = H[h]
            tok = slice(t * TT, (t + 1) * TT)
            po, bpso = pso_r.next()
            pso = self.psF[:, po, :]
            self.attn_tile(t, Kb, bK, Vb, bV, Qb[:, tok], bQm[h % 2][t], 128, 0.125, pss_r, pT_r, pso, bpso, extra_reads=[bQ])
            self.attn_finish(pso, bpso, rec_r, ost_r, self.OB[1][h // 2, (h % 2) * 64:(h % 2) * 64 + 64, tok], self.db("OMOBA", t))

    def p_ret(self, l):
        self.begin_phase()
        Qr = self.sbr(2, [64, SEQ], BF16)
        Kr = self.sbr(2, [64, SEQ], BF16)
        Vr = self.sbr(2, [128, 32, 128], BF16)
        dec = self.sb([128, 4, 128], F32)
        xi = self.sb([64, 4, 512], F32)
        zeta = self.sb([128, 4], F32)
        bc = Buf()
        self.dma("sp", dec[:], self.c_decayT.rearrange("h j i -> j h i"), w=[bc])
        self.dma("sp", xi[:], self.c_xi.rearrange("h p n -> p h n"), w=[bc])
        self.dma("sp", zeta[:], self.c_zeta, w=[bc])
        qx_r = self.sbr(2, [64, TT], BF16)
        kz_r = self.sbr(3, [128, 64], BF16)
        sc_r = self.sbr(3, [128, 128], BF16)
        Rf = self.sb([64, 128], F32)
        bRf = Buf()
        Rb_r = self.sbr(3, [64, 128], BF16)
        G_r = self.sbr(2, [128, TT], F32)
        of_r = self.sbr(2, [128, TT], F32)
        ob_r = self.sbr(2, [128, 2, TT], BF16)
        m_r = self.sbr(2, [128, TT], F32)
        v_r = self.sbr(2, [128, TT], F32)
        y_r = self.sbr(2, [128, TT], BF16)
        po_r = Ring([(0, self.pbuf[0]), (1, self.pbuf[1])])
        ps_r = Ring([(2, self.pbuf[2]), (3, self.pbuf[3])])
        pk_r = Ring([(4, self.pbuf[4]), (5, self.pbuf[5])])
        for h in range(4):
            Qb, bQ = Qr.next()
            Kb, bK = Kr.next()
            Vb, bV = Vr.next()
            self.dma("sp", Qb[:], self.RQ[h], r=self.dball("rq"), w=[bQ])
            self.dma("sp", Kb[:], self.RK[h], r=self.dball("rk"), w=[bK])
            for g4 in range(4):
                self.dma("sp", Vb[:, g4 * 8:(g4 + 1) * 8, :],
                         self.RV[g4 * 1024:(g4 + 1) * 1024, h * 128:(h + 1) * 128].rearrange("(k p) d -> p k d", p=128), r=self.dball("RV"), w=[bV])
            self.I("dve", "memset", [], [bRf], ap=Rf[:], constant=0.0)
            Rb, bRb = Rb_r.next()
            self.I("dve", "tensor_copy", [bRf], [bRb], out=Rb[:], in_=Rf[:])
            for t in range(NT):
                tok = slice(t * TT, (t + 1) * TT)
                G, bG = G_r.next()
                self.dma("sp", G[:], self.RG[h, :, tok], r=[self.db("RG", t)], w=[bG])
                qx, bqx = qx_r.next()
                self.I("pool", "tensor_tensor", [bQ, bc], [bqx], out=qx[:], in0=Qb[:, tok], in1=xi[:, h, :], op=ALU.mult)
                pi_, bpo = po_r.next()
                po = self.psF[:, pi_, :]
                for s in range(4):
                    n = 4 * t + s
                    ch = slice(n * 128, (n + 1) * 128)
                    pi_, bps = ps_r.next()
                    ps = self.psF[:, pi_, :]
                    self.I("pe", "matmul", [bK, bQ], [bps], out=ps[:, 0:128], lhsT=Kb[:, ch], rhs=Qb[:, ch], start=True, stop=True)
                    sc, bsc = sc_r.next()
                    self.I("dve", "tensor_tensor", [bps, bc], [bsc], out=sc[:], in0=ps[:, 0:128], in1=dec[:, h, :], op=ALU.mult)
                    self.I("pe", "matmul", [bV, bsc], [bpo], out=po[:, s * 128:(s + 1) * 128], lhsT=Vb[:, n, :], rhs=sc[:], start=True, stop=False)
                    self.I("pe", "matmul", [bRb, bqx], [bpo], out=po[:, s * 128:(s + 1) * 128], lhsT=Rb[:], rhs=qx[:, s * 128:(s + 1) * 128],
                           start=False, stop=True)
                    self.I("pe", "transpose", [bK, self.bconst], [self.pbufB], out=self.psB[:, 0:64], in_=Kb[:, ch], identity=self.identB[0:64, 0:64])
                    kz, bkz = kz_r.next()
                    self.I("act", "activation", [self.pbufB, bc], [bkz], out=kz[:], in_=self.psB[:, 0:64], func=AF.Copy, scale=zeta[:, h:h + 1])
                    pi_, bpk = pk_r.next()
                    pk = self.psF[:, pi_, :]
                    self.I("pe", "matmul", [bkz, bV], [bpk], out=pk[0:64, 0:128], lhsT=kz[:], rhs=Vb[:, n, :], start=True, stop=True)
                    self.I("dve", "scalar_tensor_tensor", [bRf, bpk], [bRf], out=Rf[:], in0=Rf[:], scalar=float(self.gam[h]), in1=pk[0:64, 0:128],
                           op0=ALU.mult, op1=ALU.add)
                    Rb, bRb = Rb_r.next()
                    self.I("dve", "tensor_copy", [bRf], [bRb], out=Rb[:], in_=Rf[:])
                of, bof = of_r.next()
                ob, bob = ob_r.next()
                self.I("act", "copy", [bpo], [bof], out=of[:], in_=po)
                self.I("dve", "tensor_copy", [bof], [bob], out=ob[:, 0, :], in_=of[:])
                self.I("act", "activation", [bof], [bob], out=ob[:, 1, :], in_=of[:], func=AF.Square)
                pm, bpm = self.psF[:, 6, :], self.pbuf[6]
                self.I("pe", "matmul", [bob, self.bconst], [bpm], out=pm, lhsT=self.onesB[:], rhs=ob[:, 0, :], start=True, stop=True)
                pi_, bpq = ps_r.next()
                pq = self.psF[:, pi_, :]
                self.I("pe", "matmul", [bob, self.bconst], [bpq], out=pq, lhsT=self.onesB[:], rhs=ob[:, 1, :], start=True, stop=True)
                m, bm = m_r.next()
                v, bv = v_r.next()
                self.I("act", "mul", [bpm], [bm], out=m[:], in_=pm, mul=1.0 / 128)
                self.I("dve", "tensor_tensor", [bm], [bv], out=v[:], in0=m[:], in1=m[:], op=ALU.mult)
                self.I("dve", "scalar_tensor_tensor", [bpq, bv], [bv], out=v[:], in0=pq, scalar=1.0 / 128, in1=v[:], op0=ALU.mult, op1=ALU.subtract)
                self.I("act", "activation", [bv], [bv], out=v[:], in_=v[:], func=AF.Sqrt, scale=1.0, bias=float(GN_EPS))
                self.I("dve", "reciprocal", [bv], [bv], out=v[:], in_=v[:])
                self.I("pool", "tensor_tensor", [bof, bm], [bof], out=of[:], in0=of[:], in1=m[:], op=ALU.subtract)
                self.I("dve", "tensor_tensor", [bof, bv], [bof], out=of[:], in0=of[:], in1=v[:], op=ALU.mult)
                self.I("dve", "tensor_scalar", [bof, self.bconst], [bof], out=of[:], in0=of[:], scalar1=self.vcol(l, 19 + h), scalar2=self.vcol(l, 23 + h),
                       op0=ALU.mult, op1=ALU.add)
                y, by = y_r.next()
                self.I("pool", "tensor_tensor", [bof, bG], [by], out=y[:], in0=of[:], in1=G[:], op=ALU.mult)
                self.dma("sp", self.OB[2][h, :, tok], y[:], r=[by], w=[self.db("ORET", t)])

    def p_merge(self, l):
        self.begin_phase()
        Wg = self.sb([128, 8, 3072], BF16)
        Wb = self.sb([128, 3, 4, DM], BF16)
        Wo = self.sb([128, 8, DM], BF16)
        bw = Buf()
        win = self.w_in[l].rearrange("(c p) n -> p c n", p=128)
        for j in range(3):
            self.dma("pool", Wg[:, :, j * 1024:(j + 1) * 1024], win[:, :, 3488 + j * 1024:3488 + (j + 1) * 1024], w=[bw])
            self.dma("pool", Wb[:, j, :, :], self.w_b[j][l].rearrange("(c p) n -> p c n", p=128), w=[bw])
        self.dma("pool", Wo[:], self.w_out[l].rearrange("(c p) n -> p c n", p=128), w=[bw])
        hT_r = self.sbr(2, [128, 8, TT], BF16)
        o_r = self.sbr(2, [128, 3, 4, TT], BF16)
        xT_r = self.sbr(2, [128, 8, TT], F32)
        sig_r = self.sbr(3, [128, TT], F32)
        acc_r = self.sbr(2, [128, TT], F32)
        mg_r = self.sbr(2, [128, 8, TT], BF16)
        names = ("OMLA", "OMOBA", "ORET")
        for t in range(NT):
            tok = slice(t * TT, (t + 1) * TT)
            hT, bh = hT_r.next()
            o, bo = o_r.next()
            xT, bx = xT_r.next()
            mg, bmg = mg_r.next()
            self.dma("sp", hT[:], self.HT.rearrange("c p n -> p c n")[:, :, tok], r=[self.db("HT", t)], w=[bh])
            for j in range(3):
                self.dma("sp", o[:, j, :, :], self.OB[j].rearrange("c p n -> p c n")[:, :, tok], r=[self.db(names[j], t)], w=[bo])
            self.dma("sp", xT[:], self.XT.rearrange("c p n -> p c n")[:, :, tok], r=[self.db("XT", t)], w=[bx])
            for oc in range(8):
                acc, bacc = acc_r.next()
                for j in range(3):
                    pg, bpg = self.bank()
                    for c in range(8):
                        self.I("pe", "matmul", [bh, bw], [bpg], out=pg, lhsT=Wg[:, c, j * 1024 + oc * 128:j * 1024 + (oc + 1) * 128], rhs=hT[:, c, :],
                               start=(c == 0), stop=(c == 7))
                    pb, bpb = self.bank()
                    for c in range(4):
                        self.I("pe", "matmul", [bo, bw], [bpb], out=pb, lhsT=Wb[:, j, c, oc * 128:(oc + 1) * 128], rhs=o[:, j, c, :],
                               start=(c == 0), stop=(c == 3))
                    sg, bsg = sig_r.next()
                    self.I("act", "activation", [bpg], [bsg], out=sg[:], in_=pg, func=AF.Sigmoid)
                    if j == 0:
                        self.I("dve", "tensor_tensor", [bsg, bpb], [bacc], out=acc[:], in0=pb, in1=sg[:], op=ALU.mult)
                    else:
                        self.I("dve", "tensor_tensor", [bsg, bpb], [bsg], out=sg[:], in0=pb, in1=sg[:], op=ALU.mult)
                        if j == 1:
                            self.I("pool", "tensor_tensor", [bsg, bacc], [bacc], out=acc[:], in0=acc[:], in1=sg[:], op=ALU.add)
                        else:
                            self.I("pool", "tensor_tensor", [bsg, bacc], [bmg], out=mg[:, oc, :], in0=acc[:], in1=sg[:], op=ALU.add)
            for oc in range(8):
                po, bpo = self.bank()
                for c in range(8):
                    self.I("pe", "matmul", [bmg, bw], [bpo], out=po, lhsT=Wo[:, c, oc * 128:(oc + 1) * 128], rhs=mg[:, c, :], start=(c == 0), stop=(c == 7))
                self.I("dve", "tensor_tensor", [bpo, bx], [bx], out=xT[:, oc, :], in0=po, in1=xT[:, oc, :], op=ALU.add)
            self.dma("sp", self.XT.rearrange("c p n -> p c n")[:, :, tok], xT[:], r=[bx], w=[self.db("XT", t)])

    def p_mlp_up(self, l):
        self.begin_phase()
        Wu = self.sb([128, 8, 4096], BF16)
        bw = Buf()
        wu = self.w_up[l].rearrange("(c p) n -> p c n", p=128)
        for j in range(4):
            self.dma("pool", Wu[:, :, j * 1024:(j + 1) * 1024], wu[:, :, j * 1024:(j + 1) * 1024], w=[bw])
        xT_r = self.sbr(2, [128, 8, TT], F32)
        hT_r = self.sbr(2, [128, 8, TT], BF16)
        sq = self.sb([128, 8, TT], BF16)
        bsq = Buf()
        rs_r = self.sbr(2, [128, TT], F32)
        r_r = self.sbr(3, [128, TT], F32)
        a_r = self.sbr(2, [128, 8, TT], BF16)
        for t in range(NT):
            tok = slice(t * TT, (t + 1) * TT)
            xT, bx = xT_r.next()
            hT, bh = hT_r.next()
            rs, brs = rs_r.next()
            self.dma("sp", xT[:], self.XT.rearrange("c p n -> p c n")[:, :, tok], r=[self.db("XT", t)], w=[bx])
            self.rmsnorm([xT[:, c, :] for c in range(8)], [self.vcol(l, 8 + c) for c in range(8)], DM, EPS,
                         [hT[:, c, :] for c in range(8)], [bx], [bh], sq, bsq, rs, brs)
            for fg in range(4):
                a, ba = a_r.next()
                for fj in range(8):
                    f = fg * 8 + fj
                    ps, bps = self.bank()
                    for c in range(8):
                        self.I("pe", "matmul", [bh, bw], [bps], out=ps, lhsT=Wu[:, c, f * 128:(f + 1) * 128], rhs=hT[:, c, :], start=(c == 0), stop=(c == 7))
                    r, br = r_r.next()
                    self.I("act", "activation", [bps], [br], out=r[:], in_=ps, func=AF.Relu)
                    self.I("pool" if fj % 2 else "dve", "tensor_tensor", [br], [ba], out=a[:, fj, :], in0=r[:], in1=r[:], op=ALU.mult)
                self.dma("sp", self.ACTS[fg * 8:(fg + 1) * 8].rearrange("c p n -> p c n")[:, :, tok], a[:], r=[ba], w=[self.db("ACTS%d" % fg, t)])

    def p_mlp_down(self, l):
        self.begin_phase()
        Wd = self.sb([128, 32, DM], BF16)
        bw = Buf()
        wd = self.w_down[l].rearrange("(c p) n -> p c n", p=128)
        for j in range(4):
            self.dma("pool", Wd[:, j * 8:(j + 1) * 8, :], wd[:, j * 8:(j + 1) * 8, :], w=[bw])
        xT_r = self.sbr(2, [128, 8, TT], F32)
        a_r = self.sbr(2, [128, 32, TT], BF16)
        for t in range(NT):
            tok = slice(t * TT, (t + 1) * TT)
            xT, bx = xT_r.next()
            a, ba = a_r.next()
            self.dma("sp", xT[:], self.XT.rearrange("c p n -> p c n")[:, :, tok], r=[self.db("XT", t)], w=[bx])
            for fg in range(4):
                self.dma("sp", a[:, fg * 8:(fg + 1) * 8, :], self.ACTS[fg * 8:(fg + 1) * 8].rearrange("c p n -> p c n")[:, :, tok],
                         r=[self.db("ACTS%d" % fg, t)], w=[ba])
            for oc in range(8):
                po, bpo = self.bank()
                for f in range(32):
                    self.I("pe", "matmul", [ba, bw], [bpo], out=po, lhsT=Wd[:, f, oc * 128:(oc + 1) * 128], rhs=a[:, f, :], start=(f == 0), stop=(f == 31))
                self.I("dve", "tensor_tensor", [bpo, bx], [bx], out=xT[:, oc, :], in0=po, in1=xT[:, oc, :], op=ALU.add)
            self.dma("sp", self.XT.rearrange("c p n -> p c n")[:, :, tok], xT[:], r=[bx], w=[self.db("XT", t)])

    def p_final(self):
        self.begin_phase()
        xT_r = self.sbr(2, [128, 8, TT], F32)
        yT_r = self.sbr(2, [128, 8, TT], F32)
        sq = self.sb([128, 8, TT], BF16)
        bsq = Buf()
        rs_r = self.sbr(2, [128, TT], F32)
        o_r = self.sbr(2, [128, DM], F32)
        for t in range(NT):
            tok = slice(t * TT, (t + 1) * TT)
            xT, bx = xT_r.next()
            yT, by = yT_r.next()
            rs, brs = rs_r.next()
            self.dma("sp", xT[:], self.XT.rearrange("c p n -> p c n")[:, :, tok], r=[self.db("XT", t)], w=[bx])
            fb = DEPTH * VSTR
            self.rmsnorm([xT[:, c, :] for c in range(8)], [self.vecs[:, fb + c:fb + c + 1] for c in range(8)], DM, EPS,
                         [yT[:, c, :] for c in range(8)], [bx], [by], sq, bsq, rs, brs)
            for s in range(4):
                o, bo = o_r.next()
                for g in range(2):
                    ps, bps = self.bank()
                    for j in range(4):
                        c = g * 4 + j
                        self.I("pe", "transpose", [by, self.bconst], [bps], out=ps[:, j * 128:(j + 1) * 128], in_=yT[:, c, s * 128:(s + 1) * 128],
                               identity=self.identF[:])
                    self.copy(self.evac_eng(), o[:, g * 512:(g + 1) * 512], ps, [bps], [bo])
                r0 = t * TT + s * 128
                self.dma("sp", self.out[r0:r0 + 128, :], o[:], r=[bo], w=[self.db("out", t)])

    def build(self):
        ph = self.phases
        want = lambda name: ph is None or name in ph
        if want("t_in"):
            self.p_transpose_in()
        for l in range(self.n_layers):
            if want(f"inproj{l}"):
                self.p_inproj(l)
            if want(f"mla{l}"):
                self.p_mla(l)
            if want(f"moba{l}"):
                self.p_moba(l)
            if want(f"ret{l}"):
                self.p_ret(l)
            if want(f"merge{l}"):
                self.p_merge(l)
            if want(f"mlpup{l}"):
                self.p_mlp_up(l)
            if want(f"mlpdown{l}"):
                self.p_mlp_down(l)
        if want("final"):
            self.p_final()
        self.S.barrier()
        stats = self.S.emit()
        return self.nc, stats


_CACHE = {}


def get_program():
    if "nc" not in _CACHE:
        kb = KB()
        _CACHE["nc"], _CACHE["stats"] = kb.build()
    return _CACHE["nc"]


def make_in_map(inputs, b, consts, vecs):
    m = {"x": np.ascontiguousarray(inputs["x"][b])}
    for k in ("w_in", "mla_w_q_up", "mla_w_kv_up", "w_branch_mla", "w_branch_moba", "w_branch_ret", "w_out", "w_mlp_up", "w_mlp_down"):
        m[k] = np.ascontiguousarray(inputs[k], dtype=np.float32)
    m["vecs"] = vecs
    m.update(consts)
    return m


def kernel(**inputs):
    inputs = {k: np.asarray(v) for k, v in inputs.items()}
    consts, _ = make_consts()
    vecs = pack_vecs(inputs)
    nc = get_program()
    in_maps = [make_in_map(inputs, b, consts, vecs) for b in range(NCORES)]
    res = run_bass_kernel_spmd(nc, in_maps, core_ids=list(range(NCORES)))
    out = np.stack([np.asarray(res.results[b]["out"], dtype=np.float32) for b in range(NCORES)], axis=0)
    return out
```

```python
import numpy as np
import concourse.bass as bass
import concourse.mybir as mybir
from concourse.bass_utils import run_bass_kernel_spmd

F32 = mybir.dt.float32
BF16 = mybir.dt.bfloat16
AF = mybir.ActivationFunctionType
ALU = mybir.AluOpType
AX = mybir.AxisListType

SEQ = 4096
DM = 1024
TT = 512
NT = SEQ // TT
DEPTH = 2
NCORES = 4
EPS = 1e-6
GN_EPS = 1e-5
NEGB = -30000.0


class Buf:
    __slots__ = ("name", "w", "r", "rd")

    def __init__(self, name=""):
        self.name = name
        self.w = None
        self.r = {}
        self.rd = []


class _Op:
    __slots__ = ("eng", "fn", "deps", "is_dma", "idx", "need_inc", "semidx", "semval")


class Sched:
    ENG = ("pe", "act", "dve", "pool", "sp")

    def __init__(self, nc, n_dma_sems=40, same_eng_sync=("act", "dve", "pool")):
        import os
        if os.environ.get("K_VAR", "") == "pesync":
            same_eng_sync = ("act", "dve", "pool", "pe")
        self.nc = nc
        self.ops = []
        self.n_dma_sems = n_dma_sems
        self.same_eng_sync = set(same_eng_sync)
        self.last_on_eng = {}
        self.dmas_open = []
        self.cur_barrier = None

    def op(self, eng, fn, reads=(), writes=(), dma=False):
        o = _Op()
        o.eng = eng
        o.fn = fn
        o.is_dma = dma
        o.idx = len(self.ops)
        o.need_inc = False
        deps = set()
        for b in reads:
            if b.w is not None:
                deps.add(b.w)
        for b in writes:
            if b.w is not None:
                deps.add(b.w)
            deps.update(b.r.values())
            deps.update(b.rd)
        for b in reads:
            if dma:
                b.rd.append(o.idx)
            else:
                b.r[eng] = o.idx
        for b in writes:
            b.w = o.idx
            b.r = {}
            b.rd = []
        if self.cur_barrier is not None:
            deps.add(self.cur_barrier)
        deps.discard(o.idx)
        o.deps = deps
        self.ops.append(o)
        if dma:
            self.dmas_open.append(o.idx)
        else:
            self.last_on_eng[eng] = o.idx
        return o

    def barrier(self):
        deps = set(self.last_on_eng.values()) | set(self.dmas_open)
        o = self.op("sp", lambda e: e.nop())
        o.deps |= deps
        o.deps.discard(o.idx)
        self.dmas_open = []
        self.cur_barrier = o.idx
        return o

    def emit(self):
        nc = self.nc
        ops = self.ops
        engobj = {"pe": nc.tensor, "act": nc.scalar, "dve": nc.vector, "pool": nc.gpsimd, "sp": nc.sync}
        ses = self.same_eng_sync

        def needs_sync(o, od):
            if od.is_dma or o.is_dma:
                return True
            if od.eng != o.eng:
                return True
            return od.eng in ses

        for o in ops:
            for d in o.deps:
                od = ops[d]
                if not od.is_dma and needs_sync(o, od):
                    od.need_inc = True
        cnt = {e: 0 for e in self.ENG}
        ndma = 0
        qpool = {"sp": (0, 32), "pool": (32, 24), "act": (56, 8)}
        qcnt = {q: 0 for q in qpool}
        for o in ops:
            if o.is_dma:
                base, n = qpool[o.eng]
                k = qcnt[o.eng]
                qcnt[o.eng] += 1
                o.semidx = base + (k % n)
                o.semval = 16 * (k // n + 1)
                ndma += 1
            elif o.need_inc:
                cnt[o.eng] += 1
                o.semval = cnt[o.eng]
        engsem = {e: nc.alloc_semaphore(name=f"s_{e}") for e in self.ENG}
        dmasem = [nc.alloc_semaphore(name=f"s_dma{i}") for i in range(64)]
        waited = {}
        nwaits = 0
        self.trace = {e: [] for e in self.ENG}
        for o in ops:
            e = engobj[o.eng]
            waits = {}
            for d in o.deps:
                od = ops[d]
                if not needs_sync(o, od):
                    continue
                key = ("dma", od.semidx) if od.is_dma else ("eng", od.eng)
                if waits.get(key, 0) < od.semval:
                    waits[key] = od.semval
            if o.is_dma and o.semval > 16:
                key = ("dma", o.semidx)
                if waits.get(key, 0) < o.semval - 16:
                    waits[key] = o.semval - 16
            for key, val in waits.items():
                wk = (o.eng, key)
                if waited.get(wk, 0) >= val:
                    continue
                sem = dmasem[key[1]] if key[0] == "dma" else engsem[key[1]]
                e.wait_ge(sem, val)
                self.trace[o.eng].append(("w", key, val, o.idx))
                waited[wk] = val
                nwaits += 1
            ins = o.fn(e)
            if o.is_dma:
                ins.then_inc(dmasem[o.semidx], 16)
                self.trace[o.eng].append(("i", ("dma", o.semidx), 16, o.idx))
            elif o.need_inc:
                ins.then_inc(engsem[o.eng], 1)
                self.trace[o.eng].append(("i", ("eng", o.eng), 1, o.idx))
        self.stats = dict(n_ops=len(ops), n_waits=nwaits, incs=dict(cnt), n_dma=ndma)
        return self.stats


class Ring:
    def __init__(self, items):
        self.items = items
        self.i = 0

    def next(self):
        x = self.items[self.i % len(self.items)]
        self.i += 1
        return x


def _rope_tab(dim, theta):
    inv = (1.0 / (np.float32(theta) ** (np.arange(0, dim, 2, dtype=np.float32) / np.float32(dim)))).astype(np.float32)
    ang = (np.arange(SEQ, dtype=np.float32)[:, None] * inv[None, :]).astype(np.float32)
    c = np.cos(ang).astype(np.float32).T
    s = np.sin(ang).astype(np.float32).T
    cc = np.concatenate([c, c], 0)
    ss = np.concatenate([-s, s], 0)
    return np.ascontiguousarray(np.stack([cc, ss], 0))


def make_consts():
    c = {}
    c["identF"] = np.eye(128, dtype=np.float32)
    c["rope_mla"] = _rope_tab(32, 500000.0)
    rm = _rope_tab(16, 500000.0)
    pad = np.zeros((2, 32, SEQ), np.float32)
    pad[:, 0:16] = rm
    pad[0, 16:32] = 1.0
    c["rope_moba"] = pad
    c["rope_ret"] = _rope_tab(64, 10000.0)
    k = np.arange(128)
    c["mask01"] = (k[:, None] <= k[None, :]).astype(np.float32)
    c["onehot"] = (np.arange(16)[:, None] == (np.arange(SEQ)[None, :] // 256)).astype(np.float32)
    lg = np.log(np.float32(1.0) - np.float32(2.0) ** (np.float32(-5.0) - np.arange(4, dtype=np.float32))).astype(np.float32)
    pos = np.arange(128, dtype=np.float32)
    diff = pos[:, None] - pos[None, :]
    dec = np.where(diff >= 0, np.exp(lg[:, None, None] * diff), 0.0).astype(np.float32)
    c["decayT"] = np.ascontiguousarray(dec.transpose(0, 2, 1) * np.float32(0.125)).astype(np.float32)
    xi = np.exp(lg[:, None] * (pos + 1.0)).astype(np.float32)
    c["xi"] = np.ascontiguousarray(np.broadcast_to(np.tile(xi, (1, 4))[:, None, :], (4, 64, 512))).astype(np.float32)
    zeta = np.exp(lg[:, None] * (127.0 - pos)).astype(np.float32) * np.float32(0.125)
    c["zeta"] = np.ascontiguousarray(zeta.T).astype(np.float32)
    gam = [float(np.exp(lg[h] * np.float32(128.0))) for h in range(4)]
    return c, gam


VSTR = 27
NV = DEPTH * VSTR + 8


def pack_vecs(inp):
    v = np.zeros((128, NV), np.float32)
    for l in range(DEPTH):
        b = l * VSTR
        v[:, b + 0:b + 8] = inp["attn_norm"][l].reshape(8, 128).T
        v[:, b + 8:b + 16] = inp["mlp_norm"][l].reshape(8, 128).T
        v[:, b + 16:b + 18] = inp["mla_q_norm"][l].reshape(2, 128).T
        v[:, b + 18:b + 19] = inp["mla_kv_norm"][l].reshape(1, 128).T
        v[:, b + 19:b + 23] = inp["ret_gn_w"][l].reshape(4, 128).T
        v[:, b + 23:b + 27] = inp["ret_gn_b"][l].reshape(4, 128).T
    v[:, DEPTH * VSTR:DEPTH * VSTR + 8] = inp["final_norm"].reshape(8, 128).T
    return v


class KB:
    SB_BASE = 20480
    SB_TOP = 229376

    def __init__(self, n_layers=DEPTH, debug=False, phases=None):
        self.nc = nc = bass.Bass("TRN2", target_bir_lowering=False)
        self.S = Sched(nc)
        self.debug = debug
        self.n_layers = n_layers
        self.phases = phases
        self.top = self.SB_BASE
        self.uid = 0
        L = DEPTH
        inp = lambda name, shape: nc.dram_tensor(name, shape, F32, kind="ExternalInput").ap()
        self.x = inp("x", [SEQ, DM])
        self.w_in = inp("w_in", [L, DM, 6560])
        self.w_q_up = inp("mla_w_q_up", [L, 256, 768])
        self.w_kv_up = inp("mla_w_kv_up", [L, 128, 1024])
        self.w_b = [inp("w_branch_mla", [L, 512, DM]), inp("w_branch_moba", [L, 512, DM]), inp("w_branch_ret", [L, 512, DM])]
        self.w_out = inp("w_out", [L, DM, DM])
        self.w_up = inp("w_mlp_up", [L, DM, 4096])
        self.w_down = inp("w_mlp_down", [L, 4096, DM])
        self.vecs_d = inp("vecs", [128, NV])
        self.c_identF = inp("identF", [128, 128])
        self.c_rope_mla = inp("rope_mla", [2, 32, SEQ])
        self.c_rope_moba = inp("rope_moba", [2, 32, SEQ])
        self.c_rope_ret = inp("rope_ret", [2, 64, SEQ])
        self.c_mask01 = inp("mask01", [128, 128])
        self.c_onehot = inp("onehot", [16, SEQ])
        self.c_decayT = inp("decayT", [4, 128, 128])
        self.c_xi = inp("xi", [4, 64, 512])
        self.c_zeta = inp("zeta", [128, 4])
        self.out = nc.dram_tensor("out", [SEQ, DM], F32, kind="ExternalOutput").ap()
        kind = "ExternalOutput" if debug else "Internal"
        scr = lambda name, shape, dt: nc.dram_tensor(name, shape, dt, kind=kind).ap()
        self.XT = scr("XT", [8, 128, SEQ], F32)
        self.HT = scr("HT", [8, 128, SEQ], BF16)
        self.CQ = scr("CQ", [2, 128, SEQ], BF16)
        self.CKV = scr("CKV", [128, SEQ], BF16)
        self.KPE = scr("KPE", [32, SEQ], BF16)
        self.MQ = scr("MQ", [8, 64, SEQ], BF16)
        self.MK = scr("MK", [8, 64, SEQ], BF16)
        self.MV = scr("MV", [SEQ, 512], BF16)
        self.RQ = scr("RQ", [4, 64, SEQ], BF16)
        self.RK = scr("RK", [4, 64, SEQ], BF16)
        self.RV = scr("RV", [SEQ, 512], BF16)
        self.RG = scr("RG", [4, 128, SEQ], F32)
        self.OB = [scr("OMLA", [4, 128, SEQ], BF16), scr("OMOBA", [4, 128, SEQ], BF16), scr("ORET", [4, 128, SEQ], BF16)]
        self.ACTS = scr("ACTS", [32, 128, SEQ], BF16)
        self.dbuf = {}
        self.psF = nc.alloc_psum_tensor("psF", [128, 7, 512], F32)
        self.psB = nc.alloc_psum_tensor("psB", [128, 1024], BF16)
        self.pbuf = [Buf(f"ps{i}") for i in range(7)]
        self.pbufB = Buf("psB")
        self.bank_i = 0
        _, self.gam = make_consts()
        self.identF = self.sb([128, 128], F32)
        self.identB = self.sb([128, 128], BF16)
        self.onesB = self.sb([128, 128], BF16)
        self.mask01 = self.sb([128, 128], BF16)
        self.vecs = self.sb([128, NV], F32)
        self.bconst = Buf("const")
        self.dma("sp", self.identF[:], self.c_identF, w=[self.bconst])
        self.dma("pool", self.identB[:], self.c_identF, w=[self.bconst])
        self.dma("pool", self.mask01[:], self.c_mask01, w=[self.bconst])
        self.dma("sp", self.vecs[:], self.vecs_d, w=[self.bconst])
        self.I("pool", "memset", [], [self.bconst], ap=self.onesB[:], constant=1.0)
        self.phase_base = self.top

    def sb(self, shape, dtype, name=None):
        sz = int(np.prod(shape[1:])) * (4 if dtype == F32 else 2)
        sz = (sz + 63) // 64 * 64
        self.uid += 1
        t = self.nc.alloc_sbuf_tensor_at(name or f"t{self.uid}", list(shape), dtype, offset=self.top)
        self.top += sz
        assert self.top <= self.SB_TOP, f"SBUF overflow {self.top}"
        return t

    def sbr(self, n, shape, dtype):
        return Ring([(self.sb(shape, dtype), Buf()) for _ in range(n)])

    def begin_phase(self):
        self.S.barrier()
        self.top = self.phase_base

    def I(self, eng, meth, r, w, **kw):
        return self.S.op(eng, lambda e: getattr(e, meth)(**kw), r, w)

    def dma(self, eng, out, in_, r=(), w=()):
        return self.S.op(eng, lambda e: e.dma_start(out=out, in_=in_), r, w, dma=True)

    def bank(self):
        i = self.bank_i % 7
        self.bank_i += 1
        return self.psF[:, i, :], self.pbuf[i]

    def db(self, name, t=None):
        k = (name, t)
        if k not in self.dbuf:
            self.dbuf[k] = Buf(f"{name}{t}")
        return self.dbuf[k]

    def dball(self, name):
        return [self.db(name, t) for t in range(NT)]

    def vcol(self, l, off, n=1):
        b = l * VSTR + off
        return self.vecs[:, b:b + n]

    def evac_eng(self):
        self._ev = getattr(self, "_ev", 0) + 1
        return "act" if self._ev % 2 else "dve"

    def copy(self, eng, out, in_, r, w):
        if eng == "act":
            return self.I("act", "copy", r, w, out=out, in_=in_)
        return self.I(eng, "tensor_copy", r, w, out=out, in_=in_)

    def rmsnorm(self, xaps, gaps, nfeat, eps, outaps, rb, wb, sq, bsq, rs, brs, nparts=128):
        n = len(xaps)
        for c in range(n):
            self.I("act", "activation", rb, [bsq], out=sq[0:nparts, c, :], in_=xaps[c], func=AF.Square)
        ps, bps = self.bank()
        for c in range(n):
            self.I("pe", "matmul", [bsq, self.bconst], [bps], out=ps, lhsT=self.onesB[0:nparts, :], rhs=sq[0:nparts, c, :],
                   start=(c == 0), stop=(c == n - 1))
        self.I("act", "activation", [bps], [brs], out=rs[:], in_=ps, func=AF.Sqrt, scale=1.0 / nfeat, bias=float(eps))
        self.I("dve", "reciprocal", [brs], [brs], out=rs[:], in_=rs[:])
        for c in range(n):
            self.I("dve", "scalar_tensor_tensor", list(rb) + [brs, self.bconst], wb, out=outaps[c], in0=xaps[c], scalar=gaps[c],
                   in1=rs[0:nparts, :], op0=ALU.mult, op1=ALU.mult)

    def p_transpose_in(self):
        self.begin_phase()
        xin = self.sbr(2, [128, DM], F32)
        xo = self.sbr(2, [128, 8, TT], F32)
        for t in range(NT):
            xot, bxo = xo.next()
            tiles = []
            for s in range(4):
                xt, bx = xin.next()
                r0 = t * TT + s * 128
                self.dma("sp", xt[:], self.x[r0:r0 + 128, :], w=[bx])
                for g in range(2):
                    ps, bps = self.bank()
                    for j in range(4):
                        c = g * 4 + j
                        self.I("pe", "transpose", [bx, self.bconst], [bps], out=ps[:, j * 128:(j + 1) * 128], in_=xt[:, c * 128:(c + 1) * 128],
                               identity=self.identF[:])
                    self.copy(self.evac_eng(), xot[:, g * 4:(g + 1) * 4, s * 128:(s + 1) * 128],
                              ps.rearrange("p (j n) -> p j n", n=128), [bps], [bxo])
            self.dma("sp", self.XT.rearrange("c p n -> p c n")[:, :, t * TT:(t + 1) * TT], xot[:], r=[bxo], w=[self.db("XT", t)])

    def p_inproj(self, l):
        self.begin_phase()
        W1 = self.sb([128, 8, 4288], BF16)
        bW = []
        win = self.w_in[l].rearrange("(c p) n -> p c n", p=128)

        def wl(d0, s0, n):
            b = Buf()
            self.dma("pool", W1[:, :, d0:d0 + n], win[:, :, s0:s0 + n], w=[b])
            bW.append(b)

        wl(0, 0, 416)
        wl(416, 400, 16)
        wl(432, 384, 16)
        wl(448, 416, 512)
        wl(1088, 928, 512)
        wl(1728, 1440, 512)
        wl(2240, 1952, 256)
        wl(2752, 2208, 256)
        wl(3264, 2464, 512)
        wl(3776, 2976, 512)
        for (dst, src, H, dh, half) in ((960, 416, 8, 16, 8), (1600, 928, 8, 16, 8), (2496, 1952, 4, 64, 32), (3008, 2208, 4, 64, 32)):
            for c in range(8):
                dstv = W1[:, c, dst:dst + H * dh].rearrange("p (h e) -> p h e", e=dh)
                srcv = win[:, c, src:src + H * 64].rearrange("p (h d) -> p h d", d=64)
                b = Buf()
                self.dma("pool", dstv[:, :, 0:half], srcv[:, :, half:2 * half], w=[b])
                self.dma("pool", dstv[:, :, half:2 * half], srcv[:, :, 0:half], w=[b])
                bW.append(b)
        xTr = self.sbr(1, [128, 8, TT], F32)
        hTr = self.sbr(2, [128, 8, TT], BF16)
        sq = self.sb([128, 8, TT], BF16)
        bsq = Buf()
        rsr = self.sbr(2, [128, TT], F32)
        cqf = self.sb([128, 3, TT], F32)
        bcqf = Buf()
        cqn = self.sb([128, 3, TT], BF16)
        bcqn = Buf()
        sq2 = self.sb([128, 2, TT], BF16)
        bsq2 = Buf()
        tabk = self.sb([32, 2, TT], F32)
        tabm = self.sb([32, 2, TT], F32)
        tabr = self.sb([64, 2, TT], F32)
        btab = Buf()
        t1r = self.sbr(3, [64, TT], F32)
        t2r = self.sbr(3, [64, TT], F32)
        kpe = self.sb([32, TT], BF16)
        bkpe = Buf()
        stq = self.sb([64, 8, TT], BF16)
        stk = self.sb([64, 8, TT], BF16)
        strq = self.sb([64, 4, TT], BF16)
        strk = self.sb([64, 4, TT], BF16)
        stg = self.sb([128, 4, TT], F32)
        stv = self.sbr(2, [128, 4, 512], BF16)
        bst = {k: Buf() for k in ("q", "k", "rq", "rk", "g")}
        import os
        dbg_nt = int(os.environ.get("K_NT", NT))
        dbg_stage = int(os.environ.get("K_STAGE", 99))
        for t in range(dbg_nt):
            tok = slice(t * TT, (t + 1) * TT)
            xT, bx = xTr.next()
            hT, bh = hTr.next()
            rs, brs = rsr.next()
            self.dma("sp", xT[:], self.XT.rearrange("c p n -> p c n")[:, :, tok], r=[self.db("XT", t)], w=[bx])
            self.dma("sp", tabk[:], self.c_rope_mla.rearrange("a p n -> p a n")[:, :, tok], w=[btab])
            self.dma("sp", tabm[:], self.c_rope_moba.rearrange("a p n -> p a n")[:, :, tok], w=[btab])
            self.dma("sp", tabr[:], self.c_rope_ret.rearrange("a p n -> p a n")[:, :, tok], w=[btab])
            self.rmsnorm([xT[:, c, :] for c in range(8)], [self.vcol(l, c) for c in range(8)], DM, EPS,
                         [hT[:, c, :] for c in range(8)], [bx], [bh], sq, bsq, rs, brs)
            self.dma("sp", self.HT.rearrange("c p n -> p c n")[:, :, tok], hT[:], r=[bh], w=[self.db("HT", t)])
            if dbg_stage < 1:
                continue

            def mm(M, col, N0=0, N1=TT):
                ps, bps = self.bank()
                for c in range(8):
                    self.I("pe", "matmul", [bh] + bW, [bps], out=ps[0:M, 0:N1 - N0], lhsT=W1[:, c, col:col + M], rhs=hT[:, c, N0:N1],
                           start=(c == 0), stop=(c == 7))
                return ps, bps

            for j in range(3):
                ps, bps = mm(128, j * 128)
                self.copy(self.evac_eng(), cqf[:, j, :], ps, [bps], [bcqf])
            rs2, brs2 = rsr.next()
            self.rmsnorm([cqf[:, c, :] for c in range(2)], [self.vcol(l, 16 + c) for c in range(2)], 256, EPS,
                         [cqn[:, c, :] for c in range(2)], [bcqf], [bcqn], sq2, bsq2, rs2, brs2)
            self.dma("sp", self.CQ.rearrange("c p n -> p c n")[:, :, tok], cqn[:, 0:2, :], r=[bcqn], w=[self.db("CQ", t)])
            rs3, brs3 = rsr.next()
            self.rmsnorm([cqf[:, 2, :]], [self.vcol(l, 18)], 128, EPS, [cqn[:, 2, :]], [bcqf], [bcqn], sq2, bsq2, rs3, brs3)
            self.dma("sp", self.CKV[:, tok], cqn[:, 2, :], r=[bcqn], w=[self.db("CKV", t)])

            def rope(psm, bpm, psr, bpr, n, tab, outap, wb):
                t1, b1 = t1r.next()
                t2, b2 = t2r.next()
                self.I("dve", "tensor_tensor", [bpm, btab], [b1], out=t1[0:n, :], in0=psm[0:n, :], in1=tab[0:n, 0, :], op=ALU.mult)
                self.I("dve", "tensor_tensor", [bpr, btab], [b2], out=t2[0:n, :], in0=psr[0:n, :], in1=tab[0:n, 1, :], op=ALU.mult)
                self.I("pool", "tensor_tensor", [b1, b2], wb, out=outap, in0=t1[0:n, :], in1=t2[0:n, :], op=ALU.add)

            if dbg_stage < 2:
                continue
            psm, bpm = mm(32, 384)
            psr, bpr = mm(32, 416)
            rope(psm, bpm, psr, bpr, 32, tabk, kpe[:], [bkpe])
            self.dma("sp", self.KPE[:, tok], kpe[:], r=[bkpe], w=[self.db("KPE", t)])
            if dbg_stage < 3:
                continue
            for (st, key, base, rbase, dst) in ((stq, "q", 448, 960, self.MQ), (stk, "k", 1088, 1600, self.MK)):
                for h in range(8):
                    psm, bpm = mm(64, base + h * 64)
                    psr, bpr = mm(32, rbase + h * 16)
                    self.I("act", "copy", [bpm], [bst[key]], out=st[32:64, h, :], in_=psm[32:64, :])
                    rope(psm, bpm, psr, bpr, 32, tabm, st[0:32, h, :], [bst[key]])
                self.dma("sp", dst.rearrange("h p n -> p h n")[:, :, tok], st[:], r=[bst[key]], w=[self.db("M" + key, t)])
            if dbg_stage < 4:
                continue
            for (st, key, base, rbase, dst) in ((strq, "rq", 2240, 2496, self.RQ), (strk, "rk", 2752, 3008, self.RK)):
                for h in range(4):
                    psm, bpm = mm(64, base + h * 64)
                    psr, bpr = mm(64, rbase + h * 64)
                    rope(psm, bpm, psr, bpr, 64, tabr, st[:, h, :], [bst[key]])
                self.dma("sp", dst.rearrange("h p n -> p h n")[:, :, tok], st[:], r=[bst[key]], w=[self.db(key, t)])
            if dbg_stage < 5:
                continue
            for j in range(4):
                ps, bps = mm(128, 3776 + j * 128)
                self.I("act", "activation", [bps], [bst["g"]], out=stg[:, j, :], in_=ps, func=AF.Silu)
            self.dma("sp", self.RG.rearrange("c p n -> p c n")[:, :, tok], stg[:], r=[bst["g"]], w=[self.db("RG", t)])
            if dbg_stage < 6:
                continue
            for (col, dst, key) in ((1728, self.MV, "MV"), (3264, self.RV, "RV")):
                sv, bsv = stv.next()
                for s in range(4):
                    ps, bps = self.bank()
                    for c in range(8):
                        self.I("pe", "matmul", [bh] + bW, [bps], out=ps, lhsT=hT[:, c, s * 128:(s + 1) * 128], rhs=W1[:, c, col:col + 512],
                               start=(c == 0), stop=(c == 7))
                    self.copy(self.evac_eng(), sv[:, s, :], ps, [bps], [bsv])
                self.dma("sp", dst[t * TT:(t + 1) * TT, :].rearrange("(s p) n -> p s n", p=128), sv[:], r=[bsv], w=[self.db(key, t)])

    def attn_tile(self, t, Kb, bK, Vb, bV, qT, bq, dk, scale, pss_r, pT_r, pso, bpso, LA=2, extra_reads=(), mid_hook=None):
        nkt = 4 * t + 4
        info = {}

        def emit_s(kt):
            j = kt - 4 * t
            q0 = max(0, j) * 128
            N = TT - q0
            (pi, bpi) = pss_r.next()
            pss = self.psF[:, pi, :]
            self.I("pe", "matmul", [bK, bq] + list(extra_reads), [bpi], out=pss[:, 0:N], lhsT=Kb[0:dk, kt * 128:(kt + 1) * 128], rhs=qT[0:dk, q0:TT],
                   start=True, stop=True)
            info[kt] = (pss, bpi, j, q0, N)

        for kt in range(min(LA, nkt)):
            emit_s(kt)
        for kt in range(nkt):
            if kt + LA < nkt:
                emit_s(kt + LA)
            if mid_hook is not None and kt == min(1, nkt - 1):
                mid_hook()
            pss, bpi, j, q0, N = info.pop(kt)
            pT, bpT = pT_r.next()
            self.I("act", "activation", [bpi], [bpT], out=pT[:, 0:N], in_=pss[:, 0:N], func=AF.Exp, scale=float(scale))
            if j >= 0:
                self.I("pool", "tensor_tensor", [bpT, self.bconst], [bpT], out=pT[:, 0:128], in0=pT[:, 0:128], in1=self.mask01[:], op=ALU.mult)
            self.I("pe", "matmul", [bV, bpT], [bpso], out=pso[:, q0:TT], lhsT=Vb[:, kt, :], rhs=pT[:, 0:N],
                   start=(kt == 0), stop=(kt == nkt - 1))

    def attn_finish(self, pso, bpso, rec_r, ost_r, dst_ap, dbuf):
        rec, brec = rec_r.next()
        ost, bost = ost_r.next()
        self.I("dve", "reciprocal", [bpso], [brec], out=rec[0:64, :], in_=pso[64:128, :])
        self.I("dve", "tensor_tensor", [bpso, brec], [bost], out=ost[0:64, :], in0=pso[0:64, :], in1=rec[0:64, :], op=ALU.mult)
        self.dma("sp", dst_ap, ost[0:64, :], r=[bost], w=[dbuf])

    def p_mla(self, l):
        self.begin_phase()
        CQs = self.sb([128, 2, SEQ], BF16)
        CKVs = self.sb([128, SEQ], BF16)
        bin_ = Buf()
        self.dma("sp", CQs[:], self.CQ.rearrange("c p n -> p c n"), r=self.dball("CQ"), w=[bin_])
        self.dma("sp", CKVs[:], self.CKV, r=self.dball("CKV"), w=[bin_])
        Wq = self.sb([128, 2, 8, 96], BF16)
        Wqr = self.sb([128, 2, 8, 96], BF16)
        Wkv = self.sb([128, 8, 128], BF16)
        bw = Buf()
        wq = self.w_q_up[l].rearrange("(c p) (h d) -> p c h d", p=128, d=96)
        self.I("pool", "memset", [], [bw], ap=Wqr[:], constant=0.0)
        for c in range(2):
            self.dma("pool", Wq[:, c, :, :], wq[:, c, :, :], w=[bw])
            self.dma("pool", Wqr[:, c, :, 64:80], wq[:, c, :, 80:96], w=[bw])
            self.dma("pool", Wqr[:, c, :, 80:96], wq[:, c, :, 64:80], w=[bw])
        self.dma("pool", Wkv[:], self.w_kv_up[l].rearrange("p (h d) -> p h d", d=128), w=[bw])
        tab = self.sb([96, 2, SEQ], F32)
        btab = Buf()
        self.dma("sp", tab[64:96, :, :], self.c_rope_mla.rearrange("a p n -> p a n"), w=[btab])
        Kr = self.sbr(2, [128, SEQ], BF16)
        Vr = self.sbr(2, [128, 32, 128], BF16)
        for (Kb, bK) in Kr.items:
            self.I("pool", "memset", [], [bK], ap=Kb[96:128, :], constant=0.0)
            self.dma("sp", Kb[64:96, :], self.KPE, r=self.dball("KPE"), w=[bK])
        for (Vb, bV) in Vr.items:
            self.I("pool", "memset", [], [bV], ap=Vb[:, :, 64:128], constant=1.0)
        qTr = self.sbr(3, [128, TT], BF16)
        for (q_, bq_) in qTr.items:
            self.I("pool", "memset", [], [bq_], ap=q_[96:128, :], constant=0.0)
        t1r = self.sbr(2, [96, TT], F32)
        t2r = self.sbr(2, [96, TT], F32)
        pT_r = self.sbr(4, [128, TT], BF16)
        rec_r = self.sbr(2, [64, TT], F32)
        ost_r = self.sbr(2, [64, TT], BF16)
        pss_r = Ring([(0, self.pbuf[0]), (1, self.pbuf[1]), (2, self.pbuf[2])])
        pso_r = Ring([(3, self.pbuf[3]), (4, self.pbuf[4])])
        aux_r = Ring([(5, self.pbuf[5]), (6, self.pbuf[6])])
        scale = 96 ** -0.5
        KV = {}

        def build_kv(h):
            Kb, bK = Kr.next()
            Vb, bV = Vr.next()
            KV[h] = (Kb, bK, Vb, bV)
            for t in range(NT):
                pi, bpi = aux_r.next()
                ps = self.psF[:, pi, :]
                self.I("pe", "matmul", [bin_, bw], [bpi], out=ps[0:64, :], lhsT=Wkv[:, h, 0:64], rhs=CKVs[:, t * TT:(t + 1) * TT], start=True, stop=True)
                self.copy(self.evac_eng(), Kb[0:64, t * TT:(t + 1) * TT], ps[0:64, :], [bpi], [bK])
            for g in range(4):
                pi, bpi = aux_r.next()
                ps = self.psF[:, pi, :]
                for j in range(8):
                    kt = g * 8 + j
                    self.I("pe", "matmul", [bin_, bw], [bpi], out=ps[:, j * 64:(j + 1) * 64], lhsT=CKVs[:, kt * 128:(kt + 1) * 128],
                           rhs=Wkv[:, h, 64:128], start=True, stop=True)
                self.copy(self.evac_eng(), Vb[:, g * 8:(g + 1) * 8, 0:64], ps.rearrange("p (j d) -> p j d", d=64), [bpi], [bV])

        def build_q(h, t):
            tok = slice(t * TT, (t + 1) * TT)
            qT, bq = qTr.next()
            pi, bpm = aux_r.next()
            psm = self.psF[:, pi, :]
            pi2, bpr = aux_r.next()
            psr = self.psF[:, pi2, :]
            for c in range(2):
                self.I("pe", "matmul", [bin_, bw], [bpm], out=psm[0:96, :], lhsT=Wq[:, c, h, :], rhs=CQs[:, c, tok], start=(c == 0), stop=(c == 1))
            for c in range(2):
                self.I("pe", "matmul", [bin_, bw], [bpr], out=psr[0:96, :], lhsT=Wqr[:, c, h, :], rhs=CQs[:, c, tok], start=(c == 0), stop=(c == 1))
            self.I("act", "copy", [bpm], [bq], out=qT[0:64, :], in_=psm[0:64, :])
            t1, b1 = t1r.next()
            t2, b2 = t2r.next()
            self.I("dve", "tensor_tensor", [bpm, btab], [b1], out=t1[64:96, :], in0=psm[64:96, :], in1=tab[64:96, 0, tok], op=ALU.mult)
            self.I("dve", "tensor_tensor", [bpr, btab], [b2], out=t2[64:96, :], in0=psr[64:96, :], in1=tab[64:96, 1, tok], op=ALU.mult)
            self.I("pool", "tensor_tensor", [b1, b2], [bq], out=qT[64:96, :], in0=t1[64:96, :], in1=t2[64:96, :], op=ALU.add)
            return qT, bq

        items = [(h, t) for h in range(8) for t in range(NT)]
        build_kv(0)
        prepared = {items[0]: build_q(*items[0])}
        for idx, (h, t) in enumerate(items):
            if idx + 1 < len(items):
                hn, tn = items[idx + 1]
                if tn == 0:
                    build_kv(hn)
                prepared[(hn, tn)] = build_q(hn, tn)
            qT, bq = prepared.pop((h, t))
            Kb, bK, Vb, bV = KV[h]
            tok = slice(t * TT, (t + 1) * TT)
            po, bpso = pso_r.next()
            pso = self.psF[:, po, :]
            self.attn_tile(t, Kb, bK, Vb, bV, qT, bq, 128, scale, pss_r, pT_r, pso, bpso)
            self.attn_finish(pso, bpso, rec_r, ost_r, self.OB[0][h // 2, (h % 2) * 64:(h % 2) * 64 + 64, tok], self.db("OMLA", t))

    def p_moba(self, l):
        self.begin_phase()
        Qr = self.sbr(2, [128, SEQ], BF16)
        Kr = self.sbr(2, [128, SEQ], BF16)
        Vr = self.sbr(2, [128, 32, 128], BF16)
        for (Kb, bK) in Kr.items:
            self.I("pool", "memset", [], [bK], ap=Kb[64:128, :], constant=0.0)
            self.dma("pool", Kb[64:80, :], self.c_onehot, w=[bK])
        for (Qb, bQ) in Qr.items:
            self.I("pool", "memset", [], [bQ], ap=Qb[64:128, :], constant=0.0)
        for (Vb, bV) in Vr.items:
            self.I("pool", "memset", [], [bV], ap=Vb[:, :, 64:128], constant=1.0)
        km_r = [(self.sb([64, 16], F32), self.sb([64, 16], BF16), Buf()) for _ in range(2)]
        gate_r = self.sbr(4, [128, 16], F32)
        top_r = self.sbr(4, [128, 8], F32)
        mbp_r = self.sbr(4, [128, 4, 80], BF16)
        for (m, bm) in mbp_r.items:
            self.I("pool", "memset", [], [bm], ap=m[:], constant=0.0)
        pT_r = self.sbr(4, [128, TT], BF16)
        rec_r = self.sbr(2, [64, TT], F32)
        ost_r = self.sbr(2, [64, TT], BF16)
        pss_r = Ring([(0, self.pbuf[0]), (1, self.pbuf[1]), (2, self.pbuf[2])])
        pso_r = Ring([(3, self.pbuf[3]), (4, self.pbuf[4])])
        aux_r = Ring([(5, self.pbuf[5]), (6, self.pbuf[6])])
        bQm = [[Buf() for _ in range(NT)] for _ in range(2)]
        H = {}

        def load_head(h):
            Qb, bQ = Qr.items[h % 2]
            Kb, bK = Kr.items[h % 2]
            Vb, bV = Vr.items[h % 2]
            H[h] = (Qb, bQ, Kb, bK, Vb, bV)
            self.dma("sp", Qb[0:64, :], self.MQ[h], r=self.dball("Mq"), w=[bQ])
            self.dma("sp", Kb[0:64, :], self.MK[h], r=self.dball("Mk"), w=[bK])
            for g4 in range(4):
                self.dma("sp", Vb[:, g4 * 8:(g4 + 1) * 8, 0:64],
                         self.MV[g4 * 1024:(g4 + 1) * 1024, h * 64:(h + 1) * 64].rearrange("(k p) d -> p k d", p=128), r=self.dball("MV"), w=[bV])

        def kmean(h):
            Qb, bQ, Kb, bK, Vb, bV = H[h]
            kmf, kmb, bkm = km_r[h % 2]
            self.I("dve", "tensor_reduce", [bK], [bkm], out=kmf[:], in_=Kb[0:64, :].rearrange("p (n k) -> p n k", k=256), axis=AX.X, op=ALU.add)
            self.I("dve", "tensor_copy", [bkm], [bkm], out=kmb[:], in_=kmf[:])

        def prepare(h, t):
            Qb, bQ, Kb, bK, Vb, bV = H[h]
            kmf, kmb, bkm = km_r[h % 2]
            tok = slice(t * TT, (t + 1) * TT)
            mbp, bmb = mbp_r.next()
            pi, bpg = aux_r.next()
            psg = self.psF[:, pi, :]
            for s in range(4):
                g = 4 * t + s
                blk = g // 2
                if blk <= 3:
                    self.I("pool", "memset", [], [bmb], ap=mbp[:, s, 64:80], constant=0.0)
                    continue
                self.I("pe", "matmul", [bQ, bkm], [bpg], out=psg[:, s * 16:(s + 1) * 16], lhsT=Qb[0:64, g * 128:(g + 1) * 128], rhs=kmb[:],
                       start=True, stop=True)
                gt, bg = gate_r.next()
                tp, btp = top_r.next()
                self.I("dve", "memset", [], [bg], ap=gt[:], constant=-1e30)
                self.I("dve", "tensor_copy", [bpg], [bg], out=gt[:, 0:blk], in_=psg[:, s * 16:s * 16 + blk])
                self.I("dve", "max", [bg], [btp], out=tp[:], in_=gt[:])
                self.I("dve", "tensor_scalar", [bg, btp], [bmb], out=mbp[:, s, 64:80], in0=gt[:], scalar1=tp[:, 2:3], scalar2=NEGB,
                       op0=ALU.is_lt, op1=ALU.mult)
                self.I("dve", "memset", [], [bmb], ap=mbp[:, s, 64 + blk:80], constant=0.0)
            return mbp, bmb

        def prepare2(h, t, mbp, bmb):
            Qb, bQ, Kb, bK, Vb, bV = H[h]
            tok = slice(t * TT, (t + 1) * TT)
            for s in range(4):
                self.I("pe", "transpose", [bmb, self.bconst], [self.pbufB], out=self.psB[0:80, s * 128:(s + 1) * 128], in_=mbp[:, s, :],
                       identity=self.identB[:])
            self.I("act", "copy", [self.pbufB, bQ], [bQm[h % 2][t]], out=Qb[64:80, tok], in_=self.psB[64:80, 0:TT])

        items = [(h, t) for h in range(8) for t in range(NT)]
        load_head(0)
        load_head(1)
        kmean(0)
        prepare2(0, 0, *prepare(0, 0))
        pend = None
        if len(items) > 1:
            pend = (items[1], prepare(*items[1]))
        for idx, (h, t) in enumerate(items):
            if t == 1 and h + 1 < 8 and h >= 1:
                load_head(h + 1)
            nxt = None
            if idx + 2 < len(items):
                hn, tn = items[idx + 2]
                if tn == 0:
                    kmean(hn)
                nxt = ((hn, tn), prepare(hn, tn))
            Qb, bQ, Kb, bK, Vb, bV = H[h]
            tok = slice(t * TT, (t + 1) * TT)
            po, bpso = pso_r.next()
            pso = self.psF[:, po, :]
            self.attn_tile(t, Kb, bK, Vb, bV, Qb[:, tok], bQm[h % 2][t], 128, 0.125, pss_r, pT_r, pso, bpso, extra_reads=[bQ],
                           mid_hook=(lambda p=pend: prepare2(p[0][0], p[0][1], *p[1])) if pend is not None else None)
            self.attn_finish(pso, bpso, rec_r, ost_r, self.OB[1][h // 2, (h % 2) * 64:(h % 2) * 64 + 64, tok], self.db("OMOBA", t))
            pend = nxt

    def p_ret(self, l):
        self.begin_phase()
        Qr = self.sbr(2, [64, SEQ], BF16)
        Kr = self.sbr(2, [64, SEQ], BF16)
        Vr = self.sbr(2, [128, 32, 128], BF16)
        dec = self.sb([128, 4, 128], F32)
        xi = self.sb([64, 4, 512], F32)
        zeta = self.sb([128, 4], F32)
        bc = Buf()
        self.dma("sp", dec[:], self.c_decayT.rearrange("h j i -> j h i"), w=[bc])
        self.dma("sp", xi[:], self.c_xi.rearrange("h p n -> p h n"), w=[bc])
        self.dma("sp", zeta[:], self.c_zeta, w=[bc])
        qx_r = self.sbr(2, [64, TT], BF16)
        kz_r = self.sbr(3, [128, 64], BF16)
        sc_r = self.sbr(3, [128, 128], BF16)
        Rf = self.sb([64, 128], F32)
        bRf = Buf()
        Rb_r = self.sbr(3, [64, 128], BF16)
        G_r = self.sbr(2, [128, TT], F32)
        of_r = self.sbr(2, [128, TT], F32)
        ob_r = self.sbr(2, [128, 2, TT], BF16)
        m_r = self.sbr(2, [128, TT], F32)
        v_r = self.sbr(2, [128, TT], F32)
        y_r = self.sbr(2, [128, TT], BF16)
        po_r = Ring([(0, self.pbuf[0]), (1, self.pbuf[1])])
        ps_r = Ring([(2, self.pbuf[2]), (3, self.pbuf[3])])
        pk_r = Ring([(4, self.pbuf[4]), (5, self.pbuf[5])])
        for h in range(4):
            Qb, bQ = Qr.next()
            Kb, bK = Kr.next()
            Vb, bV = Vr.next()
            self.dma("sp", Qb[:], self.RQ[h], r=self.dball("rq"), w=[bQ])
            self.dma("sp", Kb[:], self.RK[h], r=self.dball("rk"), w=[bK])
            for g4 in range(4):
                self.dma("sp", Vb[:, g4 * 8:(g4 + 1) * 8, :],
                         self.RV[g4 * 1024:(g4 + 1) * 1024, h * 128:(h + 1) * 128].rearrange("(k p) d -> p k d", p=128), r=self.dball("RV"), w=[bV])
            self.I("dve", "memset", [], [bRf], ap=Rf[:], constant=0.0)
            Rb, bRb = Rb_r.next()
            self.I("dve", "tensor_copy", [bRf], [bRb], out=Rb[:], in_=Rf[:])
            for t in range(NT):
                tok = slice(t * TT, (t + 1) * TT)
                G, bG = G_r.next()
                self.dma("sp", G[:], self.RG[h, :, tok], r=[self.db("RG", t)], w=[bG])
                qx, bqx = qx_r.next()
                self.I("pool", "tensor_tensor", [bQ, bc], [bqx], out=qx[:], in0=Qb[:, tok], in1=xi[:, h, :], op=ALU.mult)
                pi_, bpo = po_r.next()
                po = self.psF[:, pi_, :]
                for s in range(4):
                    n = 4 * t + s
                    ch = slice(n * 128, (n + 1) * 128)
                    pi_, bps = ps_r.next()
                    ps = self.psF[:, pi_, :]
                    self.I("pe", "matmul", [bK, bQ], [bps], out=ps[:, 0:128], lhsT=Kb[:, ch], rhs=Qb[:, ch], start=True, stop=True)
                    sc, bsc = sc_r.next()
                    self.I("dve", "tensor_tensor", [bps, bc], [bsc], out=sc[:], in0=ps[:, 0:128], in1=dec[:, h, :], op=ALU.mult)
                    self.I("pe", "matmul", [bV, bsc], [bpo], out=po[:, s * 128:(s + 1) * 128], lhsT=Vb[:, n, :], rhs=sc[:], start=True, stop=False)
                    self.I("pe", "matmul", [bRb, bqx], [bpo], out=po[:, s * 128:(s + 1) * 128], lhsT=Rb[:], rhs=qx[:, s * 128:(s + 1) * 128],
                           start=False, stop=True)
                    self.I("pe", "transpose", [bK, self.bconst], [self.pbufB], out=self.psB[:, 0:64], in_=Kb[:, ch], identity=self.identB[0:64, 0:64])
                    kz, bkz = kz_r.next()
                    self.I("act", "activation", [self.pbufB, bc], [bkz], out=kz[:], in_=self.psB[:, 0:64], func=AF.Copy, scale=zeta[:, h:h + 1])
                    pi_, bpk = pk_r.next()
                    pk = self.psF[:, pi_, :]
                    self.I("pe", "matmul", [bkz, bV], [bpk], out=pk[0:64, 0:128], lhsT=kz[:], rhs=Vb[:, n, :], start=True, stop=True)
                    self.I("dve", "scalar_tensor_tensor", [bRf, bpk], [bRf], out=Rf[:], in0=Rf[:], scalar=float(self.gam[h]), in1=pk[0:64, 0:128],
                           op0=ALU.mult, op1=ALU.add)
                    Rb, bRb = Rb_r.next()
                    self.I("dve", "tensor_copy", [bRf], [bRb], out=Rb[:], in_=Rf[:])
                of, bof = of_r.next()
                ob, bob = ob_r.next()
                self.I("act", "copy", [bpo], [bof], out=of[:], in_=po)
                self.I("dve", "tensor_copy", [bof], [bob], out=ob[:, 0, :], in_=of[:])
                self.I("act", "activation", [bof], [bob], out=ob[:, 1, :], in_=of[:], func=AF.Square)
                pm, bpm = self.psF[:, 6, :], self.pbuf[6]
                self.I("pe", "matmul", [bob, self.bconst], [bpm], out=pm, lhsT=self.onesB[:], rhs=ob[:, 0, :], start=True, stop=True)
                pi_, bpq = ps_r.next()
                pq = self.psF[:, pi_, :]
                self.I("pe", "matmul", [bob, self.bconst], [bpq], out=pq, lhsT=self.onesB[:], rhs=ob[:, 1, :], start=True, stop=True)
                m, bm = m_r.next()
                v, bv = v_r.next()
                self.I("act", "mul", [bpm], [bm], out=m[:], in_=pm, mul=1.0 / 128)
                self.I("dve", "tensor_tensor", [bm], [bv], out=v[:], in0=m[:], in1=m[:], op=ALU.mult)
                self.I("dve", "scalar_tensor_tensor", [bpq, bv], [bv], out=v[:], in0=pq, scalar=1.0 / 128, in1=v[:], op0=ALU.mult, op1=ALU.subtract)
                self.I("act", "activation", [bv], [bv], out=v[:], in_=v[:], func=AF.Sqrt, scale=1.0, bias=float(GN_EPS))
                self.I("dve", "reciprocal", [bv], [bv], out=v[:], in_=v[:])
                self.I("pool", "tensor_tensor", [bof, bm], [bof], out=of[:], in0=of[:], in1=m[:], op=ALU.subtract)
                self.I("dve", "tensor_tensor", [bof, bv], [bof], out=of[:], in0=of[:], in1=v[:], op=ALU.mult)
                self.I("dve", "tensor_scalar", [bof, self.bconst], [bof], out=of[:], in0=of[:], scalar1=self.vcol(l, 19 + h), scalar2=self.vcol(l, 23 + h),
                       op0=ALU.mult, op1=ALU.add)
                y, by = y_r.next()
                self.I("pool", "tensor_tensor", [bof, bG], [by], out=y[:], in0=of[:], in1=G[:], op=ALU.mult)
                self.dma("sp", self.OB[2][h, :, tok], y[:], r=[by], w=[self.db("ORET", t)])

    def p_merge(self, l):
        self.begin_phase()
        Wg = self.sb([128, 8, 3072], BF16)
        Wb = self.sb([128, 3, 4, DM], BF16)
        Wo = self.sb([128, 8, DM], BF16)
        bw = Buf()
        win = self.w_in[l].rearrange("(c p) n -> p c n", p=128)
        for j in range(3):
            self.dma("pool", Wg[:, :, j * 1024:(j + 1) * 1024], win[:, :, 3488 + j * 1024:3488 + (j + 1) * 1024], w=[bw])
            self.dma("pool", Wb[:, j, :, :], self.w_b[j][l].rearrange("(c p) n -> p c n", p=128), w=[bw])
        self.dma("pool", Wo[:], self.w_out[l].rearrange("(c p) n -> p c n", p=128), w=[bw])
        hT_r = self.sbr(2, [128, 8, TT], BF16)
        o_r = self.sbr(2, [128, 3, 4, TT], BF16)
        xT_r = self.sbr(2, [128, 8, TT], F32)
        sig_r = self.sbr(3, [128, TT], F32)
        acc_r = self.sbr(2, [128, TT], F32)
        mg_r = self.sbr(2, [128, 8, TT], BF16)
        names = ("OMLA", "OMOBA", "ORET")
        def load3(t):
            tok = slice(t * TT, (t + 1) * TT)
            hT, bh = hT_r.next()
            o, bo = o_r.next()
            xT, bx = xT_r.next()
            self.dma("sp", hT[:], self.HT.rearrange("c p n -> p c n")[:, :, tok], r=[self.db("HT", t)], w=[bh])
            for j in range(3):
                self.dma("sp", o[:, j, :, :], self.OB[j].rearrange("c p n -> p c n")[:, :, tok], r=[self.db(names[j], t)], w=[bo])
            self.dma("sp", xT[:], self.XT.rearrange("c p n -> p c n")[:, :, tok], r=[self.db("XT", t)], w=[bx])
            return hT, bh, o, bo, xT, bx

        nxt3 = load3(0)
        for t in range(NT):
            tok = slice(t * TT, (t + 1) * TT)
            hT, bh, o, bo, xT, bx = nxt3
            if t + 1 < NT:
                nxt3 = load3(t + 1)
            mg, bmg = mg_r.next()
            for oc in range(8):
                acc, bacc = acc_r.next()
                for j in range(3):
                    pg, bpg = self.bank()
                    for c in range(8):
                        self.I("pe", "matmul", [bh, bw], [bpg], out=pg, lhsT=Wg[:, c, j * 1024 + oc * 128:j * 1024 + (oc + 1) * 128], rhs=hT[:, c, :],
                               start=(c == 0), stop=(c == 7))
                    pb, bpb = self.bank()
                    for c in range(4):
                        self.I("pe", "matmul", [bo, bw], [bpb], out=pb, lhsT=Wb[:, j, c, oc * 128:(oc + 1) * 128], rhs=o[:, j, c, :],
                               start=(c == 0), stop=(c == 3))
                    sg, bsg = sig_r.next()
                    self.I("act", "activation", [bpg], [bsg], out=sg[:], in_=pg, func=AF.Sigmoid)
                    if j == 0:
                        self.I("dve", "tensor_tensor", [bsg, bpb], [bacc], out=acc[:], in0=pb, in1=sg[:], op=ALU.mult)
                    else:
                        self.I("dve", "tensor_tensor", [bsg, bpb], [bsg], out=sg[:], in0=pb, in1=sg[:], op=ALU.mult)
                        if j == 1:
                            self.I("pool", "tensor_tensor", [bsg, bacc], [bacc], out=acc[:], in0=acc[:], in1=sg[:], op=ALU.add)
                        else:
                            self.I("pool", "tensor_tensor", [bsg, bacc], [bmg], out=mg[:, oc, :], in0=acc[:], in1=sg[:], op=ALU.add)
            for oc in range(8):
                po, bpo = self.bank()
                for c in range(8):
                    self.I("pe", "matmul", [bmg, bw], [bpo], out=po, lhsT=Wo[:, c, oc * 128:(oc + 1) * 128], rhs=mg[:, c, :], start=(c == 0), stop=(c == 7))
                self.I("dve", "tensor_tensor", [bpo, bx], [bx], out=xT[:, oc, :], in0=po, in1=xT[:, oc, :], op=ALU.add)
            self.dma("sp", self.XT.rearrange("c p n -> p c n")[:, :, tok], xT[:], r=[bx], w=[self.db("XT", t)])

    def p_mlp_up(self, l):
        self.begin_phase()
        Wu = self.sb([128, 8, 4096], BF16)
        bw = Buf()
        wu = self.w_up[l].rearrange("(c p) n -> p c n", p=128)
        for j in range(4):
            self.dma("pool", Wu[:, :, j * 1024:(j + 1) * 1024], wu[:, :, j * 1024:(j + 1) * 1024], w=[bw])
        xT_r = self.sbr(2, [128, 8, TT], F32)
        hT_r = self.sbr(2, [128, 8, TT], BF16)
        sq = self.sb([128, 8, TT], BF16)
        bsq = Buf()
        rs_r = self.sbr(2, [128, TT], F32)
        r_r = self.sbr(3, [128, TT], F32)
        a_r = self.sbr(2, [128, 8, TT], BF16)
        def load4(t):
            tok = slice(t * TT, (t + 1) * TT)
            xT, bx = xT_r.next()
            hT, bh = hT_r.next()
            rs, brs = rs_r.next()
            self.dma("sp", xT[:], self.XT.rearrange("c p n -> p c n")[:, :, tok], r=[self.db("XT", t)], w=[bx])
            self.rmsnorm([xT[:, c, :] for c in range(8)], [self.vcol(l, 8 + c) for c in range(8)], DM, EPS,
                         [hT[:, c, :] for c in range(8)], [bx], [bh], sq, bsq, rs, brs)
            return hT, bh

        nxt4 = load4(0)
        for t in range(NT):
            tok = slice(t * TT, (t + 1) * TT)
            hT, bh = nxt4
            for fg in range(4):
                if fg == 2 and t + 1 < NT:
                    nxt4 = load4(t + 1)
                a, ba = a_r.next()
                for fj in range(8):
                    f = fg * 8 + fj
                    ps, bps = self.bank()
                    for c in range(8):
                        self.I("pe", "matmul", [bh, bw], [bps], out=ps, lhsT=Wu[:, c, f * 128:(f + 1) * 128], rhs=hT[:, c, :], start=(c == 0), stop=(c == 7))
                    r, br = r_r.next()
                    self.I("act", "activation", [bps], [br], out=r[:], in_=ps, func=AF.Relu)
                    self.I("pool" if fj % 2 else "dve", "tensor_tensor", [br], [ba], out=a[:, fj, :], in0=r[:], in1=r[:], op=ALU.mult)
                self.dma("sp", self.ACTS[fg * 8:(fg + 1) * 8].rearrange("c p n -> p c n")[:, :, tok], a[:], r=[ba], w=[self.db("ACTS%d" % fg, t)])

    def p_mlp_down(self, l):
        self.begin_phase()
        Wd = self.sb([128, 32, DM], BF16)
        bw = Buf()
        wd = self.w_down[l].rearrange("(c p) n -> p c n", p=128)
        for j in range(4):
            self.dma("pool", Wd[:, j * 8:(j + 1) * 8, :], wd[:, j * 8:(j + 1) * 8, :], w=[bw])
        xT_r = self.sbr(2, [128, 8, TT], F32)
        a_r = self.sbr(2, [128, 32, TT], BF16)
        def load5(t):
            tok = slice(t * TT, (t + 1) * TT)
            xT, bx = xT_r.next()
            a, ba = a_r.next()
            self.dma("sp", xT[:], self.XT.rearrange("c p n -> p c n")[:, :, tok], r=[self.db("XT", t)], w=[bx])
            for fg in range(4):
                self.dma("sp", a[:, fg * 8:(fg + 1) * 8, :], self.ACTS[fg * 8:(fg + 1) * 8].rearrange("c p n -> p c n")[:, :, tok],
                         r=[self.db("ACTS%d" % fg, t)], w=[ba])
            return xT, bx, a, ba

        nxt5 = load5(0)
        for t in range(NT):
            tok = slice(t * TT, (t + 1) * TT)
            xT, bx, a, ba = nxt5
            if t + 1 < NT:
                nxt5 = load5(t + 1)
            for oc in range(8):
                po, bpo = self.bank()
                for f in range(32):
                    self.I("pe", "matmul", [ba, bw], [bpo], out=po, lhsT=Wd[:, f, oc * 128:(oc + 1) * 128], rhs=a[:, f, :], start=(f == 0), stop=(f == 31))
                self.I("dve", "tensor_tensor", [bpo, bx], [bx], out=xT[:, oc, :], in0=po, in1=xT[:, oc, :], op=ALU.add)
            self.dma("sp", self.XT.rearrange("c p n -> p c n")[:, :, tok], xT[:], r=[bx], w=[self.db("XT", t)])

    def p_final(self):
        self.begin_phase()
        xT_r = self.sbr(2, [128, 8, TT], F32)
        yT_r = self.sbr(2, [128, 8, TT], F32)
        sq = self.sb([128, 8, TT], BF16)
        bsq = Buf()
        rs_r = self.sbr(2, [128, TT], F32)
        o_r = self.sbr(2, [128, DM], F32)
        def load6(t):
            tok = slice(t * TT, (t + 1) * TT)
            xT, bx = xT_r.next()
            self.dma("sp", xT[:], self.XT.rearrange("c p n -> p c n")[:, :, tok], r=[self.db("XT", t)], w=[bx])
            return xT, bx

        nxt6 = load6(0)
        for t in range(NT):
            tok = slice(t * TT, (t + 1) * TT)
            xT, bx = nxt6
            if t + 1 < NT:
                nxt6 = load6(t + 1)
            yT, by = yT_r.next()
            rs, brs = rs_r.next()
            fb = DEPTH * VSTR
            self.rmsnorm([xT[:, c, :] for c in range(8)], [self.vecs[:, fb + c:fb + c + 1] for c in range(8)], DM, EPS,
                         [yT[:, c, :] for c in range(8)], [bx], [by], sq, bsq, rs, brs)
            for s in range(4):
                o, bo = o_r.next()
                for g in range(2):
                    ps, bps = self.bank()
                    for j in range(4):
                        c = g * 4 + j
                        self.I("pe", "transpose", [by, self.bconst], [bps], out=ps[:, j * 128:(j + 1) * 128], in_=yT[:, c, s * 128:(s + 1) * 128],
                               identity=self.identF[:])
                    self.copy(self.evac_eng(), o[:, g * 512:(g + 1) * 512], ps, [bps], [bo])
                r0 = t * TT + s * 128
                self.dma("sp", self.out[r0:r0 + 128, :], o[:], r=[bo], w=[self.db("out", t)])

    def build(self):
        ph = self.phases
        want = lambda name: ph is None or name in ph
        if want("t_in"):
            self.p_transpose_in()
        for l in range(self.n_layers):
            if want(f"inproj{l}"):
                self.p_inproj(l)
            if want(f"mla{l}"):
                self.p_mla(l)
            if want(f"moba{l}"):
                self.p_moba(l)
            if want(f"ret{l}"):
                self.p_ret(l)
            if want(f"merge{l}"):
                self.p_merge(l)
            if want(f"mlpup{l}"):
                self.p_mlp_up(l)
            if want(f"mlpdown{l}"):
                self.p_mlp_down(l)
        if want("final"):
            self.p_final()
        self.S.barrier()
        stats = self.S.emit()
        return self.nc, stats


_CACHE = {}


def get_program():
    if "nc" not in _CACHE:
        kb = KB()
        _CACHE["nc"], _CACHE["stats"] = kb.build()
    return _CACHE["nc"]


def make_in_map(inputs, b, consts, vecs):
    m = {"x": np.ascontiguousarray(inputs["x"][b])}
    for k in ("w_in", "mla_w_q_up", "mla_w_kv_up", "w_branch_mla", "w_branch_moba", "w_branch_ret", "w_out", "w_mlp_up", "w_mlp_down"):
        m[k] = np.ascontiguousarray(inputs[k], dtype=np.float32)
    m["vecs"] = vecs
    m.update(consts)
    return m


def kernel(**inputs):
    inputs = {k: np.asarray(v) for k, v in inputs.items()}
    consts, _ = make_consts()
    vecs = pack_vecs(inputs)
    nc = get_program()
    in_maps = [make_in_map(inputs, b, consts, vecs) for b in range(NCORES)]
    res = run_bass_kernel_spmd(nc, in_maps, core_ids=list(range(NCORES)))
    out = np.stack([np.asarray(res.results[b]["out"], dtype=np.float32) for b in range(NCORES)], axis=0)
    return out
```

```python
import numpy as np
import concourse.bass as bass
import concourse.mybir as mybir
from concourse.bass_utils import run_bass_kernel_spmd

F32 = mybir.dt.float32
BF16 = mybir.dt.bfloat16
AF = mybir.ActivationFunctionType
ALU = mybir.AluOpType
AX = mybir.AxisListType

SEQ = 4096
DM = 1024
TT = 512
NT = SEQ // TT
DEPTH = 2
NCORES = 4
EPS = 1e-6
GN_EPS = 1e-5
NEGB = -30000.0


class Buf:
    __slots__ = ("name", "w", "r", "rd")

    def __init__(self, name=""):
        self.name = name
        self.w = None
        self.r = {}
        self.rd = []


class _Op:
    __slots__ = ("eng", "fn", "deps", "is_dma", "idx", "need_inc", "semidx", "semval")


class Sched:
    ENG = ("pe", "act", "dve", "pool", "sp")

    def __init__(self, nc, n_dma_sems=40, same_eng_sync=("act", "dve", "pool")):
        import os
        if os.environ.get("K_VAR", "") == "pesync":
            same_eng_sync = ("act", "dve", "pool", "pe")
        self.nc = nc
        self.ops = []
        self.n_dma_sems = n_dma_sems
        self.same_eng_sync = set(same_eng_sync)
        self.last_on_eng = {}
        self.dmas_open = []
        self.cur_barrier = None

    def op(self, eng, fn, reads=(), writes=(), dma=False):
        o = _Op()
        o.eng = eng
        o.fn = fn
        o.is_dma = dma
        o.idx = len(self.ops)
        o.need_inc = False
        deps = set()
        for b in reads:
            if b.w is not None:
                deps.add(b.w)
        for b in writes:
            if b.w is not None:
                deps.add(b.w)
            deps.update(b.r.values())
            deps.update(b.rd)
        for b in reads:
            if dma:
                b.rd.append(o.idx)
            else:
                b.r[eng] = o.idx
        for b in writes:
            b.w = o.idx
            b.r = {}
            b.rd = []
        if self.cur_barrier is not None:
            deps.add(self.cur_barrier)
        deps.discard(o.idx)
        o.deps = deps
        self.ops.append(o)
        if dma:
            self.dmas_open.append(o.idx)
        else:
            self.last_on_eng[eng] = o.idx
        return o

    def barrier(self):
        deps = set(self.last_on_eng.values()) | set(self.dmas_open)
        o = self.op("sp", lambda e: e.nop())
        o.deps |= deps
        o.deps.discard(o.idx)
        self.dmas_open = []
        self.cur_barrier = o.idx
        return o

    def emit(self):
        nc = self.nc
        ops = self.ops
        engobj = {"pe": nc.tensor, "act": nc.scalar, "dve": nc.vector, "pool": nc.gpsimd, "sp": nc.sync}
        ses = self.same_eng_sync

        def needs_sync(o, od):
            if od.is_dma or o.is_dma:
                return True
            if od.eng != o.eng:
                return True
            return od.eng in ses

        for o in ops:
            for d in o.deps:
                od = ops[d]
                if not od.is_dma and needs_sync(o, od):
                    od.need_inc = True
        cnt = {e: 0 for e in self.ENG}
        ndma = 0
        qpool = {"sp": (0, 32), "pool": (32, 24), "act": (56, 8)}
        qcnt = {q: 0 for q in qpool}
        for o in ops:
            if o.is_dma:
                base, n = qpool[o.eng]
                k = qcnt[o.eng]
                qcnt[o.eng] += 1
                o.semidx = base + (k % n)
                o.semval = 16 * (k // n + 1)
                ndma += 1
            elif o.need_inc:
                cnt[o.eng] += 1
                o.semval = cnt[o.eng]
        engsem = {e: nc.alloc_semaphore(name=f"s_{e}") for e in self.ENG}
        dmasem = [nc.alloc_semaphore(name=f"s_dma{i}") for i in range(64)]
        waited = {}
        nwaits = 0
        self.trace = {e: [] for e in self.ENG}
        for o in ops:
            e = engobj[o.eng]
            waits = {}
            for d in o.deps:
                od = ops[d]
                if not needs_sync(o, od):
                    continue
                key = ("dma", od.semidx) if od.is_dma else ("eng", od.eng)
                if waits.get(key, 0) < od.semval:
                    waits[key] = od.semval
            if o.is_dma and o.semval > 16:
                key = ("dma", o.semidx)
                if waits.get(key, 0) < o.semval - 16:
                    waits[key] = o.semval - 16
            for key, val in waits.items():
                wk = (o.eng, key)
                if waited.get(wk, 0) >= val:
                    continue
                sem = dmasem[key[1]] if key[0] == "dma" else engsem[key[1]]
                e.wait_ge(sem, val)
                self.trace[o.eng].append(("w", key, val, o.idx))
                waited[wk] = val
                nwaits += 1
            ins = o.fn(e)
            if o.is_dma:
                ins.then_inc(dmasem[o.semidx], 16)
                self.trace[o.eng].append(("i", ("dma", o.semidx), 16, o.idx))
            elif o.need_inc:
                ins.then_inc(engsem[o.eng], 1)
                self.trace[o.eng].append(("i", ("eng", o.eng), 1, o.idx))
        self.stats = dict(n_ops=len(ops), n_waits=nwaits, incs=dict(cnt), n_dma=ndma)
        return self.stats


class Ring:
    def __init__(self, items):
        self.items = items
        self.i = 0

    def next(self):
        x = self.items[self.i % len(self.items)]
        self.i += 1
        return x


def _rope_tab(dim, theta):
    inv = (1.0 / (np.float32(theta) ** (np.arange(0, dim, 2, dtype=np.float32) / np.float32(dim)))).astype(np.float32)
    ang = (np.arange(SEQ, dtype=np.float32)[:, None] * inv[None, :]).astype(np.float32)
    c = np.cos(ang).astype(np.float32).T
    s = np.sin(ang).astype(np.float32).T
    cc = np.concatenate([c, c], 0)
    ss = np.concatenate([-s, s], 0)
    return np.ascontiguousarray(np.stack([cc, ss], 0))


def make_consts():
    c = {}
    c["identF"] = np.eye(128, dtype=np.float32)
    c["rope_mla"] = _rope_tab(32, 500000.0)
    rm = _rope_tab(16, 500000.0)
    pad = np.zeros((2, 32, SEQ), np.float32)
    pad[:, 0:16] = rm
    pad[0, 16:32] = 1.0
    c["rope_moba"] = pad
    c["rope_ret"] = _rope_tab(64, 10000.0)
    k = np.arange(128)
    c["mask01"] = (k[:, None] <= k[None, :]).astype(np.float32)
    c["onehot"] = (np.arange(16)[:, None] == (np.arange(SEQ)[None, :] // 256)).astype(np.float32)
    lg = np.log(np.float32(1.0) - np.float32(2.0) ** (np.float32(-5.0) - np.arange(4, dtype=np.float32))).astype(np.float32)
    pos = np.arange(128, dtype=np.float32)
    diff = pos[:, None] - pos[None, :]
    dec = np.where(diff >= 0, np.exp(lg[:, None, None] * diff), 0.0).astype(np.float32)
    c["decayT"] = np.ascontiguousarray(dec.transpose(0, 2, 1) * np.float32(0.125)).astype(np.float32)
    xi = np.exp(lg[:, None] * (pos + 1.0)).astype(np.float32)
    c["xi"] = np.ascontiguousarray(np.broadcast_to(np.tile(xi, (1, 4))[:, None, :], (4, 64, 512))).astype(np.float32)
    zeta = np.exp(lg[:, None] * (127.0 - pos)).astype(np.float32) * np.float32(0.125)
    c["zeta"] = np.ascontiguousarray(zeta.T).astype(np.float32)
    gam = [float(np.exp(lg[h] * np.float32(128.0))) for h in range(4)]
    return c, gam


VSTR = 27
NV = DEPTH * VSTR + 8


def pack_vecs(inp):
    v = np.zeros((128, NV), np.float32)
    for l in range(DEPTH):
        b = l * VSTR
        v[:, b + 0:b + 8] = inp["attn_norm"][l].reshape(8, 128).T
        v[:, b + 8:b + 16] = inp["mlp_norm"][l].reshape(8, 128).T
        v[:, b + 16:b + 18] = inp["mla_q_norm"][l].reshape(2, 128).T
        v[:, b + 18:b + 19] = inp["mla_kv_norm"][l].reshape(1, 128).T
        v[:, b + 19:b + 23] = inp["ret_gn_w"][l].reshape(4, 128).T
        v[:, b + 23:b + 27] = inp["ret_gn_b"][l].reshape(4, 128).T
    v[:, DEPTH * VSTR:DEPTH * VSTR + 8] = inp["final_norm"].reshape(8, 128).T
    return v


class KB:
    SB_BASE = 20480
    SB_TOP = 229376

    def __init__(self, n_layers=DEPTH, debug=False, phases=None):
        self.nc = nc = bass.Bass("TRN2", target_bir_lowering=False)
        self.S = Sched(nc)
        self.debug = debug
        self.n_layers = n_layers
        self.phases = phases
        self.top = self.SB_BASE
        self.uid = 0
        L = DEPTH
        inp = lambda name, shape: nc.dram_tensor(name, shape, F32, kind="ExternalInput").ap()
        self.x = inp("x", [SEQ, DM])
        self.w_in = inp("w_in", [L, DM, 6560])
        self.w_q_up = inp("mla_w_q_up", [L, 256, 768])
        self.w_kv_up = inp("mla_w_kv_up", [L, 128, 1024])
        self.w_b = [inp("w_branch_mla", [L, 512, DM]), inp("w_branch_moba", [L, 512, DM]), inp("w_branch_ret", [L, 512, DM])]
        self.w_out = inp("w_out", [L, DM, DM])
        self.w_up = inp("w_mlp_up", [L, DM, 4096])
        self.w_down = inp("w_mlp_down", [L, 4096, DM])
        self.vecs_d = inp("vecs", [128, NV])
        self.c_identF = inp("identF", [128, 128])
        self.c_rope_mla = inp("rope_mla", [2, 32, SEQ])
        self.c_rope_moba = inp("rope_moba", [2, 32, SEQ])
        self.c_rope_ret = inp("rope_ret", [2, 64, SEQ])
        self.c_mask01 = inp("mask01", [128, 128])
        self.c_onehot = inp("onehot", [16, SEQ])
        self.c_decayT = inp("decayT", [4, 128, 128])
        self.c_xi = inp("xi", [4, 64, 512])
        self.c_zeta = inp("zeta", [128, 4])
        self.out = nc.dram_tensor("out", [SEQ, DM], F32, kind="ExternalOutput").ap()
        kind = "ExternalOutput" if debug else "Internal"
        scr = lambda name, shape, dt: nc.dram_tensor(name, shape, dt, kind=kind).ap()
        self.XT = scr("XT", [8, 128, SEQ], F32)
        self.HT = scr("HT", [8, 128, SEQ], BF16)
        self.CQ = scr("CQ", [2, 128, SEQ], BF16)
        self.CKV = scr("CKV", [128, SEQ], BF16)
        self.KPE = scr("KPE", [32, SEQ], BF16)
        self.MQ = scr("MQ", [8, 64, SEQ], BF16)
        self.MK = scr("MK", [8, 64, SEQ], BF16)
        self.MV = scr("MV", [SEQ, 512], BF16)
        self.RQ = scr("RQ", [4, 64, SEQ], BF16)
        self.RK = scr("RK", [4, 64, SEQ], BF16)
        self.RV = scr("RV", [SEQ, 512], BF16)
        self.RG = scr("RG", [4, 128, SEQ], F32)
        self.OB = [scr("OMLA", [4, 128, SEQ], BF16), scr("OMOBA", [4, 128, SEQ], BF16), scr("ORET", [4, 128, SEQ], BF16)]
        self.ACTS = scr("ACTS", [32, 128, SEQ], BF16)
        self.dbuf = {}
        self.psF = nc.alloc_psum_tensor("psF", [128, 7, 512], F32)
        self.psB = nc.alloc_psum_tensor("psB", [128, 1024], BF16)
        self.pbuf = [Buf(f"ps{i}") for i in range(7)]
        self.pbufB = Buf("psB")
        self.bank_i = 0
        _, self.gam = make_consts()
        self.identF = self.sb([128, 128], F32)
        self.identB = self.sb([128, 128], BF16)
        self.onesB = self.sb([128, 128], BF16)
        self.mask01 = self.sb([128, 128], BF16)
        self.vecs = self.sb([128, NV], F32)
        self.bconst = Buf("const")
        self.dma("sp", self.identF[:], self.c_identF, w=[self.bconst])
        self.dma("pool", self.identB[:], self.c_identF, w=[self.bconst])
        self.dma("pool", self.mask01[:], self.c_mask01, w=[self.bconst])
        self.dma("sp", self.vecs[:], self.vecs_d, w=[self.bconst])
        self.I("pool", "memset", [], [self.bconst], ap=self.onesB[:], constant=1.0)
        self.maskneg = self.sb([128, 128], BF16)
        self.I("pool", "tensor_scalar", [self.bconst], [self.bconst], out=self.maskneg[:], in0=self.mask01[:], scalar1=-1.0, scalar2=-NEGB,
               op0=ALU.add, op1=ALU.mult)
        self.phase_base = self.top

    def sb(self, shape, dtype, name=None):
        sz = int(np.prod(shape[1:])) * (4 if dtype == F32 else 2)
        sz = (sz + 63) // 64 * 64
        self.uid += 1
        t = self.nc.alloc_sbuf_tensor_at(name or f"t{self.uid}", list(shape), dtype, offset=self.top)
        self.top += sz
        assert self.top <= self.SB_TOP, f"SBUF overflow {self.top}"
        return t

    def sbr(self, n, shape, dtype):
        return Ring([(self.sb(shape, dtype), Buf()) for _ in range(n)])

    def begin_phase(self):
        self.S.barrier()
        self.top = self.phase_base

    def I(self, eng, meth, r, w, **kw):
        return self.S.op(eng, lambda e: getattr(e, meth)(**kw), r, w)

    def dma(self, eng, out, in_, r=(), w=()):
        return self.S.op(eng, lambda e: e.dma_start(out=out, in_=in_), r, w, dma=True)

    def bank(self):
        i = self.bank_i % 7
        self.bank_i += 1
        return self.psF[:, i, :], self.pbuf[i]

    def db(self, name, t=None):
        k = (name, t)
        if k not in self.dbuf:
            self.dbuf[k] = Buf(f"{name}{t}")
        return self.dbuf[k]

    def dball(self, name):
        return [self.db(name, t) for t in range(NT)]

    def vcol(self, l, off, n=1):
        b = l * VSTR + off
        return self.vecs[:, b:b + n]

    def evac_eng(self):
        self._ev = getattr(self, "_ev", 0) + 1
        return "act" if self._ev % 2 else "dve"

    def copy(self, eng, out, in_, r, w):
        if eng == "act":
            return self.I("act", "copy", r, w, out=out, in_=in_)
        return self.I(eng, "tensor_copy", r, w, out=out, in_=in_)

    def rmsnorm(self, xaps, gaps, nfeat, eps, outaps, rb, wb, sq, bsq, rs, brs, nparts=128):
        n = len(xaps)
        for c in range(n):
            self.I("act", "activation", rb, [bsq], out=sq[0:nparts, c, :], in_=xaps[c], func=AF.Square)
        ps, bps = self.bank()
        for c in range(n):
            self.I("pe", "matmul", [bsq, self.bconst], [bps], out=ps, lhsT=self.onesB[0:nparts, :], rhs=sq[0:nparts, c, :],
                   start=(c == 0), stop=(c == n - 1))
        self.I("act", "activation", [bps], [brs], out=rs[:], in_=ps, func=AF.Sqrt, scale=1.0 / nfeat, bias=float(eps))
        self.I("dve", "reciprocal", [brs], [brs], out=rs[:], in_=rs[:])
        for c in range(n):
            self.I("dve", "scalar_tensor_tensor", list(rb) + [brs, self.bconst], wb, out=outaps[c], in0=xaps[c], scalar=gaps[c],
                   in1=rs[0:nparts, :], op0=ALU.mult, op1=ALU.mult)

    def p_transpose_in(self):
        self.begin_phase()
        xin = self.sbr(2, [128, DM], F32)
        xo = self.sbr(2, [128, 8, TT], F32)
        for t in range(NT):
            xot, bxo = xo.next()
            tiles = []
            for s in range(4):
                xt, bx = xin.next()
                r0 = t * TT + s * 128
                self.dma("sp", xt[:], self.x[r0:r0 + 128, :], w=[bx])
                for g in range(2):
                    ps, bps = self.bank()
                    for j in range(4):
                        c = g * 4 + j
                        self.I("pe", "transpose", [bx, self.bconst], [bps], out=ps[:, j * 128:(j + 1) * 128], in_=xt[:, c * 128:(c + 1) * 128],
                               identity=self.identF[:])
                    self.copy(self.evac_eng(), xot[:, g * 4:(g + 1) * 4, s * 128:(s + 1) * 128],
                              ps.rearrange("p (j n) -> p j n", n=128), [bps], [bxo])
            self.dma("sp", self.XT.rearrange("c p n -> p c n")[:, :, t * TT:(t + 1) * TT], xot[:], r=[bxo], w=[self.db("XT", t)])

    def p_inproj(self, l):
        self.begin_phase()
        W1 = self.sb([128, 8, 4288], BF16)
        bW = []
        win = self.w_in[l].rearrange("(c p) n -> p c n", p=128)

        def wl(d0, s0, n):
            b = Buf()
            self.dma("pool", W1[:, :, d0:d0 + n], win[:, :, s0:s0 + n], w=[b])
            bW.append(b)

        wl(0, 0, 416)
        wl(416, 400, 16)
        wl(432, 384, 16)
        wl(448, 416, 512)
        wl(1088, 928, 512)
        wl(1728, 1440, 512)
        wl(2240, 1952, 256)
        wl(2752, 2208, 256)
        wl(3264, 2464, 512)
        wl(3776, 2976, 512)
        for (dst, src, H, dh, half) in ((960, 416, 8, 16, 8), (1600, 928, 8, 16, 8), (2496, 1952, 4, 64, 32), (3008, 2208, 4, 64, 32)):
            for c in range(8):
                dstv = W1[:, c, dst:dst + H * dh].rearrange("p (h e) -> p h e", e=dh)
                srcv = win[:, c, src:src + H * 64].rearrange("p (h d) -> p h d", d=64)
                b = Buf()
                self.dma("pool", dstv[:, :, 0:half], srcv[:, :, half:2 * half], w=[b])
                self.dma("pool", dstv[:, :, half:2 * half], srcv[:, :, 0:half], w=[b])
                bW.append(b)
        xTr = self.sbr(1, [128, 8, TT], F32)
        hTr = self.sbr(2, [128, 8, TT], BF16)
        sq = self.sb([128, 8, TT], BF16)
        bsq = Buf()
        rsr = self.sbr(2, [128, TT], F32)
        cqf = self.sb([128, 3, TT], F32)
        bcqf = Buf()
        cqn = self.sb([128, 3, TT], BF16)
        bcqn = Buf()
        sq2 = self.sb([128, 2, TT], BF16)
        bsq2 = Buf()
        tabk = self.sb([32, 2, TT], F32)
        tabm = self.sb([32, 2, TT], F32)
        tabr = self.sb([64, 2, TT], F32)
        btab = Buf()
        t1r = self.sbr(3, [64, TT], F32)
        t2r = self.sbr(3, [64, TT], F32)
        kpe = self.sb([32, TT], BF16)
        bkpe = Buf()
        stq = self.sb([64, 8, TT], BF16)
        stk = self.sb([64, 8, TT], BF16)
        strq = self.sb([64, 4, TT], BF16)
        strk = self.sb([64, 4, TT], BF16)
        stg = self.sb([128, 4, TT], F32)
        stv = self.sbr(2, [128, 4, 512], BF16)
        bst = {k: Buf() for k in ("q", "k", "rq", "rk", "g")}
        import os
        dbg_nt = int(os.environ.get("K_NT", NT))
        dbg_stage = int(os.environ.get("K_STAGE", 99))
        for t in range(dbg_nt):
            tok = slice(t * TT, (t + 1) * TT)
            xT, bx = xTr.next()
            hT, bh = hTr.next()
            rs, brs = rsr.next()
            self.dma("sp", xT[:], self.XT.rearrange("c p n -> p c n")[:, :, tok], r=[self.db("XT", t)], w=[bx])
            self.dma("sp", tabk[:], self.c_rope_mla.rearrange("a p n -> p a n")[:, :, tok], w=[btab])
            self.dma("sp", tabm[:], self.c_rope_moba.rearrange("a p n -> p a n")[:, :, tok], w=[btab])
            self.dma("sp", tabr[:], self.c_rope_ret.rearrange("a p n -> p a n")[:, :, tok], w=[btab])
            self.rmsnorm([xT[:, c, :] for c in range(8)], [self.vcol(l, c) for c in range(8)], DM, EPS,
                         [hT[:, c, :] for c in range(8)], [bx], [bh], sq, bsq, rs, brs)
            self.dma("sp", self.HT.rearrange("c p n -> p c n")[:, :, tok], hT[:], r=[bh], w=[self.db("HT", t)])
            if dbg_stage < 1:
                continue

            def mm(M, col, N0=0, N1=TT):
                ps, bps = self.bank()
                for c in range(8):
                    self.I("pe", "matmul", [bh] + bW, [bps], out=ps[0:M, 0:N1 - N0], lhsT=W1[:, c, col:col + M], rhs=hT[:, c, N0:N1],
                           start=(c == 0), stop=(c == 7))
                return ps, bps

            for j in range(3):
                ps, bps = mm(128, j * 128)
                self.copy(self.evac_eng(), cqf[:, j, :], ps, [bps], [bcqf])
            rs2, brs2 = rsr.next()
            self.rmsnorm([cqf[:, c, :] for c in range(2)], [self.vcol(l, 16 + c) for c in range(2)], 256, EPS,
                         [cqn[:, c, :] for c in range(2)], [bcqf], [bcqn], sq2, bsq2, rs2, brs2)
            self.dma("sp", self.CQ.rearrange("c p n -> p c n")[:, :, tok], cqn[:, 0:2, :], r=[bcqn], w=[self.db("CQ", t)])
            rs3, brs3 = rsr.next()
            self.rmsnorm([cqf[:, 2, :]], [self.vcol(l, 18)], 128, EPS, [cqn[:, 2, :]], [bcqf], [bcqn], sq2, bsq2, rs3, brs3)
            self.dma("sp", self.CKV[:, tok], cqn[:, 2, :], r=[bcqn], w=[self.db("CKV", t)])

            def rope(psm, bpm, psr, bpr, n, tab, outap, wb):
                t1, b1 = t1r.next()
                t2, b2 = t2r.next()
                self.I("dve", "tensor_tensor", [bpm, btab], [b1], out=t1[0:n, :], in0=psm[0:n, :], in1=tab[0:n, 0, :], op=ALU.mult)
                self.I("dve", "tensor_tensor", [bpr, btab], [b2], out=t2[0:n, :], in0=psr[0:n, :], in1=tab[0:n, 1, :], op=ALU.mult)
                self.I("pool", "tensor_tensor", [b1, b2], wb, out=outap, in0=t1[0:n, :], in1=t2[0:n, :], op=ALU.add)

            if dbg_stage < 2:
                continue
            psm, bpm = mm(32, 384)
            psr, bpr = mm(32, 416)
            rope(psm, bpm, psr, bpr, 32, tabk, kpe[:], [bkpe])
            self.dma("sp", self.KPE[:, tok], kpe[:], r=[bkpe], w=[self.db("KPE", t)])
            if dbg_stage < 3:
                continue
            for (st, key, base, rbase, dst) in ((stq, "q", 448, 960, self.MQ), (stk, "k", 1088, 1600, self.MK)):
                for h in range(8):
                    psm, bpm = mm(64, base + h * 64)
                    psr, bpr = mm(32, rbase + h * 16)
                    self.I("act", "copy", [bpm], [bst[key]], out=st[32:64, h, :], in_=psm[32:64, :])
                    rope(psm, bpm, psr, bpr, 32, tabm, st[0:32, h, :], [bst[key]])
                self.dma("sp", dst.rearrange("h p n -> p h n")[:, :, tok], st[:], r=[bst[key]], w=[self.db("M" + key, t)])
            if dbg_stage < 4:
                continue
            for (st, key, base, rbase, dst) in ((strq, "rq", 2240, 2496, self.RQ), (strk, "rk", 2752, 3008, self.RK)):
                for h in range(4):
                    psm, bpm = mm(64, base + h * 64)
                    psr, bpr = mm(64, rbase + h * 64)
                    rope(psm, bpm, psr, bpr, 64, tabr, st[:, h, :], [bst[key]])
                self.dma("sp", dst.rearrange("h p n -> p h n")[:, :, tok], st[:], r=[bst[key]], w=[self.db(key, t)])
            if dbg_stage < 5:
                continue
            for j in range(4):
                ps, bps = mm(128, 3776 + j * 128)
                self.I("act", "activation", [bps], [bst["g"]], out=stg[:, j, :], in_=ps, func=AF.Silu)
            self.dma("sp", self.RG.rearrange("c p n -> p c n")[:, :, tok], stg[:], r=[bst["g"]], w=[self.db("RG", t)])
            if dbg_stage < 6:
                continue
            for (col, dst, key) in ((1728, self.MV, "MV"), (3264, self.RV, "RV")):
                sv, bsv = stv.next()
                for s in range(4):
                    ps, bps = self.bank()
                    for c in range(8):
                        self.I("pe", "matmul", [bh] + bW, [bps], out=ps, lhsT=hT[:, c, s * 128:(s + 1) * 128], rhs=W1[:, c, col:col + 512],
                               start=(c == 0), stop=(c == 7))
                    self.copy(self.evac_eng(), sv[:, s, :], ps, [bps], [bsv])
                self.dma("sp", dst[t * TT:(t + 1) * TT, :].rearrange("(s p) n -> p s n", p=128), sv[:], r=[bsv], w=[self.db(key, t)])

    def run_attn(self, tiles, scale, pss_r, pT_r, pso_r, LA=2):
        steps = [(i, kt) for i, T in enumerate(tiles) for kt in range(4 * T["t"] + 4)]
        info = {}

        def emit_s(si):
            i, kt = steps[si]
            T = tiles[i]
            t = T["t"]
            qT, qbufs = T["get_q"]()
            j = kt - 4 * t
            q0 = max(0, j) * 128
            N = TT - q0
            (pi, bpi) = pss_r.next()
            pss = self.psF[:, pi, :]
            self.I("pe", "matmul", [T["bK"]] + list(qbufs), [bpi], out=pss[:, 0:N], lhsT=T["K"][:, kt * 128:(kt + 1) * 128], rhs=qT[:, q0:TT],
                   start=True, stop=(j < 0))
            if j >= 0:
                self.I("pe", "matmul", [self.bconst], [bpi], out=pss[:, 0:128], lhsT=self.identB[:], rhs=self.maskneg[:], start=False, stop=True)
            info[si] = (pss, bpi, q0, N)

        for si in range(min(LA, len(steps))):
            emit_s(si)
        for si, (i, kt) in enumerate(steps):
            T = tiles[i]
            nkt = 4 * T["t"] + 4
            if kt == 0:
                if T.get("start_hook") is not None:
                    T["start_hook"]()
                po, bpso = pso_r.next()
                T["pso"] = (self.psF[:, po, :], bpso)
            if si + LA < len(steps):
                emit_s(si + LA)
            if kt == 1 and T.get("mid_hook") is not None:
                T["mid_hook"]()
            pss, bpi, q0, N = info.pop(si)
            pso, bpso = T["pso"]
            pT, bpT = pT_r.next()
            self.I("act", "activation", [bpi], [bpT], out=pT[:, 0:N], in_=pss[:, 0:N], func=AF.Exp, scale=float(scale))
            self.I("pe", "matmul", [T["bV"], bpT], [bpso], out=pso[:, q0:TT], lhsT=T["V"][:, kt, :], rhs=pT[:, 0:N],
                   start=(kt == 0), stop=(kt == nkt - 1))
            if kt == nkt - 1:
                T["finish"](pso, bpso)

    def attn_finish(self, pso, bpso, rec_r, ost_r, dst_ap, dbuf):
        rec, brec = rec_r.next()
        ost, bost = ost_r.next()
        self.I("dve", "reciprocal", [bpso], [brec], out=rec[0:64, :], in_=pso[64:128, :])
        self.I("dve", "tensor_tensor", [bpso, brec], [bost], out=ost[0:64, :], in0=pso[0:64, :], in1=rec[0:64, :], op=ALU.mult)
        self.dma("sp", dst_ap, ost[0:64, :], r=[bost], w=[dbuf])

    def p_mla(self, l):
        self.begin_phase()
        CQs = self.sb([128, 2, SEQ], BF16)
        CKVs = self.sb([128, SEQ], BF16)
        bin_ = Buf()
        self.dma("sp", CQs[:], self.CQ.rearrange("c p n -> p c n"), r=self.dball("CQ"), w=[bin_])
        self.dma("sp", CKVs[:], self.CKV, r=self.dball("CKV"), w=[bin_])
        Wq = self.sb([128, 2, 8, 96], BF16)
        Wqr = self.sb([128, 2, 8, 96], BF16)
        Wkv = self.sb([128, 8, 128], BF16)
        bw = Buf()
        wq = self.w_q_up[l].rearrange("(c p) (h d) -> p c h d", p=128, d=96)
        self.I("pool", "memset", [], [bw], ap=Wqr[:], constant=0.0)
        for c in range(2):
            self.dma("pool", Wq[:, c, :, :], wq[:, c, :, :], w=[bw])
            self.dma("pool", Wqr[:, c, :, 64:80], wq[:, c, :, 80:96], w=[bw])
            self.dma("pool", Wqr[:, c, :, 80:96], wq[:, c, :, 64:80], w=[bw])
        self.dma("pool", Wkv[:], self.w_kv_up[l].rearrange("p (h d) -> p h d", d=128), w=[bw])
        tab = self.sb([96, 2, SEQ], F32)
        btab = Buf()
        self.dma("sp", tab[64:96, :, :], self.c_rope_mla.rearrange("a p n -> p a n"), w=[btab])
        Kr = self.sbr(2, [128, SEQ], BF16)
        Vr = self.sbr(2, [128, 32, 128], BF16)
        for (Kb, bK) in Kr.items:
            self.I("pool", "memset", [], [bK], ap=Kb[96:128, :], constant=0.0)
            self.dma("sp", Kb[64:96, :], self.KPE, r=self.dball("KPE"), w=[bK])
        for (Vb, bV) in Vr.items:
            self.I("pool", "memset", [], [bV], ap=Vb[:, :, 64:128], constant=1.0)
        qTr = self.sbr(3, [128, TT], BF16)
        for (q_, bq_) in qTr.items:
            self.I("pool", "memset", [], [bq_], ap=q_[96:128, :], constant=0.0)
        t1r = self.sbr(2, [96, TT], F32)
        t2r = self.sbr(2, [96, TT], F32)
        pT_r = self.sbr(4, [128, TT], BF16)
        rec_r = self.sbr(2, [64, TT], F32)
        ost_r = self.sbr(2, [64, TT], BF16)
        pss_r = Ring([(0, self.pbuf[0]), (1, self.pbuf[1]), (2, self.pbuf[2])])
        pso_r = Ring([(3, self.pbuf[3]), (4, self.pbuf[4])])
        aux_r = Ring([(5, self.pbuf[5]), (6, self.pbuf[6])])
        scale = 96 ** -0.5
        KV = {}

        def build_kv(h):
            Kb, bK = Kr.items[h % 2]
            Vb, bV = Vr.items[h % 2]
            KV[h] = (Kb, bK, Vb, bV)
            for t in range(NT):
                pi, bpi = aux_r.next()
                ps = self.psF[:, pi, :]
                self.I("pe", "matmul", [bin_, bw], [bpi], out=ps[0:64, :], lhsT=Wkv[:, h, 0:64], rhs=CKVs[:, t * TT:(t + 1) * TT], start=True, stop=True)
                self.copy(self.evac_eng(), Kb[0:64, t * TT:(t + 1) * TT], ps[0:64, :], [bpi], [bK])
            for g in range(4):
                pi, bpi = aux_r.next()
                ps = self.psF[:, pi, :]
                for j in range(8):
                    kt = g * 8 + j
                    self.I("pe", "matmul", [bin_, bw], [bpi], out=ps[:, j * 64:(j + 1) * 64], lhsT=CKVs[:, kt * 128:(kt + 1) * 128],
                           rhs=Wkv[:, h, 64:128], start=True, stop=True)
                self.copy(self.evac_eng(), Vb[:, g * 8:(g + 1) * 8, 0:64], ps.rearrange("p (j d) -> p j d", d=64), [bpi], [bV])

        def build_q(h, t):
            tok = slice(t * TT, (t + 1) * TT)
            qT, bq = qTr.next()
            pi, bpm = aux_r.next()
            psm = self.psF[:, pi, :]
            pi2, bpr = aux_r.next()
            psr = self.psF[:, pi2, :]
            for c in range(2):
                self.I("pe", "matmul", [bin_, bw], [bpm], out=psm[0:96, :], lhsT=Wq[:, c, h, :], rhs=CQs[:, c, tok], start=(c == 0), stop=(c == 1))
            for c in range(2):
                self.I("pe", "matmul", [bin_, bw], [bpr], out=psr[0:96, :], lhsT=Wqr[:, c, h, :], rhs=CQs[:, c, tok], start=(c == 0), stop=(c == 1))
            self.I("act", "copy", [bpm], [bq], out=qT[0:64, :], in_=psm[0:64, :])
            t1, b1 = t1r.next()
            t2, b2 = t2r.next()
            self.I("dve", "tensor_tensor", [bpm, btab], [b1], out=t1[64:96, :], in0=psm[64:96, :], in1=tab[64:96, 0, tok], op=ALU.mult)
            self.I("dve", "tensor_tensor", [bpr, btab], [b2], out=t2[64:96, :], in0=psr[64:96, :], in1=tab[64:96, 1, tok], op=ALU.mult)
            self.I("pool", "tensor_tensor", [b1, b2], [bq], out=qT[64:96, :], in0=t1[64:96, :], in1=t2[64:96, :], op=ALU.add)
            return qT, bq

        items = [(h, t) for h in range(8) for t in range(NT)]
        build_kv(0)
        prepared = {items[0]: build_q(*items[0])}
        tiles = []
        for idx, (h, t) in enumerate(items):
            def start_hook(idx=idx):
                if idx + 1 < len(items):
                    hn, tn = items[idx + 1]
                    if tn == 0:
                        build_kv(hn)
                    prepared[(hn, tn)] = build_q(hn, tn)

            def get_q(h=h, t=t):
                qT, bq = prepared[(h, t)]
                return qT[:, :], [bq]

            def finish(pso, bpso, h=h, t=t):
                tok = slice(t * TT, (t + 1) * TT)
                self.attn_finish(pso, bpso, rec_r, ost_r, self.OB[0][h // 2, (h % 2) * 64:(h % 2) * 64 + 64, tok], self.db("OMLA", t))

            tiles.append(dict(t=t, h=h, get_q=get_q, start_hook=start_hook, finish=finish))
        class _KV(dict):
            pass
        for T in tiles:
            T["_h"] = T["h"]
        for T in tiles:
            hh = T["h"]
            T["K"] = Kr.items[hh % 2][0]
            T["bK"] = Kr.items[hh % 2][1]
            T["V"] = Vr.items[hh % 2][0]
            T["bV"] = Vr.items[hh % 2][1]
        self.run_attn(tiles, scale, pss_r, pT_r, pso_r)

    def p_moba(self, l):
        self.begin_phase()
        Qr = self.sbr(2, [128, SEQ], BF16)
        Kr = self.sbr(2, [128, SEQ], BF16)
        Vr = self.sbr(2, [128, 32, 128], BF16)
        for (Kb, bK) in Kr.items:
            self.I("pool", "memset", [], [bK], ap=Kb[64:128, :], constant=0.0)
            self.dma("pool", Kb[64:80, :], self.c_onehot, w=[bK])
        for (Qb, bQ) in Qr.items:
            self.I("pool", "memset", [], [bQ], ap=Qb[64:128, :], constant=0.0)
        for (Vb, bV) in Vr.items:
            self.I("pool", "memset", [], [bV], ap=Vb[:, :, 64:128], constant=1.0)
        km_r = [(self.sb([64, 16], F32), self.sb([64, 16], BF16), Buf()) for _ in range(2)]
        gate_r = self.sbr(4, [128, 16], F32)
        top_r = self.sbr(4, [128, 8], F32)
        mbp_r = self.sbr(4, [128, 4, 80], BF16)
        for (m, bm) in mbp_r.items:
            self.I("pool", "memset", [], [bm], ap=m[:], constant=0.0)
        pT_r = self.sbr(4, [128, TT], BF16)
        rec_r = self.sbr(2, [64, TT], F32)
        ost_r = self.sbr(2, [64, TT], BF16)
        pss_r = Ring([(0, self.pbuf[0]), (1, self.pbuf[1]), (2, self.pbuf[2])])
        pso_r = Ring([(3, self.pbuf[3]), (4, self.pbuf[4])])
        aux_r = Ring([(5, self.pbuf[5]), (6, self.pbuf[6])])
        bQm = [[Buf() for _ in range(NT)] for _ in range(2)]
        H = {}

        def load_head(h):
            Qb, bQ = Qr.items[h % 2]
            Kb, bK = Kr.items[h % 2]
            Vb, bV = Vr.items[h % 2]
            H[h] = (Qb, bQ, Kb, bK, Vb, bV)
            self.dma("sp", Qb[0:64, :], self.MQ[h], r=self.dball("Mq"), w=[bQ])
            self.dma("sp", Kb[0:64, :], self.MK[h], r=self.dball("Mk"), w=[bK])
            for g4 in range(4):
                self.dma("sp", Vb[:, g4 * 8:(g4 + 1) * 8, 0:64],
                         self.MV[g4 * 1024:(g4 + 1) * 1024, h * 64:(h + 1) * 64].rearrange("(k p) d -> p k d", p=128), r=self.dball("MV"), w=[bV])

        def kmean(h):
            Qb, bQ, Kb, bK, Vb, bV = H[h]
            kmf, kmb, bkm = km_r[h % 2]
            self.I("dve", "tensor_reduce", [bK], [bkm], out=kmf[:], in_=Kb[0:64, :].rearrange("p (n k) -> p n k", k=256), axis=AX.X, op=ALU.add)
            self.I("dve", "tensor_copy", [bkm], [bkm], out=kmb[:], in_=kmf[:])

        def prepare(h, t):
            Qb, bQ, Kb, bK, Vb, bV = H[h]
            kmf, kmb, bkm = km_r[h % 2]
            tok = slice(t * TT, (t + 1) * TT)
            mbp, bmb = mbp_r.next()
            pi, bpg = aux_r.next()
            psg = self.psF[:, pi, :]
            for s in range(4):
                g = 4 * t + s
                blk = g // 2
                if blk <= 3:
                    self.I("pool", "memset", [], [bmb], ap=mbp[:, s, 64:80], constant=0.0)
                    continue
                self.I("pe", "matmul", [bQ, bkm], [bpg], out=psg[:, s * 16:(s + 1) * 16], lhsT=Qb[0:64, g * 128:(g + 1) * 128], rhs=kmb[:],
                       start=True, stop=True)
                gt, bg = gate_r.next()
                tp, btp = top_r.next()
                self.I("dve", "memset", [], [bg], ap=gt[:], constant=-1e30)
                self.I("dve", "tensor_copy", [bpg], [bg], out=gt[:, 0:blk], in_=psg[:, s * 16:s * 16 + blk])
                self.I("dve", "max", [bg], [btp], out=tp[:], in_=gt[:])
                self.I("dve", "tensor_scalar", [bg, btp], [bmb], out=mbp[:, s, 64:80], in0=gt[:], scalar1=tp[:, 2:3], scalar2=NEGB,
                       op0=ALU.is_lt, op1=ALU.mult)
                self.I("dve", "memset", [], [bmb], ap=mbp[:, s, 64 + blk:80], constant=0.0)
            return mbp, bmb

        def prepare2(h, t, mbp, bmb):
            Qb, bQ, Kb, bK, Vb, bV = H[h]
            tok = slice(t * TT, (t + 1) * TT)
            for s in range(4):
                self.I("pe", "transpose", [bmb, self.bconst], [self.pbufB], out=self.psB[0:80, s * 128:(s + 1) * 128], in_=mbp[:, s, :],
                       identity=self.identB[:])
            self.I("act", "copy", [self.pbufB, bQ], [bQm[h % 2][t]], out=Qb[64:80, tok], in_=self.psB[64:80, 0:TT])

        items = [(h, t) for h in range(8) for t in range(NT)]
        load_head(0)
        load_head(1)
        kmean(0)
        prepare2(0, 0, *prepare(0, 0))
        gated = {}
        if len(items) > 1:
            gated[items[1]] = prepare(*items[1])
        tiles = []
        for idx, (h, t) in enumerate(items):
            def start_hook(idx=idx, h=h, t=t):
                if t == 1 and h + 1 < 8 and h >= 1:
                    load_head(h + 1)
                if idx + 2 < len(items):
                    hn, tn = items[idx + 2]
                    if tn == 0:
                        kmean(hn)
                    gated[(hn, tn)] = prepare(hn, tn)

            def mid_hook(idx=idx):
                if idx + 1 < len(items):
                    hn, tn = items[idx + 1]
                    prepare2(hn, tn, *gated.pop((hn, tn)))

            def get_q(h=h, t=t):
                Qb, bQ = Qr.items[h % 2]
                return Qb[:, t * TT:(t + 1) * TT], [bQ, bQm[h % 2][t]]

            def finish(pso, bpso, h=h, t=t):
                tok = slice(t * TT, (t + 1) * TT)
                self.attn_finish(pso, bpso, rec_r, ost_r, self.OB[1][h // 2, (h % 2) * 64:(h % 2) * 64 + 64, tok], self.db("OMOBA", t))

            tiles.append(dict(t=t, h=h, get_q=get_q, start_hook=start_hook, mid_hook=mid_hook, finish=finish,
                              K=Kr.items[h % 2][0], bK=Kr.items[h % 2][1], V=Vr.items[h % 2][0], bV=Vr.items[h % 2][1]))
        self.run_attn(tiles, 0.125, pss_r, pT_r, pso_r)

    def p_ret(self, l):
        self.begin_phase()
        Qr = self.sbr(2, [64, SEQ], BF16)
        Kr = self.sbr(2, [64, SEQ], BF16)
        Vr = self.sbr(2, [128, 32, 128], BF16)
        dec = self.sb([128, 4, 128], F32)
        xi = self.sb([64, 4, 512], F32)
        zeta = self.sb([128, 4], F32)
        bc = Buf()
        self.dma("sp", dec[:], self.c_decayT.rearrange("h j i -> j h i"), w=[bc])
        self.dma("sp", xi[:], self.c_xi.rearrange("h p n -> p h n"), w=[bc])
        self.dma("sp", zeta[:], self.c_zeta, w=[bc])
        qx_r = self.sbr(2, [64, TT], BF16)
        kz_r = self.sbr(3, [128, 64], BF16)
        sc_r = self.sbr(3, [128, 128], BF16)
        Rf = self.sb([64, 128], F32)
        bRf = Buf()
        Rb_r = self.sbr(3, [64, 128], BF16)
        G_r = self.sbr(2, [128, TT], F32)
        of_r = self.sbr(2, [128, TT], F32)
        ob_r = self.sbr(2, [128, 2, TT], BF16)
        m_r = self.sbr(2, [128, TT], F32)
        v_r = self.sbr(2, [128, TT], F32)
        y_r = self.sbr(2, [128, TT], BF16)
        po_r = Ring([(0, self.pbuf[0]), (1, self.pbuf[1])])
        ps_r = Ring([(2, self.pbuf[2]), (3, self.pbuf[3])])
        pk_r = Ring([(4, self.pbuf[4]), (5, self.pbuf[5])])
        for h in range(4):
            Qb, bQ = Qr.next()
            Kb, bK = Kr.next()
            Vb, bV = Vr.next()
            self.dma("sp", Qb[:], self.RQ[h], r=self.dball("rq"), w=[bQ])
            self.dma("sp", Kb[:], self.RK[h], r=self.dball("rk"), w=[bK])
            for g4 in range(4):
                self.dma("sp", Vb[:, g4 * 8:(g4 + 1) * 8, :],
                         self.RV[g4 * 1024:(g4 + 1) * 1024, h * 128:(h + 1) * 128].rearrange("(k p) d -> p k d", p=128), r=self.dball("RV"), w=[bV])
            self.I("dve", "memset", [], [bRf], ap=Rf[:], constant=0.0)
            Rb, bRb = Rb_r.next()
            self.I("dve", "tensor_copy", [bRf], [bRb], out=Rb[:], in_=Rf[:])
            for t in range(NT):
                tok = slice(t * TT, (t + 1) * TT)
                G, bG = G_r.next()
                self.dma("sp", G[:], self.RG[h, :, tok], r=[self.db("RG", t)], w=[bG])
                qx, bqx = qx_r.next()
                self.I("pool", "tensor_tensor", [bQ, bc], [bqx], out=qx[:], in0=Qb[:, tok], in1=xi[:, h, :], op=ALU.mult)
                pi_, bpo = po_r.next()
                po = self.psF[:, pi_, :]
                for s in range(4):
                    n = 4 * t + s
                    ch = slice(n * 128, (n + 1) * 128)
                    pi_, bps = ps_r.next()
                    ps = self.psF[:, pi_, :]
                    self.I("pe", "matmul", [bK, bQ], [bps], out=ps[:, 0:128], lhsT=Kb[:, ch], rhs=Qb[:, ch], start=True, stop=True)
                    sc, bsc = sc_r.next()
                    self.I("dve", "tensor_tensor", [bps, bc], [bsc], out=sc[:], in0=ps[:, 0:128], in1=dec[:, h, :], op=ALU.mult)
                    self.I("pe", "matmul", [bV, bsc], [bpo], out=po[:, s * 128:(s + 1) * 128], lhsT=Vb[:, n, :], rhs=sc[:], start=True, stop=False)
                    self.I("pe", "matmul", [bRb, bqx], [bpo], out=po[:, s * 128:(s + 1) * 128], lhsT=Rb[:], rhs=qx[:, s * 128:(s + 1) * 128],
                           start=False, stop=True)
                    self.I("pe", "transpose", [bK, self.bconst], [self.pbufB], out=self.psB[:, 0:64], in_=Kb[:, ch], identity=self.identB[0:64, 0:64])
                    kz, bkz = kz_r.next()
                    self.I("act", "activation", [self.pbufB, bc], [bkz], out=kz[:], in_=self.psB[:, 0:64], func=AF.Copy, scale=zeta[:, h:h + 1])
                    pi_, bpk = pk_r.next()
                    pk = self.psF[:, pi_, :]
                    self.I("pe", "matmul", [bkz, bV], [bpk], out=pk[0:64, 0:128], lhsT=kz[:], rhs=Vb[:, n, :], start=True, stop=True)
                    self.I("dve", "scalar_tensor_tensor", [bRf, bpk], [bRf], out=Rf[:], in0=Rf[:], scalar=float(self.gam[h]), in1=pk[0:64, 0:128],
                           op0=ALU.mult, op1=ALU.add)
                    Rb, bRb = Rb_r.next()
                    self.I("dve", "tensor_copy", [bRf], [bRb], out=Rb[:], in_=Rf[:])
                of, bof = of_r.next()
                ob, bob = ob_r.next()
                self.I("act", "copy", [bpo], [bof], out=of[:], in_=po)
                self.I("dve", "tensor_copy", [bof], [bob], out=ob[:, 0, :], in_=of[:])
                self.I("act", "activation", [bof], [bob], out=ob[:, 1, :], in_=of[:], func=AF.Square)
                pm, bpm = self.psF[:, 6, :], self.pbuf[6]
                self.I("pe", "matmul", [bob, self.bconst], [bpm], out=pm, lhsT=self.onesB[:], rhs=ob[:, 0, :], start=True, stop=True)
                pi_, bpq = ps_r.next()
                pq = self.psF[:, pi_, :]
                self.I("pe", "matmul", [bob, self.bconst], [bpq], out=pq, lhsT=self.onesB[:], rhs=ob[:, 1, :], start=True, stop=True)
                m, bm = m_r.next()
                v, bv = v_r.next()
                self.I("act", "mul", [bpm], [bm], out=m[:], in_=pm, mul=1.0 / 128)
                self.I("dve", "tensor_tensor", [bm], [bv], out=v[:], in0=m[:], in1=m[:], op=ALU.mult)
                self.I("dve", "scalar_tensor_tensor", [bpq, bv], [bv], out=v[:], in0=pq, scalar=1.0 / 128, in1=v[:], op0=ALU.mult, op1=ALU.subtract)
                self.I("act", "activation", [bv], [bv], out=v[:], in_=v[:], func=AF.Sqrt, scale=1.0, bias=float(GN_EPS))
                self.I("dve", "reciprocal", [bv], [bv], out=v[:], in_=v[:])
                self.I("pool", "tensor_tensor", [bof, bm], [bof], out=of[:], in0=of[:], in1=m[:], op=ALU.subtract)
                self.I("dve", "tensor_tensor", [bof, bv], [bof], out=of[:], in0=of[:], in1=v[:], op=ALU.mult)
                self.I("dve", "tensor_scalar", [bof, self.bconst], [bof], out=of[:], in0=of[:], scalar1=self.vcol(l, 19 + h), scalar2=self.vcol(l, 23 + h),
                       op0=ALU.mult, op1=ALU.add)
                y, by = y_r.next()
                self.I("pool", "tensor_tensor", [bof, bG], [by], out=y[:], in0=of[:], in1=G[:], op=ALU.mult)
                self.dma("sp", self.OB[2][h, :, tok], y[:], r=[by], w=[self.db("ORET", t)])

    def p_merge(self, l):
        self.begin_phase()
        Wg = self.sb([128, 8, 3072], BF16)
        Wb = self.sb([128, 3, 4, DM], BF16)
        Wo = self.sb([128, 8, DM], BF16)
        bw = Buf()
        win = self.w_in[l].rearrange("(c p) n -> p c n", p=128)
        for j in range(3):
            self.dma("pool", Wg[:, :, j * 1024:(j + 1) * 1024], win[:, :, 3488 + j * 1024:3488 + (j + 1) * 1024], w=[bw])
            self.dma("pool", Wb[:, j, :, :], self.w_b[j][l].rearrange("(c p) n -> p c n", p=128), w=[bw])
        self.dma("pool", Wo[:], self.w_out[l].rearrange("(c p) n -> p c n", p=128), w=[bw])
        hT_r = self.sbr(2, [128, 8, TT], BF16)
        o_r = self.sbr(2, [128, 3, 4, TT], BF16)
        xT_r = self.sbr(2, [128, 8, TT], F32)
        sig_r = self.sbr(3, [128, TT], F32)
        acc_r = self.sbr(2, [128, TT], F32)
        mg_r = self.sbr(2, [128, 8, TT], BF16)
        names = ("OMLA", "OMOBA", "ORET")
        def load3(t):
            tok = slice(t * TT, (t + 1) * TT)
            hT, bh = hT_r.next()
            o, bo = o_r.next()
            xT, bx = xT_r.next()
            self.dma("sp", hT[:], self.HT.rearrange("c p n -> p c n")[:, :, tok], r=[self.db("HT", t)], w=[bh])
            for j in range(3):
                self.dma("sp", o[:, j, :, :], self.OB[j].rearrange("c p n -> p c n")[:, :, tok], r=[self.db(names[j], t)], w=[bo])
            self.dma("sp", xT[:], self.XT.rearrange("c p n -> p c n")[:, :, tok], r=[self.db("XT", t)], w=[bx])
            return hT, bh, o, bo, xT, bx

        nxt3 = load3(0)
        for t in range(NT):
            tok = slice(t * TT, (t + 1) * TT)
            hT, bh, o, bo, xT, bx = nxt3
            if t + 1 < NT:
                nxt3 = load3(t + 1)
            mg, bmg = mg_r.next()
            for oc in range(8):
                acc, bacc = acc_r.next()
                for j in range(3):
                    pg, bpg = self.bank()
                    for c in range(8):
                        self.I("pe", "matmul", [bh, bw], [bpg], out=pg, lhsT=Wg[:, c, j * 1024 + oc * 128:j * 1024 + (oc + 1) * 128], rhs=hT[:, c, :],
                               start=(c == 0), stop=(c == 7))
                    pb, bpb = self.bank()
                    for c in range(4):
                        self.I("pe", "matmul", [bo, bw], [bpb], out=pb, lhsT=Wb[:, j, c, oc * 128:(oc + 1) * 128], rhs=o[:, j, c, :],
                               start=(c == 0), stop=(c == 3))
                    sg, bsg = sig_r.next()
                    self.I("act", "activation", [bpg], [bsg], out=sg[:], in_=pg, func=AF.Sigmoid)
                    if j == 0:
                        self.I("dve", "tensor_tensor", [bsg, bpb], [bacc], out=acc[:], in0=pb, in1=sg[:], op=ALU.mult)
                    else:
                        self.I("dve", "tensor_tensor", [bsg, bpb], [bsg], out=sg[:], in0=pb, in1=sg[:], op=ALU.mult)
                        if j == 1:
                            self.I("pool", "tensor_tensor", [bsg, bacc], [bacc], out=acc[:], in0=acc[:], in1=sg[:], op=ALU.add)
                        else:
                            self.I("pool", "tensor_tensor", [bsg, bacc], [bmg], out=mg[:, oc, :], in0=acc[:], in1=sg[:], op=ALU.add)
            for oc in range(8):
                po, bpo = self.bank()
                for c in range(8):
                    self.I("pe", "matmul", [bmg, bw], [bpo], out=po, lhsT=Wo[:, c, oc * 128:(oc + 1) * 128], rhs=mg[:, c, :], start=(c == 0), stop=(c == 7))
                self.I("dve", "tensor_tensor", [bpo, bx], [bx], out=xT[:, oc, :], in0=po, in1=xT[:, oc, :], op=ALU.add)
            self.dma("sp", self.XT.rearrange("c p n -> p c n")[:, :, tok], xT[:], r=[bx], w=[self.db("XT", t)])

    def p_mlp_up(self, l):
        self.begin_phase()
        Wu = self.sb([128, 8, 4096], BF16)
        bw = Buf()
        wu = self.w_up[l].rearrange("(c p) n -> p c n", p=128)
        for j in range(4):
            self.dma("pool", Wu[:, :, j * 1024:(j + 1) * 1024], wu[:, :, j * 1024:(j + 1) * 1024], w=[bw])
        xT_r = self.sbr(2, [128, 8, TT], F32)
        hT_r = self.sbr(2, [128, 8, TT], BF16)
        sq = self.sb([128, 8, TT], BF16)
        bsq = Buf()
        rs_r = self.sbr(2, [128, TT], F32)
        r_r = self.sbr(3, [128, TT], F32)
        a_r = self.sbr(2, [128, 8, TT], BF16)
        def load4(t):
            tok = slice(t * TT, (t + 1) * TT)
            xT, bx = xT_r.next()
            hT, bh = hT_r.next()
            rs, brs = rs_r.next()
            self.dma("sp", xT[:], self.XT.rearrange("c p n -> p c n")[:, :, tok], r=[self.db("XT", t)], w=[bx])
            self.rmsnorm([xT[:, c, :] for c in range(8)], [self.vcol(l, 8 + c) for c in range(8)], DM, EPS,
                         [hT[:, c, :] for c in range(8)], [bx], [bh], sq, bsq, rs, brs)
            return hT, bh

        nxt4 = load4(0)
        for t in range(NT):
            tok = slice(t * TT, (t + 1) * TT)
            hT, bh = nxt4
            for fg in range(4):
                if fg == 2 and t + 1 < NT:
                    nxt4 = load4(t + 1)
                a, ba = a_r.next()
                for fj in range(8):
                    f = fg * 8 + fj
                    ps, bps = self.bank()
                    for c in range(8):
                        self.I("pe", "matmul", [bh, bw], [bps], out=ps, lhsT=Wu[:, c, f * 128:(f + 1) * 128], rhs=hT[:, c, :], start=(c == 0), stop=(c == 7))
                    r, br = r_r.next()
                    self.I("act", "activation", [bps], [br], out=r[:], in_=ps, func=AF.Relu)
                    self.I("pool" if fj % 2 else "dve", "tensor_tensor", [br], [ba], out=a[:, fj, :], in0=r[:], in1=r[:], op=ALU.mult)
                self.dma("sp", self.ACTS[fg * 8:(fg + 1) * 8].rearrange("c p n -> p c n")[:, :, tok], a[:], r=[ba], w=[self.db("ACTS%d" % fg, t)])

    def p_mlp_down(self, l):
        self.begin_phase()
        Wd = self.sb([128, 32, DM], BF16)
        bw = Buf()
        wd = self.w_down[l].rearrange("(c p) n -> p c n", p=128)
        for j in range(4):
            self.dma("pool", Wd[:, j * 8:(j + 1) * 8, :], wd[:, j * 8:(j + 1) * 8, :], w=[bw])
        xT_r = self.sbr(2, [128, 8, TT], F32)
        a_r = self.sbr(2, [128, 32, TT], BF16)
        def load5(t):
            tok = slice(t * TT, (t + 1) * TT)
            xT, bx = xT_r.next()
            a, ba = a_r.next()
            self.dma("sp", xT[:], self.XT.rearrange("c p n -> p c n")[:, :, tok], r=[self.db("XT", t)], w=[bx])
            for fg in range(4):
                self.dma("sp", a[:, fg * 8:(fg + 1) * 8, :], self.ACTS[fg * 8:(fg + 1) * 8].rearrange("c p n -> p c n")[:, :, tok],
                         r=[self.db("ACTS%d" % fg, t)], w=[ba])
            return xT, bx, a, ba

        nxt5 = load5(0)
        for t in range(NT):
            tok = slice(t * TT, (t + 1) * TT)
            xT, bx, a, ba = nxt5
            if t + 1 < NT:
                nxt5 = load5(t + 1)
            for oc in range(8):
                po, bpo = self.bank()
                for f in range(32):
                    self.I("pe", "matmul", [ba, bw], [bpo], out=po, lhsT=Wd[:, f, oc * 128:(oc + 1) * 128], rhs=a[:, f, :], start=(f == 0), stop=(f == 31))
                self.I("dve", "tensor_tensor", [bpo, bx], [bx], out=xT[:, oc, :], in0=po, in1=xT[:, oc, :], op=ALU.add)
            self.dma("sp", self.XT.rearrange("c p n -> p c n")[:, :, tok], xT[:], r=[bx], w=[self.db("XT", t)])

    def p_final(self):
        self.begin_phase()
        xT_r = self.sbr(2, [128, 8, TT], F32)
        yT_r = self.sbr(2, [128, 8, TT], F32)
        sq = self.sb([128, 8, TT], BF16)
        bsq = Buf()
        rs_r = self.sbr(2, [128, TT], F32)
        o_r = self.sbr(2, [128, DM], F32)
        def load6(t):
            tok = slice(t * TT, (t + 1) * TT)
            xT, bx = xT_r.next()
            self.dma("sp", xT[:], self.XT.rearrange("c p n -> p c n")[:, :, tok], r=[self.db("XT", t)], w=[bx])
            return xT, bx

        nxt6 = load6(0)
        for t in range(NT):
            tok = slice(t * TT, (t + 1) * TT)
            xT, bx = nxt6
            if t + 1 < NT:
                nxt6 = load6(t + 1)
            yT, by = yT_r.next()
            rs, brs = rs_r.next()
            fb = DEPTH * VSTR
            self.rmsnorm([xT[:, c, :] for c in range(8)], [self.vecs[:, fb + c:fb + c + 1] for c in range(8)], DM, EPS,
                         [yT[:, c, :] for c in range(8)], [bx], [by], sq, bsq, rs, brs)
            for s in range(4):
                o, bo = o_r.next()
                for g in range(2):
                    ps, bps = self.bank()
                    for j in range(4):
                        c = g * 4 + j
                        self.I("pe", "transpose", [by, self.bconst], [bps], out=ps[:, j * 128:(j + 1) * 128], in_=yT[:, c, s * 128:(s + 1) * 128],
                               identity=self.identF[:])
                    self.copy(self.evac_eng(), o[:, g * 512:(g + 1) * 512], ps, [bps], [bo])
                r0 = t * TT + s * 128
                self.dma("sp", self.out[r0:r0 + 128, :], o[:], r=[bo], w=[self.db("out", t)])

    def build(self):
        ph = self.phases
        want = lambda name: ph is None or name in ph
        if want("t_in"):
            self.p_transpose_in()
        for l in range(self.n_layers):
            if want(f"inproj{l}"):
                self.p_inproj(l)
            if want(f"mla{l}"):
                self.p_mla(l)
            if want(f"moba{l}"):
                self.p_moba(l)
            if want(f"ret{l}"):
                self.p_ret(l)
            if want(f"merge{l}"):
                self.p_merge(l)
            if want(f"mlpup{l}"):
                self.p_mlp_up(l)
            if want(f"mlpdown{l}"):
                self.p_mlp_down(l)
        if want("final"):
            self.p_final()
        self.S.barrier()
        stats = self.S.emit()
        return self.nc, stats


_CACHE = {}


def get_program():
    if "nc" not in _CACHE:
        kb = KB()
        _CACHE["nc"], _CACHE["stats"] = kb.build()
    return _CACHE["nc"]


def make_in_map(inputs, b, consts, vecs):
    m = {"x": np.ascontiguousarray(inputs["x"][b])}
    for k in ("w_in", "mla_w_q_up", "mla_w_kv_up", "w_branch_mla", "w_branch_moba", "w_branch_ret", "w_out", "w_mlp_up", "w_mlp_down"):
        m[k] = np.ascontiguousarray(inputs[k], dtype=np.float32)
    m["vecs"] = vecs
    m.update(consts)
    return m


def kernel(**inputs):
    inputs = {k: np.asarray(v) for k, v in inputs.items()}
    consts, _ = make_consts()
    vecs = pack_vecs(inputs)
    nc = get_program()
    in_maps = [make_in_map(inputs, b, consts, vecs) for b in range(NCORES)]
    res = run_bass_kernel_spmd(nc, in_maps, core_ids=list(range(NCORES)))
    out = np.stack([np.asarray(res.results[b]["out"], dtype=np.float32) for b in range(NCORES)], axis=0)
    return out
```

```python
import numpy as np
import concourse.bass as bass
import concourse.mybir as mybir
from concourse.bass_utils import run_bass_kernel_spmd

F32 = mybir.dt.float32
BF16 = mybir.dt.bfloat16
AF = mybir.ActivationFunctionType
ALU = mybir.AluOpType
AX = mybir.AxisListType

SEQ = 4096
DM = 1024
TT = 512
NT = SEQ // TT
DEPTH = 2
NCORES = 4
EPS = 1e-6
GN_EPS = 1e-5
NEGB = -30000.0


class Buf:
    __slots__ = ("name", "w", "r", "rd")

    def __init__(self, name=""):
        self.name = name
        self.w = None
        self.r = {}
        self.rd = []


class _Op:
    __slots__ = ("eng", "fn", "deps", "is_dma", "idx", "need_inc", "semidx", "semval")


class Sched:
    ENG = ("pe", "act", "dve", "pool", "sp")

    def __init__(self, nc, n_dma_sems=40, same_eng_sync=("act", "dve", "pool")):
        import os
        if os.environ.get("K_VAR", "") == "pesync":
            same_eng_sync = ("act", "dve", "pool", "pe")
        self.nc = nc
        self.ops = []
        self.n_dma_sems = n_dma_sems
        self.same_eng_sync = set(same_eng_sync)
        self.last_on_eng = {}
        self.dmas_open = []
        self.cur_barrier = None

    def op(self, eng, fn, reads=(), writes=(), dma=False):
        o = _Op()
        o.eng = eng
        o.fn = fn
        o.is_dma = dma
        o.idx = len(self.ops)
        o.need_inc = False
        deps = set()
        for b in reads:
            if b.w is not None:
                deps.add(b.w)
        for b in writes:
            if b.w is not None:
                deps.add(b.w)
            deps.update(b.r.values())
            deps.update(b.rd)
        for b in reads:
            if dma:
                b.rd.append(o.idx)
            else:
                b.r[eng] = o.idx
        for b in writes:
            b.w = o.idx
            b.r = {}
            b.rd = []
        if self.cur_barrier is not None:
            deps.add(self.cur_barrier)
        deps.discard(o.idx)
        o.deps = deps
        self.ops.append(o)
        if dma:
            self.dmas_open.append(o.idx)
        else:
            self.last_on_eng[eng] = o.idx
        return o

    def barrier(self):
        deps = set(self.last_on_eng.values()) | set(self.dmas_open)
        o = self.op("sp", lambda e: e.nop())
        o.deps |= deps
        o.deps.discard(o.idx)
        self.dmas_open = []
        self.cur_barrier = o.idx
        return o

    def emit(self):
        nc = self.nc
        ops = self.ops
        engobj = {"pe": nc.tensor, "act": nc.scalar, "dve": nc.vector, "pool": nc.gpsimd, "sp": nc.sync}
        ses = self.same_eng_sync

        def needs_sync(o, od):
            if od.is_dma or o.is_dma:
                return True
            if od.eng != o.eng:
                return True
            return od.eng in ses

        for o in ops:
            for d in o.deps:
                od = ops[d]
                if not od.is_dma and needs_sync(o, od):
                    od.need_inc = True
        cnt = {e: 0 for e in self.ENG}
        ndma = 0
        qpool = {"sp": (0, 32), "pool": (32, 24), "act": (56, 8)}
        qcnt = {q: 0 for q in qpool}
        for o in ops:
            if o.is_dma:
                base, n = qpool[o.eng]
                k = qcnt[o.eng]
                qcnt[o.eng] += 1
                o.semidx = base + (k % n)
                o.semval = 16 * (k // n + 1)
                ndma += 1
            elif o.need_inc:
                cnt[o.eng] += 1
                o.semval = cnt[o.eng]
        engsem = {e: nc.alloc_semaphore(name=f"s_{e}") for e in self.ENG}
        dmasem = [nc.alloc_semaphore(name=f"s_dma{i}") for i in range(64)]
        waited = {}
        nwaits = 0
        self.trace = {e: [] for e in self.ENG}
        for o in ops:
            e = engobj[o.eng]
            waits = {}
            for d in o.deps:
                od = ops[d]
                if not needs_sync(o, od):
                    continue
                key = ("dma", od.semidx) if od.is_dma else ("eng", od.eng)
                if waits.get(key, 0) < od.semval:
                    waits[key] = od.semval
            if o.is_dma and o.semval > 16:
                key = ("dma", o.semidx)
                if waits.get(key, 0) < o.semval - 16:
                    waits[key] = o.semval - 16
            for key, val in waits.items():
                wk = (o.eng, key)
                if waited.get(wk, 0) >= val:
                    continue
                sem = dmasem[key[1]] if key[0] == "dma" else engsem[key[1]]
                e.wait_ge(sem, val)
                self.trace[o.eng].append(("w", key, val, o.idx))
                waited[wk] = val
                nwaits += 1
            ins = o.fn(e)
            if o.is_dma:
                ins.then_inc(dmasem[o.semidx], 16)
                self.trace[o.eng].append(("i", ("dma", o.semidx), 16, o.idx))
            elif o.need_inc:
                ins.then_inc(engsem[o.eng], 1)
                self.trace[o.eng].append(("i", ("eng", o.eng), 1, o.idx))
        self.stats = dict(n_ops=len(ops), n_waits=nwaits, incs=dict(cnt), n_dma=ndma)
        return self.stats


class Ring:
    def __init__(self, items):
        self.items = items
        self.i = 0

    def next(self):
        x = self.items[self.i % len(self.items)]
        self.i += 1
        return x


def _rope_tab(dim, theta):
    inv = (1.0 / (np.float32(theta) ** (np.arange(0, dim, 2, dtype=np.float32) / np.float32(dim)))).astype(np.float32)
    ang = (np.arange(SEQ, dtype=np.float32)[:, None] * inv[None, :]).astype(np.float32)
    c = np.cos(ang).astype(np.float32).T
    s = np.sin(ang).astype(np.float32).T
    cc = np.concatenate([c, c], 0)
    ss = np.concatenate([-s, s], 0)
    return np.ascontiguousarray(np.stack([cc, ss], 0))


def make_consts():
    c = {}
    c["identF"] = np.eye(128, dtype=np.float32)
    c["rope_mla"] = _rope_tab(32, 500000.0)
    rm = _rope_tab(16, 500000.0)
    pad = np.zeros((2, 32, SEQ), np.float32)
    pad[:, 0:16] = rm
    pad[0, 16:32] = 1.0
    c["rope_moba"] = pad
    c["rope_ret"] = _rope_tab(64, 10000.0)
    k = np.arange(128)
    c["mask01"] = (k[:, None] <= k[None, :]).astype(np.float32)
    c["onehot"] = (np.arange(16)[:, None] == (np.arange(SEQ)[None, :] // 256)).astype(np.float32)
    lg = np.log(np.float32(1.0) - np.float32(2.0) ** (np.float32(-5.0) - np.arange(4, dtype=np.float32))).astype(np.float32)
    pos = np.arange(128, dtype=np.float32)
    diff = pos[:, None] - pos[None, :]
    dec = np.where(diff >= 0, np.exp(lg[:, None, None] * diff), 0.0).astype(np.float32)
    c["decayT"] = np.ascontiguousarray(dec.transpose(0, 2, 1) * np.float32(0.125)).astype(np.float32)
    xi = np.exp(lg[:, None] * (pos + 1.0)).astype(np.float32)
    c["xi"] = np.ascontiguousarray(np.broadcast_to(np.tile(xi, (1, 4))[:, None, :], (4, 64, 512))).astype(np.float32)
    zeta = np.exp(lg[:, None] * (127.0 - pos)).astype(np.float32) * np.float32(0.125)
    c["zeta"] = np.ascontiguousarray(zeta.T).astype(np.float32)
    gam = [float(np.exp(lg[h] * np.float32(128.0))) for h in range(4)]
    return c, gam


VSTR = 27
NV = DEPTH * VSTR + 8


def pack_vecs(inp):
    v = np.zeros((128, NV), np.float32)
    for l in range(DEPTH):
        b = l * VSTR
        v[:, b + 0:b + 8] = inp["attn_norm"][l].reshape(8, 128).T
        v[:, b + 8:b + 16] = inp["mlp_norm"][l].reshape(8, 128).T
        v[:, b + 16:b + 18] = inp["mla_q_norm"][l].reshape(2, 128).T
        v[:, b + 18:b + 19] = inp["mla_kv_norm"][l].reshape(1, 128).T
        v[:, b + 19:b + 23] = inp["ret_gn_w"][l].reshape(4, 128).T
        v[:, b + 23:b + 27] = inp["ret_gn_b"][l].reshape(4, 128).T
    v[:, DEPTH * VSTR:DEPTH * VSTR + 8] = inp["final_norm"].reshape(8, 128).T
    return v


class KB:
    SB_BASE = 20480
    SB_TOP = 229376

    def __init__(self, n_layers=DEPTH, debug=False, phases=None):
        self.nc = nc = bass.Bass("TRN2", target_bir_lowering=False)
        self.S = Sched(nc)
        self.debug = debug
        self.n_layers = n_layers
        self.phases = phases
        self.top = self.SB_BASE
        self.uid = 0
        L = DEPTH
        inp = lambda name, shape: nc.dram_tensor(name, shape, F32, kind="ExternalInput").ap()
        self.x = inp("x", [SEQ, DM])
        self.w_in = inp("w_in", [L, DM, 6560])
        self.w_q_up = inp("mla_w_q_up", [L, 256, 768])
        self.w_kv_up = inp("mla_w_kv_up", [L, 128, 1024])
        self.w_b = [inp("w_branch_mla", [L, 512, DM]), inp("w_branch_moba", [L, 512, DM]), inp("w_branch_ret", [L, 512, DM])]
        self.w_out = inp("w_out", [L, DM, DM])
        self.w_up = inp("w_mlp_up", [L, DM, 4096])
        self.w_down = inp("w_mlp_down", [L, 4096, DM])
        self.vecs_d = inp("vecs", [128, NV])
        self.c_identF = inp("identF", [128, 128])
        self.c_rope_mla = inp("rope_mla", [2, 32, SEQ])
        self.c_rope_moba = inp("rope_moba", [2, 32, SEQ])
        self.c_rope_ret = inp("rope_ret", [2, 64, SEQ])
        self.c_mask01 = inp("mask01", [128, 128])
        self.c_onehot = inp("onehot", [16, SEQ])
        self.c_decayT = inp("decayT", [4, 128, 128])
        self.c_xi = inp("xi", [4, 64, 512])
        self.c_zeta = inp("zeta", [128, 4])
        self.out = nc.dram_tensor("out", [SEQ, DM], F32, kind="ExternalOutput").ap()
        kind = "ExternalOutput" if debug else "Internal"
        scr = lambda name, shape, dt: nc.dram_tensor(name, shape, dt, kind=kind).ap()
        self.XT = scr("XT", [8, 128, SEQ], F32)
        self.HT = scr("HT", [8, 128, SEQ], BF16)
        self.CQ = scr("CQ", [2, 128, SEQ], BF16)
        self.CKV = scr("CKV", [128, SEQ], BF16)
        self.KPE = scr("KPE", [32, SEQ], BF16)
        self.MQ = scr("MQ", [8, 64, SEQ], BF16)
        self.MK = scr("MK", [8, 64, SEQ], BF16)
        self.MV = scr("MV", [SEQ, 512], BF16)
        self.RQ = scr("RQ", [4, 64, SEQ], BF16)
        self.RK = scr("RK", [4, 64, SEQ], BF16)
        self.RV = scr("RV", [SEQ, 512], BF16)
        self.RG = scr("RG", [4, 128, SEQ], F32)
        self.OB = [scr("OMLA", [4, 128, SEQ], BF16), scr("OMOBA", [4, 128, SEQ], BF16), scr("ORET", [4, 128, SEQ], BF16)]
        self.ACTS = scr("ACTS", [32, 128, SEQ], BF16)
        self.dbuf = {}
        self.psF = nc.alloc_psum_tensor("psF", [128, 7, 512], F32)
        self.psB = nc.alloc_psum_tensor("psB", [128, 1024], BF16)
        self.pbuf = [Buf(f"ps{i}") for i in range(7)]
        self.pbufB = Buf("psB")
        self.bank_i = 0
        _, self.gam = make_consts()
        self.identF = self.sb([128, 128], F32)
        self.identB = self.sb([128, 128], BF16)
        self.onesB = self.sb([128, 128], BF16)
        self.mask01 = self.sb([128, 128], BF16)
        self.vecs = self.sb([128, NV], F32)
        self.bconst = Buf("const")
        self.dma("sp", self.identF[:], self.c_identF, w=[self.bconst])
        self.dma("pool", self.identB[:], self.c_identF, w=[self.bconst])
        self.dma("pool", self.mask01[:], self.c_mask01, w=[self.bconst])
        self.dma("sp", self.vecs[:], self.vecs_d, w=[self.bconst])
        self.I("pool", "memset", [], [self.bconst], ap=self.onesB[:], constant=1.0)
        self.maskneg = self.sb([128, 128], BF16)
        self.I("pool", "tensor_scalar", [self.bconst], [self.bconst], out=self.maskneg[:], in0=self.mask01[:], scalar1=-1.0, scalar2=-NEGB,
               op0=ALU.add, op1=ALU.mult)
        self.phase_base = self.top

    def sb(self, shape, dtype, name=None):
        sz = int(np.prod(shape[1:])) * (4 if dtype == F32 else 2)
        sz = (sz + 63) // 64 * 64
        self.uid += 1
        t = self.nc.alloc_sbuf_tensor_at(name or f"t{self.uid}", list(shape), dtype, offset=self.top)
        self.top += sz
        assert self.top <= self.SB_TOP, f"SBUF overflow {self.top}"
        return t

    def sbr(self, n, shape, dtype):
        return Ring([(self.sb(shape, dtype), Buf()) for _ in range(n)])

    def begin_phase(self):
        self.S.barrier()
        self.top = self.phase_base

    def I(self, eng, meth, r, w, **kw):
        return self.S.op(eng, lambda e: getattr(e, meth)(**kw), r, w)

    def dma(self, eng, out, in_, r=(), w=()):
        return self.S.op(eng, lambda e: e.dma_start(out=out, in_=in_), r, w, dma=True)

    def bank(self):
        i = self.bank_i % 7
        self.bank_i += 1
        return self.psF[:, i, :], self.pbuf[i]

    def db(self, name, t=None):
        k = (name, t)
        if k not in self.dbuf:
            self.dbuf[k] = Buf(f"{name}{t}")
        return self.dbuf[k]

    def dball(self, name):
        return [self.db(name, t) for t in range(NT)]

    def vcol(self, l, off, n=1):
        b = l * VSTR + off
        return self.vecs[:, b:b + n]

    def evac_eng(self):
        self._ev = getattr(self, "_ev", 0) + 1
        return "act" if self._ev % 2 else "dve"

    def copy(self, eng, out, in_, r, w):
        if eng == "act":
            return self.I("act", "copy", r, w, out=out, in_=in_)
        return self.I(eng, "tensor_copy", r, w, out=out, in_=in_)

    def rmsnorm(self, xaps, gaps, nfeat, eps, outaps, rb, wb, sq, bsq, rs, brs, nparts=128):
        n = len(xaps)
        for c in range(n):
            self.I("act", "activation", rb, [bsq], out=sq[0:nparts, c, :], in_=xaps[c], func=AF.Square)
        ps, bps = self.bank()
        for c in range(n):
            self.I("pe", "matmul", [bsq, self.bconst], [bps], out=ps, lhsT=self.onesB[0:nparts, :], rhs=sq[0:nparts, c, :],
                   start=(c == 0), stop=(c == n - 1))
        self.I("act", "activation", [bps], [brs], out=rs[:], in_=ps, func=AF.Sqrt, scale=1.0 / nfeat, bias=float(eps))
        self.I("dve", "reciprocal", [brs], [brs], out=rs[:], in_=rs[:])
        for c in range(n):
            self.I("dve", "scalar_tensor_tensor", list(rb) + [brs, self.bconst], wb, out=outaps[c], in0=xaps[c], scalar=gaps[c],
                   in1=rs[0:nparts, :], op0=ALU.mult, op1=ALU.mult)

    def p_transpose_in(self):
        self.begin_phase()
        xin = self.sbr(2, [128, DM], F32)
        xo = self.sbr(2, [128, 8, TT], F32)
        for t in range(NT):
            xot, bxo = xo.next()
            tiles = []
            for s in range(4):
                xt, bx = xin.next()
                r0 = t * TT + s * 128
                self.dma("sp", xt[:], self.x[r0:r0 + 128, :], w=[bx])
                for g in range(2):
                    ps, bps = self.bank()
                    for j in range(4):
                        c = g * 4 + j
                        self.I("pe", "transpose", [bx, self.bconst], [bps], out=ps[:, j * 128:(j + 1) * 128], in_=xt[:, c * 128:(c + 1) * 128],
                               identity=self.identF[:])
                    self.copy(self.evac_eng(), xot[:, g * 4:(g + 1) * 4, s * 128:(s + 1) * 128],
                              ps.rearrange("p (j n) -> p j n", n=128), [bps], [bxo])
            self.dma("sp", self.XT.rearrange("c p n -> p c n")[:, :, t * TT:(t + 1) * TT], xot[:], r=[bxo], w=[self.db("XT", t)])

    def p_inproj(self, l):
        self.begin_phase()
        W1 = self.sb([128, 8, 4288], BF16)
        bW = []
        win = self.w_in[l].rearrange("(c p) n -> p c n", p=128)

        def wl(d0, s0, n):
            b = Buf()
            self.dma("pool", W1[:, :, d0:d0 + n], win[:, :, s0:s0 + n], w=[b])
            bW.append(b)

        wl(0, 0, 416)
        wl(416, 400, 16)
        wl(432, 384, 16)
        wl(448, 416, 512)
        wl(1088, 928, 512)
        wl(1728, 1440, 512)
        wl(2240, 1952, 256)
        wl(2752, 2208, 256)
        wl(3264, 2464, 512)
        wl(3776, 2976, 512)
        for (dst, src, H, dh, half) in ((960, 416, 8, 16, 8), (1600, 928, 8, 16, 8), (2496, 1952, 4, 64, 32), (3008, 2208, 4, 64, 32)):
            for c in range(8):
                dstv = W1[:, c, dst:dst + H * dh].rearrange("p (h e) -> p h e", e=dh)
                srcv = win[:, c, src:src + H * 64].rearrange("p (h d) -> p h d", d=64)
                b = Buf()
                self.dma("pool", dstv[:, :, 0:half], srcv[:, :, half:2 * half], w=[b])
                self.dma("pool", dstv[:, :, half:2 * half], srcv[:, :, 0:half], w=[b])
                bW.append(b)
        xTr = self.sbr(1, [128, 8, TT], F32)
        hTr = self.sbr(2, [128, 8, TT], BF16)
        sq = self.sb([128, 8, TT], BF16)
        bsq = Buf()
        rsr = self.sbr(2, [128, TT], F32)
        cqf = self.sb([128, 3, TT], F32)
        bcqf = Buf()
        cqn = self.sb([128, 3, TT], BF16)
        bcqn = Buf()
        sq2 = self.sb([128, 2, TT], BF16)
        bsq2 = Buf()
        tabk = self.sb([32, 2, TT], F32)
        tabm = self.sb([32, 2, TT], F32)
        tabr = self.sb([64, 2, TT], F32)
        btab = Buf()
        t1r = self.sbr(3, [64, TT], F32)
        t2r = self.sbr(3, [64, TT], F32)
        kpe = self.sb([32, TT], BF16)
        bkpe = Buf()
        stq = self.sb([64, 8, TT], BF16)
        stk = self.sb([64, 8, TT], BF16)
        strq = self.sb([64, 4, TT], BF16)
        strk = self.sb([64, 4, TT], BF16)
        stg = self.sb([128, 4, TT], F32)
        stv = self.sbr(2, [128, 4, 512], BF16)
        bst = {k: Buf() for k in ("q", "k", "rq", "rk", "g")}
        import os
        dbg_nt = int(os.environ.get("K_NT", NT))
        dbg_stage = int(os.environ.get("K_STAGE", 99))
        for t in range(dbg_nt):
            tok = slice(t * TT, (t + 1) * TT)
            xT, bx = xTr.next()
            hT, bh = hTr.next()
            rs, brs = rsr.next()
            self.dma("sp", xT[:], self.XT.rearrange("c p n -> p c n")[:, :, tok], r=[self.db("XT", t)], w=[bx])
            self.dma("sp", tabk[:], self.c_rope_mla.rearrange("a p n -> p a n")[:, :, tok], w=[btab])
            self.dma("sp", tabm[:], self.c_rope_moba.rearrange("a p n -> p a n")[:, :, tok], w=[btab])
            self.dma("sp", tabr[:], self.c_rope_ret.rearrange("a p n -> p a n")[:, :, tok], w=[btab])
            self.rmsnorm([xT[:, c, :] for c in range(8)], [self.vcol(l, c) for c in range(8)], DM, EPS,
                         [hT[:, c, :] for c in range(8)], [bx], [bh], sq, bsq, rs, brs)
            self.dma("sp", self.HT.rearrange("c p n -> p c n")[:, :, tok], hT[:], r=[bh], w=[self.db("HT", t)])
            if dbg_stage < 1:
                continue

            def mm(M, col, N0=0, N1=TT):
                ps, bps = self.bank()
                for c in range(8):
                    self.I("pe", "matmul", [bh] + bW, [bps], out=ps[0:M, 0:N1 - N0], lhsT=W1[:, c, col:col + M], rhs=hT[:, c, N0:N1],
                           start=(c == 0), stop=(c == 7))
                return ps, bps

            for j in range(3):
                ps, bps = mm(128, j * 128)
                self.copy(self.evac_eng(), cqf[:, j, :], ps, [bps], [bcqf])
            rs2, brs2 = rsr.next()
            self.rmsnorm([cqf[:, c, :] for c in range(2)], [self.vcol(l, 16 + c) for c in range(2)], 256, EPS,
                         [cqn[:, c, :] for c in range(2)], [bcqf], [bcqn], sq2, bsq2, rs2, brs2)
            self.dma("sp", self.CQ.rearrange("c p n -> p c n")[:, :, tok], cqn[:, 0:2, :], r=[bcqn], w=[self.db("CQ", t)])
            rs3, brs3 = rsr.next()
            self.rmsnorm([cqf[:, 2, :]], [self.vcol(l, 18)], 128, EPS, [cqn[:, 2, :]], [bcqf], [bcqn], sq2, bsq2, rs3, brs3)
            self.dma("sp", self.CKV[:, tok], cqn[:, 2, :], r=[bcqn], w=[self.db("CKV", t)])

            def rope(psm, bpm, psr, bpr, n, tab, outap, wb):
                t1, b1 = t1r.next()
                t2, b2 = t2r.next()
                self.I("dve", "tensor_tensor", [bpm, btab], [b1], out=t1[0:n, :], in0=psm[0:n, :], in1=tab[0:n, 0, :], op=ALU.mult)
                self.I("dve", "tensor_tensor", [bpr, btab], [b2], out=t2[0:n, :], in0=psr[0:n, :], in1=tab[0:n, 1, :], op=ALU.mult)
                self.I("pool", "tensor_tensor", [b1, b2], wb, out=outap, in0=t1[0:n, :], in1=t2[0:n, :], op=ALU.add)

            if dbg_stage < 2:
                continue
            psm, bpm = mm(32, 384)
            psr, bpr = mm(32, 416)
            rope(psm, bpm, psr, bpr, 32, tabk, kpe[:], [bkpe])
            self.dma("sp", self.KPE[:, tok], kpe[:], r=[bkpe], w=[self.db("KPE", t)])
            if dbg_stage < 3:
                continue
            for (st, key, base, rbase, dst) in ((stq, "q", 448, 960, self.MQ), (stk, "k", 1088, 1600, self.MK)):
                for h in range(8):
                    psm, bpm = mm(64, base + h * 64)
                    psr, bpr = mm(32, rbase + h * 16)
                    self.I("act", "copy", [bpm], [bst[key]], out=st[32:64, h, :], in_=psm[32:64, :])
                    rope(psm, bpm, psr, bpr, 32, tabm, st[0:32, h, :], [bst[key]])
                self.dma("sp", dst.rearrange("h p n -> p h n")[:, :, tok], st[:], r=[bst[key]], w=[self.db("M" + key, t)])
            if dbg_stage < 4:
                continue
            for (st, key, base, rbase, dst) in ((strq, "rq", 2240, 2496, self.RQ), (strk, "rk", 2752, 3008, self.RK)):
                for h in range(4):
                    psm, bpm = mm(64, base + h * 64)
                    psr, bpr = mm(64, rbase + h * 64)
                    rope(psm, bpm, psr, bpr, 64, tabr, st[:, h, :], [bst[key]])
                self.dma("sp", dst.rearrange("h p n -> p h n")[:, :, tok], st[:], r=[bst[key]], w=[self.db(key, t)])
            if dbg_stage < 5:
                continue
            for j in range(4):
                ps, bps = mm(128, 3776 + j * 128)
                self.I("act", "activation", [bps], [bst["g"]], out=stg[:, j, :], in_=ps, func=AF.Silu)
            self.dma("sp", self.RG.rearrange("c p n -> p c n")[:, :, tok], stg[:], r=[bst["g"]], w=[self.db("RG", t)])
            if dbg_stage < 6:
                continue
            for (col, dst, key) in ((1728, self.MV, "MV"), (3264, self.RV, "RV")):
                sv, bsv = stv.next()
                for s in range(4):
                    ps, bps = self.bank()
                    for c in range(8):
                        self.I("pe", "matmul", [bh] + bW, [bps], out=ps, lhsT=hT[:, c, s * 128:(s + 1) * 128], rhs=W1[:, c, col:col + 512],
                               start=(c == 0), stop=(c == 7))
                    self.copy(self.evac_eng(), sv[:, s, :], ps, [bps], [bsv])
                self.dma("sp", dst[t * TT:(t + 1) * TT, :].rearrange("(s p) n -> p s n", p=128), sv[:], r=[bsv], w=[self.db(key, t)])

    def run_attn(self, tiles, scale, pss_r, pT_r, pso_r, LA=2):
        steps = [(i, kt) for i, T in enumerate(tiles) for kt in range(4 * T["t"] + 4)]
        info = {}

        def emit_s(si):
            i, kt = steps[si]
            T = tiles[i]
            t = T["t"]
            qT, qbufs = T["get_q"]()
            j = kt - 4 * t
            q0 = max(0, j) * 128
            N = TT - q0
            (pi, bpi) = pss_r.next()
            pss = self.psF[:, pi, :]
            self.I("pe", "matmul", [T["bK"]] + list(qbufs), [bpi], out=pss[:, 0:N], lhsT=T["K"][:, kt * 128:(kt + 1) * 128], rhs=qT[:, q0:TT],
                   start=True, stop=(j < 0))
            if j >= 0:
                self.I("pe", "matmul", [self.bconst], [bpi], out=pss[:, 0:128], lhsT=self.identB[:], rhs=self.maskneg[:], start=False, stop=True)
            info[si] = (pss, bpi, q0, N)

        for si in range(min(LA, len(steps))):
            emit_s(si)
        for si, (i, kt) in enumerate(steps):
            T = tiles[i]
            nkt = 4 * T["t"] + 4
            if kt == 0:
                if T.get("start_hook") is not None:
                    T["start_hook"]()
                po, bpso = pso_r.next()
                T["pso"] = (self.psF[:, po, :], bpso)
            if si + LA < len(steps):
                emit_s(si + LA)
            if kt == 1 and T.get("mid_hook") is not None:
                T["mid_hook"]()
            pss, bpi, q0, N = info.pop(si)
            pso, bpso = T["pso"]
            pT, bpT = pT_r.next()
            self.I("act", "activation", [bpi], [bpT], out=pT[:, 0:N], in_=pss[:, 0:N], func=AF.Exp, scale=float(scale))
            self.I("pe", "matmul", [T["bV"], bpT], [bpso], out=pso[:, q0:TT], lhsT=T["V"][:, kt, :], rhs=pT[:, 0:N],
                   start=(kt == 0), stop=(kt == nkt - 1))
            if kt == nkt - 1:
                T["finish"](pso, bpso)

    def attn_finish(self, pso, bpso, rec_r, ost_r, dst_ap, dbuf):
        rec, brec = rec_r.next()
        ost, bost = ost_r.next()
        self.I("dve", "reciprocal", [bpso], [brec], out=rec[0:64, :], in_=pso[64:128, :])
        self.I("dve", "tensor_tensor", [bpso, brec], [bost], out=ost[0:64, :], in0=pso[0:64, :], in1=rec[0:64, :], op=ALU.mult)
        self.dma("sp", dst_ap, ost[0:64, :], r=[bost], w=[dbuf])

    def p_mla(self, l):
        self.begin_phase()
        CQs = self.sb([128, 2, SEQ], BF16)
        CKVs = self.sb([128, SEQ], BF16)
        bin_ = Buf()
        self.dma("sp", CQs[:], self.CQ.rearrange("c p n -> p c n"), r=self.dball("CQ"), w=[bin_])
        self.dma("sp", CKVs[:], self.CKV, r=self.dball("CKV"), w=[bin_])
        Wq = self.sb([128, 2, 8, 96], BF16)
        Wqr = self.sb([128, 2, 8, 96], BF16)
        Wkv = self.sb([128, 8, 128], BF16)
        bw = Buf()
        wq = self.w_q_up[l].rearrange("(c p) (h d) -> p c h d", p=128, d=96)
        self.I("pool", "memset", [], [bw], ap=Wqr[:], constant=0.0)
        for c in range(2):
            self.dma("pool", Wq[:, c, :, :], wq[:, c, :, :], w=[bw])
            self.dma("pool", Wqr[:, c, :, 64:80], wq[:, c, :, 80:96], w=[bw])
            self.dma("pool", Wqr[:, c, :, 80:96], wq[:, c, :, 64:80], w=[bw])
        self.dma("pool", Wkv[:], self.w_kv_up[l].rearrange("p (h d) -> p h d", d=128), w=[bw])
        tab = self.sb([96, 2, SEQ], F32)
        btab = Buf()
        self.dma("sp", tab[64:96, :, :], self.c_rope_mla.rearrange("a p n -> p a n"), w=[btab])
        Kr = self.sbr(2, [128, SEQ], BF16)
        Vr = self.sbr(2, [128, 32, 128], BF16)
        for (Kb, bK) in Kr.items:
            self.I("pool", "memset", [], [bK], ap=Kb[96:128, :], constant=0.0)
            self.dma("sp", Kb[64:96, :], self.KPE, r=self.dball("KPE"), w=[bK])
        for (Vb, bV) in Vr.items:
            self.I("pool", "memset", [], [bV], ap=Vb[:, :, 64:128], constant=1.0)
        qTr = self.sbr(3, [128, TT], BF16)
        for (q_, bq_) in qTr.items:
            self.I("pool", "memset", [], [bq_], ap=q_[96:128, :], constant=0.0)
        t1r = self.sbr(2, [96, TT], F32)
        t2r = self.sbr(2, [96, TT], F32)
        pT_r = self.sbr(4, [128, TT], BF16)
        rec_r = self.sbr(2, [64, TT], F32)
        ost_r = self.sbr(2, [64, TT], BF16)
        pss_r = Ring([(0, self.pbuf[0]), (1, self.pbuf[1]), (2, self.pbuf[2])])
        pso_r = Ring([(3, self.pbuf[3]), (4, self.pbuf[4])])
        aux_r = Ring([(5, self.pbuf[5]), (6, self.pbuf[6])])
        scale = 96 ** -0.5
        KV = {}

        def build_kv(h):
            Kb, bK = Kr.items[h % 2]
            Vb, bV = Vr.items[h % 2]
            KV[h] = (Kb, bK, Vb, bV)
            for t in range(NT):
                pi, bpi = aux_r.next()
                ps = self.psF[:, pi, :]
                self.I("pe", "matmul", [bin_, bw], [bpi], out=ps[0:64, :], lhsT=Wkv[:, h, 0:64], rhs=CKVs[:, t * TT:(t + 1) * TT], start=True, stop=True)
                self.copy(self.evac_eng(), Kb[0:64, t * TT:(t + 1) * TT], ps[0:64, :], [bpi], [bK])
            for g in range(4):
                pi, bpi = aux_r.next()
                ps = self.psF[:, pi, :]
                for j in range(8):
                    kt = g * 8 + j
                    self.I("pe", "matmul", [bin_, bw], [bpi], out=ps[:, j * 64:(j + 1) * 64], lhsT=CKVs[:, kt * 128:(kt + 1) * 128],
                           rhs=Wkv[:, h, 64:128], start=True, stop=True)
                self.copy(self.evac_eng(), Vb[:, g * 8:(g + 1) * 8, 0:64], ps.rearrange("p (j d) -> p j d", d=64), [bpi], [bV])

        def build_q(h, t):
            tok = slice(t * TT, (t + 1) * TT)
            qT, bq = qTr.next()
            pi, bpm = aux_r.next()
            psm = self.psF[:, pi, :]
            pi2, bpr = aux_r.next()
            psr = self.psF[:, pi2, :]
            for c in range(2):
                self.I("pe", "matmul", [bin_, bw], [bpm], out=psm[0:96, :], lhsT=Wq[:, c, h, :], rhs=CQs[:, c, tok], start=(c == 0), stop=(c == 1))
            for c in range(2):
                self.I("pe", "matmul", [bin_, bw], [bpr], out=psr[0:96, :], lhsT=Wqr[:, c, h, :], rhs=CQs[:, c, tok], start=(c == 0), stop=(c == 1))
            self.I("act", "copy", [bpm], [bq], out=qT[0:64, :], in_=psm[0:64, :])
            t1, b1 = t1r.next()
            t2, b2 = t2r.next()
            self.I("dve", "tensor_tensor", [bpm, btab], [b1], out=t1[64:96, :], in0=psm[64:96, :], in1=tab[64:96, 0, tok], op=ALU.mult)
            self.I("dve", "tensor_tensor", [bpr, btab], [b2], out=t2[64:96, :], in0=psr[64:96, :], in1=tab[64:96, 1, tok], op=ALU.mult)
            self.I("pool", "tensor_tensor", [b1, b2], [bq], out=qT[64:96, :], in0=t1[64:96, :], in1=t2[64:96, :], op=ALU.add)
            return qT, bq

        items = [(h, t) for h in range(8) for t in range(NT)]
        build_kv(0)
        prepared = {items[0]: build_q(*items[0])}
        tiles = []
        for idx, (h, t) in enumerate(items):
            def start_hook(idx=idx):
                if idx + 1 < len(items):
                    hn, tn = items[idx + 1]
                    if tn == 0:
                        build_kv(hn)
                    prepared[(hn, tn)] = build_q(hn, tn)

            def get_q(h=h, t=t):
                qT, bq = prepared[(h, t)]
                return qT[:, :], [bq]

            def finish(pso, bpso, h=h, t=t):
                tok = slice(t * TT, (t + 1) * TT)
                self.attn_finish(pso, bpso, rec_r, ost_r, self.OB[0][h // 2, (h % 2) * 64:(h % 2) * 64 + 64, tok], self.db("OMLA", t))

            tiles.append(dict(t=t, h=h, get_q=get_q, start_hook=start_hook, finish=finish))
        class _KV(dict):
            pass
        for T in tiles:
            T["_h"] = T["h"]
        for T in tiles:
            hh = T["h"]
            T["K"] = Kr.items[hh % 2][0]
            T["bK"] = Kr.items[hh % 2][1]
            T["V"] = Vr.items[hh % 2][0]
            T["bV"] = Vr.items[hh % 2][1]
        self.run_attn(tiles, scale, pss_r, pT_r, pso_r)

    def p_moba(self, l):
        self.begin_phase()
        Qr = self.sbr(2, [128, SEQ], BF16)
        Kr = self.sbr(2, [128, SEQ], BF16)
        Vr = self.sbr(2, [128, 32, 128], BF16)
        for (Kb, bK) in Kr.items:
            self.I("pool", "memset", [], [bK], ap=Kb[64:128, :], constant=0.0)
            self.dma("pool", Kb[64:80, :], self.c_onehot, w=[bK])
        for (Qb, bQ) in Qr.items:
            self.I("pool", "memset", [], [bQ], ap=Qb[64:128, :], constant=0.0)
        for (Vb, bV) in Vr.items:
            self.I("pool", "memset", [], [bV], ap=Vb[:, :, 64:128], constant=1.0)
        km_r = [(self.sb([64, 16], F32), self.sb([64, 16], BF16), Buf()) for _ in range(2)]
        gate_r = self.sbr(4, [128, 16], F32)
        top_r = self.sbr(4, [128, 8], F32)
        mbp_r = self.sbr(4, [128, 4, 80], BF16)
        for (m, bm) in mbp_r.items:
            self.I("pool", "memset", [], [bm], ap=m[:], constant=0.0)
        pT_r = self.sbr(4, [128, TT], BF16)
        rec_r = self.sbr(2, [64, TT], F32)
        ost_r = self.sbr(2, [64, TT], BF16)
        pss_r = Ring([(0, self.pbuf[0]), (1, self.pbuf[1]), (2, self.pbuf[2])])
        pso_r = Ring([(3, self.pbuf[3]), (4, self.pbuf[4])])
        aux_r = Ring([(5, self.pbuf[5]), (6, self.pbuf[6])])
        bQm = [[Buf() for _ in range(NT)] for _ in range(2)]
        H = {}

        def load_head(h):
            Qb, bQ = Qr.items[h % 2]
            Kb, bK = Kr.items[h % 2]
            Vb, bV = Vr.items[h % 2]
            H[h] = (Qb, bQ, Kb, bK, Vb, bV)
            self.dma("sp", Qb[0:64, :], self.MQ[h], r=self.dball("Mq"), w=[bQ])
            self.dma("sp", Kb[0:64, :], self.MK[h], r=self.dball("Mk"), w=[bK])
            for g4 in range(4):
                self.dma("sp", Vb[:, g4 * 8:(g4 + 1) * 8, 0:64],
                         self.MV[g4 * 1024:(g4 + 1) * 1024, h * 64:(h + 1) * 64].rearrange("(k p) d -> p k d", p=128), r=self.dball("MV"), w=[bV])

        def kmean(h):
            Qb, bQ, Kb, bK, Vb, bV = H[h]
            kmf, kmb, bkm = km_r[h % 2]
            self.I("dve", "tensor_reduce", [bK], [bkm], out=kmf[:], in_=Kb[0:64, :].rearrange("p (n k) -> p n k", k=256), axis=AX.X, op=ALU.add)
            self.I("dve", "tensor_copy", [bkm], [bkm], out=kmb[:], in_=kmf[:])

        def prepare(h, t):
            Qb, bQ, Kb, bK, Vb, bV = H[h]
            kmf, kmb, bkm = km_r[h % 2]
            tok = slice(t * TT, (t + 1) * TT)
            mbp, bmb = mbp_r.next()
            pi, bpg = aux_r.next()
            psg = self.psF[:, pi, :]
            for s in range(4):
                g = 4 * t + s
                blk = g // 2
                if blk <= 3:
                    continue
                self.I("pe", "matmul", [bQ, bkm], [bpg], out=psg[:, s * 16:(s + 1) * 16], lhsT=Qb[0:64, g * 128:(g + 1) * 128], rhs=kmb[:],
                       start=True, stop=True)
            for s in range(4):
                g = 4 * t + s
                blk = g // 2
                if blk <= 3:
                    self.I("pool", "memset", [], [bmb], ap=mbp[:, s, 64:80], constant=0.0)
                    continue
                gt, bg = gate_r.next()
                tp, btp = top_r.next()
                self.I("dve", "memset", [], [bg], ap=gt[:], constant=-1e30)
                self.I("dve", "tensor_copy", [bpg], [bg], out=gt[:, 0:blk], in_=psg[:, s * 16:s * 16 + blk])
                self.I("dve", "max", [bg], [btp], out=tp[:], in_=gt[:])
                self.I("dve", "tensor_scalar", [bg, btp], [bmb], out=mbp[:, s, 64:80], in0=gt[:], scalar1=tp[:, 2:3], scalar2=NEGB,
                       op0=ALU.is_lt, op1=ALU.mult)
                self.I("dve", "memset", [], [bmb], ap=mbp[:, s, 64 + blk:80], constant=0.0)
            return mbp, bmb

        def prepare2(h, t, mbp, bmb):
            Qb, bQ, Kb, bK, Vb, bV = H[h]
            tok = slice(t * TT, (t + 1) * TT)
            for s in range(4):
                self.I("pe", "transpose", [bmb, self.bconst], [self.pbufB], out=self.psB[0:80, s * 128:(s + 1) * 128], in_=mbp[:, s, :],
                       identity=self.identB[:])
            self.I("act", "copy", [self.pbufB, bQ], [bQm[h % 2][t]], out=Qb[64:80, tok], in_=self.psB[64:80, 0:TT])

        items = [(h, t) for h in range(8) for t in range(NT)]
        load_head(0)
        load_head(1)
        kmean(0)
        prepare2(0, 0, *prepare(0, 0))
        gated = {}
        if len(items) > 1:
            gated[items[1]] = prepare(*items[1])
        tiles = []
        for idx, (h, t) in enumerate(items):
            def start_hook(idx=idx, h=h, t=t):
                if t == 1 and h + 1 < 8 and h >= 1:
                    load_head(h + 1)
                if idx + 2 < len(items):
                    hn, tn = items[idx + 2]
                    if tn == 0:
                        kmean(hn)
                    gated[(hn, tn)] = prepare(hn, tn)

            def mid_hook(idx=idx):
                if idx + 1 < len(items):
                    hn, tn = items[idx + 1]
                    prepare2(hn, tn, *gated.pop((hn, tn)))

            def get_q(h=h, t=t):
                Qb, bQ = Qr.items[h % 2]
                return Qb[:, t * TT:(t + 1) * TT], [bQ, bQm[h % 2][t]]

            def finish(pso, bpso, h=h, t=t):
                tok = slice(t * TT, (t + 1) * TT)
                self.attn_finish(pso, bpso, rec_r, ost_r, self.OB[1][h // 2, (h % 2) * 64:(h % 2) * 64 + 64, tok], self.db("OMOBA", t))

            tiles.append(dict(t=t, h=h, get_q=get_q, start_hook=start_hook, mid_hook=mid_hook, finish=finish,
                              K=Kr.items[h % 2][0], bK=Kr.items[h % 2][1], V=Vr.items[h % 2][0], bV=Vr.items[h % 2][1]))
        self.run_attn(tiles, 0.125, pss_r, pT_r, pso_r)

    def p_ret(self, l):
        self.begin_phase()
        Qr = self.sbr(2, [64, SEQ], BF16)
        Kr = self.sbr(2, [64, SEQ], BF16)
        Vr = self.sbr(2, [128, 32, 128], BF16)
        dec = self.sb([128, 4, 128], F32)
        xi = self.sb([64, 4, 512], F32)
        zeta = self.sb([128, 4], F32)
        bc = Buf()
        self.dma("sp", dec[:], self.c_decayT.rearrange("h j i -> j h i"), w=[bc])
        self.dma("sp", xi[:], self.c_xi.rearrange("h p n -> p h n"), w=[bc])
        self.dma("sp", zeta[:], self.c_zeta, w=[bc])
        qx_r = self.sbr(2, [64, TT], BF16)
        kz_r = self.sbr(4, [128, 64], BF16)
        sc_r = self.sbr(4, [128, 128], BF16)
        KV_r = self.sbr(2, [64, 32, 128], F32)
        Rall_r = self.sbr(2, [64, 32, 128], BF16)
        Rf = self.sb([64, 128], F32)
        bRf = Buf()
        G_r = self.sbr(2, [128, TT], F32)
        of_r = self.sbr(2, [128, TT], F32)
        ob_r = self.sbr(2, [128, 2, TT], BF16)
        m_r = self.sbr(2, [128, TT], F32)
        v_r = self.sbr(2, [128, TT], F32)
        y_r = self.sbr(2, [128, TT], BF16)
        po_r = Ring([(0, self.pbuf[0]), (1, self.pbuf[1])])
        ps_r = Ring([(2, self.pbuf[2]), (3, self.pbuf[3])])
        pk_r = Ring([(4, self.pbuf[4]), (5, self.pbuf[5])])
        HB = {}

        def load_head(h):
            Qb, bQ = Qr.items[h % 2]
            Kb, bK = Kr.items[h % 2]
            Vb, bV = Vr.items[h % 2]
            HB[h] = (Qb, bQ, Kb, bK, Vb, bV)
            self.dma("sp", Qb[:], self.RQ[h], r=self.dball("rq"), w=[bQ])
            self.dma("sp", Kb[:], self.RK[h], r=self.dball("rk"), w=[bK])
            for g4 in range(4):
                self.dma("sp", Vb[:, g4 * 8:(g4 + 1) * 8, :],
                         self.RV[g4 * 1024:(g4 + 1) * 1024, h * 128:(h + 1) * 128].rearrange("(k p) d -> p k d", p=128), r=self.dball("RV"), w=[bV])

        def stage1(h):
            Qb, bQ, Kb, bK, Vb, bV = HB[h]
            KVs, bKV = KV_r.items[h % 2]
            for g in range(8):
                pi_, bpk = pk_r.next()
                pk = self.psF[:, pi_, :]
                for j in range(4):
                    n = g * 4 + j
                    ch = slice(n * 128, (n + 1) * 128)
                    self.I("pe", "transpose", [bK, self.bconst], [self.pbufB], out=self.psB[:, j * 64:(j + 1) * 64], in_=Kb[:, ch],
                           identity=self.identB[0:64, 0:64])
                kzs = []
                for j in range(4):
                    kz, bkz = kz_r.next()
                    self.I("act", "activation", [self.pbufB, bc], [bkz], out=kz[:], in_=self.psB[:, j * 64:(j + 1) * 64], func=AF.Copy,
                           scale=zeta[:, h:h + 1])
                    kzs.append((kz, bkz))
                for j in range(4):
                    n = g * 4 + j
                    kz, bkz = kzs[j]
                    self.I("pe", "matmul", [bkz, bV], [bpk], out=pk[0:64, j * 128:(j + 1) * 128], lhsT=kz[:], rhs=Vb[:, n, :], start=True, stop=True)
                self.copy("dve" if g % 2 else "act", KVs[:, g * 4:(g + 1) * 4, :], pk[0:64, :].rearrange("p (j e) -> p j e", e=128), [bpk], [bKV])

        def stage2(h):
            KVs, bKV = KV_r.items[h % 2]
            Rall, bR = Rall_r.items[h % 2]
            self.I("pool", "memset", [], [bRf], ap=Rf[:], constant=0.0)
            self.I("pool", "memset", [], [bR], ap=Rall[:, 0, :], constant=0.0)
            for n in range(31):
                self.I("dve", "scalar_tensor_tensor", [bRf, bKV], [bRf], out=Rf[:], in0=Rf[:], scalar=float(self.gam[h]), in1=KVs[:, n, :],
                       op0=ALU.mult, op1=ALU.add)
                self.I("dve", "tensor_copy", [bRf], [bR], out=Rall[:, n + 1, :], in_=Rf[:])

        def stage3(h):
            Qb, bQ, Kb, bK, Vb, bV = HB[h]
            Rall, bR = Rall_r.items[h % 2]
            for t in range(NT):
                tok = slice(t * TT, (t + 1) * TT)
                G, bG = G_r.next()
                self.dma("sp", G[:], self.RG[h, :, tok], r=[self.db("RG", t)], w=[bG])
                qx, bqx = qx_r.next()
                self.I("pool", "tensor_tensor", [bQ, bc], [bqx], out=qx[:], in0=Qb[:, tok], in1=xi[:, h, :], op=ALU.mult)
                pi_, bpo = po_r.next()
                po = self.psF[:, pi_, :]
                pi_, bps = ps_r.next()
                ps = self.psF[:, pi_, :]
                for s in range(4):
                    n = 4 * t + s
                    ch = slice(n * 128, (n + 1) * 128)
                    self.I("pe", "matmul", [bK, bQ], [bps], out=ps[:, s * 128:(s + 1) * 128], lhsT=Kb[:, ch], rhs=Qb[:, ch], start=True, stop=True)
                scs = []
                for s in range(4):
                    sc, bsc = sc_r.next()
                    self.I("dve", "tensor_tensor", [bps, bc], [bsc], out=sc[:], in0=ps[:, s * 128:(s + 1) * 128], in1=dec[:, h, :], op=ALU.mult)
                    scs.append((sc, bsc))
                for s in range(4):
                    n = 4 * t + s
                    sc, bsc = scs[s]
                    self.I("pe", "matmul", [bV, bsc], [bpo], out=po[:, s * 128:(s + 1) * 128], lhsT=Vb[:, n, :], rhs=sc[:], start=True, stop=False)
                    self.I("pe", "matmul", [bR, bqx], [bpo], out=po[:, s * 128:(s + 1) * 128], lhsT=Rall[:, n, :], rhs=qx[:, s * 128:(s + 1) * 128],
                           start=False, stop=True)
                of, bof = of_r.next()
                ob, bob = ob_r.next()
                self.I("act", "copy", [bpo], [bof], out=of[:], in_=po)
                self.I("pool", "tensor_copy", [bof], [bob], out=ob[:, 0, :], in_=of[:])
                self.I("act", "activation", [bof], [bob], out=ob[:, 1, :], in_=of[:], func=AF.Square)
                pm, bpm = self.psF[:, 6, :], self.pbuf[6]
                self.I("pe", "matmul", [bob, self.bconst], [bpm], out=pm, lhsT=self.onesB[:], rhs=ob[:, 0, :], start=True, stop=True)
                pi_, bpq = ps_r.next()
                pq = self.psF[:, pi_, :]
                self.I("pe", "matmul", [bob, self.bconst], [bpq], out=pq, lhsT=self.onesB[:], rhs=ob[:, 1, :], start=True, stop=True)
                m, bm = m_r.next()
                v, bv = v_r.next()
                self.I("act", "mul", [bpm], [bm], out=m[:], in_=pm, mul=1.0 / 128)
                self.I("dve", "tensor_tensor", [bm], [bv], out=v[:], in0=m[:], in1=m[:], op=ALU.mult)
                self.I("dve", "scalar_tensor_tensor", [bpq, bv], [bv], out=v[:], in0=pq, scalar=1.0 / 128, in1=v[:], op0=ALU.mult, op1=ALU.subtract)
                self.I("act", "activation", [bv], [bv], out=v[:], in_=v[:], func=AF.Sqrt, scale=1.0, bias=float(GN_EPS))
                self.I("dve", "reciprocal", [bv], [bv], out=v[:], in_=v[:])
                self.I("pool", "tensor_tensor", [bof, bm], [bof], out=of[:], in0=of[:], in1=m[:], op=ALU.subtract)
                self.I("dve", "tensor_tensor", [bof, bv], [bof], out=of[:], in0=of[:], in1=v[:], op=ALU.mult)
                self.I("dve", "tensor_scalar", [bof, self.bconst], [bof], out=of[:], in0=of[:], scalar1=self.vcol(l, 19 + h), scalar2=self.vcol(l, 23 + h),
                       op0=ALU.mult, op1=ALU.add)
                y, by = y_r.next()
                self.I("pool", "tensor_tensor", [bof, bG], [by], out=y[:], in0=of[:], in1=G[:], op=ALU.mult)
                self.dma("sp", self.OB[2][h, :, tok], y[:], r=[by], w=[self.db("ORET", t)])

        load_head(0)
        load_head(1)
        stage1(0)
        stage2(0)
        for h in range(4):
            if h + 1 < 4:
                stage1(h + 1)
                stage2(h + 1)
            stage3(h)
            if h + 2 < 4:
                load_head(h + 2)

    def p_merge(self, l):
        self.begin_phase()
        Wg = self.sb([128, 8, 3072], BF16)
        Wb = self.sb([128, 3, 4, DM], BF16)
        Wo = self.sb([128, 8, DM], BF16)
        bw = Buf()
        win = self.w_in[l].rearrange("(c p) n -> p c n", p=128)
        for j in range(3):
            self.dma("pool", Wg[:, :, j * 1024:(j + 1) * 1024], win[:, :, 3488 + j * 1024:3488 + (j + 1) * 1024], w=[bw])
            self.dma("pool", Wb[:, j, :, :], self.w_b[j][l].rearrange("(c p) n -> p c n", p=128), w=[bw])
        self.dma("pool", Wo[:], self.w_out[l].rearrange("(c p) n -> p c n", p=128), w=[bw])
        hT_r = self.sbr(2, [128, 8, TT], BF16)
        o_r = self.sbr(2, [128, 3, 4, TT], BF16)
        xT_r = self.sbr(2, [128, 8, TT], F32)
        sig_r = self.sbr(3, [128, TT], F32)
        acc_r = self.sbr(2, [128, TT], F32)
        mg_r = self.sbr(2, [128, 8, TT], BF16)
        names = ("OMLA", "OMOBA", "ORET")
        def load3(t):
            tok = slice(t * TT, (t + 1) * TT)
            hT, bh = hT_r.next()
            o, bo = o_r.next()
            xT, bx = xT_r.next()
            self.dma("sp", hT[:], self.HT.rearrange("c p n -> p c n")[:, :, tok], r=[self.db("HT", t)], w=[bh])
            for j in range(3):
                self.dma("sp", o[:, j, :, :], self.OB[j].rearrange("c p n -> p c n")[:, :, tok], r=[self.db(names[j], t)], w=[bo])
            self.dma("sp", xT[:], self.XT.rearrange("c p n -> p c n")[:, :, tok], r=[self.db("XT", t)], w=[bx])
            return hT, bh, o, bo, xT, bx

        nxt3 = load3(0)
        for t in range(NT):
            tok = slice(t * TT, (t + 1) * TT)
            hT, bh, o, bo, xT, bx = nxt3
            if t + 1 < NT:
                nxt3 = load3(t + 1)
            mg, bmg = mg_r.next()
            for oc in range(8):
                acc, bacc = acc_r.next()
                for j in range(3):
                    pg, bpg = self.bank()
                    for c in range(8):
                        self.I("pe", "matmul", [bh, bw], [bpg], out=pg, lhsT=Wg[:, c, j * 1024 + oc * 128:j * 1024 + (oc + 1) * 128], rhs=hT[:, c, :],
                               start=(c == 0), stop=(c == 7))
                    pb, bpb = self.bank()
                    for c in range(4):
                        self.I("pe", "matmul", [bo, bw], [bpb], out=pb, lhsT=Wb[:, j, c, oc * 128:(oc + 1) * 128], rhs=o[:, j, c, :],
                               start=(c == 0), stop=(c == 3))
                    sg, bsg = sig_r.next()
                    self.I("act", "activation", [bpg], [bsg], out=sg[:], in_=pg, func=AF.Sigmoid)
                    if j == 0:
                        self.I("dve", "tensor_tensor", [bsg, bpb], [bacc], out=acc[:], in0=pb, in1=sg[:], op=ALU.mult)
                    else:
                        self.I("dve", "tensor_tensor", [bsg, bpb], [bsg], out=sg[:], in0=pb, in1=sg[:], op=ALU.mult)
                        if j == 1:
                            self.I("pool", "tensor_tensor", [bsg, bacc], [bacc], out=acc[:], in0=acc[:], in1=sg[:], op=ALU.add)
                        else:
                            self.I("pool", "tensor_tensor", [bsg, bacc], [bmg], out=mg[:, oc, :], in0=acc[:], in1=sg[:], op=ALU.add)
            for oc in range(8):
                po, bpo = self.bank()
                for c in range(8):
                    self.I("pe", "matmul", [bmg, bw], [bpo], out=po, lhsT=Wo[:, c, oc * 128:(oc + 1) * 128], rhs=mg[:, c, :], start=(c == 0), stop=(c == 7))
                self.I("dve", "tensor_tensor", [bpo, bx], [bx], out=xT[:, oc, :], in0=po, in1=xT[:, oc, :], op=ALU.add)
            self.dma("sp", self.XT.rearrange("c p n -> p c n")[:, :, tok], xT[:], r=[bx], w=[self.db("XT", t)])

    def p_mlp_up(self, l):
        self.begin_phase()
        Wu = self.sb([128, 8, 4096], BF16)
        bw = Buf()
        wu = self.w_up[l].rearrange("(c p) n -> p c n", p=128)
        for j in range(4):
            self.dma("pool", Wu[:, :, j * 1024:(j + 1) * 1024], wu[:, :, j * 1024:(j + 1) * 1024], w=[bw])
        xT_r = self.sbr(2, [128, 8, TT], F32)
        hT_r = self.sbr(2, [128, 8, TT], BF16)
        sq = self.sb([128, 8, TT], BF16)
        bsq = Buf()
        rs_r = self.sbr(2, [128, TT], F32)
        r_r = self.sbr(3, [128, TT], F32)
        a_r = self.sbr(2, [128, 8, TT], BF16)
        def load4(t):
            tok = slice(t * TT, (t + 1) * TT)
            xT, bx = xT_r.next()
            hT, bh = hT_r.next()
            rs, brs = rs_r.next()
            self.dma("sp", xT[:], self.XT.rearrange("c p n -> p c n")[:, :, tok], r=[self.db("XT", t)], w=[bx])
            self.rmsnorm([xT[:, c, :] for c in range(8)], [self.vcol(l, 8 + c) for c in range(8)], DM, EPS,
                         [hT[:, c, :] for c in range(8)], [bx], [bh], sq, bsq, rs, brs)
            return hT, bh

        nxt4 = load4(0)
        for t in range(NT):
            tok = slice(t * TT, (t + 1) * TT)
            hT, bh = nxt4
            for fg in range(4):
                if fg == 2 and t + 1 < NT:
                    nxt4 = load4(t + 1)
                a, ba = a_r.next()
                for fj in range(8):
                    f = fg * 8 + fj
                    ps, bps = self.bank()
                    for c in range(8):
                        self.I("pe", "matmul", [bh, bw], [bps], out=ps, lhsT=Wu[:, c, f * 128:(f + 1) * 128], rhs=hT[:, c, :], start=(c == 0), stop=(c == 7))
                    r, br = r_r.next()
                    self.I("act", "activation", [bps], [br], out=r[:], in_=ps, func=AF.Relu)
                    self.I("pool" if fj % 2 else "dve", "tensor_tensor", [br], [ba], out=a[:, fj, :], in0=r[:], in1=r[:], op=ALU.mult)
                self.dma("sp", self.ACTS[fg * 8:(fg + 1) * 8].rearrange("c p n -> p c n")[:, :, tok], a[:], r=[ba], w=[self.db("ACTS%d" % fg, t)])

    def p_mlp_down(self, l):
        self.begin_phase()
        Wd = self.sb([128, 32, DM], BF16)
        bw = Buf()
        wd = self.w_down[l].rearrange("(c p) n -> p c n", p=128)
        for j in range(4):
            self.dma("pool", Wd[:, j * 8:(j + 1) * 8, :], wd[:, j * 8:(j + 1) * 8, :], w=[bw])
        xT_r = self.sbr(2, [128, 8, TT], F32)
        a_r = self.sbr(2, [128, 32, TT], BF16)
        def load5(t):
            tok = slice(t * TT, (t + 1) * TT)
            xT, bx = xT_r.next()
            a, ba = a_r.next()
            self.dma("sp", xT[:], self.XT.rearrange("c p n -> p c n")[:, :, tok], r=[self.db("XT", t)], w=[bx])
            for fg in range(4):
                self.dma("sp", a[:, fg * 8:(fg + 1) * 8, :], self.ACTS[fg * 8:(fg + 1) * 8].rearrange("c p n -> p c n")[:, :, tok],
                         r=[self.db("ACTS%d" % fg, t)], w=[ba])
            return xT, bx, a, ba

        nxt5 = load5(0)
        for t in range(NT):
            tok = slice(t * TT, (t + 1) * TT)
            xT, bx, a, ba = nxt5
            if t + 1 < NT:
                nxt5 = load5(t + 1)
            for oc in range(8):
                po, bpo = self.bank()
                for f in range(32):
                    self.I("pe", "matmul", [ba, bw], [bpo], out=po, lhsT=Wd[:, f, oc * 128:(oc + 1) * 128], rhs=a[:, f, :], start=(f == 0), stop=(f == 31))
                self.I("dve", "tensor_tensor", [bpo, bx], [bx], out=xT[:, oc, :], in0=po, in1=xT[:, oc, :], op=ALU.add)
            self.dma("sp", self.XT.rearrange("c p n -> p c n")[:, :, tok], xT[:], r=[bx], w=[self.db("XT", t)])

    def p_final(self):
        self.begin_phase()
        xT_r = self.sbr(2, [128, 8, TT], F32)
        yT_r = self.sbr(2, [128, 8, TT], F32)
        sq = self.sb([128, 8, TT], BF16)
        bsq = Buf()
        rs_r = self.sbr(2, [128, TT], F32)
        o_r = self.sbr(2, [128, DM], F32)
        def load6(t):
            tok = slice(t * TT, (t + 1) * TT)
            xT, bx = xT_r.next()
            self.dma("sp", xT[:], self.XT.rearrange("c p n -> p c n")[:, :, tok], r=[self.db("XT", t)], w=[bx])
            return xT, bx

        nxt6 = load6(0)
        for t in range(NT):
            tok = slice(t * TT, (t + 1) * TT)
            xT, bx = nxt6
            if t + 1 < NT:
                nxt6 = load6(t + 1)
            yT, by = yT_r.next()
            rs, brs = rs_r.next()
            fb = DEPTH * VSTR
            self.rmsnorm([xT[:, c, :] for c in range(8)], [self.vecs[:, fb + c:fb + c + 1] for c in range(8)], DM, EPS,
                         [yT[:, c, :] for c in range(8)], [bx], [by], sq, bsq, rs, brs)
            for s in range(4):
                o, bo = o_r.next()
                for g in range(2):
                    ps, bps = self.bank()
                    for j in range(4):
                        c = g * 4 + j
                        self.I("pe", "transpose", [by, self.bconst], [bps], out=ps[:, j * 128:(j + 1) * 128], in_=yT[:, c, s * 128:(s + 1) * 128],
                               identity=self.identF[:])
                    self.copy(self.evac_eng(), o[:, g * 512:(g + 1) * 512], ps, [bps], [bo])
                r0 = t * TT + s * 128
                self.dma("sp", self.out[r0:r0 + 128, :], o[:], r=[bo], w=[self.db("out", t)])

    def build(self):
        ph = self.phases
        want = lambda name: ph is None or name in ph
        if want("t_in"):
            self.p_transpose_in()
        for l in range(self.n_layers):
            if want(f"inproj{l}"):
                self.p_inproj(l)
            if want(f"mla{l}"):
                self.p_mla(l)
            if want(f"moba{l}"):
                self.p_moba(l)
            if want(f"ret{l}"):
                self.p_ret(l)
            if want(f"merge{l}"):
                self.p_merge(l)
            if want(f"mlpup{l}"):
                self.p_mlp_up(l)
            if want(f"mlpdown{l}"):
                self.p_mlp_down(l)
        if want("final"):
            self.p_final()
        self.S.barrier()
        stats = self.S.emit()
        return self.nc, stats


_CACHE = {}


def get_program():
    if "nc" not in _CACHE:
        kb = KB()
        _CACHE["nc"], _CACHE["stats"] = kb.build()
    return _CACHE["nc"]


def make_in_map(inputs, b, consts, vecs):
    m = {"x": np.ascontiguousarray(inputs["x"][b])}
    for k in ("w_in", "mla_w_q_up", "mla_w_kv_up", "w_branch_mla", "w_branch_moba", "w_branch_ret", "w_out", "w_mlp_up", "w_mlp_down"):
        m[k] = np.ascontiguousarray(inputs[k], dtype=np.float32)
    m["vecs"] = vecs
    m.update(consts)
    return m


def kernel(**inputs):
    inputs = {k: np.asarray(v) for k, v in inputs.items()}
    consts, _ = make_consts()
    vecs = pack_vecs(inputs)
    nc = get_program()
    in_maps = [make_in_map(inputs, b, consts, vecs) for b in range(NCORES)]
    res = run_bass_kernel_spmd(nc, in_maps, core_ids=list(range(NCORES)))
    out = np.stack([np.asarray(res.results[b]["out"], dtype=np.float32) for b in range(NCORES)], axis=0)
    return out
```

```python
import numpy as np
import concourse.bass as bass
import concourse.mybir as mybir
from concourse.bass_utils import run_bass_kernel_spmd

F32 = mybir.dt.float32
BF16 = mybir.dt.bfloat16
AF = mybir.ActivationFunctionType
ALU = mybir.AluOpType
AX = mybir.AxisListType

SEQ = 4096
DM = 1024
TT = 512
NT = SEQ // TT
DEPTH = 2
NCORES = 4
EPS = 1e-6
GN_EPS = 1e-5
NEGB = -30000.0


class Buf:
    __slots__ = ("name", "w", "r", "rd")

    def __init__(self, name=""):
        self.name = name
        self.w = None
        self.r = {}
        self.rd = []


class _Op:
    __slots__ = ("eng", "fn", "deps", "is_dma", "idx", "need_inc", "semidx", "semval")


class Sched:
    ENG = ("pe", "act", "dve", "pool", "sp")

    def __init__(self, nc, n_dma_sems=40, same_eng_sync=("act", "dve", "pool")):
        import os
        if os.environ.get("K_VAR", "") == "pesync":
            same_eng_sync = ("act", "dve", "pool", "pe")
        self.nc = nc
        self.ops = []
        self.n_dma_sems = n_dma_sems
        self.same_eng_sync = set(same_eng_sync)
        self.last_on_eng = {}
        self.dmas_open = []
        self.cur_barrier = None

    def op(self, eng, fn, reads=(), writes=(), dma=False):
        o = _Op()
        o.eng = eng
        o.fn = fn
        o.is_dma = dma
        o.idx = len(self.ops)
        o.need_inc = False
        deps = set()
        for b in reads:
            if b.w is not None:
                deps.add(b.w)
        for b in writes:
            if b.w is not None:
                deps.add(b.w)
            deps.update(b.r.values())
            deps.update(b.rd)
        for b in reads:
            if dma:
                b.rd.append(o.idx)
            else:
                b.r[eng] = o.idx
        for b in writes:
            b.w = o.idx
            b.r = {}
            b.rd = []
        if self.cur_barrier is not None:
            deps.add(self.cur_barrier)
        deps.discard(o.idx)
        o.deps = deps
        self.ops.append(o)
        if dma:
            self.dmas_open.append(o.idx)
        else:
            self.last_on_eng[eng] = o.idx
        return o

    def barrier(self):
        deps = set(self.last_on_eng.values()) | set(self.dmas_open)
        o = self.op("sp", lambda e: e.nop())
        o.deps |= deps
        o.deps.discard(o.idx)
        self.dmas_open = []
        self.cur_barrier = o.idx
        return o

    def emit(self):
        nc = self.nc
        ops = self.ops
        engobj = {"pe": nc.tensor, "act": nc.scalar, "dve": nc.vector, "pool": nc.gpsimd, "sp": nc.sync}
        ses = self.same_eng_sync

        def needs_sync(o, od):
            if od.is_dma or o.is_dma:
                return True
            if od.eng != o.eng:
                return True
            return od.eng in ses

        for o in ops:
            for d in o.deps:
                od = ops[d]
                if not od.is_dma and needs_sync(o, od):
                    od.need_inc = True
        cnt = {e: 0 for e in self.ENG}
        ndma = 0
        qpool = {"sp": (0, 32), "pool": (32, 24), "act": (56, 8)}
        qcnt = {q: 0 for q in qpool}
        for o in ops:
            if o.is_dma:
                base, n = qpool[o.eng]
                k = qcnt[o.eng]
                qcnt[o.eng] += 1
                o.semidx = base + (k % n)
                o.semval = 16 * (k // n + 1)
                ndma += 1
            elif o.need_inc:
                cnt[o.eng] += 1
                o.semval = cnt[o.eng]
        engsem = {e: nc.alloc_semaphore(name=f"s_{e}") for e in self.ENG}
        dmasem = [nc.alloc_semaphore(name=f"s_dma{i}") for i in range(64)]
        waited = {}
        nwaits = 0
        self.trace = {e: [] for e in self.ENG}
        for o in ops:
            e = engobj[o.eng]
            waits = {}
            for d in o.deps:
                od = ops[d]
                if not needs_sync(o, od):
                    continue
                key = ("dma", od.semidx) if od.is_dma else ("eng", od.eng)
                if waits.get(key, 0) < od.semval:
                    waits[key] = od.semval
            if o.is_dma and o.semval > 16:
                key = ("dma", o.semidx)
                if waits.get(key, 0) < o.semval - 16:
                    waits[key] = o.semval - 16
            for key, val in waits.items():
                wk = (o.eng, key)
                if waited.get(wk, 0) >= val:
                    continue
                sem = dmasem[key[1]] if key[0] == "dma" else engsem[key[1]]
                e.wait_ge(sem, val)
                self.trace[o.eng].append(("w", key, val, o.idx))
                waited[wk] = val
                nwaits += 1
            ins = o.fn(e)
            if o.is_dma:
                ins.then_inc(dmasem[o.semidx], 16)
                self.trace[o.eng].append(("i", ("dma", o.semidx), 16, o.idx))
            elif o.need_inc:
                ins.then_inc(engsem[o.eng], 1)
                self.trace[o.eng].append(("i", ("eng", o.eng), 1, o.idx))
        self.stats = dict(n_ops=len(ops), n_waits=nwaits, incs=dict(cnt), n_dma=ndma)
        return self.stats


class Ring:
    def __init__(self, items):
        self.items = items
        self.i = 0

    def next(self):
        x = self.items[self.i % len(self.items)]
        self.i += 1
        return x


def _rope_tab(dim, theta):
    inv = (1.0 / (np.float32(theta) ** (np.arange(0, dim, 2, dtype=np.float32) / np.float32(dim)))).astype(np.float32)
    ang = (np.arange(SEQ, dtype=np.float32)[:, None] * inv[None, :]).astype(np.float32)
    c = np.cos(ang).astype(np.float32).T
    s = np.sin(ang).astype(np.float32).T
    cc = np.concatenate([c, c], 0)
    ss = np.concatenate([-s, s], 0)
    return np.ascontiguousarray(np.stack([cc, ss], 0))


def make_consts():
    c = {}
    c["identF"] = np.eye(128, dtype=np.float32)
    c["rope_mla"] = _rope_tab(32, 500000.0)
    rm = _rope_tab(16, 500000.0)
    pad = np.zeros((2, 128, SEQ), np.float32)
    pad[0] = 1.0
    pad[:, 0:16] = rm
    pad[:, 64:80] = rm
    c["rope_moba"] = pad
    rr = _rope_tab(64, 10000.0)
    c["rope_ret"] = np.ascontiguousarray(np.concatenate([rr, rr], axis=1))
    k = np.arange(128)
    c["mask01"] = (k[:, None] <= k[None, :]).astype(np.float32)
    c["onehot"] = (np.arange(16)[:, None] == (np.arange(SEQ)[None, :] // 256)).astype(np.float32)
    lg = np.log(np.float32(1.0) - np.float32(2.0) ** (np.float32(-5.0) - np.arange(4, dtype=np.float32))).astype(np.float32)
    pos = np.arange(128, dtype=np.float32)
    diff = pos[:, None] - pos[None, :]
    dec = np.where(diff >= 0, np.exp(lg[:, None, None] * diff), 0.0).astype(np.float32)
    c["decayT"] = np.ascontiguousarray(dec.transpose(0, 2, 1) * np.float32(0.125)).astype(np.float32)
    xi = np.exp(lg[:, None] * (pos + 1.0)).astype(np.float32)
    c["xi"] = np.ascontiguousarray(np.broadcast_to(np.tile(xi, (1, 4))[:, None, :], (4, 64, 512))).astype(np.float32)
    zeta = np.exp(lg[:, None] * (127.0 - pos)).astype(np.float32) * np.float32(0.125)
    c["zeta"] = np.ascontiguousarray(zeta.T).astype(np.float32)
    gam = [float(np.exp(lg[h] * np.float32(128.0))) for h in range(4)]
    return c, gam


VSTR = 27
NV = DEPTH * VSTR + 8


def pack_vecs(inp):
    v = np.zeros((128, NV), np.float32)
    for l in range(DEPTH):
        b = l * VSTR
        v[:, b + 0:b + 8] = inp["attn_norm"][l].reshape(8, 128).T
        v[:, b + 8:b + 16] = inp["mlp_norm"][l].reshape(8, 128).T
        v[:, b + 16:b + 18] = inp["mla_q_norm"][l].reshape(2, 128).T
        v[:, b + 18:b + 19] = inp["mla_kv_norm"][l].reshape(1, 128).T
        v[:, b + 19:b + 23] = inp["ret_gn_w"][l].reshape(4, 128).T
        v[:, b + 23:b + 27] = inp["ret_gn_b"][l].reshape(4, 128).T
    v[:, DEPTH * VSTR:DEPTH * VSTR + 8] = inp["final_norm"].reshape(8, 128).T
    return v


class KB:
    SB_BASE = 20480
    SB_TOP = 229376

    def __init__(self, n_layers=DEPTH, debug=False, phases=None):
        self.nc = nc = bass.Bass("TRN2", target_bir_lowering=False)
        self.S = Sched(nc)
        self.debug = debug
        self.n_layers = n_layers
        self.phases = phases
        self.top = self.SB_BASE
        self.uid = 0
        L = DEPTH
        inp = lambda name, shape: nc.dram_tensor(name, shape, F32, kind="ExternalInput").ap()
        self.x = inp("x", [SEQ, DM])
        self.w_in = inp("w_in", [L, DM, 6560])
        self.w_q_up = inp("mla_w_q_up", [L, 256, 768])
        self.w_kv_up = inp("mla_w_kv_up", [L, 128, 1024])
        self.w_b = [inp("w_branch_mla", [L, 512, DM]), inp("w_branch_moba", [L, 512, DM]), inp("w_branch_ret", [L, 512, DM])]
        self.w_out = inp("w_out", [L, DM, DM])
        self.w_up = inp("w_mlp_up", [L, DM, 4096])
        self.w_down = inp("w_mlp_down", [L, 4096, DM])
        self.vecs_d = inp("vecs", [128, NV])
        self.c_identF = inp("identF", [128, 128])
        self.c_rope_mla = inp("rope_mla", [2, 32, SEQ])
        self.c_rope_moba = inp("rope_moba", [2, 128, SEQ])
        self.c_rope_ret = inp("rope_ret", [2, 128, SEQ])
        self.c_mask01 = inp("mask01", [128, 128])
        self.c_onehot = inp("onehot", [16, SEQ])
        self.c_decayT = inp("decayT", [4, 128, 128])
        self.c_xi = inp("xi", [4, 64, 512])
        self.c_zeta = inp("zeta", [128, 4])
        self.out = nc.dram_tensor("out", [SEQ, DM], F32, kind="ExternalOutput").ap()
        kind = "ExternalOutput" if debug else "Internal"
        scr = lambda name, shape, dt: nc.dram_tensor(name, shape, dt, kind=kind).ap()
        self.XT = scr("XT", [8, 128, SEQ], F32)
        self.HT = scr("HT", [8, 128, SEQ], BF16)
        self.CQ = scr("CQ", [2, 128, SEQ], BF16)
        self.CKV = scr("CKV", [128, SEQ], BF16)
        self.KPE = scr("KPE", [32, SEQ], BF16)
        self.MQ = scr("MQ", [8, 64, SEQ], BF16)
        self.MK = scr("MK", [8, 64, SEQ], BF16)
        self.MV = scr("MV", [SEQ, 512], BF16)
        self.RQ = scr("RQ", [4, 64, SEQ], BF16)
        self.RK = scr("RK", [4, 64, SEQ], BF16)
        self.RV = scr("RV", [SEQ, 512], BF16)
        self.RG = scr("RG", [4, 128, SEQ], F32)
        self.OB = [scr("OMLA", [4, 128, SEQ], BF16), scr("OMOBA", [4, 128, SEQ], BF16), scr("ORET", [4, 128, SEQ], BF16)]
        self.ACTS = scr("ACTS", [32, 128, SEQ], BF16)
        self.dbuf = {}
        self.psF = nc.alloc_psum_tensor("psF", [128, 7, 512], F32)
        self.psB = nc.alloc_psum_tensor("psB", [128, 1024], BF16)
        self.pbuf = [Buf(f"ps{i}") for i in range(7)]
        self.pbufB = Buf("psB")
        self.bank_i = 0
        _, self.gam = make_consts()
        self.identF = self.sb([128, 128], F32)
        self.identB = self.sb([128, 128], BF16)
        self.onesB = self.sb([128, 128], BF16)
        self.mask01 = self.sb([128, 128], BF16)
        self.vecs = self.sb([128, NV], F32)
        self.bconst = Buf("const")
        self.dma("sp", self.identF[:], self.c_identF, w=[self.bconst])
        self.dma("pool", self.identB[:], self.c_identF, w=[self.bconst])
        self.dma("pool", self.mask01[:], self.c_mask01, w=[self.bconst])
        self.dma("sp", self.vecs[:], self.vecs_d, w=[self.bconst])
        self.I("pool", "memset", [], [self.bconst], ap=self.onesB[:], constant=1.0)
        self.maskneg = self.sb([128, 128], BF16)
        self.I("pool", "tensor_scalar", [self.bconst], [self.bconst], out=self.maskneg[:], in0=self.mask01[:], scalar1=-1.0, scalar2=-NEGB,
               op0=ALU.add, op1=ALU.mult)
        self.phase_base = self.top

    def sb(self, shape, dtype, name=None):
        sz = int(np.prod(shape[1:])) * (4 if dtype == F32 else 2)
        sz = (sz + 63) // 64 * 64
        self.uid += 1
        t = self.nc.alloc_sbuf_tensor_at(name or f"t{self.uid}", list(shape), dtype, offset=self.top)
        self.top += sz
        assert self.top <= self.SB_TOP, f"SBUF overflow {self.top}"
        return t

    def sbr(self, n, shape, dtype):
        return Ring([(self.sb(shape, dtype), Buf()) for _ in range(n)])

    def begin_phase(self):
        self.S.barrier()
        self.top = self.phase_base

    def I(self, eng, meth, r, w, **kw):
        return self.S.op(eng, lambda e: getattr(e, meth)(**kw), r, w)

    def dma(self, eng, out, in_, r=(), w=()):
        return self.S.op(eng, lambda e: e.dma_start(out=out, in_=in_), r, w, dma=True)

    def bank(self):
        i = self.bank_i % 7
        self.bank_i += 1
        return self.psF[:, i, :], self.pbuf[i]

    def db(self, name, t=None):
        k = (name, t)
        if k not in self.dbuf:
            self.dbuf[k] = Buf(f"{name}{t}")
        return self.dbuf[k]

    def dball(self, name):
        return [self.db(name, t) for t in range(NT)]

    def vcol(self, l, off, n=1):
        b = l * VSTR + off
        return self.vecs[:, b:b + n]

    def evac_eng(self):
        self._ev = getattr(self, "_ev", 0) + 1
        return "act" if self._ev % 2 else "dve"

    def copy(self, eng, out, in_, r, w):
        if eng == "act":
            return self.I("act", "copy", r, w, out=out, in_=in_)
        return self.I(eng, "tensor_copy", r, w, out=out, in_=in_)

    def rmsnorm_a(self, xaps, rb, sq, bsq, nparts=128):
        for c in range(len(xaps)):
            self.I("act", "activation", rb, [bsq], out=sq[0:nparts, c, :], in_=xaps[c], func=AF.Square)

    def rmsnorm_b(self, xaps, gaps, nfeat, eps, outaps, rb, wb, sq, bsq, rs, brs, nparts=128):
        n = len(xaps)
        ps, bps = self.bank()
        for c in range(n):
            self.I("pe", "matmul", [bsq, self.bconst], [bps], out=ps, lhsT=self.onesB[0:nparts, :], rhs=sq[0:nparts, c, :],
                   start=(c == 0), stop=(c == n - 1))
        self.I("act", "activation", [bps], [brs], out=rs[:], in_=ps, func=AF.Sqrt, scale=1.0 / nfeat, bias=float(eps))
        self.I("dve", "reciprocal", [brs], [brs], out=rs[:], in_=rs[:])
        for c in range(n):
            self.I("dve", "scalar_tensor_tensor", list(rb) + [brs, self.bconst], wb, out=outaps[c], in0=xaps[c], scalar=gaps[c],
                   in1=rs[0:nparts, :], op0=ALU.mult, op1=ALU.mult)

    def rmsnorm(self, xaps, gaps, nfeat, eps, outaps, rb, wb, sq, bsq, rs, brs, nparts=128):
        self.rmsnorm_a(xaps, rb, sq, bsq, nparts)
        self.rmsnorm_b(xaps, gaps, nfeat, eps, outaps, rb, wb, sq, bsq, rs, brs, nparts)

    def p_transpose_in(self):
        self.begin_phase()
        xin = self.sbr(2, [128, DM], F32)
        xo = self.sbr(2, [128, 8, TT], F32)
        for t in range(NT):
            xot, bxo = xo.next()
            tiles = []
            for s in range(4):
                xt, bx = xin.next()
                r0 = t * TT + s * 128
                self.dma("sp", xt[:], self.x[r0:r0 + 128, :], w=[bx])
                for g in range(2):
                    ps, bps = self.bank()
                    for j in range(4):
                        c = g * 4 + j
                        self.I("pe", "transpose", [bx, self.bconst], [bps], out=ps[:, j * 128:(j + 1) * 128], in_=xt[:, c * 128:(c + 1) * 128],
                               identity=self.identF[:])
                    self.copy(self.evac_eng(), xot[:, g * 4:(g + 1) * 4, s * 128:(s + 1) * 128],
                              ps.rearrange("p (j n) -> p j n", n=128), [bps], [bxo])
            self.dma("sp", self.XT.rearrange("c p n -> p c n")[:, :, t * TT:(t + 1) * TT], xot[:], r=[bxo], w=[self.db("XT", t)])

    def p_inproj(self, l):
        self.begin_phase()
        NC1 = 5056
        W1 = self.sb([128, 8, NC1], BF16)
        bW = []
        win = self.w_in[l].rearrange("(c p) n -> p c n", p=128)

        def wl(d0, s0, n):
            b = Buf()
            self.dma("pool", W1[:, :, d0:d0 + n], win[:, :, s0:s0 + n], w=[b])
            bW.append(b)

        C_MQ, C_MQR, C_MK, C_MKR, C_MV = 448, 960, 1472, 1984, 2496
        C_RQ, C_RQR, C_RK, C_RKR, C_RV, C_RG = 3008, 3264, 3520, 3776, 4032, 4544
        wl(0, 0, 416)
        wl(416, 400, 16)
        wl(432, 384, 16)
        wl(C_MQ, 416, 512)
        wl(C_MK, 928, 512)
        wl(C_MV, 1440, 512)
        wl(C_RQ, 1952, 256)
        wl(C_RK, 2208, 256)
        wl(C_RV, 2464, 512)
        wl(C_RG, 2976, 512)
        bz = Buf()
        for c in range(8):
            self.I("pool", "memset", [], [bz], ap=W1[:, c, C_MQR:C_MQR + 512], constant=0.0)
            self.I("pool", "memset", [], [bz], ap=W1[:, c, C_MKR:C_MKR + 512], constant=0.0)
        bW.append(bz)
        for (dst, src, H, half) in ((C_MQR, 416, 8, 8), (C_MKR, 928, 8, 8), (C_RQR, 1952, 4, 32), (C_RKR, 2208, 4, 32)):
            for c in range(8):
                dstv = W1[:, c, dst:dst + H * 64].rearrange("p (h e) -> p h e", e=64)
                srcv = win[:, c, src:src + H * 64].rearrange("p (h d) -> p h d", d=64)
                b = Buf()
                self.dma("pool", dstv[:, :, 0:half], srcv[:, :, half:2 * half], r=[bz], w=[b])
                self.dma("pool", dstv[:, :, half:2 * half], srcv[:, :, 0:half], r=[bz], w=[b])
                bW.append(b)
        xTr = self.sbr(2, [128, 8, TT], F32)
        hTr = self.sbr(2, [128, 8, TT], BF16)
        sq = self.sb([128, 8, TT], BF16)
        bsq = Buf()
        rsr = self.sbr(3, [128, TT], F32)
        cqf = self.sb([128, 3, TT], F32)
        bcqf = Buf()
        cqn = self.sb([128, 3, TT], BF16)
        bcqn = Buf()
        sq2 = self.sb([128, 3, TT], BF16)
        bsq2 = [Buf(), Buf()]
        tabk = self.sb([32, 2, TT], F32)
        tabm = self.sb([128, 2, TT], F32)
        tabr = self.sb([128, 2, TT], F32)
        btab = Buf()
        t1r = self.sbr(2, [128, TT], F32)
        t2r = self.sbr(2, [128, TT], F32)
        kpe = self.sb([32, TT], BF16)
        bkpe = Buf()
        stq = self.sb([128, 4, TT], BF16)
        stk = self.sb([128, 4, TT], BF16)
        strq = self.sb([128, 2, TT], BF16)
        strk = self.sb([128, 2, TT], BF16)
        stg = self.sb([128, 4, TT], F32)
        stv = self.sbr(2, [128, 4, 512], BF16)
        bst = {k: Buf() for k in ("q", "k", "rq", "rk", "g")}

        def load_x(t):
            tok = slice(t * TT, (t + 1) * TT)
            xT, bx = xTr.next()
            hT, bh = hTr.next()
            self.dma("sp", xT[:], self.XT.rearrange("c p n -> p c n")[:, :, tok], r=[self.db("XT", t)], w=[bx])
            return xT, bx, hT, bh

        def xnorm_a(st_):
            xT, bx, hT, bh = st_
            self.rmsnorm_a([xT[:, c, :] for c in range(8)], [bx], sq, bsq)

        def xnorm_b(st_, t):
            xT, bx, hT, bh = st_
            rs, brs = rsr.next()
            self.rmsnorm_b([xT[:, c, :] for c in range(8)], [self.vcol(l, c) for c in range(8)], DM, EPS,
                           [hT[:, c, :] for c in range(8)], [bx], [bh], sq, bsq, rs, brs)
            self.dma("sp", self.HT.rearrange("c p n -> p c n")[:, :, t * TT:(t + 1) * TT], hT[:], r=[bh], w=[self.db("HT", t)])

        cur = load_x(0)
        xnorm_a(cur)
        xnorm_b(cur, 0)
        for t in range(NT):
            tok = slice(t * TT, (t + 1) * TT)
            xT, bx, hT, bh = cur
            nxt = load_x(t + 1) if t + 1 < NT else None
            self.dma("sp", tabk[:], self.c_rope_mla.rearrange("a p n -> p a n")[:, :, tok], w=[btab])
            self.dma("sp", tabm[:], self.c_rope_moba.rearrange("a p n -> p a n")[:, :, tok], w=[btab])
            self.dma("sp", tabr[:], self.c_rope_ret.rearrange("a p n -> p a n")[:, :, tok], w=[btab])

            def mm(M, col):
                ps, bps = self.bank()
                for c in range(8):
                    self.I("pe", "matmul", [bh] + bW, [bps], out=ps[0:M, :], lhsT=W1[:, c, col:col + M], rhs=hT[:, c, :],
                           start=(c == 0), stop=(c == 7))
                return ps, bps

            def rope(psm, bpm, psr, bpr, n, tab, outap, wb):
                t1, b1 = t1r.next()
                t2, b2 = t2r.next()
                self.I("dve", "tensor_tensor", [bpm, btab], [b1], out=t1[0:n, :], in0=psm[0:n, :], in1=tab[0:n, 0, :], op=ALU.mult)
                self.I("dve", "tensor_tensor", [bpr, btab], [b2], out=t2[0:n, :], in0=psr[0:n, :], in1=tab[0:n, 1, :], op=ALU.mult)
                self.I("pool", "tensor_tensor", [b1, b2], wb, out=outap, in0=t1[0:n, :], in1=t2[0:n, :], op=ALU.add)

            for j in range(3):
                ps, bps = mm(128, j * 128)
                self.copy(self.evac_eng(), cqf[:, j, :], ps, [bps], [bcqf])
            self.rmsnorm_a([cqf[:, c, :] for c in range(2)], [bcqf], sq2[:, 0:2, :], bsq2[0])
            self.rmsnorm_a([cqf[:, 2, :]], [bcqf], sq2[:, 2:3, :], bsq2[1])
            psm, bpm = mm(32, 384)
            psr, bpr = mm(32, 416)
            rope(psm, bpm, psr, bpr, 32, tabk, kpe[:], [bkpe])
            self.dma("sp", self.KPE[:, tok], kpe[:], r=[bkpe], w=[self.db("KPE", t)])
            if nxt is not None:
                xnorm_a(nxt)
            for (st, key, base, rbase, dst) in ((stq, "q", C_MQ, C_MQR, self.MQ), (stk, "k", C_MK, C_MKR, self.MK)):
                for j in range(4):
                    psm, bpm = mm(128, base + j * 128)
                    psr, bpr = mm(128, rbase + j * 128)
                    rope(psm, bpm, psr, bpr, 128, tabm, st[:, j, :], [bst[key]])
                self.dma("sp", dst.rearrange("(j two) p n -> (two p) j n", two=2)[:, :, tok], st[:], r=[bst[key]], w=[self.db("M" + key, t)])
            rs2, brs2 = rsr.next()
            self.rmsnorm_b([cqf[:, c, :] for c in range(2)], [self.vcol(l, 16 + c) for c in range(2)], 256, EPS,
                           [cqn[:, c, :] for c in range(2)], [bcqf], [bcqn], sq2[:, 0:2, :], bsq2[0], rs2, brs2)
            self.dma("sp", self.CQ.rearrange("c p n -> p c n")[:, :, tok], cqn[:, 0:2, :], r=[bcqn], w=[self.db("CQ", t)])
            rs3, brs3 = rsr.next()
            self.rmsnorm_b([cqf[:, 2, :]], [self.vcol(l, 18)], 128, EPS, [cqn[:, 2, :]], [bcqf], [bcqn], sq2[:, 2:3, :], bsq2[1], rs3, brs3)
            self.dma("sp", self.CKV[:, tok], cqn[:, 2, :], r=[bcqn], w=[self.db("CKV", t)])
            for (st, key, base, rbase, dst) in ((strq, "rq", C_RQ, C_RQR, self.RQ), (strk, "rk", C_RK, C_RKR, self.RK)):
                for j in range(2):
                    psm, bpm = mm(128, base + j * 128)
                    psr, bpr = mm(128, rbase + j * 128)
                    rope(psm, bpm, psr, bpr, 128, tabr, st[:, j, :], [bst[key]])
                self.dma("sp", dst.rearrange("(j two) p n -> (two p) j n", two=2)[:, :, tok], st[:], r=[bst[key]], w=[self.db(key, t)])
            for j in range(4):
                ps, bps = mm(128, C_RG + j * 128)
                self.I("act", "activation", [bps], [bst["g"]], out=stg[:, j, :], in_=ps, func=AF.Silu)
            self.dma("sp", self.RG.rearrange("c p n -> p c n")[:, :, tok], stg[:], r=[bst["g"]], w=[self.db("RG", t)])
            if nxt is not None:
                xnorm_b(nxt, t + 1)
            for (col, dst, key) in ((C_MV, self.MV, "MV"), (C_RV, self.RV, "RV")):
                sv, bsv = stv.next()
                for s_ in range(4):
                    ps, bps = self.bank()
                    for c in range(8):
                        self.I("pe", "matmul", [bh] + bW, [bps], out=ps, lhsT=hT[:, c, s_ * 128:(s_ + 1) * 128], rhs=W1[:, c, col:col + 512],
                               start=(c == 0), stop=(c == 7))
                    self.copy(self.evac_eng(), sv[:, s_, :], ps, [bps], [bsv])
                self.dma("sp", dst[t * TT:(t + 1) * TT, :].rearrange("(s p) n -> p s n", p=128), sv[:], r=[bsv], w=[self.db(key, t)])
            cur = nxt

    def run_attn(self, tiles, scale, pss_r, pT_r, pso_r, LA=2):
        steps = [(i, kt) for i, T in enumerate(tiles) for kt in range(4 * T["t"] + 4)]
        info = {}

        def emit_s(si):
            i, kt = steps[si]
            T = tiles[i]
            t = T["t"]
            qT, qbufs = T["get_q"]()
            j = kt - 4 * t
            q0 = max(0, j) * 128
            N = TT - q0
            (pi, bpi) = pss_r.next()
            pss = self.psF[:, pi, :]
            self.I("pe", "matmul", [T["bK"]] + list(qbufs), [bpi], out=pss[:, 0:N], lhsT=T["K"][:, kt * 128:(kt + 1) * 128], rhs=qT[:, q0:TT],
                   start=True, stop=(j < 0))
            if j >= 0:
                self.I("pe", "matmul", [self.bconst], [bpi], out=pss[:, 0:128], lhsT=self.identB[:], rhs=self.maskneg[:], start=False, stop=True)
            info[si] = (pss, bpi, q0, N)

        for si in range(min(LA, len(steps))):
            emit_s(si)
        for si, (i, kt) in enumerate(steps):
            T = tiles[i]
            nkt = 4 * T["t"] + 4
            if kt == 0:
                if T.get("start_hook") is not None:
                    T["start_hook"]()
                po, bpso = pso_r.next()
                T["pso"] = (self.psF[:, po, :], bpso)
            if si + LA < len(steps):
                emit_s(si + LA)
            if kt == 1 and T.get("mid_hook") is not None:
                T["mid_hook"]()
            pss, bpi, q0, N = info.pop(si)
            pso, bpso = T["pso"]
            pT, bpT = pT_r.next()
            self.I("act", "activation", [bpi], [bpT], out=pT[:, 0:N], in_=pss[:, 0:N], func=AF.Exp, scale=float(scale))
            self.I("pe", "matmul", [T["bV"], bpT], [bpso], out=pso[:, q0:TT], lhsT=T["V"][:, kt, :], rhs=pT[:, 0:N],
                   start=(kt == 0), stop=(kt == nkt - 1))
            if kt == nkt - 1:
                T["finish"](pso, bpso)

    def attn_finish(self, pso, bpso, rec_r, ost_r, dst_ap, dbuf):
        rec, brec = rec_r.next()
        ost, bost = ost_r.next()
        self.I("dve", "reciprocal", [bpso], [brec], out=rec[0:64, :], in_=pso[64:128, :])
        self.I("dve", "tensor_tensor", [bpso, brec], [bost], out=ost[0:64, :], in0=pso[0:64, :], in1=rec[0:64, :], op=ALU.mult)
        self.dma("sp", dst_ap, ost[0:64, :], r=[bost], w=[dbuf])

    def p_mla(self, l):
        self.begin_phase()
        CQs = self.sb([128, 2, SEQ], BF16)
        CKVs = self.sb([128, SEQ], BF16)
        bin_ = Buf()
        self.dma("sp", CQs[:], self.CQ.rearrange("c p n -> p c n"), r=self.dball("CQ"), w=[bin_])
        self.dma("sp", CKVs[:], self.CKV, r=self.dball("CKV"), w=[bin_])
        Wq = self.sb([128, 2, 8, 96], BF16)
        Wqr = self.sb([128, 2, 8, 96], BF16)
        Wkv = self.sb([128, 8, 128], BF16)
        bw = Buf()
        wq = self.w_q_up[l].rearrange("(c p) (h d) -> p c h d", p=128, d=96)
        self.I("pool", "memset", [], [bw], ap=Wqr[:], constant=0.0)
        for c in range(2):
            self.dma("pool", Wq[:, c, :, :], wq[:, c, :, :], w=[bw])
            self.dma("pool", Wqr[:, c, :, 64:80], wq[:, c, :, 80:96], w=[bw])
            self.dma("pool", Wqr[:, c, :, 80:96], wq[:, c, :, 64:80], w=[bw])
        self.dma("pool", Wkv[:], self.w_kv_up[l].rearrange("p (h d) -> p h d", d=128), w=[bw])
        tab = self.sb([96, 2, SEQ], F32)
        btab = Buf()
        self.dma("sp", tab[64:96, :, :], self.c_rope_mla.rearrange("a p n -> p a n"), w=[btab])
        Kr = self.sbr(2, [128, SEQ], BF16)
        Vr = self.sbr(2, [128, 32, 128], BF16)
        for (Kb, bK) in Kr.items:
            self.I("pool", "memset", [], [bK], ap=Kb[96:128, :], constant=0.0)
            self.dma("sp", Kb[64:96, :], self.KPE, r=self.dball("KPE"), w=[bK])
        for (Vb, bV) in Vr.items:
            self.I("pool", "memset", [], [bV], ap=Vb[:, :, 64:128], constant=1.0)
        qTr = self.sbr(3, [128, TT], BF16)
        for (q_, bq_) in qTr.items:
            self.I("pool", "memset", [], [bq_], ap=q_[96:128, :], constant=0.0)
        t1r = self.sbr(2, [96, TT], F32)
        t2r = self.sbr(2, [96, TT], F32)
        pT_r = self.sbr(4, [128, TT], BF16)
        rec_r = self.sbr(2, [64, TT], F32)
        ost_r = self.sbr(2, [64, TT], BF16)
        pss_r = Ring([(0, self.pbuf[0]), (1, self.pbuf[1]), (2, self.pbuf[2])])
        pso_r = Ring([(3, self.pbuf[3]), (4, self.pbuf[4])])
        aux_r = Ring([(5, self.pbuf[5]), (6, self.pbuf[6])])
        scale = 96 ** -0.5
        KV = {}

        def build_kv(h):
            Kb, bK = Kr.items[h % 2]
            Vb, bV = Vr.items[h % 2]
            KV[h] = (Kb, bK, Vb, bV)
            for t in range(NT):
                pi, bpi = aux_r.next()
                ps = self.psF[:, pi, :]
                self.I("pe", "matmul", [bin_, bw], [bpi], out=ps[0:64, :], lhsT=Wkv[:, h, 0:64], rhs=CKVs[:, t * TT:(t + 1) * TT], start=True, stop=True)
                self.copy(self.evac_eng(), Kb[0:64, t * TT:(t + 1) * TT], ps[0:64, :], [bpi], [bK])
            for g in range(4):
                pi, bpi = aux_r.next()
                ps = self.psF[:, pi, :]
                for j in range(8):
                    kt = g * 8 + j
                    self.I("pe", "matmul", [bin_, bw], [bpi], out=ps[:, j * 64:(j + 1) * 64], lhsT=CKVs[:, kt * 128:(kt + 1) * 128],
                           rhs=Wkv[:, h, 64:128], start=True, stop=True)
                self.copy(self.evac_eng(), Vb[:, g * 8:(g + 1) * 8, 0:64], ps.rearrange("p (j d) -> p j d", d=64), [bpi], [bV])

        def build_q(h, t):
            tok = slice(t * TT, (t + 1) * TT)
            qT, bq = qTr.next()
            pi, bpm = aux_r.next()
            psm = self.psF[:, pi, :]
            pi2, bpr = aux_r.next()
            psr = self.psF[:, pi2, :]
            for c in range(2):
                self.I("pe", "matmul", [bin_, bw], [bpm], out=psm[0:96, :], lhsT=Wq[:, c, h, :], rhs=CQs[:, c, tok], start=(c == 0), stop=(c == 1))
            for c in range(2):
                self.I("pe", "matmul", [bin_, bw], [bpr], out=psr[0:96, :], lhsT=Wqr[:, c, h, :], rhs=CQs[:, c, tok], start=(c == 0), stop=(c == 1))
            self.I("act", "copy", [bpm], [bq], out=qT[0:64, :], in_=psm[0:64, :])
            t1, b1 = t1r.next()
            t2, b2 = t2r.next()
            self.I("dve", "tensor_tensor", [bpm, btab], [b1], out=t1[64:96, :], in0=psm[64:96, :], in1=tab[64:96, 0, tok], op=ALU.mult)
            self.I("dve", "tensor_tensor", [bpr, btab], [b2], out=t2[64:96, :], in0=psr[64:96, :], in1=tab[64:96, 1, tok], op=ALU.mult)
            self.I("pool", "tensor_tensor", [b1, b2], [bq], out=qT[64:96, :], in0=t1[64:96, :], in1=t2[64:96, :], op=ALU.add)
            return qT, bq

        items = [(h, t) for h in range(8) for t in range(NT)]
        build_kv(0)
        prepared = {items[0]: build_q(*items[0])}
        tiles = []
        for idx, (h, t) in enumerate(items):
            def start_hook(idx=idx):
                if idx + 1 < len(items):
                    hn, tn = items[idx + 1]
                    if tn == 0:
                        build_kv(hn)
                    prepared[(hn, tn)] = build_q(hn, tn)

            def get_q(h=h, t=t):
                qT, bq = prepared[(h, t)]
                return qT[:, :], [bq]

            def finish(pso, bpso, h=h, t=t):
                tok = slice(t * TT, (t + 1) * TT)
                self.attn_finish(pso, bpso, rec_r, ost_r, self.OB[0][h // 2, (h % 2) * 64:(h % 2) * 64 + 64, tok], self.db("OMLA", t))

            tiles.append(dict(t=t, h=h, get_q=get_q, start_hook=start_hook, finish=finish))
        class _KV(dict):
            pass
        for T in tiles:
            T["_h"] = T["h"]
        for T in tiles:
            hh = T["h"]
            T["K"] = Kr.items[hh % 2][0]
            T["bK"] = Kr.items[hh % 2][1]
            T["V"] = Vr.items[hh % 2][0]
            T["bV"] = Vr.items[hh % 2][1]
        self.run_attn(tiles, scale, pss_r, pT_r, pso_r)

    def p_moba(self, l):
        self.begin_phase()
        Qr = self.sbr(2, [128, SEQ], BF16)
        Kr = self.sbr(2, [128, SEQ], BF16)
        Vr = self.sbr(2, [128, 32, 128], BF16)
        for (Kb, bK) in Kr.items:
            self.I("pool", "memset", [], [bK], ap=Kb[64:128, :], constant=0.0)
            self.dma("pool", Kb[64:80, :], self.c_onehot, w=[bK])
        for (Qb, bQ) in Qr.items:
            self.I("pool", "memset", [], [bQ], ap=Qb[64:128, :], constant=0.0)
        for (Vb, bV) in Vr.items:
            self.I("pool", "memset", [], [bV], ap=Vb[:, :, 64:128], constant=1.0)
        km_r = [(self.sb([64, 16], F32), self.sb([64, 16], BF16), Buf()) for _ in range(2)]
        gate_r = self.sbr(4, [128, 16], F32)
        top_r = self.sbr(4, [128, 8], F32)
        mbp_r = self.sbr(4, [128, 4, 80], BF16)
        for (m, bm) in mbp_r.items:
            self.I("pool", "memset", [], [bm], ap=m[:], constant=0.0)
        pT_r = self.sbr(4, [128, TT], BF16)
        rec_r = self.sbr(2, [64, TT], F32)
        ost_r = self.sbr(2, [64, TT], BF16)
        pss_r = Ring([(0, self.pbuf[0]), (1, self.pbuf[1]), (2, self.pbuf[2])])
        pso_r = Ring([(3, self.pbuf[3]), (4, self.pbuf[4])])
        aux_r = Ring([(5, self.pbuf[5]), (6, self.pbuf[6])])
        bQm = [[Buf() for _ in range(NT)] for _ in range(2)]
        H = {}

        def load_head(h):
            Qb, bQ = Qr.items[h % 2]
            Kb, bK = Kr.items[h % 2]
            Vb, bV = Vr.items[h % 2]
            H[h] = (Qb, bQ, Kb, bK, Vb, bV)
            self.dma("sp", Qb[0:64, :], self.MQ[h], r=self.dball("Mq"), w=[bQ])
            self.dma("sp", Kb[0:64, :], self.MK[h], r=self.dball("Mk"), w=[bK])
            for g4 in range(4):
                self.dma("sp", Vb[:, g4 * 8:(g4 + 1) * 8, 0:64],
                         self.MV[g4 * 1024:(g4 + 1) * 1024, h * 64:(h + 1) * 64].rearrange("(k p) d -> p k d", p=128), r=self.dball("MV"), w=[bV])

        def kmean(h):
            Qb, bQ, Kb, bK, Vb, bV = H[h]
            kmf, kmb, bkm = km_r[h % 2]
            self.I("dve", "tensor_reduce", [bK], [bkm], out=kmf[:], in_=Kb[0:64, :].rearrange("p (n k) -> p n k", k=256), axis=AX.X, op=ALU.add)
            self.I("dve", "tensor_copy", [bkm], [bkm], out=kmb[:], in_=kmf[:])

        def prepare(h, t):
            Qb, bQ, Kb, bK, Vb, bV = H[h]
            kmf, kmb, bkm = km_r[h % 2]
            tok = slice(t * TT, (t + 1) * TT)
            mbp, bmb = mbp_r.next()
            pi, bpg = aux_r.next()
            psg = self.psF[:, pi, :]
            for s in range(4):
                g = 4 * t + s
                blk = g // 2
                if blk <= 3:
                    continue
                self.I("pe", "matmul", [bQ, bkm], [bpg], out=psg[:, s * 16:(s + 1) * 16], lhsT=Qb[0:64, g * 128:(g + 1) * 128], rhs=kmb[:],
                       start=True, stop=True)
            for s in range(4):
                g = 4 * t + s
                blk = g // 2
                if blk <= 3:
                    self.I("pool", "memset", [], [bmb], ap=mbp[:, s, 64:80], constant=0.0)
                    continue
                gt, bg = gate_r.next()
                tp, btp = top_r.next()
                self.I("dve", "memset", [], [bg], ap=gt[:], constant=-1e30)
                self.I("dve", "tensor_copy", [bpg], [bg], out=gt[:, 0:blk], in_=psg[:, s * 16:s * 16 + blk])
                self.I("dve", "max", [bg], [btp], out=tp[:], in_=gt[:])
                self.I("dve", "tensor_scalar", [bg, btp], [bmb], out=mbp[:, s, 64:80], in0=gt[:], scalar1=tp[:, 2:3], scalar2=NEGB,
                       op0=ALU.is_lt, op1=ALU.mult)
                self.I("dve", "memset", [], [bmb], ap=mbp[:, s, 64 + blk:80], constant=0.0)
            return mbp, bmb

        def prepare2(h, t, mbp, bmb):
            Qb, bQ, Kb, bK, Vb, bV = H[h]
            tok = slice(t * TT, (t + 1) * TT)
            for s in range(4):
                self.I("pe", "transpose", [bmb, self.bconst], [self.pbufB], out=self.psB[0:80, s * 128:(s + 1) * 128], in_=mbp[:, s, :],
                       identity=self.identB[:])
            self.I("act", "copy", [self.pbufB, bQ], [bQm[h % 2][t]], out=Qb[64:80, tok], in_=self.psB[64:80, 0:TT])

        items = [(h, t) for h in range(8) for t in range(NT)]
        load_head(0)
        load_head(1)
        kmean(0)
        prepare2(0, 0, *prepare(0, 0))
        gated = {}
        if len(items) > 1:
            gated[items[1]] = prepare(*items[1])
        tiles = []
        for idx, (h, t) in enumerate(items):
            def start_hook(idx=idx, h=h, t=t):
                if t == 1 and h + 1 < 8 and h >= 1:
                    load_head(h + 1)
                if idx + 2 < len(items):
                    hn, tn = items[idx + 2]
                    if tn == 0:
                        kmean(hn)
                    gated[(hn, tn)] = prepare(hn, tn)

            def mid_hook(idx=idx):
                if idx + 1 < len(items):
                    hn, tn = items[idx + 1]
                    prepare2(hn, tn, *gated.pop((hn, tn)))

            def get_q(h=h, t=t):
                Qb, bQ = Qr.items[h % 2]
                return Qb[:, t * TT:(t + 1) * TT], [bQ, bQm[h % 2][t]]

            def finish(pso, bpso, h=h, t=t):
                tok = slice(t * TT, (t + 1) * TT)
                self.attn_finish(pso, bpso, rec_r, ost_r, self.OB[1][h // 2, (h % 2) * 64:(h % 2) * 64 + 64, tok], self.db("OMOBA", t))

            tiles.append(dict(t=t, h=h, get_q=get_q, start_hook=start_hook, mid_hook=mid_hook, finish=finish,
                              K=Kr.items[h % 2][0], bK=Kr.items[h % 2][1], V=Vr.items[h % 2][0], bV=Vr.items[h % 2][1]))
        self.run_attn(tiles, 0.125, pss_r, pT_r, pso_r)

    def p_ret(self, l):
        self.begin_phase()
        Qr = self.sbr(2, [64, SEQ], BF16)
        Kr = self.sbr(2, [64, SEQ], BF16)
        Vr = self.sbr(2, [128, 32, 128], BF16)
        dec = self.sb([128, 4, 128], F32)
        xi = self.sb([64, 4, 512], F32)
        zeta = self.sb([128, 4], F32)
        bc = Buf()
        self.dma("sp", dec[:], self.c_decayT.rearrange("h j i -> j h i"), w=[bc])
        self.dma("sp", xi[:], self.c_xi.rearrange("h p n -> p h n"), w=[bc])
        self.dma("sp", zeta[:], self.c_zeta, w=[bc])
        qx_r = self.sbr(2, [64, TT], BF16)
        kz_r = self.sbr(4, [128, 64], BF16)
        sc_r = self.sbr(4, [128, 128], BF16)
        KV_r = self.sbr(2, [64, 32, 128], F32)
        Rall_r = self.sbr(2, [64, 32, 128], BF16)
        Rf = self.sb([64, 128], F32)
        bRf = Buf()
        G_r = self.sbr(2, [128, TT], F32)
        of_r = self.sbr(2, [128, TT], F32)
        ob_r = self.sbr(2, [128, 2, TT], BF16)
        m_r = self.sbr(2, [128, TT], F32)
        v_r = self.sbr(2, [128, TT], F32)
        y_r = self.sbr(2, [128, TT], BF16)
        po_r = Ring([(0, self.pbuf[0]), (1, self.pbuf[1])])
        ps_r = Ring([(2, self.pbuf[2]), (3, self.pbuf[3])])
        pk_r = Ring([(4, self.pbuf[4]), (5, self.pbuf[5])])
        HB = {}

        def load_head(h):
            Qb, bQ = Qr.items[h % 2]
            Kb, bK = Kr.items[h % 2]
            Vb, bV = Vr.items[h % 2]
            HB[h] = (Qb, bQ, Kb, bK, Vb, bV)
            self.dma("sp", Qb[:], self.RQ[h], r=self.dball("rq"), w=[bQ])
            self.dma("sp", Kb[:], self.RK[h], r=self.dball("rk"), w=[bK])
            for g4 in range(4):
                self.dma("sp", Vb[:, g4 * 8:(g4 + 1) * 8, :],
                         self.RV[g4 * 1024:(g4 + 1) * 1024, h * 128:(h + 1) * 128].rearrange("(k p) d -> p k d", p=128), r=self.dball("RV"), w=[bV])

        def stage1(h):
            Qb, bQ, Kb, bK, Vb, bV = HB[h]
            KVs, bKV = KV_r.items[h % 2]
            for g in range(8):
                pi_, bpk = pk_r.next()
                pk = self.psF[:, pi_, :]
                for j in range(4):
                    n = g * 4 + j
                    ch = slice(n * 128, (n + 1) * 128)
                    self.I("pe", "transpose", [bK, self.bconst], [self.pbufB], out=self.psB[:, j * 64:(j + 1) * 64], in_=Kb[:, ch],
                           identity=self.identB[0:64, 0:64])
                kzs = []
                for j in range(4):
                    kz, bkz = kz_r.next()
                    self.I("act", "activation", [self.pbufB, bc], [bkz], out=kz[:], in_=self.psB[:, j * 64:(j + 1) * 64], func=AF.Copy,
                           scale=zeta[:, h:h + 1])
                    kzs.append((kz, bkz))
                for j in range(4):
                    n = g * 4 + j
                    kz, bkz = kzs[j]
                    self.I("pe", "matmul", [bkz, bV], [bpk], out=pk[0:64, j * 128:(j + 1) * 128], lhsT=kz[:], rhs=Vb[:, n, :], start=True, stop=True)
                self.copy("dve" if g % 2 else "act", KVs[:, g * 4:(g + 1) * 4, :], pk[0:64, :].rearrange("p (j e) -> p j e", e=128), [bpk], [bKV])

        def stage2(h):
            KVs, bKV = KV_r.items[h % 2]
            Rall, bR = Rall_r.items[h % 2]
            self.I("pool", "memset", [], [bRf], ap=Rf[:], constant=0.0)
            self.I("pool", "memset", [], [bR], ap=Rall[:, 0, :], constant=0.0)
            for n in range(31):
                self.I("dve", "scalar_tensor_tensor", [bRf, bKV], [bRf], out=Rf[:], in0=Rf[:], scalar=float(self.gam[h]), in1=KVs[:, n, :],
                       op0=ALU.mult, op1=ALU.add)
                self.I("dve", "tensor_copy", [bRf], [bR], out=Rall[:, n + 1, :], in_=Rf[:])

        def stage3(h):
            Qb, bQ, Kb, bK, Vb, bV = HB[h]
            Rall, bR = Rall_r.items[h % 2]
            for t in range(NT):
                tok = slice(t * TT, (t + 1) * TT)
                G, bG = G_r.next()
                self.dma("sp", G[:], self.RG[h, :, tok], r=[self.db("RG", t)], w=[bG])
                qx, bqx = qx_r.next()
                self.I("pool", "tensor_tensor", [bQ, bc], [bqx], out=qx[:], in0=Qb[:, tok], in1=xi[:, h, :], op=ALU.mult)
                pi_, bpo = po_r.next()
                po = self.psF[:, pi_, :]
                pi_, bps = ps_r.next()
                ps = self.psF[:, pi_, :]
                for s in range(4):
                    n = 4 * t + s
                    ch = slice(n * 128, (n + 1) * 128)
                    self.I("pe", "matmul", [bK, bQ], [bps], out=ps[:, s * 128:(s + 1) * 128], lhsT=Kb[:, ch], rhs=Qb[:, ch], start=True, stop=True)
                scs = []
                for s in range(4):
                    sc, bsc = sc_r.next()
                    self.I("dve", "tensor_tensor", [bps, bc], [bsc], out=sc[:], in0=ps[:, s * 128:(s + 1) * 128], in1=dec[:, h, :], op=ALU.mult)
                    scs.append((sc, bsc))
                for s in range(4):
                    n = 4 * t + s
                    sc, bsc = scs[s]
                    self.I("pe", "matmul", [bV, bsc], [bpo], out=po[:, s * 128:(s + 1) * 128], lhsT=Vb[:, n, :], rhs=sc[:], start=True, stop=False)
                    self.I("pe", "matmul", [bR, bqx], [bpo], out=po[:, s * 128:(s + 1) * 128], lhsT=Rall[:, n, :], rhs=qx[:, s * 128:(s + 1) * 128],
                           start=False, stop=True)
                of, bof = of_r.next()
                ob, bob = ob_r.next()
                self.I("act", "copy", [bpo], [bof], out=of[:], in_=po)
                self.I("pool", "tensor_copy", [bof], [bob], out=ob[:, 0, :], in_=of[:])
                self.I("act", "activation", [bof], [bob], out=ob[:, 1, :], in_=of[:], func=AF.Square)
                pm, bpm = self.psF[:, 6, :], self.pbuf[6]
                self.I("pe", "matmul", [bob, self.bconst], [bpm], out=pm, lhsT=self.onesB[:], rhs=ob[:, 0, :], start=True, stop=True)
                pi_, bpq = ps_r.next()
                pq = self.psF[:, pi_, :]
                self.I("pe", "matmul", [bob, self.bconst], [bpq], out=pq, lhsT=self.onesB[:], rhs=ob[:, 1, :], start=True, stop=True)
                m, bm = m_r.next()
                v, bv = v_r.next()
                self.I("act", "mul", [bpm], [bm], out=m[:], in_=pm, mul=1.0 / 128)
                self.I("dve", "tensor_tensor", [bm], [bv], out=v[:], in0=m[:], in1=m[:], op=ALU.mult)
                self.I("dve", "scalar_tensor_tensor", [bpq, bv], [bv], out=v[:], in0=pq, scalar=1.0 / 128, in1=v[:], op0=ALU.mult, op1=ALU.subtract)
                self.I("act", "activation", [bv], [bv], out=v[:], in_=v[:], func=AF.Sqrt, scale=1.0, bias=float(GN_EPS))
                self.I("dve", "reciprocal", [bv], [bv], out=v[:], in_=v[:])
                self.I("pool", "tensor_tensor", [bof, bm], [bof], out=of[:], in0=of[:], in1=m[:], op=ALU.subtract)
                self.I("dve", "tensor_tensor", [bof, bv], [bof], out=of[:], in0=of[:], in1=v[:], op=ALU.mult)
                self.I("dve", "tensor_scalar", [bof, self.bconst], [bof], out=of[:], in0=of[:], scalar1=self.vcol(l, 19 + h), scalar2=self.vcol(l, 23 + h),
                       op0=ALU.mult, op1=ALU.add)
                y, by = y_r.next()
                self.I("pool", "tensor_tensor", [bof, bG], [by], out=y[:], in0=of[:], in1=G[:], op=ALU.mult)
                self.dma("sp", self.OB[2][h, :, tok], y[:], r=[by], w=[self.db("ORET", t)])

        load_head(0)
        load_head(1)
        stage1(0)
        stage2(0)
        for h in range(4):
            if h + 1 < 4:
                stage1(h + 1)
                stage2(h + 1)
            stage3(h)
            if h + 2 < 4:
                load_head(h + 2)

    def p_merge(self, l):
        self.begin_phase()
        Wg = self.sb([128, 8, 3072], BF16)
        Wb = self.sb([128, 3, 4, DM], BF16)
        Wo = self.sb([128, 8, DM], BF16)
        bw = Buf()
        win = self.w_in[l].rearrange("(c p) n -> p c n", p=128)
        for j in range(3):
            self.dma("pool", Wg[:, :, j * 1024:(j + 1) * 1024], win[:, :, 3488 + j * 1024:3488 + (j + 1) * 1024], w=[bw])
            self.dma("pool", Wb[:, j, :, :], self.w_b[j][l].rearrange("(c p) n -> p c n", p=128), w=[bw])
        self.dma("pool", Wo[:], self.w_out[l].rearrange("(c p) n -> p c n", p=128), w=[bw])
        hT_r = self.sbr(2, [128, 8, TT], BF16)
        o_r = self.sbr(2, [128, 3, 4, TT], BF16)
        xT_r = self.sbr(2, [128, 8, TT], F32)
        sig_r = self.sbr(3, [128, TT], F32)
        acc_r = self.sbr(2, [128, TT], F32)
        mg_r = self.sbr(2, [128, 8, TT], BF16)
        names = ("OMLA", "OMOBA", "ORET")
        def load3(t):
            tok = slice(t * TT, (t + 1) * TT)
            hT, bh = hT_r.next()
            o, bo = o_r.next()
            xT, bx = xT_r.next()
            self.dma("sp", hT[:], self.HT.rearrange("c p n -> p c n")[:, :, tok], r=[self.db("HT", t)], w=[bh])
            for j in range(3):
                self.dma("sp", o[:, j, :, :], self.OB[j].rearrange("c p n -> p c n")[:, :, tok], r=[self.db(names[j], t)], w=[bo])
            self.dma("sp", xT[:], self.XT.rearrange("c p n -> p c n")[:, :, tok], r=[self.db("XT", t)], w=[bx])
            return hT, bh, o, bo, xT, bx

        nxt3 = load3(0)
        for t in range(NT):
            tok = slice(t * TT, (t + 1) * TT)
            hT, bh, o, bo, xT, bx = nxt3
            if t + 1 < NT:
                nxt3 = load3(t + 1)
            mg, bmg = mg_r.next()
            for oc in range(8):
                acc, bacc = acc_r.next()
                for j in range(3):
                    pg, bpg = self.bank()
                    for c in range(8):
                        self.I("pe", "matmul", [bh, bw], [bpg], out=pg, lhsT=Wg[:, c, j * 1024 + oc * 128:j * 1024 + (oc + 1) * 128], rhs=hT[:, c, :],
                               start=(c == 0), stop=(c == 7))
                    pb, bpb = self.bank()
                    for c in range(4):
                        self.I("pe", "matmul", [bo, bw], [bpb], out=pb, lhsT=Wb[:, j, c, oc * 128:(oc + 1) * 128], rhs=o[:, j, c, :],
                               start=(c == 0), stop=(c == 3))
                    sg, bsg = sig_r.next()
                    self.I("act", "activation", [bpg], [bsg], out=sg[:], in_=pg, func=AF.Sigmoid)
                    if j == 0:
                        self.I("dve", "tensor_tensor", [bsg, bpb], [bacc], out=acc[:], in0=pb, in1=sg[:], op=ALU.mult)
                    else:
                        self.I("dve", "tensor_tensor", [bsg, bpb], [bsg], out=sg[:], in0=pb, in1=sg[:], op=ALU.mult)
                        if j == 1:
                            self.I("pool", "tensor_tensor", [bsg, bacc], [bacc], out=acc[:], in0=acc[:], in1=sg[:], op=ALU.add)
                        else:
                            self.I("pool", "tensor_tensor", [bsg, bacc], [bmg], out=mg[:, oc, :], in0=acc[:], in1=sg[:], op=ALU.add)
            for oc in range(8):
                po, bpo = self.bank()
                for c in range(8):
                    self.I("pe", "matmul", [bmg, bw], [bpo], out=po, lhsT=Wo[:, c, oc * 128:(oc + 1) * 128], rhs=mg[:, c, :], start=(c == 0), stop=(c == 7))
                self.I("dve", "tensor_tensor", [bpo, bx], [bx], out=xT[:, oc, :], in0=po, in1=xT[:, oc, :], op=ALU.add)
            self.dma("sp", self.XT.rearrange("c p n -> p c n")[:, :, tok], xT[:], r=[bx], w=[self.db("XT", t)])

    def p_mlp_up(self, l):
        self.begin_phase()
        Wu = self.sb([128, 8, 4096], BF16)
        bw = Buf()
        wu = self.w_up[l].rearrange("(c p) n -> p c n", p=128)
        for j in range(4):
            self.dma("pool", Wu[:, :, j * 1024:(j + 1) * 1024], wu[:, :, j * 1024:(j + 1) * 1024], w=[bw])
        xT_r = self.sbr(2, [128, 8, TT], F32)
        hT_r = self.sbr(2, [128, 8, TT], BF16)
        sq = self.sb([128, 8, TT], BF16)
        bsq = Buf()
        rs_r = self.sbr(2, [128, TT], F32)
        r_r = self.sbr(3, [128, TT], F32)
        a_r = self.sbr(2, [128, 8, TT], BF16)
        def load4(t):
            tok = slice(t * TT, (t + 1) * TT)
            xT, bx = xT_r.next()
            hT, bh = hT_r.next()
            self.dma("sp", xT[:], self.XT.rearrange("c p n -> p c n")[:, :, tok], r=[self.db("XT", t)], w=[bx])
            return xT, bx, hT, bh

        def norm4a(st_):
            xT, bx, hT, bh = st_
            self.rmsnorm_a([xT[:, c, :] for c in range(8)], [bx], sq, bsq)

        def norm4b(st_):
            xT, bx, hT, bh = st_
            rs, brs = rs_r.next()
            self.rmsnorm_b([xT[:, c, :] for c in range(8)], [self.vcol(l, 8 + c) for c in range(8)], DM, EPS,
                           [hT[:, c, :] for c in range(8)], [bx], [bh], sq, bsq, rs, brs)

        cur4 = load4(0)
        norm4a(cur4)
        norm4b(cur4)
        for t in range(NT):
            tok = slice(t * TT, (t + 1) * TT)
            xT, bx, hT, bh = cur4
            nxt4 = None
            for fg in range(4):
                if t + 1 < NT:
                    if fg == 0:
                        nxt4 = load4(t + 1)
                    elif fg == 1:
                        norm4a(nxt4)
                    elif fg == 3:
                        norm4b(nxt4)
                a, ba = a_r.next()
                for fj in range(8):
                    f = fg * 8 + fj
                    ps, bps = self.bank()
                    for c in range(8):
                        self.I("pe", "matmul", [bh, bw], [bps], out=ps, lhsT=Wu[:, c, f * 128:(f + 1) * 128], rhs=hT[:, c, :], start=(c == 0), stop=(c == 7))
                    r, br = r_r.next()
                    self.I("act", "activation", [bps], [br], out=r[:], in_=ps, func=AF.Relu)
                    self.I("pool" if fj % 2 else "dve", "tensor_tensor", [br], [ba], out=a[:, fj, :], in0=r[:], in1=r[:], op=ALU.mult)
                self.dma("sp", self.ACTS[fg * 8:(fg + 1) * 8].rearrange("c p n -> p c n")[:, :, tok], a[:], r=[ba], w=[self.db("ACTS%d" % fg, t)])
            cur4 = nxt4

    def p_mlp_down(self, l):
        self.begin_phase()
        Wd = self.sb([128, 32, DM], BF16)
        bw = Buf()
        wd = self.w_down[l].rearrange("(c p) n -> p c n", p=128)
        for j in range(4):
            self.dma("pool", Wd[:, j * 8:(j + 1) * 8, :], wd[:, j * 8:(j + 1) * 8, :], w=[bw])
        xT_r = self.sbr(2, [128, 8, TT], F32)
        a_r = self.sbr(2, [128, 32, TT], BF16)
        def load5(t):
            tok = slice(t * TT, (t + 1) * TT)
            xT, bx = xT_r.next()
            a, ba = a_r.next()
            self.dma("sp", xT[:], self.XT.rearrange("c p n -> p c n")[:, :, tok], r=[self.db("XT", t)], w=[bx])
            for fg in range(4):
                self.dma("sp", a[:, fg * 8:(fg + 1) * 8, :], self.ACTS[fg * 8:(fg + 1) * 8].rearrange("c p n -> p c n")[:, :, tok],
                         r=[self.db("ACTS%d" % fg, t)], w=[ba])
            return xT, bx, a, ba

        nxt5 = load5(0)
        for t in range(NT):
            tok = slice(t * TT, (t + 1) * TT)
            xT, bx, a, ba = nxt5
            if t + 1 < NT:
                nxt5 = load5(t + 1)
            for oc in range(8):
                po, bpo = self.bank()
                for f in range(32):
                    self.I("pe", "matmul", [ba, bw], [bpo], out=po, lhsT=Wd[:, f, oc * 128:(oc + 1) * 128], rhs=a[:, f, :], start=(f == 0), stop=(f == 31))
                self.I("dve", "tensor_tensor", [bpo, bx], [bx], out=xT[:, oc, :], in0=po, in1=xT[:, oc, :], op=ALU.add)
            self.dma("sp", self.XT.rearrange("c p n -> p c n")[:, :, tok], xT[:], r=[bx], w=[self.db("XT", t)])

    def p_final(self):
        self.begin_phase()
        xT_r = self.sbr(2, [128, 8, TT], F32)
        yT_r = self.sbr(2, [128, 8, TT], F32)
        sq = self.sb([128, 8, TT], BF16)
        bsq = Buf()
        rs_r = self.sbr(2, [128, TT], F32)
        o_r = self.sbr(2, [128, DM], F32)
        def load6(t):
            tok = slice(t * TT, (t + 1) * TT)
            xT, bx = xT_r.next()
            self.dma("sp", xT[:], self.XT.rearrange("c p n -> p c n")[:, :, tok], r=[self.db("XT", t)], w=[bx])
            return xT, bx

        nxt6 = load6(0)
        for t in range(NT):
            tok = slice(t * TT, (t + 1) * TT)
            xT, bx = nxt6
            if t + 1 < NT:
                nxt6 = load6(t + 1)
            yT, by = yT_r.next()
            rs, brs = rs_r.next()
            fb = DEPTH * VSTR
            self.rmsnorm([xT[:, c, :] for c in range(8)], [self.vecs[:, fb + c:fb + c + 1] for c in range(8)], DM, EPS,
                         [yT[:, c, :] for c in range(8)], [bx], [by], sq, bsq, rs, brs)
            for s in range(4):
                o, bo = o_r.next()
                for g in range(2):
                    ps, bps = self.bank()
                    for j in range(4):
                        c = g * 4 + j
                        self.I("pe", "transpose", [by, self.bconst], [bps], out=ps[:, j * 128:(j + 1) * 128], in_=yT[:, c, s * 128:(s + 1) * 128],
                               identity=self.identF[:])
                    self.copy(self.evac_eng(), o[:, g * 512:(g + 1) * 512], ps, [bps], [bo])
                r0 = t * TT + s * 128
                self.dma("sp", self.out[r0:r0 + 128, :], o[:], r=[bo], w=[self.db("out", t)])

    def build(self):
        ph = self.phases
        want = lambda name: ph is None or name in ph
        if want("t_in"):
            self.p_transpose_in()
        for l in range(self.n_layers):
            if want(f"inproj{l}"):
                self.p_inproj(l)
            if want(f"mla{l}"):
                self.p_mla(l)
            if want(f"moba{l}"):
                self.p_moba(l)
            if want(f"ret{l}"):
                self.p_ret(l)
            if want(f"merge{l}"):
                self.p_merge(l)
            if want(f"mlpup{l}"):
                self.p_mlp_up(l)
            if want(f"mlpdown{l}"):
                self.p_mlp_down(l)
        if want("final"):
            self.p_final()
        self.S.barrier()
        stats = self.S.emit()
        return self.nc, stats


_CACHE = {}


def get_program():
    if "nc" not in _CACHE:
        kb = KB()
        _CACHE["nc"], _CACHE["stats"] = kb.build()
    return _CACHE["nc"]


def make_in_map(inputs, b, consts, vecs):
    m = {"x": np.ascontiguousarray(inputs["x"][b])}
    for k in ("w_in", "mla_w_q_up", "mla_w_kv_up", "w_branch_mla", "w_branch_moba", "w_branch_ret", "w_out", "w_mlp_up", "w_mlp_down"):
        m[k] = np.ascontiguousarray(inputs[k], dtype=np.float32)
    m["vecs"] = vecs
    m.update(consts)
    return m


def kernel(**inputs):
    inputs = {k: np.asarray(v) for k, v in inputs.items()}
    consts, _ = make_consts()
    vecs = pack_vecs(inputs)
    nc = get_program()
    in_maps = [make_in_map(inputs, b, consts, vecs) for b in range(NCORES)]
    res = run_bass_kernel_spmd(nc, in_maps, core_ids=list(range(NCORES)))
    out = np.stack([np.asarray(res.results[b]["out"], dtype=np.float32) for b in range(NCORES)], axis=0)
    return out
```

```python
import numpy as np
import concourse.bass as bass
import concourse.mybir as mybir
from concourse.bass_utils import run_bass_kernel_spmd

F32 = mybir.dt.float32
BF16 = mybir.dt.bfloat16
AF = mybir.ActivationFunctionType
ALU = mybir.AluOpType
AX = mybir.AxisListType

SEQ = 4096
DM = 1024
TT = 512
NT = SEQ // TT
DEPTH = 2
NCORES = 4
EPS = 1e-6
GN_EPS = 1e-5
NEGB = -30000.0


class Buf:
    __slots__ = ("name", "w", "r", "rd")

    def __init__(self, name=""):
        self.name = name
        self.w = None
        self.r = {}
        self.rd = []


class _Op:
    __slots__ = ("eng", "fn", "deps", "is_dma", "idx", "need_inc", "semidx", "semval")


class Sched:
    ENG = ("pe", "act", "dve", "pool", "sp")

    def __init__(self, nc, n_dma_sems=40, same_eng_sync=("act", "dve", "pool")):
        import os
        if os.environ.get("K_VAR", "") == "pesync":
            same_eng_sync = ("act", "dve", "pool", "pe")
        self.nc = nc
        self.ops = []
        self.n_dma_sems = n_dma_sems
        self.same_eng_sync = set(same_eng_sync)
        self.last_on_eng = {}
        self.dmas_open = []
        self.cur_barrier = None

    def op(self, eng, fn, reads=(), writes=(), dma=False):
        o = _Op()
        o.eng = eng
        o.fn = fn
        o.is_dma = dma
        o.idx = len(self.ops)
        o.need_inc = False
        deps = set()
        for b in reads:
            if b.w is not None:
                deps.add(b.w)
        for b in writes:
            if b.w is not None:
                deps.add(b.w)
            deps.update(b.r.values())
            deps.update(b.rd)
        for b in reads:
            if dma:
                b.rd.append(o.idx)
            else:
                b.r[eng] = o.idx
        for b in writes:
            b.w = o.idx
            b.r = {}
            b.rd = []
        if self.cur_barrier is not None:
            deps.add(self.cur_barrier)
        deps.discard(o.idx)
        o.deps = deps
        self.ops.append(o)
        if dma:
            self.dmas_open.append(o.idx)
        else:
            self.last_on_eng[eng] = o.idx
        return o

    def barrier(self):
        deps = set(self.last_on_eng.values()) | set(self.dmas_open)
        o = self.op("sp", lambda e: e.nop())
        o.deps |= deps
        o.deps.discard(o.idx)
        self.dmas_open = []
        self.cur_barrier = o.idx
        return o

    def emit(self):
        nc = self.nc
        ops = self.ops
        engobj = {"pe": nc.tensor, "act": nc.scalar, "dve": nc.vector, "pool": nc.gpsimd, "sp": nc.sync}
        ses = self.same_eng_sync

        def needs_sync(o, od):
            if od.is_dma or o.is_dma:
                return True
            if od.eng != o.eng:
                return True
            return od.eng in ses

        for o in ops:
            for d in o.deps:
                od = ops[d]
                if not od.is_dma and needs_sync(o, od):
                    od.need_inc = True
        cnt = {e: 0 for e in self.ENG}
        ndma = 0
        qpool = {"sp": (0, 32), "pool": (32, 24), "act": (56, 8)}
        qcnt = {q: 0 for q in qpool}
        for o in ops:
            if o.is_dma:
                base, n = qpool[o.eng]
                k = qcnt[o.eng]
                qcnt[o.eng] += 1
                o.semidx = base + (k % n)
                o.semval = 16 * (k // n + 1)
                ndma += 1
            elif o.need_inc:
                cnt[o.eng] += 1
                o.semval = cnt[o.eng]
        engsem = {e: nc.alloc_semaphore(name=f"s_{e}") for e in self.ENG}
        dmasem = [nc.alloc_semaphore(name=f"s_dma{i}") for i in range(64)]
        waited = {}
        nwaits = 0
        self.trace = {e: [] for e in self.ENG}
        for o in ops:
            e = engobj[o.eng]
            waits = {}
            for d in o.deps:
                od = ops[d]
                if not needs_sync(o, od):
                    continue
                key = ("dma", od.semidx) if od.is_dma else ("eng", od.eng)
                if waits.get(key, 0) < od.semval:
                    waits[key] = od.semval
            if o.is_dma and o.semval > 16:
                key = ("dma", o.semidx)
                if waits.get(key, 0) < o.semval - 16:
                    waits[key] = o.semval - 16
            for key, val in waits.items():
                wk = (o.eng, key)
                if waited.get(wk, 0) >= val:
                    continue
                sem = dmasem[key[1]] if key[0] == "dma" else engsem[key[1]]
                e.wait_ge(sem, val)
                self.trace[o.eng].append(("w", key, val, o.idx))
                waited[wk] = val
                nwaits += 1
            ins = o.fn(e)
            if o.is_dma:
                ins.then_inc(dmasem[o.semidx], 16)
                self.trace[o.eng].append(("i", ("dma", o.semidx), 16, o.idx))
            elif o.need_inc:
                ins.then_inc(engsem[o.eng], 1)
                self.trace[o.eng].append(("i", ("eng", o.eng), 1, o.idx))
        self.stats = dict(n_ops=len(ops), n_waits=nwaits, incs=dict(cnt), n_dma=ndma)
        return self.stats


class Ring:
    def __init__(self, items):
        self.items = items
        self.i = 0

    def next(self):
        x = self.items[self.i % len(self.items)]
        self.i += 1
        return x


def _rope_tab(dim, theta):
    inv = (1.0 / (np.float32(theta) ** (np.arange(0, dim, 2, dtype=np.float32) / np.float32(dim)))).astype(np.float32)
    ang = (np.arange(SEQ, dtype=np.float32)[:, None] * inv[None, :]).astype(np.float32)
    c = np.cos(ang).astype(np.float32).T
    s = np.sin(ang).astype(np.float32).T
    cc = np.concatenate([c, c], 0)
    ss = np.concatenate([-s, s], 0)
    return np.ascontiguousarray(np.stack([cc, ss], 0))


def make_consts():
    c = {}
    c["identF"] = np.eye(128, dtype=np.float32)
    c["rope_mla"] = _rope_tab(32, 500000.0)
    rm = _rope_tab(16, 500000.0)
    pad = np.zeros((2, 128, SEQ), np.float32)
    pad[0] = 1.0
    pad[:, 0:16] = rm
    pad[:, 64:80] = rm
    c["rope_moba"] = pad
    rr = _rope_tab(64, 10000.0)
    c["rope_ret"] = np.ascontiguousarray(np.concatenate([rr, rr], axis=1))
    k = np.arange(128)
    c["mask01"] = (k[:, None] <= k[None, :]).astype(np.float32)
    c["onehot"] = (np.arange(16)[:, None] == (np.arange(SEQ)[None, :] // 256)).astype(np.float32)
    lg = np.log(np.float32(1.0) - np.float32(2.0) ** (np.float32(-5.0) - np.arange(4, dtype=np.float32))).astype(np.float32)
    pos = np.arange(128, dtype=np.float32)
    diff = pos[:, None] - pos[None, :]
    dec = np.where(diff >= 0, np.exp(lg[:, None, None] * diff), 0.0).astype(np.float32)
    c["decayT"] = np.ascontiguousarray(dec.transpose(0, 2, 1) * np.float32(0.125)).astype(np.float32)
    xi = np.exp(lg[:, None] * (pos + 1.0)).astype(np.float32)
    c["xi"] = np.ascontiguousarray(np.broadcast_to(np.tile(xi, (1, 4))[:, None, :], (4, 64, 512))).astype(np.float32)
    zeta = np.exp(lg[:, None] * (127.0 - pos)).astype(np.float32) * np.float32(0.125)
    c["zeta"] = np.ascontiguousarray(zeta.T).astype(np.float32)
    gam = [float(np.exp(lg[h] * np.float32(128.0))) for h in range(4)]
    return c, gam


VSTR = 27
NV = DEPTH * VSTR + 8


def pack_vecs(inp):
    v = np.zeros((128, NV), np.float32)
    for l in range(DEPTH):
        b = l * VSTR
        v[:, b + 0:b + 8] = inp["attn_norm"][l].reshape(8, 128).T
        v[:, b + 8:b + 16] = inp["mlp_norm"][l].reshape(8, 128).T
        v[:, b + 16:b + 18] = inp["mla_q_norm"][l].reshape(2, 128).T
        v[:, b + 18:b + 19] = inp["mla_kv_norm"][l].reshape(1, 128).T
        v[:, b + 19:b + 23] = inp["ret_gn_w"][l].reshape(4, 128).T
        v[:, b + 23:b + 27] = inp["ret_gn_b"][l].reshape(4, 128).T
    v[:, DEPTH * VSTR:DEPTH * VSTR + 8] = inp["final_norm"].reshape(8, 128).T
    return v


class KB:
    SB_BASE = 20480
    SB_TOP = 229376

    def __init__(self, n_layers=DEPTH, debug=False, phases=None):
        self.nc = nc = bass.Bass("TRN2", target_bir_lowering=False)
        self.S = Sched(nc)
        self.debug = debug
        self.n_layers = n_layers
        self.phases = phases
        self.top = self.SB_BASE
        self.uid = 0
        L = DEPTH
        inp = lambda name, shape: nc.dram_tensor(name, shape, F32, kind="ExternalInput").ap()
        self.x = inp("x", [SEQ, DM])
        self.w_in = inp("w_in", [L, DM, 6560])
        self.w_q_up = inp("mla_w_q_up", [L, 256, 768])
        self.w_kv_up = inp("mla_w_kv_up", [L, 128, 1024])
        self.w_b = [inp("w_branch_mla", [L, 512, DM]), inp("w_branch_moba", [L, 512, DM]), inp("w_branch_ret", [L, 512, DM])]
        self.w_out = inp("w_out", [L, DM, DM])
        self.w_up = inp("w_mlp_up", [L, DM, 4096])
        self.w_down = inp("w_mlp_down", [L, 4096, DM])
        self.vecs_d = inp("vecs", [128, NV])
        self.c_identF = inp("identF", [128, 128])
        self.c_rope_mla = inp("rope_mla", [2, 32, SEQ])
        self.c_rope_moba = inp("rope_moba", [2, 128, SEQ])
        self.c_rope_ret = inp("rope_ret", [2, 128, SEQ])
        self.c_mask01 = inp("mask01", [128, 128])
        self.c_onehot = inp("onehot", [16, SEQ])
        self.c_decayT = inp("decayT", [4, 128, 128])
        self.c_xi = inp("xi", [4, 64, 512])
        self.c_zeta = inp("zeta", [128, 4])
        self.out = nc.dram_tensor("out", [SEQ, DM], F32, kind="ExternalOutput").ap()
        kind = "ExternalOutput" if debug else "Internal"
        scr = lambda name, shape, dt: nc.dram_tensor(name, shape, dt, kind=kind).ap()
        self.XT = scr("XT", [8, 128, SEQ], F32)
        self.HT = scr("HT", [8, 128, SEQ], BF16)
        self.CQ = scr("CQ", [2, 128, SEQ], BF16)
        self.CKV = scr("CKV", [128, SEQ], BF16)
        self.KPE = scr("KPE", [32, SEQ], BF16)
        self.MQ = scr("MQ", [8, 64, SEQ], BF16)
        self.MK = scr("MK", [8, 64, SEQ], BF16)
        self.MV = scr("MV", [SEQ, 512], BF16)
        self.RQ = scr("RQ", [4, 64, SEQ], BF16)
        self.RK = scr("RK", [4, 64, SEQ], BF16)
        self.RV = scr("RV", [SEQ, 512], BF16)
        self.RG = scr("RG", [4, 128, SEQ], F32)
        self.OB = [scr("OMLA", [4, 128, SEQ], BF16), scr("OMOBA", [4, 128, SEQ], BF16), scr("ORET", [4, 128, SEQ], BF16)]
        self.ACTS = scr("ACTS", [32, 128, SEQ], BF16)
        self.dbuf = {}
        self.psF = nc.alloc_psum_tensor("psF", [128, 7, 512], F32)
        self.psB = nc.alloc_psum_tensor("psB", [128, 1024], BF16)
        self.pbuf = [Buf(f"ps{i}") for i in range(7)]
        self.pbufB = Buf("psB")
        self.bank_i = 0
        _, self.gam = make_consts()
        self.identF = self.sb([128, 128], F32)
        self.identB = self.sb([128, 128], BF16)
        self.onesB = self.sb([128, 128], BF16)
        self.mask01 = self.sb([128, 128], BF16)
        self.vecs = self.sb([128, NV], F32)
        self.bconst = Buf("const")
        self.dma("sp", self.identF[:], self.c_identF, w=[self.bconst])
        self.dma("pool", self.identB[:], self.c_identF, w=[self.bconst])
        self.dma("pool", self.mask01[:], self.c_mask01, w=[self.bconst])
        self.dma("sp", self.vecs[:], self.vecs_d, w=[self.bconst])
        self.I("pool", "memset", [], [self.bconst], ap=self.onesB[:], constant=1.0)
        self.maskneg = self.sb([128, 128], BF16)
        self.I("pool", "tensor_scalar", [self.bconst], [self.bconst], out=self.maskneg[:], in0=self.mask01[:], scalar1=-1.0, scalar2=-NEGB,
               op0=ALU.add, op1=ALU.mult)
        self.phase_base = self.top

    def sb(self, shape, dtype, name=None):
        sz = int(np.prod(shape[1:])) * (4 if dtype == F32 else 2)
        sz = (sz + 63) // 64 * 64
        self.uid += 1
        t = self.nc.alloc_sbuf_tensor_at(name or f"t{self.uid}", list(shape), dtype, offset=self.top)
        self.top += sz
        assert self.top <= self.SB_TOP, f"SBUF overflow {self.top}"
        return t

    def sbr(self, n, shape, dtype):
        return Ring([(self.sb(shape, dtype), Buf()) for _ in range(n)])

    def begin_phase(self):
        self.S.barrier()
        self.top = self.phase_base

    def I(self, eng, meth, r, w, **kw):
        return self.S.op(eng, lambda e: getattr(e, meth)(**kw), r, w)

    def dma(self, eng, out, in_, r=(), w=()):
        return self.S.op(eng, lambda e: e.dma_start(out=out, in_=in_), r, w, dma=True)

    def bank(self):
        i = self.bank_i % 7
        self.bank_i += 1
        return self.psF[:, i, :], self.pbuf[i]

    def db(self, name, t=None):
        k = (name, t)
        if k not in self.dbuf:
            self.dbuf[k] = Buf(f"{name}{t}")
        return self.dbuf[k]

    def dball(self, name):
        return [self.db(name, t) for t in range(NT)]

    def vcol(self, l, off, n=1):
        b = l * VSTR + off
        return self.vecs[:, b:b + n]

    def evac_eng(self):
        self._ev = getattr(self, "_ev", 0) + 1
        return "act" if self._ev % 2 else "dve"

    def copy(self, eng, out, in_, r, w):
        if eng == "act":
            return self.I("act", "copy", r, w, out=out, in_=in_)
        return self.I(eng, "tensor_copy", r, w, out=out, in_=in_)

    def rmsnorm_a(self, xaps, rb, sq, bsq, nparts=128):
        for c in range(len(xaps)):
            self.I("act", "activation", rb, [bsq], out=sq[0:nparts, c, :], in_=xaps[c], func=AF.Square)

    def rmsnorm_b(self, xaps, gaps, nfeat, eps, outaps, rb, wb, sq, bsq, rs, brs, nparts=128):
        n = len(xaps)
        ps, bps = self.bank()
        for c in range(n):
            self.I("pe", "matmul", [bsq, self.bconst], [bps], out=ps, lhsT=self.onesB[0:nparts, :], rhs=sq[0:nparts, c, :],
                   start=(c == 0), stop=(c == n - 1))
        self.I("act", "activation", [bps], [brs], out=rs[:], in_=ps, func=AF.Sqrt, scale=1.0 / nfeat, bias=float(eps))
        self.I("dve", "reciprocal", [brs], [brs], out=rs[:], in_=rs[:])
        for c in range(n):
            self.I("dve", "scalar_tensor_tensor", list(rb) + [brs, self.bconst], wb, out=outaps[c], in0=xaps[c], scalar=gaps[c],
                   in1=rs[0:nparts, :], op0=ALU.mult, op1=ALU.mult)

    def rmsnorm(self, xaps, gaps, nfeat, eps, outaps, rb, wb, sq, bsq, rs, brs, nparts=128):
        self.rmsnorm_a(xaps, rb, sq, bsq, nparts)
        self.rmsnorm_b(xaps, gaps, nfeat, eps, outaps, rb, wb, sq, bsq, rs, brs, nparts)

    def p_transpose_in(self):
        self.begin_phase()
        xin = self.sbr(2, [128, DM], F32)
        xo = self.sbr(2, [128, 8, TT], F32)
        for t in range(NT):
            xot, bxo = xo.next()
            tiles = []
            for s in range(4):
                xt, bx = xin.next()
                r0 = t * TT + s * 128
                self.dma("sp", xt[:], self.x[r0:r0 + 128, :], w=[bx])
                for g in range(2):
                    ps, bps = self.bank()
                    for j in range(4):
                        c = g * 4 + j
                        self.I("pe", "transpose", [bx, self.bconst], [bps], out=ps[:, j * 128:(j + 1) * 128], in_=xt[:, c * 128:(c + 1) * 128],
                               identity=self.identF[:])
                    self.copy(self.evac_eng(), xot[:, g * 4:(g + 1) * 4, s * 128:(s + 1) * 128],
                              ps.rearrange("p (j n) -> p j n", n=128), [bps], [bxo])
            self.dma("sp", self.XT.rearrange("c p n -> p c n")[:, :, t * TT:(t + 1) * TT], xot[:], r=[bxo], w=[self.db("XT", t)])

    def p_inproj(self, l):
        self.begin_phase()
        NC1 = 5056
        W1 = self.sb([128, 8, NC1], BF16)
        bW = []
        win = self.w_in[l].rearrange("(c p) n -> p c n", p=128)

        segs = []

        def wl(d0, s0, n):
            b = Buf()
            self.dma("pool", W1[:, :, d0:d0 + n], win[:, :, s0:s0 + n], w=[b])
            bW.append(b)
            segs.append((d0, d0 + n, [b]))

        def wbufs(col, M):
            out = []
            for (c0, c1, bl) in segs:
                if c0 < col + M and col < c1:
                    out.extend(bl)
            return out

        C_MQ, C_MQR, C_MK, C_MKR, C_MV = 448, 960, 1472, 1984, 2496
        C_RQ, C_RQR, C_RK, C_RKR, C_RV, C_RG = 3008, 3264, 3520, 3776, 4032, 4544
        bz = Buf()
        for c in range(8):
            self.I("pool", "memset", [], [bz], ap=W1[:, c, C_MQR:C_MQR + 512], constant=0.0)
            self.I("pool", "memset", [], [bz], ap=W1[:, c, C_MKR:C_MKR + 512], constant=0.0)
        bW.append(bz)

        def wrot(dst, src, H, half):
            bl = [bz]
            for c in range(8):
                dstv = W1[:, c, dst:dst + H * 64].rearrange("p (h e) -> p h e", e=64)
                srcv = win[:, c, src:src + H * 64].rearrange("p (h d) -> p h d", d=64)
                b = Buf()
                self.dma("pool", dstv[:, :, 0:half], srcv[:, :, half:2 * half], r=[bz], w=[b])
                self.dma("pool", dstv[:, :, half:2 * half], srcv[:, :, 0:half], r=[bz], w=[b])
                bW.append(b)
                bl.append(b)
            segs.append((dst, dst + H * 64, bl))

        wl(0, 0, 416)
        wl(416, 400, 16)
        wl(432, 384, 16)
        wl(C_MQ, 416, 512)
        wrot(C_MQR, 416, 8, 8)
        wl(C_MK, 928, 512)
        wrot(C_MKR, 928, 8, 8)
        wl(C_RQ, 1952, 256)
        wrot(C_RQR, 1952, 4, 32)
        wl(C_RK, 2208, 256)
        wrot(C_RKR, 2208, 4, 32)
        wl(C_RG, 2976, 512)
        wl(C_MV, 1440, 512)
        wl(C_RV, 2464, 512)
        xTr = self.sbr(2, [128, 8, TT], F32)
        hTr = self.sbr(2, [128, 8, TT], BF16)
        sq = self.sb([128, 8, TT], BF16)
        bsq = Buf()
        rsr = self.sbr(3, [128, TT], F32)
        cqf = self.sb([128, 3, TT], F32)
        bcqf = Buf()
        cqn = self.sb([128, 3, TT], BF16)
        bcqn = Buf()
        sq2 = self.sb([128, 3, TT], BF16)
        bsq2 = [Buf(), Buf()]
        tabk = self.sb([32, 2, TT], F32)
        tabm = self.sb([128, 2, TT], F32)
        tabr = self.sb([128, 2, TT], F32)
        btab = Buf()
        t1r = self.sbr(2, [128, TT], F32)
        t2r = self.sbr(2, [128, TT], F32)
        kpe = self.sb([32, TT], BF16)
        bkpe = Buf()
        stq = self.sb([128, 4, TT], BF16)
        stk = self.sb([128, 4, TT], BF16)
        strq = self.sb([128, 2, TT], BF16)
        strk = self.sb([128, 2, TT], BF16)
        stg = self.sb([128, 4, TT], F32)
        stv = self.sbr(2, [128, 4, 512], BF16)
        bst = {k: Buf() for k in ("q", "k", "rq", "rk", "g")}

        def load_x(t):
            tok = slice(t * TT, (t + 1) * TT)
            xT, bx = xTr.next()
            hT, bh = hTr.next()
            self.dma("sp", xT[:], self.XT.rearrange("c p n -> p c n")[:, :, tok], r=[self.db("XT", t)], w=[bx])
            return xT, bx, hT, bh

        def xnorm_a(st_):
            xT, bx, hT, bh = st_
            self.rmsnorm_a([xT[:, c, :] for c in range(8)], [bx], sq, bsq)

        def xnorm_b(st_, t):
            xT, bx, hT, bh = st_
            rs, brs = rsr.next()
            self.rmsnorm_b([xT[:, c, :] for c in range(8)], [self.vcol(l, c) for c in range(8)], DM, EPS,
                           [hT[:, c, :] for c in range(8)], [bx], [bh], sq, bsq, rs, brs)
            self.dma("sp", self.HT.rearrange("c p n -> p c n")[:, :, t * TT:(t + 1) * TT], hT[:], r=[bh], w=[self.db("HT", t)])

        cur = load_x(0)
        xnorm_a(cur)
        xnorm_b(cur, 0)
        for t in range(NT):
            tok = slice(t * TT, (t + 1) * TT)
            xT, bx, hT, bh = cur
            nxt = load_x(t + 1) if t + 1 < NT else None
            self.dma("sp", tabk[:], self.c_rope_mla.rearrange("a p n -> p a n")[:, :, tok], w=[btab])
            self.dma("sp", tabm[:], self.c_rope_moba.rearrange("a p n -> p a n")[:, :, tok], w=[btab])
            self.dma("sp", tabr[:], self.c_rope_ret.rearrange("a p n -> p a n")[:, :, tok], w=[btab])

            def mm(M, col):
                ps, bps = self.bank()
                wb_ = wbufs(col, M)
                for c in range(8):
                    self.I("pe", "matmul", [bh] + wb_, [bps], out=ps[0:M, :], lhsT=W1[:, c, col:col + M], rhs=hT[:, c, :],
                           start=(c == 0), stop=(c == 7))
                return ps, bps

            def rope(psm, bpm, psr, bpr, n, tab, outap, wb):
                t1, b1 = t1r.next()
                t2, b2 = t2r.next()
                self.I("dve", "tensor_tensor", [bpm, btab], [b1], out=t1[0:n, :], in0=psm[0:n, :], in1=tab[0:n, 0, :], op=ALU.mult)
                self.I("dve", "tensor_tensor", [bpr, btab], [b2], out=t2[0:n, :], in0=psr[0:n, :], in1=tab[0:n, 1, :], op=ALU.mult)
                self.I("pool", "tensor_tensor", [b1, b2], wb, out=outap, in0=t1[0:n, :], in1=t2[0:n, :], op=ALU.add)

            for j in range(3):
                ps, bps = mm(128, j * 128)
                self.copy(self.evac_eng(), cqf[:, j, :], ps, [bps], [bcqf])
            self.rmsnorm_a([cqf[:, c, :] for c in range(2)], [bcqf], sq2[:, 0:2, :], bsq2[0])
            self.rmsnorm_a([cqf[:, 2, :]], [bcqf], sq2[:, 2:3, :], bsq2[1])
            psm, bpm = mm(32, 384)
            psr, bpr = mm(32, 416)
            rope(psm, bpm, psr, bpr, 32, tabk, kpe[:], [bkpe])
            self.dma("sp", self.KPE[:, tok], kpe[:], r=[bkpe], w=[self.db("KPE", t)])
            if nxt is not None:
                xnorm_a(nxt)
            for (st, key, base, rbase, dst) in ((stq, "q", C_MQ, C_MQR, self.MQ), (stk, "k", C_MK, C_MKR, self.MK)):
                for j in range(4):
                    psm, bpm = mm(128, base + j * 128)
                    psr, bpr = mm(128, rbase + j * 128)
                    rope(psm, bpm, psr, bpr, 128, tabm, st[:, j, :], [bst[key]])
                self.dma("sp", dst.rearrange("(j two) p n -> (two p) j n", two=2)[:, :, tok], st[:], r=[bst[key]], w=[self.db("M" + key, t)])
            rs2, brs2 = rsr.next()
            self.rmsnorm_b([cqf[:, c, :] for c in range(2)], [self.vcol(l, 16 + c) for c in range(2)], 256, EPS,
                           [cqn[:, c, :] for c in range(2)], [bcqf], [bcqn], sq2[:, 0:2, :], bsq2[0], rs2, brs2)
            self.dma("sp", self.CQ.rearrange("c p n -> p c n")[:, :, tok], cqn[:, 0:2, :], r=[bcqn], w=[self.db("CQ", t)])
            rs3, brs3 = rsr.next()
            self.rmsnorm_b([cqf[:, 2, :]], [self.vcol(l, 18)], 128, EPS, [cqn[:, 2, :]], [bcqf], [bcqn], sq2[:, 2:3, :], bsq2[1], rs3, brs3)
            self.dma("sp", self.CKV[:, tok], cqn[:, 2, :], r=[bcqn], w=[self.db("CKV", t)])
            for (st, key, base, rbase, dst) in ((strq, "rq", C_RQ, C_RQR, self.RQ), (strk, "rk", C_RK, C_RKR, self.RK)):
                for j in range(2):
                    psm, bpm = mm(128, base + j * 128)
                    psr, bpr = mm(128, rbase + j * 128)
                    rope(psm, bpm, psr, bpr, 128, tabr, st[:, j, :], [bst[key]])
                self.dma("sp", dst.rearrange("(j two) p n -> (two p) j n", two=2)[:, :, tok], st[:], r=[bst[key]], w=[self.db(key, t)])
            for j in range(4):
                ps, bps = mm(128, C_RG + j * 128)
                self.I("act", "activation", [bps], [bst["g"]], out=stg[:, j, :], in_=ps, func=AF.Silu)
            self.dma("sp", self.RG.rearrange("c p n -> p c n")[:, :, tok], stg[:], r=[bst["g"]], w=[self.db("RG", t)])
            if nxt is not None:
                xnorm_b(nxt, t + 1)
            for (col, dst, key) in ((C_MV, self.MV, "MV"), (C_RV, self.RV, "RV")):
                sv, bsv = stv.next()
                for s_ in range(4):
                    ps, bps = self.bank()
                    for c in range(8):
                        self.I("pe", "matmul", [bh] + wbufs(col, 512), [bps], out=ps, lhsT=hT[:, c, s_ * 128:(s_ + 1) * 128], rhs=W1[:, c, col:col + 512],
                               start=(c == 0), stop=(c == 7))
                    self.copy(self.evac_eng(), sv[:, s_, :], ps, [bps], [bsv])
                self.dma("sp", dst[t * TT:(t + 1) * TT, :].rearrange("(s p) n -> p s n", p=128), sv[:], r=[bsv], w=[self.db(key, t)])
            cur = nxt

    def run_attn(self, tiles, scale, pss_r, pT_r, pso_r, LA=2):
        steps = [(i, kt) for i, T in enumerate(tiles) for kt in range(4 * T["t"] + 4)]
        info = {}

        def emit_s(si):
            i, kt = steps[si]
            T = tiles[i]
            t = T["t"]
            qT, qbufs = T["get_q"]()
            j = kt - 4 * t
            q0 = max(0, j) * 128
            N = TT - q0
            (pi, bpi) = pss_r.next()
            pss = self.psF[:, pi, :]
            self.I("pe", "matmul", [T["bK"]] + list(qbufs), [bpi], out=pss[:, 0:N], lhsT=T["K"][:, kt * 128:(kt + 1) * 128], rhs=qT[:, q0:TT],
                   start=True, stop=(j < 0))
            if j >= 0:
                self.I("pe", "matmul", [self.bconst], [bpi], out=pss[:, 0:128], lhsT=self.identB[:], rhs=self.maskneg[:], start=False, stop=True)
            info[si] = (pss, bpi, q0, N)

        for si in range(min(LA, len(steps))):
            emit_s(si)
        for si, (i, kt) in enumerate(steps):
            T = tiles[i]
            nkt = 4 * T["t"] + 4
            if kt == 0:
                if T.get("start_hook") is not None:
                    T["start_hook"]()
                po, bpso = pso_r.next()
                T["pso"] = (self.psF[:, po, :], bpso)
            if si + LA < len(steps):
                emit_s(si + LA)
            if kt == 1 and T.get("mid_hook") is not None:
                T["mid_hook"]()
            pss, bpi, q0, N = info.pop(si)
            pso, bpso = T["pso"]
            pT, bpT = pT_r.next()
            self.I("act", "activation", [bpi], [bpT], out=pT[:, 0:N], in_=pss[:, 0:N], func=AF.Exp, scale=float(scale))
            self.I("pe", "matmul", [T["bV"], bpT], [bpso], out=pso[:, q0:TT], lhsT=T["V"][:, kt, :], rhs=pT[:, 0:N],
                   start=(kt == 0), stop=(kt == nkt - 1))
            if kt == nkt - 1:
                T["finish"](pso, bpso)

    def attn_finish(self, pso, bpso, rec_r, ost_r, dst_ap, dbuf):
        rec, brec = rec_r.next()
        ost, bost = ost_r.next()
        self.I("dve", "reciprocal", [bpso], [brec], out=rec[0:64, :], in_=pso[64:128, :])
        self.I("dve", "tensor_tensor", [bpso, brec], [bost], out=ost[0:64, :], in0=pso[0:64, :], in1=rec[0:64, :], op=ALU.mult)
        self.dma("sp", dst_ap, ost[0:64, :], r=[bost], w=[dbuf])

    def p_mla(self, l):
        self.begin_phase()
        CQs = self.sb([128, 2, SEQ], BF16)
        CKVs = self.sb([128, SEQ], BF16)
        bin_ = Buf()
        self.dma("sp", CQs[:], self.CQ.rearrange("c p n -> p c n"), r=self.dball("CQ"), w=[bin_])
        self.dma("sp", CKVs[:], self.CKV, r=self.dball("CKV"), w=[bin_])
        Wq = self.sb([128, 2, 8, 96], BF16)
        Wqr = self.sb([128, 2, 8, 96], BF16)
        Wkv = self.sb([128, 8, 128], BF16)
        bw = Buf()
        wq = self.w_q_up[l].rearrange("(c p) (h d) -> p c h d", p=128, d=96)
        self.I("pool", "memset", [], [bw], ap=Wqr[:], constant=0.0)
        for c in range(2):
            self.dma("pool", Wq[:, c, :, :], wq[:, c, :, :], w=[bw])
            self.dma("pool", Wqr[:, c, :, 64:80], wq[:, c, :, 80:96], w=[bw])
            self.dma("pool", Wqr[:, c, :, 80:96], wq[:, c, :, 64:80], w=[bw])
        self.dma("pool", Wkv[:], self.w_kv_up[l].rearrange("p (h d) -> p h d", d=128), w=[bw])
        tab = self.sb([96, 2, SEQ], F32)
        btab = Buf()
        self.dma("sp", tab[64:96, :, :], self.c_rope_mla.rearrange("a p n -> p a n"), w=[btab])
        Kr = self.sbr(2, [128, SEQ], BF16)
        Vr = self.sbr(2, [128, 32, 128], BF16)
        for (Kb, bK) in Kr.items:
            self.I("pool", "memset", [], [bK], ap=Kb[96:128, :], constant=0.0)
            self.dma("sp", Kb[64:96, :], self.KPE, r=self.dball("KPE"), w=[bK])
        for (Vb, bV) in Vr.items:
            self.I("pool", "memset", [], [bV], ap=Vb[:, :, 64:128], constant=1.0)
        qTr = self.sbr(3, [128, TT], BF16)
        for (q_, bq_) in qTr.items:
            self.I("pool", "memset", [], [bq_], ap=q_[96:128, :], constant=0.0)
        t1r = self.sbr(2, [96, TT], F32)
        t2r = self.sbr(2, [96, TT], F32)
        pT_r = self.sbr(4, [128, TT], BF16)
        rec_r = self.sbr(2, [64, TT], F32)
        ost_r = self.sbr(2, [64, TT], BF16)
        pss_r = Ring([(0, self.pbuf[0]), (1, self.pbuf[1]), (2, self.pbuf[2])])
        pso_r = Ring([(3, self.pbuf[3]), (4, self.pbuf[4])])
        aux_r = Ring([(5, self.pbuf[5]), (6, self.pbuf[6])])
        scale = 96 ** -0.5
        KV = {}

        def build_kv(h):
            Kb, bK = Kr.items[h % 2]
            Vb, bV = Vr.items[h % 2]
            KV[h] = (Kb, bK, Vb, bV)
            for t in range(NT):
                pi, bpi = aux_r.next()
                ps = self.psF[:, pi, :]
                self.I("pe", "matmul", [bin_, bw], [bpi], out=ps[0:64, :], lhsT=Wkv[:, h, 0:64], rhs=CKVs[:, t * TT:(t + 1) * TT], start=True, stop=True)
                self.copy(self.evac_eng(), Kb[0:64, t * TT:(t + 1) * TT], ps[0:64, :], [bpi], [bK])
            for g in range(4):
                pi, bpi = aux_r.next()
                ps = self.psF[:, pi, :]
                for j in range(8):
                    kt = g * 8 + j
                    self.I("pe", "matmul", [bin_, bw], [bpi], out=ps[:, j * 64:(j + 1) * 64], lhsT=CKVs[:, kt * 128:(kt + 1) * 128],
                           rhs=Wkv[:, h, 64:128], start=True, stop=True)
                self.copy(self.evac_eng(), Vb[:, g * 8:(g + 1) * 8, 0:64], ps.rearrange("p (j d) -> p j d", d=64), [bpi], [bV])

        def build_q(h, t):
            tok = slice(t * TT, (t + 1) * TT)
            qT, bq = qTr.next()
            pi, bpm = aux_r.next()
            psm = self.psF[:, pi, :]
            pi2, bpr = aux_r.next()
            psr = self.psF[:, pi2, :]
            for c in range(2):
                self.I("pe", "matmul", [bin_, bw], [bpm], out=psm[0:96, :], lhsT=Wq[:, c, h, :], rhs=CQs[:, c, tok], start=(c == 0), stop=(c == 1))
            for c in range(2):
                self.I("pe", "matmul", [bin_, bw], [bpr], out=psr[0:96, :], lhsT=Wqr[:, c, h, :], rhs=CQs[:, c, tok], start=(c == 0), stop=(c == 1))
            self.I("act", "copy", [bpm], [bq], out=qT[0:64, :], in_=psm[0:64, :])
            t1, b1 = t1r.next()
            t2, b2 = t2r.next()
            self.I("dve", "tensor_tensor", [bpm, btab], [b1], out=t1[64:96, :], in0=psm[64:96, :], in1=tab[64:96, 0, tok], op=ALU.mult)
            self.I("dve", "tensor_tensor", [bpr, btab], [b2], out=t2[64:96, :], in0=psr[64:96, :], in1=tab[64:96, 1, tok], op=ALU.mult)
            self.I("pool", "tensor_tensor", [b1, b2], [bq], out=qT[64:96, :], in0=t1[64:96, :], in1=t2[64:96, :], op=ALU.add)
            return qT, bq

        items = [(h, t) for h in range(8) for t in range(NT)]
        build_kv(0)
        prepared = {items[0]: build_q(*items[0])}
        tiles = []
        for idx, (h, t) in enumerate(items):
            def start_hook(idx=idx):
                if idx + 1 < len(items):
                    hn, tn = items[idx + 1]
                    if tn == 0:
                        build_kv(hn)
                    prepared[(hn, tn)] = build_q(hn, tn)

            def get_q(h=h, t=t):
                qT, bq = prepared[(h, t)]
                return qT[:, :], [bq]

            def finish(pso, bpso, h=h, t=t):
                tok = slice(t * TT, (t + 1) * TT)
                self.attn_finish(pso, bpso, rec_r, ost_r, self.OB[0][h // 2, (h % 2) * 64:(h % 2) * 64 + 64, tok], self.db("OMLA", t))

            tiles.append(dict(t=t, h=h, get_q=get_q, start_hook=start_hook, finish=finish))
        class _KV(dict):
            pass
        for T in tiles:
            T["_h"] = T["h"]
        for T in tiles:
            hh = T["h"]
            T["K"] = Kr.items[hh % 2][0]
            T["bK"] = Kr.items[hh % 2][1]
            T["V"] = Vr.items[hh % 2][0]
            T["bV"] = Vr.items[hh % 2][1]
        self.run_attn(tiles, scale, pss_r, pT_r, pso_r)

    def p_moba(self, l):
        self.begin_phase()
        Qr = self.sbr(2, [128, SEQ], BF16)
        Kr = self.sbr(2, [128, SEQ], BF16)
        Vr = self.sbr(2, [128, 32, 128], BF16)
        for (Kb, bK) in Kr.items:
            self.I("pool", "memset", [], [bK], ap=Kb[64:128, :], constant=0.0)
            self.dma("pool", Kb[64:80, :], self.c_onehot, w=[bK])
        for (Qb, bQ) in Qr.items:
            self.I("pool", "memset", [], [bQ], ap=Qb[64:128, :], constant=0.0)
        for (Vb, bV) in Vr.items:
            self.I("pool", "memset", [], [bV], ap=Vb[:, :, 64:128], constant=1.0)
        km_r = [(self.sb([64, 16], F32), self.sb([64, 16], BF16), Buf()) for _ in range(2)]
        gate_r = self.sbr(4, [128, 16], F32)
        top_r = self.sbr(4, [128, 8], F32)
        mbp_r = self.sbr(4, [128, 4, 80], BF16)
        for (m, bm) in mbp_r.items:
            self.I("pool", "memset", [], [bm], ap=m[:], constant=0.0)
        pT_r = self.sbr(4, [128, TT], BF16)
        rec_r = self.sbr(2, [64, TT], F32)
        ost_r = self.sbr(2, [64, TT], BF16)
        pss_r = Ring([(0, self.pbuf[0]), (1, self.pbuf[1]), (2, self.pbuf[2])])
        pso_r = Ring([(3, self.pbuf[3]), (4, self.pbuf[4])])
        aux_r = Ring([(5, self.pbuf[5]), (6, self.pbuf[6])])
        bQm = [[Buf() for _ in range(NT)] for _ in range(2)]
        H = {}

        def load_head(h):
            Qb, bQ = Qr.items[h % 2]
            Kb, bK = Kr.items[h % 2]
            Vb, bV = Vr.items[h % 2]
            H[h] = (Qb, bQ, Kb, bK, Vb, bV)
            self.dma("sp", Qb[0:64, :], self.MQ[h], r=self.dball("Mq"), w=[bQ])
            self.dma("sp", Kb[0:64, :], self.MK[h], r=self.dball("Mk"), w=[bK])
            for g4 in range(4):
                self.dma("sp", Vb[:, g4 * 8:(g4 + 1) * 8, 0:64],
                         self.MV[g4 * 1024:(g4 + 1) * 1024, h * 64:(h + 1) * 64].rearrange("(k p) d -> p k d", p=128), r=self.dball("MV"), w=[bV])

        def kmean(h):
            Qb, bQ, Kb, bK, Vb, bV = H[h]
            kmf, kmb, bkm = km_r[h % 2]
            self.I("dve", "tensor_reduce", [bK], [bkm], out=kmf[:], in_=Kb[0:64, :].rearrange("p (n k) -> p n k", k=256), axis=AX.X, op=ALU.add)
            self.I("dve", "tensor_copy", [bkm], [bkm], out=kmb[:], in_=kmf[:])

        def prepare(h, t):
            Qb, bQ, Kb, bK, Vb, bV = H[h]
            kmf, kmb, bkm = km_r[h % 2]
            tok = slice(t * TT, (t + 1) * TT)
            mbp, bmb = mbp_r.next()
            pi, bpg = aux_r.next()
            psg = self.psF[:, pi, :]
            for s in range(4):
                g = 4 * t + s
                blk = g // 2
                if blk <= 3:
                    continue
                self.I("pe", "matmul", [bQ, bkm], [bpg], out=psg[:, s * 16:(s + 1) * 16], lhsT=Qb[0:64, g * 128:(g + 1) * 128], rhs=kmb[:],
                       start=True, stop=True)
            for s in range(4):
                g = 4 * t + s
                blk = g // 2
                if blk <= 3:
                    self.I("pool", "memset", [], [bmb], ap=mbp[:, s, 64:80], constant=0.0)
                    continue
                gt, bg = gate_r.next()
                tp, btp = top_r.next()
                self.I("dve", "memset", [], [bg], ap=gt[:], constant=-1e30)
                self.I("dve", "tensor_copy", [bpg], [bg], out=gt[:, 0:blk], in_=psg[:, s * 16:s * 16 + blk])
                self.I("dve", "max", [bg], [btp], out=tp[:], in_=gt[:])
                self.I("dve", "tensor_scalar", [bg, btp], [bmb], out=mbp[:, s, 64:80], in0=gt[:], scalar1=tp[:, 2:3], scalar2=NEGB,
                       op0=ALU.is_lt, op1=ALU.mult)
                self.I("dve", "memset", [], [bmb], ap=mbp[:, s, 64 + blk:80], constant=0.0)
            return mbp, bmb

        def prepare2(h, t, mbp, bmb):
            Qb, bQ, Kb, bK, Vb, bV = H[h]
            tok = slice(t * TT, (t + 1) * TT)
            for s in range(4):
                self.I("pe", "transpose", [bmb, self.bconst], [self.pbufB], out=self.psB[0:80, s * 128:(s + 1) * 128], in_=mbp[:, s, :],
                       identity=self.identB[:])
            self.I("act", "copy", [self.pbufB, bQ], [bQm[h % 2][t]], out=Qb[64:80, tok], in_=self.psB[64:80, 0:TT])

        items = [(h, t) for h in range(8) for t in range(NT)]
        load_head(0)
        load_head(1)
        kmean(0)
        prepare2(0, 0, *prepare(0, 0))
        gated = {}
        if len(items) > 1:
            gated[items[1]] = prepare(*items[1])
        tiles = []
        for idx, (h, t) in enumerate(items):
            def start_hook(idx=idx, h=h, t=t):
                if t == 1 and h + 1 < 8 and h >= 1:
                    load_head(h + 1)
                if idx + 2 < len(items):
                    hn, tn = items[idx + 2]
                    if tn == 0:
                        kmean(hn)
                    gated[(hn, tn)] = prepare(hn, tn)

            def mid_hook(idx=idx):
                if idx + 1 < len(items):
                    hn, tn = items[idx + 1]
                    prepare2(hn, tn, *gated.pop((hn, tn)))

            def get_q(h=h, t=t):
                Qb, bQ = Qr.items[h % 2]
                return Qb[:, t * TT:(t + 1) * TT], [bQ, bQm[h % 2][t]]

            def finish(pso, bpso, h=h, t=t):
                tok = slice(t * TT, (t + 1) * TT)
                self.attn_finish(pso, bpso, rec_r, ost_r, self.OB[1][h // 2, (h % 2) * 64:(h % 2) * 64 + 64, tok], self.db("OMOBA", t))

            tiles.append(dict(t=t, h=h, get_q=get_q, start_hook=start_hook, mid_hook=mid_hook, finish=finish,
                              K=Kr.items[h % 2][0], bK=Kr.items[h % 2][1], V=Vr.items[h % 2][0], bV=Vr.items[h % 2][1]))
        self.run_attn(tiles, 0.125, pss_r, pT_r, pso_r)

    def p_ret(self, l):
        self.begin_phase()
        Qr = self.sbr(2, [64, SEQ], BF16)
        Kr = self.sbr(2, [64, SEQ], BF16)
        Vr = self.sbr(2, [128, 32, 128], BF16)
        dec = self.sb([128, 4, 128], F32)
        xi = self.sb([64, 4, 512], F32)
        zeta = self.sb([128, 4], F32)
        bc = Buf()
        self.dma("sp", dec[:], self.c_decayT.rearrange("h j i -> j h i"), w=[bc])
        self.dma("sp", xi[:], self.c_xi.rearrange("h p n -> p h n"), w=[bc])
        self.dma("sp", zeta[:], self.c_zeta, w=[bc])
        qx_r = self.sbr(2, [64, TT], BF16)
        kz_r = self.sbr(4, [128, 64], BF16)
        sc_r = self.sbr(4, [128, 128], BF16)
        KV_r = self.sbr(2, [64, 32, 128], F32)
        Rall_r = self.sbr(2, [64, 32, 128], BF16)
        Rf = self.sb([64, 128], F32)
        bRf = Buf()
        G_r = self.sbr(2, [128, TT], F32)
        of_r = self.sbr(2, [128, TT], F32)
        ob_r = self.sbr(2, [128, 2, TT], BF16)
        m_r = self.sbr(2, [128, TT], F32)
        v_r = self.sbr(2, [128, TT], F32)
        y_r = self.sbr(2, [128, TT], BF16)
        po_r = Ring([(0, self.pbuf[0]), (1, self.pbuf[1])])
        ps_r = Ring([(2, self.pbuf[2]), (3, self.pbuf[3])])
        pk_r = Ring([(4, self.pbuf[4]), (5, self.pbuf[5])])
        HB = {}

        def load_head(h):
            Qb, bQ = Qr.items[h % 2]
            Kb, bK = Kr.items[h % 2]
            Vb, bV = Vr.items[h % 2]
            HB[h] = (Qb, bQ, Kb, bK, Vb, bV)
            self.dma("sp", Qb[:], self.RQ[h], r=self.dball("rq"), w=[bQ])
            self.dma("sp", Kb[:], self.RK[h], r=self.dball("rk"), w=[bK])
            for g4 in range(4):
                self.dma("sp", Vb[:, g4 * 8:(g4 + 1) * 8, :],
                         self.RV[g4 * 1024:(g4 + 1) * 1024, h * 128:(h + 1) * 128].rearrange("(k p) d -> p k d", p=128), r=self.dball("RV"), w=[bV])

        def stage1(h):
            Qb, bQ, Kb, bK, Vb, bV = HB[h]
            KVs, bKV = KV_r.items[h % 2]
            for g in range(8):
                pi_, bpk = pk_r.next()
                pk = self.psF[:, pi_, :]
                for j in range(4):
                    n = g * 4 + j
                    ch = slice(n * 128, (n + 1) * 128)
                    self.I("pe", "transpose", [bK, self.bconst], [self.pbufB], out=self.psB[:, j * 64:(j + 1) * 64], in_=Kb[:, ch],
                           identity=self.identB[0:64, 0:64])
                kzs = []
                for j in range(4):
                    kz, bkz = kz_r.next()
                    self.I("act", "activation", [self.pbufB, bc], [bkz], out=kz[:], in_=self.psB[:, j * 64:(j + 1) * 64], func=AF.Copy,
                           scale=zeta[:, h:h + 1])
                    kzs.append((kz, bkz))
                for j in range(4):
                    n = g * 4 + j
                    kz, bkz = kzs[j]
                    self.I("pe", "matmul", [bkz, bV], [bpk], out=pk[0:64, j * 128:(j + 1) * 128], lhsT=kz[:], rhs=Vb[:, n, :], start=True, stop=True)
                self.copy("dve" if g % 2 else "act", KVs[:, g * 4:(g + 1) * 4, :], pk[0:64, :].rearrange("p (j e) -> p j e", e=128), [bpk], [bKV])

        def stage2(h):
            KVs, bKV = KV_r.items[h % 2]
            Rall, bR = Rall_r.items[h % 2]
            self.I("pool", "memset", [], [bRf], ap=Rf[:], constant=0.0)
            self.I("pool", "memset", [], [bR], ap=Rall[:, 0, :], constant=0.0)
            for n in range(31):
                self.I("dve", "scalar_tensor_tensor", [bRf, bKV], [bRf], out=Rf[:], in0=Rf[:], scalar=float(self.gam[h]), in1=KVs[:, n, :],
                       op0=ALU.mult, op1=ALU.add)
                self.I("dve", "tensor_copy", [bRf], [bR], out=Rall[:, n + 1, :], in_=Rf[:])

        def stage3(h):
            Qb, bQ, Kb, bK, Vb, bV = HB[h]
            Rall, bR = Rall_r.items[h % 2]
            for t in range(NT):
                tok = slice(t * TT, (t + 1) * TT)
                G, bG = G_r.next()
                self.dma("sp", G[:], self.RG[h, :, tok], r=[self.db("RG", t)], w=[bG])
                qx, bqx = qx_r.next()
                self.I("pool", "tensor_tensor", [bQ, bc], [bqx], out=qx[:], in0=Qb[:, tok], in1=xi[:, h, :], op=ALU.mult)
                pi_, bpo = po_r.next()
                po = self.psF[:, pi_, :]
                pi_, bps = ps_r.next()
                ps = self.psF[:, pi_, :]
                for s in range(4):
                    n = 4 * t + s
                    ch = slice(n * 128, (n + 1) * 128)
                    self.I("pe", "matmul", [bK, bQ], [bps], out=ps[:, s * 128:(s + 1) * 128], lhsT=Kb[:, ch], rhs=Qb[:, ch], start=True, stop=True)
                scs = []
                for s in range(4):
                    sc, bsc = sc_r.next()
                    self.I("dve", "tensor_tensor", [bps, bc], [bsc], out=sc[:], in0=ps[:, s * 128:(s + 1) * 128], in1=dec[:, h, :], op=ALU.mult)
                    scs.append((sc, bsc))
                for s in range(4):
                    n = 4 * t + s
                    sc, bsc = scs[s]
                    self.I("pe", "matmul", [bV, bsc], [bpo], out=po[:, s * 128:(s + 1) * 128], lhsT=Vb[:, n, :], rhs=sc[:], start=True, stop=False)
                    self.I("pe", "matmul", [bR, bqx], [bpo], out=po[:, s * 128:(s + 1) * 128], lhsT=Rall[:, n, :], rhs=qx[:, s * 128:(s + 1) * 128],
                           start=False, stop=True)
                of, bof = of_r.next()
                ob, bob = ob_r.next()
                self.I("act", "copy", [bpo], [bof], out=of[:], in_=po)
                self.I("pool", "tensor_copy", [bof], [bob], out=ob[:, 0, :], in_=of[:])
                self.I("act", "activation", [bof], [bob], out=ob[:, 1, :], in_=of[:], func=AF.Square)
                pm, bpm = self.psF[:, 6, :], self.pbuf[6]
                self.I("pe", "matmul", [bob, self.bconst], [bpm], out=pm, lhsT=self.onesB[:], rhs=ob[:, 0, :], start=True, stop=True)
                pi_, bpq = ps_r.next()
                pq = self.psF[:, pi_, :]
                self.I("pe", "matmul", [bob, self.bconst], [bpq], out=pq, lhsT=self.onesB[:], rhs=ob[:, 1, :], start=True, stop=True)
                m, bm = m_r.next()
                v, bv = v_r.next()
                self.I("act", "mul", [bpm], [bm], out=m[:], in_=pm, mul=1.0 / 128)
                self.I("dve", "tensor_tensor", [bm], [bv], out=v[:], in0=m[:], in1=m[:], op=ALU.mult)
                self.I("dve", "scalar_tensor_tensor", [bpq, bv], [bv], out=v[:], in0=pq, scalar=1.0 / 128, in1=v[:], op0=ALU.mult, op1=ALU.subtract)
                self.I("act", "activation", [bv], [bv], out=v[:], in_=v[:], func=AF.Sqrt, scale=1.0, bias=float(GN_EPS))
                self.I("dve", "reciprocal", [bv], [bv], out=v[:], in_=v[:])
                self.I("pool", "tensor_tensor", [bof, bm], [bof], out=of[:], in0=of[:], in1=m[:], op=ALU.subtract)
                self.I("dve", "tensor_tensor", [bof, bv], [bof], out=of[:], in0=of[:], in1=v[:], op=ALU.mult)
                self.I("dve", "tensor_scalar", [bof, self.bconst], [bof], out=of[:], in0=of[:], scalar1=self.vcol(l, 19 + h), scalar2=self.vcol(l, 23 + h),
                       op0=ALU.mult, op1=ALU.add)
                y, by = y_r.next()
                self.I("pool", "tensor_tensor", [bof, bG], [by], out=y[:], in0=of[:], in1=G[:], op=ALU.mult)
                self.dma("sp", self.OB[2][h, :, tok], y[:], r=[by], w=[self.db("ORET", t)])

        load_head(0)
        load_head(1)
        stage1(0)
        stage2(0)
        for h in range(4):
            if h + 1 < 4:
                stage1(h + 1)
                stage2(h + 1)
            stage3(h)
            if h + 2 < 4:
                load_head(h + 2)

    def p_merge(self, l):
        self.begin_phase()
        Wg = self.sb([128, 8, 3072], BF16)
        Wb = self.sb([128, 3, 4, DM], BF16)
        Wo = self.sb([128, 8, DM], BF16)
        bw = Buf()
        bwo = [Buf() for _ in range(8)]
        win = self.w_in[l].rearrange("(c p) n -> p c n", p=128)
        for oc in range(8):
            for j in range(3):
                self.dma("pool", Wg[:, :, j * 1024 + oc * 128:j * 1024 + (oc + 1) * 128],
                         win[:, :, 3488 + j * 1024 + oc * 128:3488 + j * 1024 + (oc + 1) * 128], w=[bwo[oc]])
                self.dma("pool", Wb[:, j, :, oc * 128:(oc + 1) * 128], self.w_b[j][l].rearrange("(c p) n -> p c n", p=128)[:, :, oc * 128:(oc + 1) * 128],
                         w=[bwo[oc]])
        self.dma("pool", Wo[:], self.w_out[l].rearrange("(c p) n -> p c n", p=128), w=[bw])
        hT_r = self.sbr(2, [128, 8, TT], BF16)
        o_r = self.sbr(2, [128, 3, 4, TT], BF16)
        xT_r = self.sbr(2, [128, 8, TT], F32)
        sig_r = self.sbr(3, [128, TT], F32)
        acc_r = self.sbr(2, [128, TT], F32)
        mg_r = self.sbr(2, [128, 8, TT], BF16)
        names = ("OMLA", "OMOBA", "ORET")
        def load3(t):
            tok = slice(t * TT, (t + 1) * TT)
            hT, bh = hT_r.next()
            o, bo = o_r.next()
            xT, bx = xT_r.next()
            self.dma("sp", hT[:], self.HT.rearrange("c p n -> p c n")[:, :, tok], r=[self.db("HT", t)], w=[bh])
            for j in range(3):
                self.dma("sp", o[:, j, :, :], self.OB[j].rearrange("c p n -> p c n")[:, :, tok], r=[self.db(names[j], t)], w=[bo])
            self.dma("sp", xT[:], self.XT.rearrange("c p n -> p c n")[:, :, tok], r=[self.db("XT", t)], w=[bx])
            return hT, bh, o, bo, xT, bx

        nxt3 = load3(0)
        for t in range(NT):
            tok = slice(t * TT, (t + 1) * TT)
            hT, bh, o, bo, xT, bx = nxt3
            if t + 1 < NT:
                nxt3 = load3(t + 1)
            mg, bmg = mg_r.next()
            for oc in range(8):
                acc, bacc = acc_r.next()
                for j in range(3):
                    pg, bpg = self.bank()
                    for c in range(8):
                        self.I("pe", "matmul", [bh, bwo[oc]], [bpg], out=pg, lhsT=Wg[:, c, j * 1024 + oc * 128:j * 1024 + (oc + 1) * 128], rhs=hT[:, c, :],
                               start=(c == 0), stop=(c == 7))
                    pb, bpb = self.bank()
                    for c in range(4):
                        self.I("pe", "matmul", [bo, bwo[oc]], [bpb], out=pb, lhsT=Wb[:, j, c, oc * 128:(oc + 1) * 128], rhs=o[:, j, c, :],
                               start=(c == 0), stop=(c == 3))
                    sg, bsg = sig_r.next()
                    self.I("act", "activation", [bpg], [bsg], out=sg[:], in_=pg, func=AF.Sigmoid)
                    if j == 0:
                        self.I("dve", "tensor_tensor", [bsg, bpb], [bacc], out=acc[:], in0=pb, in1=sg[:], op=ALU.mult)
                    else:
                        self.I("dve", "tensor_tensor", [bsg, bpb], [bsg], out=sg[:], in0=pb, in1=sg[:], op=ALU.mult)
                        if j == 1:
                            self.I("pool", "tensor_tensor", [bsg, bacc], [bacc], out=acc[:], in0=acc[:], in1=sg[:], op=ALU.add)
                        else:
                            self.I("pool", "tensor_tensor", [bsg, bacc], [bmg], out=mg[:, oc, :], in0=acc[:], in1=sg[:], op=ALU.add)
            for oc in range(8):
                po, bpo = self.bank()
                for c in range(8):
                    self.I("pe", "matmul", [bmg, bw], [bpo], out=po, lhsT=Wo[:, c, oc * 128:(oc + 1) * 128], rhs=mg[:, c, :], start=(c == 0), stop=(c == 7))
                self.I("dve", "tensor_tensor", [bpo, bx], [bx], out=xT[:, oc, :], in0=po, in1=xT[:, oc, :], op=ALU.add)
            self.dma("sp", self.XT.rearrange("c p n -> p c n")[:, :, tok], xT[:], r=[bx], w=[self.db("XT", t)])

    def p_mlp_up(self, l):
        self.begin_phase()
        Wu = self.sb([128, 8, 4096], BF16)
        bw = Buf()
        wu = self.w_up[l].rearrange("(c p) n -> p c n", p=128)
        bwu = [Buf() for _ in range(8)]
        for j in range(8):
            self.dma("pool", Wu[:, :, j * 512:(j + 1) * 512], wu[:, :, j * 512:(j + 1) * 512], w=[bwu[j]])
        xT_r = self.sbr(2, [128, 8, TT], F32)
        hT_r = self.sbr(2, [128, 8, TT], BF16)
        sq = self.sb([128, 8, TT], BF16)
        bsq = Buf()
        rs_r = self.sbr(2, [128, TT], F32)
        r_r = self.sbr(3, [128, TT], F32)
        a_r = self.sbr(2, [128, 8, TT], BF16)
        def load4(t):
            tok = slice(t * TT, (t + 1) * TT)
            xT, bx = xT_r.next()
            hT, bh = hT_r.next()
            self.dma("sp", xT[:], self.XT.rearrange("c p n -> p c n")[:, :, tok], r=[self.db("XT", t)], w=[bx])
            return xT, bx, hT, bh

        def norm4a(st_):
            xT, bx, hT, bh = st_
            self.rmsnorm_a([xT[:, c, :] for c in range(8)], [bx], sq, bsq)

        def norm4b(st_):
            xT, bx, hT, bh = st_
            rs, brs = rs_r.next()
            self.rmsnorm_b([xT[:, c, :] for c in range(8)], [self.vcol(l, 8 + c) for c in range(8)], DM, EPS,
                           [hT[:, c, :] for c in range(8)], [bx], [bh], sq, bsq, rs, brs)

        cur4 = load4(0)
        norm4a(cur4)
        norm4b(cur4)
        for t in range(NT):
            tok = slice(t * TT, (t + 1) * TT)
            xT, bx, hT, bh = cur4
            nxt4 = None
            for fg in range(4):
                if t + 1 < NT:
                    if fg == 0:
                        nxt4 = load4(t + 1)
                    elif fg == 1:
                        norm4a(nxt4)
                    elif fg == 3:
                        norm4b(nxt4)
                a, ba = a_r.next()
                for fj in range(8):
                    f = fg * 8 + fj
                    ps, bps = self.bank()
                    for c in range(8):
                        self.I("pe", "matmul", [bh, bwu[f // 4]], [bps], out=ps, lhsT=Wu[:, c, f * 128:(f + 1) * 128], rhs=hT[:, c, :], start=(c == 0), stop=(c == 7))
                    r, br = r_r.next()
                    self.I("act", "activation", [bps], [br], out=r[:], in_=ps, func=AF.Relu)
                    self.I("pool" if fj % 2 else "dve", "tensor_tensor", [br], [ba], out=a[:, fj, :], in0=r[:], in1=r[:], op=ALU.mult)
                self.dma("sp", self.ACTS[fg * 8:(fg + 1) * 8].rearrange("c p n -> p c n")[:, :, tok], a[:], r=[ba], w=[self.db("ACTS%d" % fg, t)])
            cur4 = nxt4

    def p_mlp_down(self, l):
        self.begin_phase()
        Wd = self.sb([128, 32, DM], BF16)
        bw = Buf()
        wd = self.w_down[l].rearrange("(c p) n -> p c n", p=128)
        bwd = [Buf() for _ in range(8)]
        for oc in range(8):
            for j in range(2):
                self.dma("pool", Wd[:, j * 16:(j + 1) * 16, oc * 128:(oc + 1) * 128], wd[:, j * 16:(j + 1) * 16, oc * 128:(oc + 1) * 128], w=[bwd[oc]])
        xT_r = self.sbr(2, [128, 8, TT], F32)
        a_r = self.sbr(2, [128, 32, TT], BF16)
        def load5(t):
            tok = slice(t * TT, (t + 1) * TT)
            xT, bx = xT_r.next()
            a, ba = a_r.next()
            self.dma("sp", xT[:], self.XT.rearrange("c p n -> p c n")[:, :, tok], r=[self.db("XT", t)], w=[bx])
            for fg in range(4):
                self.dma("sp", a[:, fg * 8:(fg + 1) * 8, :], self.ACTS[fg * 8:(fg + 1) * 8].rearrange("c p n -> p c n")[:, :, tok],
                         r=[self.db("ACTS%d" % fg, t)], w=[ba])
            return xT, bx, a, ba

        nxt5 = load5(0)
        for t in range(NT):
            tok = slice(t * TT, (t + 1) * TT)
            xT, bx, a, ba = nxt5
            if t + 1 < NT:
                nxt5 = load5(t + 1)
            for oc in range(8):
                po, bpo = self.bank()
                for f in range(32):
                    self.I("pe", "matmul", [ba, bwd[oc]], [bpo], out=po, lhsT=Wd[:, f, oc * 128:(oc + 1) * 128], rhs=a[:, f, :], start=(f == 0), stop=(f == 31))
                self.I("dve", "tensor_tensor", [bpo, bx], [bx], out=xT[:, oc, :], in0=po, in1=xT[:, oc, :], op=ALU.add)
            self.dma("sp", self.XT.rearrange("c p n -> p c n")[:, :, tok], xT[:], r=[bx], w=[self.db("XT", t)])

    def p_final(self):
        self.begin_phase()
        xT_r = self.sbr(2, [128, 8, TT], F32)
        yT_r = self.sbr(2, [128, 8, TT], F32)
        sq = self.sb([128, 8, TT], BF16)
        bsq = Buf()
        rs_r = self.sbr(2, [128, TT], F32)
        o_r = self.sbr(2, [128, DM], F32)
        def load6(t):
            tok = slice(t * TT, (t + 1) * TT)
            xT, bx = xT_r.next()
            self.dma("sp", xT[:], self.XT.rearrange("c p n -> p c n")[:, :, tok], r=[self.db("XT", t)], w=[bx])
            return xT, bx

        nxt6 = load6(0)
        for t in range(NT):
            tok = slice(t * TT, (t + 1) * TT)
            xT, bx = nxt6
            if t + 1 < NT:
                nxt6 = load6(t + 1)
            yT, by = yT_r.next()
            rs, brs = rs_r.next()
            fb = DEPTH * VSTR
            self.rmsnorm([xT[:, c, :] for c in range(8)], [self.vecs[:, fb + c:fb + c + 1] for c in range(8)], DM, EPS,
                         [yT[:, c, :] for c in range(8)], [bx], [by], sq, bsq, rs, brs)
            for s in range(4):
                o, bo = o_r.next()
                for g in range(2):
                    ps, bps = self.bank()
                    for j in range(4):
                        c = g * 4 + j
                        self.I("pe", "transpose", [by, self.bconst], [bps], out=ps[:, j * 128:(j + 1) * 128], in_=yT[:, c, s * 128:(s + 1) * 128],
                               identity=self.identF[:])
                    self.copy(self.evac_eng(), o[:, g * 512:(g + 1) * 512], ps, [bps], [bo])
                r0 = t * TT + s * 128
                self.dma("sp", self.out[r0:r0 + 128, :], o[:], r=[bo], w=[self.db("out", t)])

    def build(self):
        ph = self.phases
        want = lambda name: ph is None or name in ph
        if want("t_in"):
            self.p_transpose_in()
        for l in range(self.n_layers):
            if want(f"inproj{l}"):
                self.p_inproj(l)
            if want(f"mla{l}"):
                self.p_mla(l)
            if want(f"moba{l}"):
                self.p_moba(l)
            if want(f"ret{l}"):
                self.p_ret(l)
            if want(f"merge{l}"):
                self.p_merge(l)
            if want(f"mlpup{l}"):
                self.p_mlp_up(l)
            if want(f"mlpdown{l}"):
                self.p_mlp_down(l)
        if want("final"):
            self.p_final()
        self.S.barrier()
        stats = self.S.emit()
        return self.nc, stats


_CACHE = {}


def get_program():
    if "nc" not in _CACHE:
        kb = KB()
        _CACHE["nc"], _CACHE["stats"] = kb.build()
    return _CACHE["nc"]


def make_in_map(inputs, b, consts, vecs):
    m = {"x": np.ascontiguousarray(inputs["x"][b])}
    for k in ("w_in", "mla_w_q_up", "mla_w_kv_up", "w_branch_mla", "w_branch_moba", "w_branch_ret", "w_out", "w_mlp_up", "w_mlp_down"):
        m[k] = np.ascontiguousarray(inputs[k], dtype=np.float32)
    m["vecs"] = vecs
    m.update(consts)
    return m


def kernel(**inputs):
    inputs = {k: np.asarray(v) for k, v in inputs.items()}
    consts, _ = make_consts()
    vecs = pack_vecs(inputs)
    nc = get_program()
    in_maps = [make_in_map(inputs, b, consts, vecs) for b in range(NCORES)]
    res = run_bass_kernel_spmd(nc, in_maps, core_ids=list(range(NCORES)))
    out = np.stack([np.asarray(res.results[b]["out"], dtype=np.float32) for b in range(NCORES)], axis=0)
    return out
```

```python
import numpy as np
import concourse.bass as bass
import concourse.mybir as mybir
from concourse.bass_utils import run_bass_kernel_spmd

F32 = mybir.dt.float32
BF16 = mybir.dt.bfloat16
AF = mybir.ActivationFunctionType
ALU = mybir.AluOpType
AX = mybir.AxisListType

SEQ = 4096
DM = 1024
TT = 512
NT = SEQ // TT
DEPTH = 2
NCORES = 4
EPS = 1e-6
GN_EPS = 1e-5
NEGB = -30000.0


class Buf:
    __slots__ = ("name", "w", "r", "rd")

    def __init__(self, name=""):
        self.name = name
        self.w = None
        self.r = {}
        self.rd = []


class _Op:
    __slots__ = ("eng", "fn", "deps", "is_dma", "idx", "need_inc", "semidx", "semval")


class Sched:
    ENG = ("pe", "act", "dve", "pool", "sp")

    def __init__(self, nc, n_dma_sems=40, same_eng_sync=("act", "dve", "pool")):
        import os
        if os.environ.get("K_VAR", "") == "pesync":
            same_eng_sync = ("act", "dve", "pool", "pe")
        self.nc = nc
        self.ops = []
        self.n_dma_sems = n_dma_sems
        self.same_eng_sync = set(same_eng_sync)
        self.last_on_eng = {}
        self.dmas_open = []
        self.cur_barrier = None

    def op(self, eng, fn, reads=(), writes=(), dma=False):
        o = _Op()
        o.eng = eng
        o.fn = fn
        o.is_dma = dma
        o.idx = len(self.ops)
        o.need_inc = False
        deps = set()
        for b in reads:
            if b.w is not None:
                deps.add(b.w)
        for b in writes:
            if b.w is not None:
                deps.add(b.w)
            deps.update(b.r.values())
            deps.update(b.rd)
        for b in reads:
            if dma:
                b.rd.append(o.idx)
            else:
                b.r[eng] = o.idx
        for b in writes:
            b.w = o.idx
            b.r = {}
            b.rd = []
        if self.cur_barrier is not None:
            deps.add(self.cur_barrier)
        deps.discard(o.idx)
        o.deps = deps
        self.ops.append(o)
        if dma:
            self.dmas_open.append(o.idx)
        else:
            self.last_on_eng[eng] = o.idx
        return o

    def barrier(self):
        deps = set(self.last_on_eng.values()) | set(self.dmas_open)
        o = self.op("sp", lambda e: e.nop())
        o.deps |= deps
        o.deps.discard(o.idx)
        self.dmas_open = []
        self.cur_barrier = o.idx
        return o

    def emit(self):
        nc = self.nc
        ops = self.ops
        engobj = {"pe": nc.tensor, "act": nc.scalar, "dve": nc.vector, "pool": nc.gpsimd, "sp": nc.sync}
        ses = self.same_eng_sync

        def needs_sync(o, od):
            if od.is_dma or o.is_dma:
                return True
            if od.eng != o.eng:
                return True
            return od.eng in ses

        for o in ops:
            for d in o.deps:
                od = ops[d]
                if not od.is_dma and needs_sync(o, od):
                    od.need_inc = True
        cnt = {e: 0 for e in self.ENG}
        ndma = 0
        qpool = {"sp": (0, 32), "pool": (32, 24), "act": (56, 8)}
        qcnt = {q: 0 for q in qpool}
        for o in ops:
            if o.is_dma:
                base, n = qpool[o.eng]
                k = qcnt[o.eng]
                qcnt[o.eng] += 1
                o.semidx = base + (k % n)
                o.semval = 16 * (k // n + 1)
                ndma += 1
            elif o.need_inc:
                cnt[o.eng] += 1
                o.semval = cnt[o.eng]
        engsem = {e: nc.alloc_semaphore(name=f"s_{e}") for e in self.ENG}
        dmasem = [nc.alloc_semaphore(name=f"s_dma{i}") for i in range(64)]
        waited = {}
        nwaits = 0
        self.trace = {e: [] for e in self.ENG}
        for o in ops:
            e = engobj[o.eng]
            waits = {}
            for d in o.deps:
                od = ops[d]
                if not needs_sync(o, od):
                    continue
                key = ("dma", od.semidx) if od.is_dma else ("eng", od.eng)
                if waits.get(key, 0) < od.semval:
                    waits[key] = od.semval
            if o.is_dma and o.semval > 16:
                key = ("dma", o.semidx)
                if waits.get(key, 0) < o.semval - 16:
                    waits[key] = o.semval - 16
            for key, val in waits.items():
                wk = (o.eng, key)
                if waited.get(wk, 0) >= val:
                    continue
                sem = dmasem[key[1]] if key[0] == "dma" else engsem[key[1]]
                e.wait_ge(sem, val)
                self.trace[o.eng].append(("w", key, val, o.idx))
                waited[wk] = val
                nwaits += 1
            ins = o.fn(e)
            if o.is_dma:
                ins.then_inc(dmasem[o.semidx], 16)
                self.trace[o.eng].append(("i", ("dma", o.semidx), 16, o.idx))
            elif o.need_inc:
                ins.then_inc(engsem[o.eng], 1)
                self.trace[o.eng].append(("i", ("eng", o.eng), 1, o.idx))
        self.stats = dict(n_ops=len(ops), n_waits=nwaits, incs=dict(cnt), n_dma=ndma)
        return self.stats


class Ring:
    def __init__(self, items):
        self.items = items
        self.i = 0

    def next(self):
        x = self.items[self.i % len(self.items)]
        self.i += 1
        return x


def _rope_tab(dim, theta):
    inv = (1.0 / (np.float32(theta) ** (np.arange(0, dim, 2, dtype=np.float32) / np.float32(dim)))).astype(np.float32)
    ang = (np.arange(SEQ, dtype=np.float32)[:, None] * inv[None, :]).astype(np.float32)
    c = np.cos(ang).astype(np.float32).T
    s = np.sin(ang).astype(np.float32).T
    cc = np.concatenate([c, c], 0)
    ss = np.concatenate([-s, s], 0)
    return np.ascontiguousarray(np.stack([cc, ss], 0))


def make_consts():
    c = {}
    c["identF"] = np.eye(128, dtype=np.float32)
    c["rope_mla"] = _rope_tab(32, 500000.0)
    rm = _rope_tab(16, 500000.0)
    pad = np.zeros((2, 128, SEQ), np.float32)
    pad[0] = 1.0
    pad[:, 0:16] = rm
    pad[:, 64:80] = rm
    c["rope_moba"] = pad
    rr = _rope_tab(64, 10000.0)
    c["rope_ret"] = np.ascontiguousarray(np.concatenate([rr, rr], axis=1))
    k = np.arange(128)
    c["mask01"] = (k[:, None] <= k[None, :]).astype(np.float32)
    c["onehot"] = (np.arange(16)[:, None] == (np.arange(SEQ)[None, :] // 256)).astype(np.float32)
    lg = np.log(np.float32(1.0) - np.float32(2.0) ** (np.float32(-5.0) - np.arange(4, dtype=np.float32))).astype(np.float32)
    pos = np.arange(128, dtype=np.float32)
    diff = pos[:, None] - pos[None, :]
    dec = np.where(diff >= 0, np.exp(lg[:, None, None] * diff), 0.0).astype(np.float32)
    c["decayT"] = np.ascontiguousarray(dec.transpose(0, 2, 1) * np.float32(0.125)).astype(np.float32)
    xi = np.exp(lg[:, None] * (pos + 1.0)).astype(np.float32)
    c["xi"] = np.ascontiguousarray(np.broadcast_to(np.tile(xi, (1, 4))[:, None, :], (4, 64, 512))).astype(np.float32)
    zeta = np.exp(lg[:, None] * (127.0 - pos)).astype(np.float32) * np.float32(0.125)
    c["zeta"] = np.ascontiguousarray(zeta.T).astype(np.float32)
    gam = [float(np.exp(lg[h] * np.float32(128.0))) for h in range(4)]
    return c, gam


VSTR = 27
NV = DEPTH * VSTR + 8


def pack_vecs(inp):
    v = np.zeros((128, NV), np.float32)
    for l in range(DEPTH):
        b = l * VSTR
        v[:, b + 0:b + 8] = inp["attn_norm"][l].reshape(8, 128).T
        v[:, b + 8:b + 16] = inp["mlp_norm"][l].reshape(8, 128).T
        v[:, b + 16:b + 18] = inp["mla_q_norm"][l].reshape(2, 128).T
        v[:, b + 18:b + 19] = inp["mla_kv_norm"][l].reshape(1, 128).T
        v[:, b + 19:b + 23] = inp["ret_gn_w"][l].reshape(4, 128).T
        v[:, b + 23:b + 27] = inp["ret_gn_b"][l].reshape(4, 128).T
    v[:, DEPTH * VSTR:DEPTH * VSTR + 8] = inp["final_norm"].reshape(8, 128).T
    return v


class KB:
    SB_BASE = 20480
    SB_TOP = 229376

    def __init__(self, n_layers=DEPTH, debug=False, phases=None):
        self.nc = nc = bass.Bass("TRN2", target_bir_lowering=False)
        self.S = Sched(nc)
        self.debug = debug
        self.n_layers = n_layers
        self.phases = phases
        self.top = self.SB_BASE
        self.uid = 0
        L = DEPTH
        inp = lambda name, shape: nc.dram_tensor(name, shape, F32, kind="ExternalInput").ap()
        self.x = inp("x", [SEQ, DM])
        self.w_in = inp("w_in", [L, DM, 6560])
        self.w_q_up = inp("mla_w_q_up", [L, 256, 768])
        self.w_kv_up = inp("mla_w_kv_up", [L, 128, 1024])
        self.w_b = [inp("w_branch_mla", [L, 512, DM]), inp("w_branch_moba", [L, 512, DM]), inp("w_branch_ret", [L, 512, DM])]
        self.w_out = inp("w_out", [L, DM, DM])
        self.w_up = inp("w_mlp_up", [L, DM, 4096])
        self.w_down = inp("w_mlp_down", [L, 4096, DM])
        self.vecs_d = inp("vecs", [128, NV])
        self.c_identF = inp("identF", [128, 128])
        self.c_rope_mla = inp("rope_mla", [2, 32, SEQ])
        self.c_rope_moba = inp("rope_moba", [2, 128, SEQ])
        self.c_rope_ret = inp("rope_ret", [2, 128, SEQ])
        self.c_mask01 = inp("mask01", [128, 128])
        self.c_onehot = inp("onehot", [16, SEQ])
        self.c_decayT = inp("decayT", [4, 128, 128])
        self.c_xi = inp("xi", [4, 64, 512])
        self.c_zeta = inp("zeta", [128, 4])
        self.out = nc.dram_tensor("out", [SEQ, DM], F32, kind="ExternalOutput").ap()
        kind = "ExternalOutput" if debug else "Internal"
        scr = lambda name, shape, dt: nc.dram_tensor(name, shape, dt, kind=kind).ap()
        self.XT = scr("XT", [8, 128, SEQ], F32)
        self.HT = scr("HT", [8, 128, SEQ], BF16)
        self.CQ = scr("CQ", [2, 128, SEQ], BF16)
        self.CKV = scr("CKV", [128, SEQ], BF16)
        self.KPE = scr("KPE", [32, SEQ], BF16)
        self.MQ = scr("MQ", [8, 64, SEQ], BF16)
        self.MK = scr("MK", [8, 64, SEQ], BF16)
        self.MV = scr("MV", [SEQ, 512], BF16)
        self.RQ = scr("RQ", [4, 64, SEQ], BF16)
        self.RK = scr("RK", [4, 64, SEQ], BF16)
        self.RV = scr("RV", [SEQ, 512], BF16)
        self.RG = scr("RG", [4, 128, SEQ], F32)
        self.OB = [scr("OMLA", [4, 128, SEQ], BF16), scr("OMOBA", [4, 128, SEQ], BF16), scr("ORET", [4, 128, SEQ], BF16)]
        self.ACTS = scr("ACTS", [32, 128, SEQ], BF16)
        self.dbuf = {}
        self.psF = nc.alloc_psum_tensor("psF", [128, 7, 512], F32)
        self.psB = nc.alloc_psum_tensor("psB", [128, 1024], BF16)
        self.pbuf = [Buf(f"ps{i}") for i in range(7)]
        self.pbufB = Buf("psB")
        self.bank_i = 0
        _, self.gam = make_consts()
        self.identF = self.sb([128, 128], F32)
        self.identB = self.sb([128, 128], BF16)
        self.onesB = self.sb([128, 128], BF16)
        self.mask01 = self.sb([128, 128], BF16)
        self.vecs = self.sb([128, NV], F32)
        self.bconst = Buf("const")
        self.dma("sp", self.identF[:], self.c_identF, w=[self.bconst])
        self.dma("pool", self.identB[:], self.c_identF, w=[self.bconst])
        self.dma("pool", self.mask01[:], self.c_mask01, w=[self.bconst])
        self.dma("sp", self.vecs[:], self.vecs_d, w=[self.bconst])
        self.I("pool", "memset", [], [self.bconst], ap=self.onesB[:], constant=1.0)
        self.maskneg = self.sb([128, 128], BF16)
        self.I("pool", "tensor_scalar", [self.bconst], [self.bconst], out=self.maskneg[:], in0=self.mask01[:], scalar1=-1.0, scalar2=-NEGB,
               op0=ALU.add, op1=ALU.mult)
        self.phase_base = self.top

    def sb(self, shape, dtype, name=None):
        sz = int(np.prod(shape[1:])) * (4 if dtype == F32 else 2)
        sz = (sz + 63) // 64 * 64
        self.uid += 1
        t = self.nc.alloc_sbuf_tensor_at(name or f"t{self.uid}", list(shape), dtype, offset=self.top)
        self.top += sz
        assert self.top <= self.SB_TOP, f"SBUF overflow {self.top}"
        return t

    def sbr(self, n, shape, dtype):
        return Ring([(self.sb(shape, dtype), Buf()) for _ in range(n)])

    def begin_phase(self):
        self.S.barrier()
        self.top = self.phase_base

    def I(self, eng, meth, r, w, **kw):
        return self.S.op(eng, lambda e: getattr(e, meth)(**kw), r, w)

    def dma(self, eng, out, in_, r=(), w=()):
        return self.S.op(eng, lambda e: e.dma_start(out=out, in_=in_), r, w, dma=True)

    def bank(self):
        i = self.bank_i % 7
        self.bank_i += 1
        return self.psF[:, i, :], self.pbuf[i]

    def db(self, name, t=None):
        k = (name, t)
        if k not in self.dbuf:
            self.dbuf[k] = Buf(f"{name}{t}")
        return self.dbuf[k]

    def dball(self, name):
        return [self.db(name, t) for t in range(NT)]

    def vcol(self, l, off, n=1):
        b = l * VSTR + off
        return self.vecs[:, b:b + n]

    def evac_eng(self):
        self._ev = getattr(self, "_ev", 0) + 1
        return "act" if self._ev % 2 else "dve"

    def copy(self, eng, out, in_, r, w):
        if eng == "act":
            return self.I("act", "copy", r, w, out=out, in_=in_)
        return self.I(eng, "tensor_copy", r, w, out=out, in_=in_)

    def rmsnorm_a(self, xaps, rb, sq, bsq, nparts=128):
        for c in range(len(xaps)):
            self.I("act", "activation", rb, [bsq], out=sq[0:nparts, c, :], in_=xaps[c], func=AF.Square)

    def rmsnorm_b(self, xaps, gaps, nfeat, eps, outaps, rb, wb, sq, bsq, rs, brs, nparts=128):
        n = len(xaps)
        ps, bps = self.bank()
        for c in range(n):
            self.I("pe", "matmul", [bsq, self.bconst], [bps], out=ps, lhsT=self.onesB[0:nparts, :], rhs=sq[0:nparts, c, :],
                   start=(c == 0), stop=(c == n - 1))
        self.I("act", "activation", [bps], [brs], out=rs[:], in_=ps, func=AF.Sqrt, scale=1.0 / nfeat, bias=float(eps))
        self.I("dve", "reciprocal", [brs], [brs], out=rs[:], in_=rs[:])
        for c in range(n):
            self.I("dve", "scalar_tensor_tensor", list(rb) + [brs, self.bconst], wb, out=outaps[c], in0=xaps[c], scalar=gaps[c],
                   in1=rs[0:nparts, :], op0=ALU.mult, op1=ALU.mult)

    def rmsnorm(self, xaps, gaps, nfeat, eps, outaps, rb, wb, sq, bsq, rs, brs, nparts=128):
        self.rmsnorm_a(xaps, rb, sq, bsq, nparts)
        self.rmsnorm_b(xaps, gaps, nfeat, eps, outaps, rb, wb, sq, bsq, rs, brs, nparts)

    def p_transpose_in(self):
        self.begin_phase()
        xin = self.sbr(2, [128, DM], F32)
        xo = self.sbr(2, [128, 8, TT], F32)
        for t in range(NT):
            xot, bxo = xo.next()
            tiles = []
            for s in range(4):
                xt, bx = xin.next()
                r0 = t * TT + s * 128
                self.dma("sp", xt[:], self.x[r0:r0 + 128, :], w=[bx])
                for g in range(2):
                    ps, bps = self.bank()
                    for j in range(4):
                        c = g * 4 + j
                        self.I("pe", "transpose", [bx, self.bconst], [bps], out=ps[:, j * 128:(j + 1) * 128], in_=xt[:, c * 128:(c + 1) * 128],
                               identity=self.identF[:])
                    self.copy(self.evac_eng(), xot[:, g * 4:(g + 1) * 4, s * 128:(s + 1) * 128],
                              ps.rearrange("p (j n) -> p j n", n=128), [bps], [bxo])
            self.dma("sp", self.XT.rearrange("c p n -> p c n")[:, :, t * TT:(t + 1) * TT], xot[:], r=[bxo], w=[self.db("XT", t)])

    def p_inproj(self, l):
        self.begin_phase()
        NC1 = 5056
        W1 = self.sb([128, 8, NC1], BF16)
        bW = []
        win = self.w_in[l].rearrange("(c p) n -> p c n", p=128)

        segs = []

        def wl(d0, s0, n):
            b = Buf()
            self.dma("pool", W1[:, :, d0:d0 + n], win[:, :, s0:s0 + n], w=[b])
            bW.append(b)
            segs.append((d0, d0 + n, [b]))

        def wbufs(col, M):
            out = []
            for (c0, c1, bl) in segs:
                if c0 < col + M and col < c1:
                    out.extend(bl)
            return out

        C_MQ, C_MQR, C_MK, C_MKR, C_MV = 448, 960, 1472, 1984, 2496
        C_RQ, C_RQR, C_RK, C_RKR, C_RV, C_RG = 3008, 3264, 3520, 3776, 4032, 4544
        bz = Buf()
        for c in range(8):
            self.I("pool", "memset", [], [bz], ap=W1[:, c, C_MQR:C_MQR + 512], constant=0.0)
            self.I("pool", "memset", [], [bz], ap=W1[:, c, C_MKR:C_MKR + 512], constant=0.0)
        bW.append(bz)

        def wrot(dst, src, H, half):
            bl = [bz]
            for c in range(8):
                dstv = W1[:, c, dst:dst + H * 64].rearrange("p (h e) -> p h e", e=64)
                srcv = win[:, c, src:src + H * 64].rearrange("p (h d) -> p h d", d=64)
                b = Buf()
                self.dma("pool", dstv[:, :, 0:half], srcv[:, :, half:2 * half], r=[bz], w=[b])
                self.dma("pool", dstv[:, :, half:2 * half], srcv[:, :, 0:half], r=[bz], w=[b])
                bW.append(b)
                bl.append(b)
            segs.append((dst, dst + H * 64, bl))

        wl(0, 0, 416)
        wl(416, 400, 16)
        wl(432, 384, 16)
        wl(C_MQ, 416, 512)
        wrot(C_MQR, 416, 8, 8)
        wl(C_MK, 928, 512)
        wrot(C_MKR, 928, 8, 8)
        wl(C_RQ, 1952, 256)
        wrot(C_RQR, 1952, 4, 32)
        wl(C_RK, 2208, 256)
        wrot(C_RKR, 2208, 4, 32)
        wl(C_RG, 2976, 512)
        wl(C_MV, 1440, 512)
        wl(C_RV, 2464, 512)
        xTr = self.sbr(2, [128, 8, TT], F32)
        hTr = self.sbr(2, [128, 8, TT], BF16)
        sq = self.sb([128, 8, TT], BF16)
        bsq = Buf()
        rsr = self.sbr(3, [128, TT], F32)
        cqf = self.sb([128, 3, TT], F32)
        bcqf = Buf()
        cqn = self.sb([128, 3, TT], BF16)
        bcqn = Buf()
        sq2 = self.sb([128, 3, TT], BF16)
        bsq2 = [Buf(), Buf()]
        tabk = self.sb([32, 2, TT], F32)
        tabm = self.sb([128, 2, TT], F32)
        tabr = self.sb([128, 2, TT], F32)
        btab = Buf()
        t1r = self.sbr(2, [128, TT], F32)
        t2r = self.sbr(2, [128, TT], F32)
        kpe = self.sb([32, TT], BF16)
        bkpe = Buf()
        stq = self.sb([128, 4, TT], BF16)
        stk = self.sb([128, 4, TT], BF16)
        strq = self.sb([128, 2, TT], BF16)
        strk = self.sb([128, 2, TT], BF16)
        stg = self.sb([128, 4, TT], F32)
        stv = self.sbr(2, [128, 4, 512], BF16)
        bst = {k: Buf() for k in ("q", "k", "rq", "rk", "g")}

        def load_x(t):
            tok = slice(t * TT, (t + 1) * TT)
            xT, bx = xTr.next()
            hT, bh = hTr.next()
            self.dma("sp", xT[:], self.XT.rearrange("c p n -> p c n")[:, :, tok], r=[self.db("XT", t)], w=[bx])
            return xT, bx, hT, bh

        def xnorm_a(st_):
            xT, bx, hT, bh = st_
            self.rmsnorm_a([xT[:, c, :] for c in range(8)], [bx], sq, bsq)

        def xnorm_b(st_, t):
            xT, bx, hT, bh = st_
            rs, brs = rsr.next()
            self.rmsnorm_b([xT[:, c, :] for c in range(8)], [self.vcol(l, c) for c in range(8)], DM, EPS,
                           [hT[:, c, :] for c in range(8)], [bx], [bh], sq, bsq, rs, brs)
            self.dma("sp", self.HT.rearrange("c p n -> p c n")[:, :, t * TT:(t + 1) * TT], hT[:], r=[bh], w=[self.db("HT", t)])

        cur = load_x(0)
        xnorm_a(cur)
        xnorm_b(cur, 0)
        for t in range(NT):
            tok = slice(t * TT, (t + 1) * TT)
            xT, bx, hT, bh = cur
            nxt = load_x(t + 1) if t + 1 < NT else None
            self.dma("sp", tabk[:], self.c_rope_mla.rearrange("a p n -> p a n")[:, :, tok], w=[btab])
            self.dma("sp", tabm[:], self.c_rope_moba.rearrange("a p n -> p a n")[:, :, tok], w=[btab])
            self.dma("sp", tabr[:], self.c_rope_ret.rearrange("a p n -> p a n")[:, :, tok], w=[btab])

            def mm(M, col):
                ps, bps = self.bank()
                wb_ = wbufs(col, M)
                for c in range(8):
                    self.I("pe", "matmul", [bh] + wb_, [bps], out=ps[0:M, :], lhsT=W1[:, c, col:col + M], rhs=hT[:, c, :],
                           start=(c == 0), stop=(c == 7))
                return ps, bps

            def rope(psm, bpm, psr, bpr, n, tab, outap, wb):
                t1, b1 = t1r.next()
                t2, b2 = t2r.next()
                self.I("dve", "tensor_tensor", [bpm, btab], [b1], out=t1[0:n, :], in0=psm[0:n, :], in1=tab[0:n, 0, :], op=ALU.mult)
                self.I("dve", "tensor_tensor", [bpr, btab], [b2], out=t2[0:n, :], in0=psr[0:n, :], in1=tab[0:n, 1, :], op=ALU.mult)
                self.I("pool", "tensor_tensor", [b1, b2], wb, out=outap, in0=t1[0:n, :], in1=t2[0:n, :], op=ALU.add)

            for j in range(3):
                ps, bps = mm(128, j * 128)
                self.copy(self.evac_eng(), cqf[:, j, :], ps, [bps], [bcqf])
            self.rmsnorm_a([cqf[:, c, :] for c in range(2)], [bcqf], sq2[:, 0:2, :], bsq2[0])
            self.rmsnorm_a([cqf[:, 2, :]], [bcqf], sq2[:, 2:3, :], bsq2[1])
            psm, bpm = mm(32, 384)
            psr, bpr = mm(32, 416)
            rope(psm, bpm, psr, bpr, 32, tabk, kpe[:], [bkpe])
            self.dma("sp", self.KPE[:, tok], kpe[:], r=[bkpe], w=[self.db("KPE", t)])
            if nxt is not None:
                xnorm_a(nxt)
            for (st, key, base, rbase, dst) in ((stq, "q", C_MQ, C_MQR, self.MQ), (stk, "k", C_MK, C_MKR, self.MK)):
                for j in range(4):
                    psm, bpm = mm(128, base + j * 128)
                    psr, bpr = mm(128, rbase + j * 128)
                    rope(psm, bpm, psr, bpr, 128, tabm, st[:, j, :], [bst[key]])
                self.dma("sp", dst.rearrange("(j two) p n -> (two p) j n", two=2)[:, :, tok], st[:], r=[bst[key]], w=[self.db("M" + key, t)])
            rs2, brs2 = rsr.next()
            self.rmsnorm_b([cqf[:, c, :] for c in range(2)], [self.vcol(l, 16 + c) for c in range(2)], 256, EPS,
                           [cqn[:, c, :] for c in range(2)], [bcqf], [bcqn], sq2[:, 0:2, :], bsq2[0], rs2, brs2)
            self.dma("sp", self.CQ.rearrange("c p n -> p c n")[:, :, tok], cqn[:, 0:2, :], r=[bcqn], w=[self.db("CQ", t)])
            rs3, brs3 = rsr.next()
            self.rmsnorm_b([cqf[:, 2, :]], [self.vcol(l, 18)], 128, EPS, [cqn[:, 2, :]], [bcqf], [bcqn], sq2[:, 2:3, :], bsq2[1], rs3, brs3)
            self.dma("sp", self.CKV[:, tok], cqn[:, 2, :], r=[bcqn], w=[self.db("CKV", t)])
            for (st, key, base, rbase, dst) in ((strq, "rq", C_RQ, C_RQR, self.RQ), (strk, "rk", C_RK, C_RKR, self.RK)):
                for j in range(2):
                    psm, bpm = mm(128, base + j * 128)
                    psr, bpr = mm(128, rbase + j * 128)
                    rope(psm, bpm, psr, bpr, 128, tabr, st[:, j, :], [bst[key]])
                self.dma("sp", dst.rearrange("(j two) p n -> (two p) j n", two=2)[:, :, tok], st[:], r=[bst[key]], w=[self.db(key, t)])
            for j in range(4):
                ps, bps = mm(128, C_RG + j * 128)
                self.I("act", "activation", [bps], [bst["g"]], out=stg[:, j, :], in_=ps, func=AF.Silu)
            self.dma("sp", self.RG.rearrange("c p n -> p c n")[:, :, tok], stg[:], r=[bst["g"]], w=[self.db("RG", t)])
            if nxt is not None:
                xnorm_b(nxt, t + 1)
            for (col, dst, key) in ((C_MV, self.MV, "MV"), (C_RV, self.RV, "RV")):
                sv, bsv = stv.next()
                for s_ in range(4):
                    ps, bps = self.bank()
                    for c in range(8):
                        self.I("pe", "matmul", [bh] + wbufs(col, 512), [bps], out=ps, lhsT=hT[:, c, s_ * 128:(s_ + 1) * 128], rhs=W1[:, c, col:col + 512],
                               start=(c == 0), stop=(c == 7))
                    self.copy(self.evac_eng(), sv[:, s_, :], ps, [bps], [bsv])
                self.dma("sp", dst[t * TT:(t + 1) * TT, :].rearrange("(s p) n -> p s n", p=128), sv[:], r=[bsv], w=[self.db(key, t)])
            cur = nxt

    def run_attn(self, tiles, scale, pss_r, pT_r, pso_r, LA=2):
        steps = [(i, kt) for i, T in enumerate(tiles) for kt in range(4 * T["t"] + 4)]
        info = {}

        def emit_s(si):
            i, kt = steps[si]
            T = tiles[i]
            t = T["t"]
            qT, qbufs = T["get_q"]()
            j = kt - 4 * t
            q0 = max(0, j) * 128
            N = TT - q0
            (pi, bpi) = pss_r.next()
            pss = self.psF[:, pi, :]
            self.I("pe", "matmul", [T["bK"]] + list(qbufs), [bpi], out=pss[:, 0:N], lhsT=T["K"][:, kt * 128:(kt + 1) * 128], rhs=qT[:, q0:TT],
                   start=True, stop=(j < 0))
            if j >= 0:
                self.I("pe", "matmul", [self.bconst], [bpi], out=pss[:, 0:128], lhsT=self.identB[:], rhs=self.maskneg[:], start=False, stop=True)
            info[si] = (pss, bpi, q0, N)

        for si in range(min(LA, len(steps))):
            emit_s(si)
        for si, (i, kt) in enumerate(steps):
            T = tiles[i]
            nkt = 4 * T["t"] + 4
            if kt == 0:
                if T.get("start_hook") is not None:
                    T["start_hook"]()
                po, bpso = pso_r.next()
                T["pso"] = (self.psF[:, po, :], bpso)
            if si + LA < len(steps):
                emit_s(si + LA)
            if kt == 1 and T.get("mid_hook") is not None:
                T["mid_hook"]()
            pss, bpi, q0, N = info.pop(si)
            pso, bpso = T["pso"]
            pT, bpT = pT_r.next()
            self.I("act", "activation", [bpi], [bpT], out=pT[:, 0:N], in_=pss[:, 0:N], func=AF.Exp, scale=float(scale))
            self.I("pe", "matmul", [T["bV"], bpT], [bpso], out=pso[:, q0:TT], lhsT=T["V"][:, kt, :], rhs=pT[:, 0:N],
                   start=(kt == 0), stop=(kt == nkt - 1))
            if kt == nkt - 1:
                T["finish"](pso, bpso)

    def attn_finish(self, pso, bpso, rec_r, ost_r, dst_ap, dbuf):
        rec, brec = rec_r.next()
        ost, bost = ost_r.next()
        self.I("dve", "reciprocal", [bpso], [brec], out=rec[0:64, :], in_=pso[64:128, :])
        self.I("dve", "tensor_tensor", [bpso, brec], [bost], out=ost[0:64, :], in0=pso[0:64, :], in1=rec[0:64, :], op=ALU.mult)
        self.dma("sp", dst_ap, ost[0:64, :], r=[bost], w=[dbuf])

    def p_mla(self, l):
        self.begin_phase()
        CQs = self.sb([128, 2, SEQ], BF16)
        CKVs = self.sb([128, SEQ], BF16)
        bin_ = Buf()
        self.dma("sp", CQs[:], self.CQ.rearrange("c p n -> p c n"), r=self.dball("CQ"), w=[bin_])
        self.dma("sp", CKVs[:], self.CKV, r=self.dball("CKV"), w=[bin_])
        Wq = self.sb([128, 2, 8, 96], BF16)
        Wqr = self.sb([128, 2, 8, 96], BF16)
        Wkv = self.sb([128, 8, 128], BF16)
        bw = Buf()
        wq = self.w_q_up[l].rearrange("(c p) (h d) -> p c h d", p=128, d=96)
        self.I("pool", "memset", [], [bw], ap=Wqr[:], constant=0.0)
        for c in range(2):
            self.dma("pool", Wq[:, c, :, :], wq[:, c, :, :], w=[bw])
            self.dma("pool", Wqr[:, c, :, 64:80], wq[:, c, :, 80:96], w=[bw])
            self.dma("pool", Wqr[:, c, :, 80:96], wq[:, c, :, 64:80], w=[bw])
        self.dma("pool", Wkv[:], self.w_kv_up[l].rearrange("p (h d) -> p h d", d=128), w=[bw])
        tab = self.sb([96, 2, SEQ], F32)
        btab = Buf()
        self.dma("sp", tab[64:96, :, :], self.c_rope_mla.rearrange("a p n -> p a n"), w=[btab])
        Kr = self.sbr(2, [128, SEQ], BF16)
        Vr = self.sbr(2, [128, 32, 128], BF16)
        for (Kb, bK) in Kr.items:
            self.I("pool", "memset", [], [bK], ap=Kb[96:128, :], constant=0.0)
            self.dma("sp", Kb[64:96, :], self.KPE, r=self.dball("KPE"), w=[bK])
        for (Vb, bV) in Vr.items:
            self.I("pool", "memset", [], [bV], ap=Vb[:, :, 64:128], constant=1.0)
        qTr = self.sbr(3, [128, TT], BF16)
        for (q_, bq_) in qTr.items:
            self.I("pool", "memset", [], [bq_], ap=q_[96:128, :], constant=0.0)
        t1r = self.sbr(2, [96, TT], F32)
        t2r = self.sbr(2, [96, TT], F32)
        pT_r = self.sbr(4, [128, TT], BF16)
        rec_r = self.sbr(2, [64, TT], F32)
        ost_r = self.sbr(2, [64, TT], BF16)
        pss_r = Ring([(0, self.pbuf[0]), (1, self.pbuf[1]), (2, self.pbuf[2])])
        pso_r = Ring([(3, self.pbuf[3]), (4, self.pbuf[4])])
        aux_r = Ring([(5, self.pbuf[5]), (6, self.pbuf[6])])
        scale = 96 ** -0.5
        KV = {}

        def build_kv(h):
            Kb, bK = Kr.items[h % 2]
            Vb, bV = Vr.items[h % 2]
            KV[h] = (Kb, bK, Vb, bV)
            for t in range(NT):
                pi, bpi = aux_r.next()
                ps = self.psF[:, pi, :]
                self.I("pe", "matmul", [bin_, bw], [bpi], out=ps[0:64, :], lhsT=Wkv[:, h, 0:64], rhs=CKVs[:, t * TT:(t + 1) * TT], start=True, stop=True)
                self.copy(self.evac_eng(), Kb[0:64, t * TT:(t + 1) * TT], ps[0:64, :], [bpi], [bK])
            for g in range(4):
                pi, bpi = aux_r.next()
                ps = self.psF[:, pi, :]
                for j in range(8):
                    kt = g * 8 + j
                    self.I("pe", "matmul", [bin_, bw], [bpi], out=ps[:, j * 64:(j + 1) * 64], lhsT=CKVs[:, kt * 128:(kt + 1) * 128],
                           rhs=Wkv[:, h, 64:128], start=True, stop=True)
                self.copy(self.evac_eng(), Vb[:, g * 8:(g + 1) * 8, 0:64], ps.rearrange("p (j d) -> p j d", d=64), [bpi], [bV])

        def build_q(h, t):
            tok = slice(t * TT, (t + 1) * TT)
            qT, bq = qTr.next()
            pi, bpm = aux_r.next()
            psm = self.psF[:, pi, :]
            pi2, bpr = aux_r.next()
            psr = self.psF[:, pi2, :]
            for c in range(2):
                self.I("pe", "matmul", [bin_, bw], [bpm], out=psm[0:96, :], lhsT=Wq[:, c, h, :], rhs=CQs[:, c, tok], start=(c == 0), stop=(c == 1))
            for c in range(2):
                self.I("pe", "matmul", [bin_, bw], [bpr], out=psr[0:96, :], lhsT=Wqr[:, c, h, :], rhs=CQs[:, c, tok], start=(c == 0), stop=(c == 1))
            self.I("act", "copy", [bpm], [bq], out=qT[0:64, :], in_=psm[0:64, :])
            t1, b1 = t1r.next()
            t2, b2 = t2r.next()
            self.I("dve", "tensor_tensor", [bpm, btab], [b1], out=t1[64:96, :], in0=psm[64:96, :], in1=tab[64:96, 0, tok], op=ALU.mult)
            self.I("dve", "tensor_tensor", [bpr, btab], [b2], out=t2[64:96, :], in0=psr[64:96, :], in1=tab[64:96, 1, tok], op=ALU.mult)
            self.I("pool", "tensor_tensor", [b1, b2], [bq], out=qT[64:96, :], in0=t1[64:96, :], in1=t2[64:96, :], op=ALU.add)
            return qT, bq

        items = [(h, t) for h in range(8) for t in range(NT)]
        build_kv(0)
        prepared = {items[0]: build_q(*items[0])}
        tiles = []
        for idx, (h, t) in enumerate(items):
            def start_hook(idx=idx):
                if idx + 1 < len(items):
                    hn, tn = items[idx + 1]
                    if tn == 0:
                        build_kv(hn)
                    prepared[(hn, tn)] = build_q(hn, tn)

            def get_q(h=h, t=t):
                qT, bq = prepared[(h, t)]
                return qT[:, :], [bq]

            def finish(pso, bpso, h=h, t=t):
                tok = slice(t * TT, (t + 1) * TT)
                self.attn_finish(pso, bpso, rec_r, ost_r, self.OB[0][h // 2, (h % 2) * 64:(h % 2) * 64 + 64, tok], self.db("OMLA", t))

            tiles.append(dict(t=t, h=h, get_q=get_q, start_hook=start_hook, finish=finish))
        class _KV(dict):
            pass
        for T in tiles:
            T["_h"] = T["h"]
        for T in tiles:
            hh = T["h"]
            T["K"] = Kr.items[hh % 2][0]
            T["bK"] = Kr.items[hh % 2][1]
            T["V"] = Vr.items[hh % 2][0]
            T["bV"] = Vr.items[hh % 2][1]
        self.run_attn(tiles, scale, pss_r, pT_r, pso_r)

    def p_moba(self, l):
        self.begin_phase()
        Qr = self.sbr(2, [128, SEQ], BF16)
        Kr = self.sbr(2, [128, SEQ], BF16)
        Vr = self.sbr(2, [128, 32, 128], BF16)
        for (Kb, bK) in Kr.items:
            self.I("pool", "memset", [], [bK], ap=Kb[64:128, :], constant=0.0)
            self.dma("pool", Kb[64:80, :], self.c_onehot, w=[bK])
        for (Qb, bQ) in Qr.items:
            self.I("pool", "memset", [], [bQ], ap=Qb[64:128, :], constant=0.0)
        for (Vb, bV) in Vr.items:
            self.I("pool", "memset", [], [bV], ap=Vb[:, :, 64:128], constant=1.0)
        km_r = [(self.sb([64, 16], F32), self.sb([64, 16], BF16), Buf()) for _ in range(2)]
        gate_r = self.sbr(4, [128, 16], F32)
        top_r = self.sbr(4, [128, 8], F32)
        mbp_r = self.sbr(4, [128, 4, 80], BF16)
        for (m, bm) in mbp_r.items:
            self.I("pool", "memset", [], [bm], ap=m[:], constant=0.0)
        pT_r = self.sbr(4, [128, TT], BF16)
        rec_r = self.sbr(2, [64, TT], F32)
        ost_r = self.sbr(2, [64, TT], BF16)
        pss_r = Ring([(0, self.pbuf[0]), (1, self.pbuf[1]), (2, self.pbuf[2])])
        pso_r = Ring([(3, self.pbuf[3]), (4, self.pbuf[4])])
        aux_r = Ring([(5, self.pbuf[5]), (6, self.pbuf[6])])
        bQm = [[Buf() for _ in range(NT)] for _ in range(2)]
        H = {}

        def load_head(h):
            Qb, bQ = Qr.items[h % 2]
            Kb, bK = Kr.items[h % 2]
            Vb, bV = Vr.items[h % 2]
            H[h] = (Qb, bQ, Kb, bK, Vb, bV)
            self.dma("sp", Qb[0:64, :], self.MQ[h], r=self.dball("Mq"), w=[bQ])
            self.dma("sp", Kb[0:64, :], self.MK[h], r=self.dball("Mk"), w=[bK])
            for g4 in range(4):
                self.dma("sp", Vb[:, g4 * 8:(g4 + 1) * 8, 0:64],
                         self.MV[g4 * 1024:(g4 + 1) * 1024, h * 64:(h + 1) * 64].rearrange("(k p) d -> p k d", p=128), r=self.dball("MV"), w=[bV])

        def kmean(h):
            Qb, bQ, Kb, bK, Vb, bV = H[h]
            kmf, kmb, bkm = km_r[h % 2]
            self.I("dve", "tensor_reduce", [bK], [bkm], out=kmf[:], in_=Kb[0:64, :].rearrange("p (n k) -> p n k", k=256), axis=AX.X, op=ALU.add)
            self.I("dve", "tensor_copy", [bkm], [bkm], out=kmb[:], in_=kmf[:])

        def prepare(h, t):
            Qb, bQ, Kb, bK, Vb, bV = H[h]
            kmf, kmb, bkm = km_r[h % 2]
            tok = slice(t * TT, (t + 1) * TT)
            mbp, bmb = mbp_r.next()
            pi, bpg = aux_r.next()
            psg = self.psF[:, pi, :]
            for s in range(4):
                g = 4 * t + s
                blk = g // 2
                if blk <= 3:
                    continue
                self.I("pe", "matmul", [bQ, bkm], [bpg], out=psg[:, s * 16:(s + 1) * 16], lhsT=Qb[0:64, g * 128:(g + 1) * 128], rhs=kmb[:],
                       start=True, stop=True)
            for s in range(4):
                g = 4 * t + s
                blk = g // 2
                if blk <= 3:
                    self.I("pool", "memset", [], [bmb], ap=mbp[:, s, 64:80], constant=0.0)
                    continue
                gt, bg = gate_r.next()
                tp, btp = top_r.next()
                self.I("dve", "memset", [], [bg], ap=gt[:], constant=-1e30)
                self.I("dve", "tensor_copy", [bpg], [bg], out=gt[:, 0:blk], in_=psg[:, s * 16:s * 16 + blk])
                self.I("dve", "max", [bg], [btp], out=tp[:], in_=gt[:])
                self.I("dve", "tensor_scalar", [bg, btp], [bmb], out=mbp[:, s, 64:80], in0=gt[:], scalar1=tp[:, 2:3], scalar2=NEGB,
                       op0=ALU.is_lt, op1=ALU.mult)
                self.I("dve", "memset", [], [bmb], ap=mbp[:, s, 64 + blk:80], constant=0.0)
            return mbp, bmb

        def prepare2(h, t, mbp, bmb):
            Qb, bQ, Kb, bK, Vb, bV = H[h]
            tok = slice(t * TT, (t + 1) * TT)
            for s in range(4):
                self.I("pe", "transpose", [bmb, self.bconst], [self.pbufB], out=self.psB[0:80, s * 128:(s + 1) * 128], in_=mbp[:, s, :],
                       identity=self.identB[:])
            self.I("act", "copy", [self.pbufB, bQ], [bQm[h % 2][t]], out=Qb[64:80, tok], in_=self.psB[64:80, 0:TT])

        items = [(h, t) for h in range(8) for t in range(NT)]
        load_head(0)
        load_head(1)
        kmean(0)
        prepare2(0, 0, *prepare(0, 0))
        gated = {}
        if len(items) > 1:
            gated[items[1]] = prepare(*items[1])
        tiles = []
        for idx, (h, t) in enumerate(items):
            def start_hook(idx=idx, h=h, t=t):
                if t == 1 and h + 1 < 8 and h >= 1:
                    load_head(h + 1)
                if idx + 2 < len(items):
                    hn, tn = items[idx + 2]
                    if tn == 0:
                        kmean(hn)
                    gated[(hn, tn)] = prepare(hn, tn)

            def mid_hook(idx=idx):
                if idx + 1 < len(items):
                    hn, tn = items[idx + 1]
                    prepare2(hn, tn, *gated.pop((hn, tn)))

            def get_q(h=h, t=t):
                Qb, bQ = Qr.items[h % 2]
                return Qb[:, t * TT:(t + 1) * TT], [bQ, bQm[h % 2][t]]

            def finish(pso, bpso, h=h, t=t):
                tok = slice(t * TT, (t + 1) * TT)
                self.attn_finish(pso, bpso, rec_r, ost_r, self.OB[1][h // 2, (h % 2) * 64:(h % 2) * 64 + 64, tok], self.db("OMOBA", t))

            tiles.append(dict(t=t, h=h, get_q=get_q, start_hook=start_hook, mid_hook=mid_hook, finish=finish,
                              K=Kr.items[h % 2][0], bK=Kr.items[h % 2][1], V=Vr.items[h % 2][0], bV=Vr.items[h % 2][1]))
        self.run_attn(tiles, 0.125, pss_r, pT_r, pso_r)

    def p_ret(self, l):
        self.begin_phase()
        Qr = self.sbr(2, [64, SEQ], BF16)
        Kr = self.sbr(2, [64, SEQ], BF16)
        Vr = self.sbr(2, [128, 32, 128], BF16)
        dec = self.sb([128, 4, 128], F32)
        xi = self.sb([64, 4, 512], F32)
        zeta = self.sb([128, 4], F32)
        bc = Buf()
        self.dma("sp", dec[:], self.c_decayT.rearrange("h j i -> j h i"), w=[bc])
        self.dma("sp", xi[:], self.c_xi.rearrange("h p n -> p h n"), w=[bc])
        self.dma("sp", zeta[:], self.c_zeta, w=[bc])
        qx_r = self.sbr(2, [64, TT], BF16)
        kz_r = self.sbr(4, [128, 64], BF16)
        sc_r = self.sbr(4, [128, 128], BF16)
        KV_r = self.sbr(2, [64, 32, 128], F32)
        Rall_r = self.sbr(2, [64, 32, 128], BF16)
        Rf = self.sb([64, 128], F32)
        bRf = Buf()
        G_r = self.sbr(2, [128, TT], F32)
        of_r = self.sbr(2, [128, TT], F32)
        ob_r = self.sbr(2, [128, 2, TT], BF16)
        m_r = self.sbr(2, [128, TT], F32)
        v_r = self.sbr(2, [128, TT], F32)
        y_r = self.sbr(2, [128, TT], BF16)
        po_r = Ring([(0, self.pbuf[0]), (1, self.pbuf[1])])
        ps_r = Ring([(2, self.pbuf[2]), (3, self.pbuf[3])])
        pk_r = Ring([(4, self.pbuf[4]), (5, self.pbuf[5])])
        HB = {}

        def load_head(h):
            Qb, bQ = Qr.items[h % 2]
            Kb, bK = Kr.items[h % 2]
            Vb, bV = Vr.items[h % 2]
            HB[h] = (Qb, bQ, Kb, bK, Vb, bV)
            self.dma("sp", Qb[:], self.RQ[h], r=self.dball("rq"), w=[bQ])
            self.dma("sp", Kb[:], self.RK[h], r=self.dball("rk"), w=[bK])
            for g4 in range(4):
                self.dma("sp", Vb[:, g4 * 8:(g4 + 1) * 8, :],
                         self.RV[g4 * 1024:(g4 + 1) * 1024, h * 128:(h + 1) * 128].rearrange("(k p) d -> p k d", p=128), r=self.dball("RV"), w=[bV])

        def stage1(h):
            Qb, bQ, Kb, bK, Vb, bV = HB[h]
            KVs, bKV = KV_r.items[h % 2]
            for g in range(8):
                pi_, bpk = pk_r.next()
                pk = self.psF[:, pi_, :]
                for j in range(4):
                    n = g * 4 + j
                    ch = slice(n * 128, (n + 1) * 128)
                    self.I("pe", "transpose", [bK, self.bconst], [self.pbufB], out=self.psB[:, j * 64:(j + 1) * 64], in_=Kb[:, ch],
                           identity=self.identB[0:64, 0:64])
                kzs = []
                for j in range(4):
                    kz, bkz = kz_r.next()
                    self.I("act", "activation", [self.pbufB, bc], [bkz], out=kz[:], in_=self.psB[:, j * 64:(j + 1) * 64], func=AF.Copy,
                           scale=zeta[:, h:h + 1])
                    kzs.append((kz, bkz))
                for j in range(4):
                    n = g * 4 + j
                    kz, bkz = kzs[j]
                    self.I("pe", "matmul", [bkz, bV], [bpk], out=pk[0:64, j * 128:(j + 1) * 128], lhsT=kz[:], rhs=Vb[:, n, :], start=True, stop=True)
                self.copy("dve" if g % 2 else "act", KVs[:, g * 4:(g + 1) * 4, :], pk[0:64, :].rearrange("p (j e) -> p j e", e=128), [bpk], [bKV])

        def stage2_steps(h):
            KVs, bKV = KV_r.items[h % 2]
            Rall, bR = Rall_r.items[h % 2]
            steps = []

            def init():
                self.I("pool", "memset", [], [bRf], ap=Rf[:], constant=0.0)
                self.I("pool", "memset", [], [bR], ap=Rall[:, 0, :], constant=0.0)
            steps.append(init)
            for n in range(31):
                def st(n=n):
                    self.I("dve", "scalar_tensor_tensor", [bRf, bKV], [bRf], out=Rf[:], in0=Rf[:], scalar=float(self.gam[h]), in1=KVs[:, n, :],
                           op0=ALU.mult, op1=ALU.add)
                    self.I("dve", "tensor_copy", [bRf], [bR], out=Rall[:, n + 1, :], in_=Rf[:])
                steps.append(st)
            return steps

        Cm = self.sb([128, 128], BF16)
        self.I("pool", "tensor_scalar", [self.bconst], [bc], out=Cm[:], in0=self.identB[:], scalar1=-1.0 / 128, scalar2=None, op0=ALU.add)
        sd_r = self.sbr(2, [128, TT], F32)

        def stage3(h, inter):
            Qb, bQ, Kb, bK, Vb, bV = HB[h]
            Rall, bR = Rall_r.items[h % 2]
            for t in range(NT):
                tok = slice(t * TT, (t + 1) * TT)
                G, bG = G_r.next()
                self.dma("sp", G[:], self.RG[h, :, tok], r=[self.db("RG", t)], w=[bG])
                qx, bqx = qx_r.next()
                self.I("pool", "tensor_tensor", [bQ, bc], [bqx], out=qx[:], in0=Qb[:, tok], in1=xi[:, h, :], op=ALU.mult)
                pi_, bpo = po_r.next()
                po = self.psF[:, pi_, :]
                pi_, bps = ps_r.next()
                ps = self.psF[:, pi_, :]
                for s_ in range(4):
                    n = 4 * t + s_
                    ch = slice(n * 128, (n + 1) * 128)
                    self.I("pe", "matmul", [bK, bQ], [bps], out=ps[:, s_ * 128:(s_ + 1) * 128], lhsT=Kb[:, ch], rhs=Qb[:, ch], start=True, stop=True)
                scs = []
                for s_ in range(4):
                    sc, bsc = sc_r.next()
                    self.I("dve", "tensor_tensor", [bps, bc], [bsc], out=sc[:], in0=ps[:, s_ * 128:(s_ + 1) * 128], in1=dec[:, h, :], op=ALU.mult)
                    scs.append((sc, bsc))
                for s_ in range(4):
                    n = 4 * t + s_
                    sc, bsc = scs[s_]
                    self.I("pe", "matmul", [bV, bsc], [bpo], out=po[:, s_ * 128:(s_ + 1) * 128], lhsT=Vb[:, n, :], rhs=sc[:], start=True, stop=False)
                    self.I("pe", "matmul", [bR, bqx], [bpo], out=po[:, s_ * 128:(s_ + 1) * 128], lhsT=Rall[:, n, :], rhs=qx[:, s_ * 128:(s_ + 1) * 128],
                           start=False, stop=True)
                for _ in range(5):
                    if inter:
                        inter.pop(0)()
                ob, bob = ob_r.next()
                self.I("act", "copy", [bpo], [bob], out=ob[:, 0, :], in_=po)
                pc, bpc = self.psF[:, 6, :], self.pbuf[6]
                self.I("pe", "matmul", [bob, bc], [bpc], out=pc, lhsT=Cm[:], rhs=ob[:, 0, :], start=True, stop=True)
                self.I("act", "activation", [bpc], [bob], out=ob[:, 1, :], in_=pc, func=AF.Square)
                pi_, bpq = ps_r.next()
                pq = self.psF[:, pi_, :]
                self.I("pe", "matmul", [bob, self.bconst], [bpq], out=pq, lhsT=self.onesB[:], rhs=ob[:, 1, :], start=True, stop=True)
                sd, bsd = sd_r.next()
                self.I("act", "activation", [bpq], [bsd], out=sd[:], in_=pq, func=AF.Sqrt, scale=1.0 / 128, bias=float(GN_EPS))
                self.I("dve", "reciprocal", [bsd], [bsd], out=sd[:], in_=sd[:])
                of, bof = of_r.next()
                self.I("dve", "tensor_tensor", [bpc, bsd], [bof], out=of[:], in0=pc, in1=sd[:], op=ALU.mult)
                self.I("dve", "tensor_scalar", [bof, self.bconst], [bof], out=of[:], in0=of[:], scalar1=self.vcol(l, 19 + h), scalar2=self.vcol(l, 23 + h),
                       op0=ALU.mult, op1=ALU.add)
                y, by = y_r.next()
                self.I("pool", "tensor_tensor", [bof, bG], [by], out=y[:], in0=of[:], in1=G[:], op=ALU.mult)
                self.dma("sp", self.OB[2][h, :, tok], y[:], r=[by], w=[self.db("ORET", t)])
            while inter:
                inter.pop(0)()

        load_head(0)
        load_head(1)
        stage1(0)
        for st in stage2_steps(0):
            st()
        for h in range(4):
            inter = []
            if h + 1 < 4:
                stage1(h + 1)
                inter = stage2_steps(h + 1)
            stage3(h, inter)
            if h + 2 < 4:
                load_head(h + 2)

    def p_merge(self, l):
        self.begin_phase()
        Wg = self.sb([128, 8, 3072], BF16)
        Wb = self.sb([128, 3, 4, DM], BF16)
        Wo = self.sb([128, 8, DM], BF16)
        bw = Buf()
        bwo = [Buf() for _ in range(8)]
        win = self.w_in[l].rearrange("(c p) n -> p c n", p=128)
        for oc in range(8):
            for j in range(3):
                self.dma("pool", Wg[:, :, j * 1024 + oc * 128:j * 1024 + (oc + 1) * 128],
                         win[:, :, 3488 + j * 1024 + oc * 128:3488 + j * 1024 + (oc + 1) * 128], w=[bwo[oc]])
                self.dma("pool", Wb[:, j, :, oc * 128:(oc + 1) * 128], self.w_b[j][l].rearrange("(c p) n -> p c n", p=128)[:, :, oc * 128:(oc + 1) * 128],
                         w=[bwo[oc]])
        self.dma("pool", Wo[:], self.w_out[l].rearrange("(c p) n -> p c n", p=128), w=[bw])
        hT_r = self.sbr(2, [128, 8, TT], BF16)
        o_r = self.sbr(2, [128, 3, 4, TT], BF16)
        xT_r = self.sbr(2, [128, 8, TT], F32)
        sig_r = self.sbr(3, [128, TT], F32)
        acc_r = self.sbr(2, [128, TT], F32)
        mg_r = self.sbr(2, [128, 8, TT], BF16)
        names = ("OMLA", "OMOBA", "ORET")
        def load3(t):
            tok = slice(t * TT, (t + 1) * TT)
            hT, bh = hT_r.next()
            o, bo = o_r.next()
            xT, bx = xT_r.next()
            self.dma("sp", hT[:], self.HT.rearrange("c p n -> p c n")[:, :, tok], r=[self.db("HT", t)], w=[bh])
            for j in range(3):
                self.dma("sp", o[:, j, :, :], self.OB[j].rearrange("c p n -> p c n")[:, :, tok], r=[self.db(names[j], t)], w=[bo])
            self.dma("sp", xT[:], self.XT.rearrange("c p n -> p c n")[:, :, tok], r=[self.db("XT", t)], w=[bx])
            return hT, bh, o, bo, xT, bx

        nxt3 = load3(0)
        for t in range(NT):
            tok = slice(t * TT, (t + 1) * TT)
            hT, bh, o, bo, xT, bx = nxt3
            if t + 1 < NT:
                nxt3 = load3(t + 1)
            mg, bmg = mg_r.next()
            for oc in range(8):
                acc, bacc = acc_r.next()
                for j in range(3):
                    pg, bpg = self.bank()
                    for c in range(8):
                        self.I("pe", "matmul", [bh, bwo[oc]], [bpg], out=pg, lhsT=Wg[:, c, j * 1024 + oc * 128:j * 1024 + (oc + 1) * 128], rhs=hT[:, c, :],
                               start=(c == 0), stop=(c == 7))
                    pb, bpb = self.bank()
                    for c in range(4):
                        self.I("pe", "matmul", [bo, bwo[oc]], [bpb], out=pb, lhsT=Wb[:, j, c, oc * 128:(oc + 1) * 128], rhs=o[:, j, c, :],
                               start=(c == 0), stop=(c == 3))
                    sg, bsg = sig_r.next()
                    self.I("act", "activation", [bpg], [bsg], out=sg[:], in_=pg, func=AF.Sigmoid)
                    if j == 0:
                        self.I("dve", "tensor_tensor", [bsg, bpb], [bacc], out=acc[:], in0=pb, in1=sg[:], op=ALU.mult)
                    else:
                        self.I("dve", "tensor_tensor", [bsg, bpb], [bsg], out=sg[:], in0=pb, in1=sg[:], op=ALU.mult)
                        if j == 1:
                            self.I("pool", "tensor_tensor", [bsg, bacc], [bacc], out=acc[:], in0=acc[:], in1=sg[:], op=ALU.add)
                        else:
                            self.I("pool", "tensor_tensor", [bsg, bacc], [bmg], out=mg[:, oc, :], in0=acc[:], in1=sg[:], op=ALU.add)
            for oc in range(8):
                po, bpo = self.bank()
                for c in range(8):
                    self.I("pe", "matmul", [bmg, bw], [bpo], out=po, lhsT=Wo[:, c, oc * 128:(oc + 1) * 128], rhs=mg[:, c, :], start=(c == 0), stop=(c == 7))
                self.I("dve", "tensor_tensor", [bpo, bx], [bx], out=xT[:, oc, :], in0=po, in1=xT[:, oc, :], op=ALU.add)
            self.dma("sp", self.XT.rearrange("c p n -> p c n")[:, :, tok], xT[:], r=[bx], w=[self.db("XT", t)])

    def p_mlp_up(self, l):
        self.begin_phase()
        Wu = self.sb([128, 8, 4096], BF16)
        bw = Buf()
        wu = self.w_up[l].rearrange("(c p) n -> p c n", p=128)
        bwu = [Buf() for _ in range(8)]
        for j in range(8):
            self.dma("pool", Wu[:, :, j * 512:(j + 1) * 512], wu[:, :, j * 512:(j + 1) * 512], w=[bwu[j]])
        xT_r = self.sbr(2, [128, 8, TT], F32)
        hT_r = self.sbr(2, [128, 8, TT], BF16)
        sq = self.sb([128, 8, TT], BF16)
        bsq = Buf()
        rs_r = self.sbr(2, [128, TT], F32)
        r_r = self.sbr(3, [128, TT], F32)
        a_r = self.sbr(2, [128, 8, TT], BF16)
        def load4(t):
            tok = slice(t * TT, (t + 1) * TT)
            xT, bx = xT_r.next()
            hT, bh = hT_r.next()
            self.dma("sp", xT[:], self.XT.rearrange("c p n -> p c n")[:, :, tok], r=[self.db("XT", t)], w=[bx])
            return xT, bx, hT, bh

        def norm4a(st_):
            xT, bx, hT, bh = st_
            self.rmsnorm_a([xT[:, c, :] for c in range(8)], [bx], sq, bsq)

        def norm4b(st_):
            xT, bx, hT, bh = st_
            rs, brs = rs_r.next()
            self.rmsnorm_b([xT[:, c, :] for c in range(8)], [self.vcol(l, 8 + c) for c in range(8)], DM, EPS,
                           [hT[:, c, :] for c in range(8)], [bx], [bh], sq, bsq, rs, brs)

        cur4 = load4(0)
        norm4a(cur4)
        norm4b(cur4)
        for t in range(NT):
            tok = slice(t * TT, (t + 1) * TT)
            xT, bx, hT, bh = cur4
            nxt4 = None
            for fg in range(4):
                if t + 1 < NT:
                    if fg == 0:
                        nxt4 = load4(t + 1)
                    elif fg == 1:
                        norm4a(nxt4)
                    elif fg == 3:
                        norm4b(nxt4)
                a, ba = a_r.next()
                for fj in range(8):
                    f = fg * 8 + fj
                    ps, bps = self.bank()
                    for c in range(8):
                        self.I("pe", "matmul", [bh, bwu[f // 4]], [bps], out=ps, lhsT=Wu[:, c, f * 128:(f + 1) * 128], rhs=hT[:, c, :], start=(c == 0), stop=(c == 7))
                    r, br = r_r.next()
                    self.I("act", "activation", [bps], [br], out=r[:], in_=ps, func=AF.Relu)
                    self.I("pool" if fj % 2 else "dve", "tensor_tensor", [br], [ba], out=a[:, fj, :], in0=r[:], in1=r[:], op=ALU.mult)
                self.dma("sp", self.ACTS[fg * 8:(fg + 1) * 8].rearrange("c p n -> p c n")[:, :, tok], a[:], r=[ba], w=[self.db("ACTS%d" % fg, t)])
            cur4 = nxt4

    def p_mlp_down(self, l):
        self.begin_phase()
        Wd = self.sb([128, 32, DM], BF16)
        bw = Buf()
        wd = self.w_down[l].rearrange("(c p) n -> p c n", p=128)
        bwd = [Buf() for _ in range(8)]
        for oc in range(8):
            for j in range(2):
                self.dma("pool", Wd[:, j * 16:(j + 1) * 16, oc * 128:(oc + 1) * 128], wd[:, j * 16:(j + 1) * 16, oc * 128:(oc + 1) * 128], w=[bwd[oc]])
        xT_r = self.sbr(2, [128, 8, TT], F32)
        a_r = self.sbr(2, [128, 32, TT], BF16)
        def load5(t):
            tok = slice(t * TT, (t + 1) * TT)
            xT, bx = xT_r.next()
            a, ba = a_r.next()
            self.dma("sp", xT[:], self.XT.rearrange("c p n -> p c n")[:, :, tok], r=[self.db("XT", t)], w=[bx])
            for fg in range(4):
                self.dma("sp", a[:, fg * 8:(fg + 1) * 8, :], self.ACTS[fg * 8:(fg + 1) * 8].rearrange("c p n -> p c n")[:, :, tok],
                         r=[self.db("ACTS%d" % fg, t)], w=[ba])
            return xT, bx, a, ba

        nxt5 = load5(0)
        for t in range(NT):
            tok = slice(t * TT, (t + 1) * TT)
            xT, bx, a, ba = nxt5
            if t + 1 < NT:
                nxt5 = load5(t + 1)
            for oc in range(8):
                po, bpo = self.bank()
                for f in range(32):
                    self.I("pe", "matmul", [ba, bwd[oc]], [bpo], out=po, lhsT=Wd[:, f, oc * 128:(oc + 1) * 128], rhs=a[:, f, :], start=(f == 0), stop=(f == 31))
                self.I("dve", "tensor_tensor", [bpo, bx], [bx], out=xT[:, oc, :], in0=po, in1=xT[:, oc, :], op=ALU.add)
            self.dma("sp", self.XT.rearrange("c p n -> p c n")[:, :, tok], xT[:], r=[bx], w=[self.db("XT", t)])

    def p_final(self):
        self.begin_phase()
        xT_r = self.sbr(2, [128, 8, TT], F32)
        yT_r = self.sbr(2, [128, 8, TT], F32)
        sq = self.sb([128, 8, TT], BF16)
        bsq = Buf()
        rs_r = self.sbr(2, [128, TT], F32)
        o_r = self.sbr(2, [128, DM], F32)
        def load6(t):
            tok = slice(t * TT, (t + 1) * TT)
            xT, bx = xT_r.next()
            self.dma("sp", xT[:], self.XT.rearrange("c p n -> p c n")[:, :, tok], r=[self.db("XT", t)], w=[bx])
            return xT, bx

        nxt6 = load6(0)
        for t in range(NT):
            tok = slice(t * TT, (t + 1) * TT)
            xT, bx = nxt6
            if t + 1 < NT:
                nxt6 = load6(t + 1)
            yT, by = yT_r.next()
            rs, brs = rs_r.next()
            fb = DEPTH * VSTR
            self.rmsnorm([xT[:, c, :] for c in range(8)], [self.vecs[:, fb + c:fb + c + 1] for c in range(8)], DM, EPS,
                         [yT[:, c, :] for c in range(8)], [bx], [by], sq, bsq, rs, brs)
            for s in range(4):
                o, bo = o_r.next()
                for g in range(2):
                    ps, bps = self.bank()
                    for j in range(4):
                        c = g * 4 + j
                        self.I("pe", "transpose", [by, self.bconst], [bps], out=ps[:, j * 128:(j + 1) * 128], in_=yT[:, c, s * 128:(s + 1) * 128],
                               identity=self.identF[:])
                    self.copy(self.evac_eng(), o[:, g * 512:(g + 1) * 512], ps, [bps], [bo])
                r0 = t * TT + s * 128
                self.dma("sp", self.out[r0:r0 + 128, :], o[:], r=[bo], w=[self.db("out", t)])

    def build(self):
        ph = self.phases
        want = lambda name: ph is None or name in ph
        if want("t_in"):
            self.p_transpose_in()
        for l in range(self.n_layers):
            if want(f"inproj{l}"):
                self.p_inproj(l)
            if want(f"mla{l}"):
                self.p_mla(l)
            if want(f"moba{l}"):
                self.p_moba(l)
            if want(f"ret{l}"):
                self.p_ret(l)
            if want(f"merge{l}"):
                self.p_merge(l)
            if want(f"mlpup{l}"):
                self.p_mlp_up(l)
            if want(f"mlpdown{l}"):
                self.p_mlp_down(l)
        if want("final"):
            self.p_final()
        self.S.barrier()
        stats = self.S.emit()
        return self.nc, stats


_CACHE = {}


def get_program():
    if "nc" not in _CACHE:
        kb = KB()
        _CACHE["nc"], _CACHE["stats"] = kb.build()
    return _CACHE["nc"]


def make_in_map(inputs, b, consts, vecs):
    m = {"x": np.ascontiguousarray(inputs["x"][b])}
    for k in ("w_in", "mla_w_q_up", "mla_w_kv_up", "w_branch_mla", "w_branch_moba", "w_branch_ret", "w_out", "w_mlp_up", "w_mlp_down"):
        m[k] = np.ascontiguousarray(inputs[k], dtype=np.float32)
    m["vecs"] = vecs
    m.update(consts)
    return m


def kernel(**inputs):
    inputs = {k: np.asarray(v) for k, v in inputs.items()}
    consts, _ = make_consts()
    vecs = pack_vecs(inputs)
    nc = get_program()
    in_maps = [make_in_map(inputs, b, consts, vecs) for b in range(NCORES)]
    res = run_bass_kernel_spmd(nc, in_maps, core_ids=list(range(NCORES)))
    out = np.stack([np.asarray(res.results[b]["out"], dtype=np.float32) for b in range(NCORES)], axis=0)
    return out
```
